# Optimizing a Trainium2 kernel written in Bass

```python
import math
import numpy as np
import jax
import jax.numpy as jnp
from jax import lax

D_MODEL = 2048
BATCH = 8
SEQ = 2048
DEPTH = 2
DEC_BATCH = 2
DEC_SEQ = 4096
PAST_LEN = 128

BRANCH_W = D_MODEL // 2
N_BRANCH = 4
ROPE_THETA = 500000.0
EPS = 1e-6
Q_BLOCK = 128

MLA_HEADS = 8
MLA_NOPE = 128
MLA_ROPE = 64
MLA_V = BRANCH_W // MLA_HEADS
MLA_Q_LORA = 512
MLA_KV_LORA = 256

DIFF_HEADS = 4
DIFF_HD = BRANCH_W // (2 * DIFF_HEADS)
DIFF_ROT = DIFF_HD // 4

SSD_P = 64
SSD_HEADS = BRANCH_W // SSD_P
SSD_N = 128
SSD_G = 2
SSD_CONV = 4
SSD_CHUNK = 128
SSD_CONV_DIM = BRANCH_W + 2 * SSD_G * SSD_N

POOL_WINDOWS = (2, 4, 8, 16)
POOL_GROUP = BRANCH_W // 4

IN_SIZES = (MLA_Q_LORA, MLA_KV_LORA, MLA_ROPE, BRANCH_W,
            2 * BRANCH_W // 2 * 1, BRANCH_W, BRANCH_W, BRANCH_W,
            BRANCH_W, SSD_CONV_DIM, 2 * SSD_HEADS,
            BRANCH_W, BRANCH_W,
            N_BRANCH * D_MODEL)
IN_DIM = sum(IN_SIZES)
IN_SPLITS = tuple(int(v) for v in np.cumsum(IN_SIZES)[:-1])

kernel_name = 'hybrid_bidir_encoder'


def _rmsnorm(x, w):
    xf = x.astype(jnp.float32)
    y = xf * lax.rsqrt(jnp.mean(xf * xf, axis=-1, keepdims=True) + EPS)
    return (y * w.astype(jnp.float32)).astype(x.dtype)


def _rope(x, rot_dim):
    s = x.shape[1]
    half = rot_dim // 2
    inv_freq = jnp.power(ROPE_THETA, -jnp.arange(half, dtype=jnp.float32) * 2.0 / rot_dim)
    ang = jnp.arange(s, dtype=jnp.float32)[:, None] * inv_freq[None, :]
    cos = jnp.cos(ang)[None, :, None, :]
    sin = jnp.sin(ang)[None, :, None, :]
    xf = x.astype(jnp.float32)
    x1 = xf[..., :half]
    x2 = xf[..., half:rot_dim]
    out = jnp.concatenate([x1 * cos - x2 * sin, x1 * sin + x2 * cos, xf[..., rot_dim:]], axis=-1)
    return out.astype(x.dtype)


def _to_blocks(t):
    b, s, h, d = t.shape
    return t.reshape(b, s // Q_BLOCK, Q_BLOCK, h, d).transpose(1, 0, 3, 2, 4)


def _from_blocks(o):
    nb, b, h, qb, d = o.shape
    return o.transpose(1, 0, 3, 2, 4).reshape(b, nb * qb, h, d)


def _mla(cq, ckv, kr, gate, q_norm, w_uq, kv_norm, w_ukv):
    b, s, _ = cq.shape
    q = (_rmsnorm(cq, q_norm) @ w_uq).reshape(b, s, MLA_HEADS, MLA_NOPE + MLA_ROPE)
    q_nope = q[..., :MLA_NOPE]
    q_pe = _rope(q[..., MLA_NOPE:], MLA_ROPE)
    kv = (_rmsnorm(ckv, kv_norm) @ w_ukv).reshape(b, s, MLA_HEADS, MLA_NOPE + MLA_V)
    k_nope = kv[..., :MLA_NOPE]
    v = kv[..., MLA_NOPE:]
    k_pe = _rope(kr[:, :, None, :], MLA_ROPE)[:, :, 0]
    scale = (MLA_NOPE + MLA_ROPE) ** -0.5

    def attend(blk):
        qn, qp = blk
        sc = (jnp.einsum('bhqd,bkhd->bhqk', qn, k_nope)
              + jnp.einsum('bhqd,bkd->bhqk', qp, k_pe))
        p = jax.nn.softmax(sc.astype(jnp.float32) * scale, axis=-1)
        return jnp.einsum('bhqk,bkhd->bhqd', p.astype(v.dtype), v)

    o = _from_blocks(lax.map(attend, (_to_blocks(q_nope), _to_blocks(q_pe))))
    return o.reshape(b, s, BRANCH_W) * jax.nn.silu(gate)


def _diff_attn(dq, dk, dv, gate, lam_params, subln, lambda_init):
    b, s, _ = dq.shape
    q = _rope(dq.reshape(b, s, 2 * DIFF_HEADS, DIFF_HD), DIFF_ROT).reshape(b, s, DIFF_HEADS, 2, DIFF_HD)
    k = _rope(dk.reshape(b, s, 2 * DIFF_HEADS, DIFF_HD), DIFF_ROT).reshape(b, s, DIFF_HEADS, 2, DIFF_HD)
    v = dv.reshape(b, s, DIFF_HEADS, 2 * DIFF_HD)
    k1 = k[..., 0, :]
    k2 = k[..., 1, :]
    lp = lam_params.astype(jnp.float32)
    lam = jnp.exp(jnp.sum(lp[0] * lp[1])) - jnp.exp(jnp.sum(lp[2] * lp[3])) + lambda_init
    scale = DIFF_HD ** -0.5

    def attend(blk):
        q1, q2 = blk
        p1 = jax.nn.softmax(jnp.einsum('bhqd,bkhd->bhqk', q1, k1).astype(jnp.float32) * scale, axis=-1)
        p2 = jax.nn.softmax(jnp.einsum('bhqd,bkhd->bhqk', q2, k2).astype(jnp.float32) * scale, axis=-1)
        p = p1 - lam * p2
        return jnp.einsum('bhqk,bkhd->bhqd', p.astype(v.dtype), v)

    o = _from_blocks(lax.map(attend, (_to_blocks(q[..., 0, :]), _to_blocks(q[..., 1, :]))))
    o = _rmsnorm(o, subln) * (1.0 - lambda_init)
    return o.reshape(b, s, BRANCH_W) * jax.nn.silu(gate)


def _ssd_scan(x, dt, a, b_mat, c_mat):
    bsz, s, h, p = x.shape
    g, n = b_mat.shape[2], b_mat.shape[3]
    r = h // g
    nc = s // SSD_CHUNK
    xd = (x.astype(jnp.float32) * dt[..., None]).reshape(bsz, nc, SSD_CHUNK, g, r, p)
    da = (dt * a).reshape(bsz, nc, SSD_CHUNK, g, r).transpose(0, 3, 4, 1, 2)
    bc = b_mat.astype(jnp.float32).reshape(bsz, nc, SSD_CHUNK, g, n)
    cc = c_mat.astype(jnp.float32).reshape(bsz, nc, SSD_CHUNK, g, n)
    a_cs = jnp.cumsum(da, axis=-1)
    mask = jnp.tril(jnp.ones((SSD_CHUNK, SSD_CHUNK), dtype=bool))
    seg = a_cs[..., :, None] - a_cs[..., None, :]
    l_mat = jnp.exp(jnp.where(mask, seg, -jnp.inf))
    cb = jnp.einsum('bclgn,bcsgn->bcgls', cc, bc)
    y_diag = jnp.einsum('bcgls,bgrcls,bcsgrp->bclgrp', cb, l_mat, xd)
    decay_states = jnp.exp(a_cs[..., -1:] - a_cs)
    states = jnp.einsum('bclgn,bgrcl,bclgrp->bcgrpn', bc, decay_states, xd)
    chunk_decay = jnp.exp(a_cs[..., -1])

    def step(hst, inp):
        st, dec = inp
        return hst * dec[..., None, None] + st, hst

    h0 = jnp.zeros((bsz, g, r, p, n), jnp.float32)
    _, prev = lax.scan(step, h0, (jnp.moveaxis(states, 1, 0), jnp.moveaxis(chunk_decay, 3, 0)))
    prev = jnp.moveaxis(prev, 0, 1)
    y_off = jnp.einsum('bclgn,bcgrpn,bgrcl->bclgrp', cc, prev, jnp.exp(a_cs))
    return (y_diag + y_off).reshape(bsz, s, h, p)


def _ssd(z, xbc, dt_raw, conv_w, conv_b, dt_bias, a_log, d_skip, norm_w):
    b, s, _ = xbc.shape
    pad_l = SSD_CONV // 2
    pad_r = SSD_CONV - 1 - pad_l
    xbc = lax.conv_general_dilated(xbc, conv_w[:, None, :], window_strides=(1,), padding=[(pad_l, pad_r)],
                                   dimension_numbers=('NWC', 'WIO', 'NWC'),
                                   feature_group_count=SSD_CONV_DIM) + conv_b
    xbc = jax.nn.silu(xbc)
    xs = xbc[..., :BRANCH_W].reshape(b, s, SSD_HEADS, SSD_P)
    bm = xbc[..., BRANCH_W:BRANCH_W + SSD_G * SSD_N].reshape(b, s, SSD_G, SSD_N)
    cm = xbc[..., BRANCH_W + SSD_G * SSD_N:].reshape(b, s, SSD_G, SSD_N)
    dt = jax.nn.softplus(dt_raw.astype(jnp.float32).reshape(b, s, 2, SSD_HEADS) + dt_bias.astype(jnp.float32))
    a = -jnp.exp(a_log.astype(jnp.float32))
    y_f = _ssd_scan(xs, dt[:, :, 0], a[0], bm, cm)
    y_b = jnp.flip(_ssd_scan(jnp.flip(xs, 1), jnp.flip(dt[:, :, 1], 1), a[1],
                             jnp.flip(bm, 1), jnp.flip(cm, 1)), 1)
    y = y_f + y_b + xs.astype(jnp.float32) * d_skip.astype(jnp.float32)[:, None]
    y = y.reshape(b, s, BRANCH_W) * jax.nn.silu(z.astype(jnp.float32))
    y = _rmsnorm(y.reshape(b, s, SSD_G, BRANCH_W // SSD_G), norm_w.reshape(SSD_G, BRANCH_W // SSD_G))
    return y.reshape(b, s, BRANCH_W).astype(z.dtype)


def _pool(u, gate, pool_w, pool_scale):
    b, s, _ = u.shape
    uf = u.astype(jnp.float32).reshape(b, s, 4, POOL_GROUP)
    cs = jnp.concatenate([jnp.zeros((b, 1, 4, POOL_GROUP), jnp.float32), jnp.cumsum(uf, axis=1)], axis=1)
    t = np.arange(s)
    means = []
    for i, w in enumerate(POOL_WINDOWS):
        lo = w // 2
        hi = w - 1 - lo
        start = np.clip(t - lo, 0, s)
        end = np.clip(t + hi + 1, 0, s)
        cnt = jnp.asarray((end - start).astype(np.float32))
        means.append((cs[:, end, i] - cs[:, start, i]) / cnt[None, :, None])
    pooled = jnp.stack(means, axis=2) - uf
    mixed = jnp.einsum('bsgc,gcd->bsgd', pooled.astype(u.dtype), pool_w).reshape(b, s, BRANCH_W)
    return mixed * pool_scale * jax.nn.silu(gate)


def _trunk(x, norm_w, w_in, mla_q_norm, mla_w_uq, mla_kv_norm, mla_w_ukv, diff_lambda, diff_subln,
           ssd_conv_w, ssd_conv_b, ssd_dt_bias, ssd_a_log, ssd_d, ssd_norm, pool_w, pool_scale,
           w_branch, w_out, final_norm):
    b, s, _ = x.shape
    for l in range(DEPTH):
        h = _rmsnorm(x, norm_w[l])
        (cq, ckv, kr, g_mla, dq, dk, dv, g_diff, z, xbc, dt_raw, u, g_pool, mg) = jnp.split(
            h @ w_in[l], list(IN_SPLITS), axis=-1)
        lambda_init = 0.8 - 0.6 * math.exp(-0.3 * l)
        br_mla = _mla(cq, ckv, kr, g_mla, mla_q_norm[l], mla_w_uq[l], mla_kv_norm[l], mla_w_ukv[l])
        br_diff = _diff_attn(dq, dk, dv, g_diff, diff_lambda[l], diff_subln[l], lambda_init)
        br_ssd = _ssd(z, xbc, dt_raw, ssd_conv_w[l], ssd_conv_b[l], ssd_dt_bias[l], ssd_a_log[l],
                      ssd_d[l], ssd_norm[l])
        br_pool = _pool(u, g_pool, pool_w[l], pool_scale[l])
        gates = jax.nn.sigmoid(mg.reshape(b, s, N_BRANCH, D_MODEL))
        merged = gates[:, :, 0] * (br_mla @ w_branch[l, 0])
        merged = merged + gates[:, :, 1] * (br_diff @ w_branch[l, 1])
        merged = merged + gates[:, :, 2] * (br_ssd @ w_branch[l, 2])
        merged = merged + gates[:, :, 3] * (br_pool @ w_branch[l, 3])
        x = x + merged @ w_out[l]
    return _rmsnorm(x, final_norm)


def setup_inputs(seed: int = 0) -> dict:
    key = jax.random.key(seed)
    ks = jax.random.split(key, 24)
    f32 = jnp.float32
    nrm = lambda k, shape, sc: jax.random.normal(k, shape, f32) * sc
    dt0 = jnp.exp(jax.random.uniform(ks[12], (DEPTH, 2, SSD_HEADS), f32, math.log(1e-3), math.log(1e-1)))
    return {
        'x_prompt': jax.random.normal(ks[0], (BATCH, SEQ, D_MODEL), f32),
        'x_sample': jax.random.normal(ks[1], (DEC_BATCH, DEC_SEQ, D_MODEL), f32),
        'norm_w': 1.0 + nrm(ks[2], (DEPTH, D_MODEL), 0.02),
        'w_in': nrm(ks[3], (DEPTH, D_MODEL, IN_DIM), D_MODEL ** -0.5),
        'mla_q_norm': 1.0 + nrm(ks[4], (DEPTH, MLA_Q_LORA), 0.02),
        'mla_w_uq': nrm(ks[5], (DEPTH, MLA_Q_LORA, MLA_HEADS * (MLA_NOPE + MLA_ROPE)), MLA_Q_LORA ** -0.5),
        'mla_kv_norm': 1.0 + nrm(ks[6], (DEPTH, MLA_KV_LORA), 0.02),
        'mla_w_ukv': nrm(ks[7], (DEPTH, MLA_KV_LORA, MLA_HEADS * (MLA_NOPE + MLA_V)), MLA_KV_LORA ** -0.5),
        'diff_lambda': nrm(ks[8], (DEPTH, 4, DIFF_HD), 0.1),
        'diff_subln': 1.0 + nrm(ks[9], (DEPTH, 2 * DIFF_HD), 0.02),
        'ssd_conv_w': nrm(ks[10], (DEPTH, SSD_CONV, SSD_CONV_DIM), SSD_CONV ** -0.5),
        'ssd_conv_b': nrm(ks[11], (DEPTH, SSD_CONV_DIM), 0.01),
        'ssd_dt_bias': dt0 + jnp.log(-jnp.expm1(-dt0)),
        'ssd_a_log': jnp.log(jax.random.uniform(ks[13], (DEPTH, 2, SSD_HEADS), f32, 1.0, 16.0)),
        'ssd_d': 1.0 + nrm(ks[14], (DEPTH, SSD_HEADS), 0.02),
        'ssd_norm': 1.0 + nrm(ks[15], (DEPTH, BRANCH_W), 0.02),
        'pool_w': nrm(ks[16], (DEPTH, 4, POOL_GROUP, POOL_GROUP), POOL_GROUP ** -0.5),
        'pool_scale': 1.0 + nrm(ks[17], (DEPTH, BRANCH_W), 0.02),
        'w_branch': nrm(ks[18], (DEPTH, N_BRANCH, BRANCH_W, D_MODEL), BRANCH_W ** -0.5),
        'w_out': nrm(ks[19], (DEPTH, D_MODEL, D_MODEL), D_MODEL ** -0.5),
        'final_norm': 1.0 + nrm(ks[20], (D_MODEL,), 0.02),
    }


def reference(x_prompt, x_sample, norm_w, w_in, mla_q_norm, mla_w_uq, mla_kv_norm, mla_w_ukv,
              diff_lambda, diff_subln, ssd_conv_w, ssd_conv_b, ssd_dt_bias, ssd_a_log, ssd_d, ssd_norm,
              pool_w, pool_scale, w_branch, w_out, final_norm):
    weights = (norm_w, w_in, mla_q_norm, mla_w_uq, mla_kv_norm, mla_w_ukv, diff_lambda, diff_subln,
               ssd_conv_w, ssd_conv_b, ssd_dt_bias, ssd_a_log, ssd_d, ssd_norm, pool_w, pool_scale,
               w_branch, w_out, final_norm)
    y_prompt = _trunk(x_prompt, *weights)
    y_sample = _trunk(x_sample, *weights)
    return (y_prompt, y_sample)
```

```python
import contextlib
import math
import numpy as np
import concourse.bass as bass
import concourse.mybir as mybir
from concourse.bass_utils import run_bass_kernel_spmd

F32 = mybir.dt.float32
BF16 = mybir.dt.bfloat16
AF = mybir.ActivationFunctionType
ALU = mybir.AluOpType
AX = mybir.AxisListType

D = 2048
BW = 1024
IN_DIM = 18784
DEPTH = 2
EPS = 1e-6
SEM_CAP = 32000
NEG = -30000.0


class Buf:
    __slots__ = ("name", "last_w", "readers", "const")

    def __init__(self, name="", const=False):
        self.name = name
        self.last_w = None
        self.readers = []
        self.const = const


class TL:
    __slots__ = ("ap", "buf")

    def __init__(self, ap, buf):
        self.ap = ap
        self.buf = buf

    def __getitem__(self, k):
        return TL(self.ap[k], self.buf)

    def v(self, fn):
        return TL(fn(self.ap), self.buf)


class Op:
    __slots__ = ("eng", "fn", "deps", "signal", "sigval", "is_dma", "dsem", "dtarget")

    def __init__(self, eng, fn, is_dma=False):
        self.eng = eng
        self.fn = fn
        self.deps = []
        self.signal = False
        self.sigval = None
        self.is_dma = is_dma
        self.dsem = None
        self.dtarget = None


class DmaSem:
    def __init__(self):
        self.count = 0


class Prog:
    ENGS = ("pe", "act", "dve", "pool", "sp")

    def __init__(self, nc):
        self.nc = nc
        self.eng_obj = {"pe": nc.tensor, "act": nc.scalar, "dve": nc.vector,
                        "pool": nc.gpsimd, "sp": nc.sync}
        self.ops = {e: [] for e in self.ENGS}
        self.dma_last = {}
        self.dsem_pool = []
        self.dsem_i = 0

    def dsem(self):
        if self.dsem_i >= len(self.dsem_pool):
            self.dsem_pool.append(DmaSem())
        s = self.dsem_pool[self.dsem_i]
        self.dsem_i += 1
        return s

    def _track(self, o, reads, writes, nowaw=False):
        seen = set()
        for b in reads:
            w = b.last_w
            if w is not None and id(w) not in seen:
                o.deps.append((w, "raw", w.dsem.count if w.is_dma else 0))
                seen.add(id(w))
        for b in writes:
            w = b.last_w
            if w is not None and id(w) not in seen:
                if not (nowaw and w.is_dma and w.dsem is o.dsem):
                    o.deps.append((w, "waw", w.dsem.count if w.is_dma else 0))
                    seen.add(id(w))
            for r in b.readers:
                if id(r) not in seen:
                    o.deps.append((r, "war", r.dsem.count if r.is_dma else 0))
                    seen.add(id(r))
        for b in reads:
            if not b.const:
                b.readers.append(o)
        for b in writes:
            b.last_w = o
            b.readers = []
        self.ops[o.eng].append(o)

    def op(self, eng, fn, reads=(), writes=()):
        o = Op(eng, fn)
        self._track(o, [t.buf for t in reads], [t.buf for t in writes])
        return o

    def dma(self, eng, out, in_, dsem, reads=(), writes=()):
        nc = self.nc
        eo = self.eng_obj[eng]
        oa = out.ap if isinstance(out, TL) else out
        ia = in_.ap if isinstance(in_, TL) else in_
        o = Op(eng, lambda: eo.dma_start(out=oa, in_=ia, allow_slow_non_contiguous=True), is_dma=True)
        o.dsem = dsem
        o.dtarget = dsem.count + 1
        rd = [t.buf for t in reads] + ([in_.buf] if isinstance(in_, TL) else [])
        wr = [t.buf for t in writes] + ([out.buf] if isinstance(out, TL) else [])
        self._track(o, rd, wr, nowaw=True)
        dsem.count += 1
        self.dma_last[id(dsem)] = o
        return o

    def barrier(self):
        lasts = []
        for e in self.ENGS:
            for o in reversed(self.ops[e]):
                if not o.is_dma and o.fn is not None:
                    lasts.append(o)
                    break
        lasts += list(self.dma_last.values())
        for e in self.ENGS:
            o = Op(e, None)
            for l in lasts:
                o.deps.append((l, "raw", l.dsem.count if l.is_dma else 0))
            self.ops[e].append(o)
        self.dsem_i = 0

    @staticmethod
    def _needs_sem(p, c, kind):
        if p.is_dma:
            return True
        if p.eng == c.eng:
            if p.eng in ("pe", "sp"):
                return False
            return kind == "raw"
        return True

    def emit(self, es):
        nc = self.nc
        for e in self.ENGS:
            for c in self.ops[e]:
                for (p, kind, cnt) in c.deps:
                    if not p.is_dma and self._needs_sem(p, c, kind):
                        p.signal = True
        pool = {}

        def getsem(key):
            if key not in pool:
                pool[key] = es.enter_context(nc.semaphore(f"s{len(pool)}"))
            return pool[key]

        for e in self.ENGS:
            k = 0
            for o in self.ops[e]:
                if not o.is_dma and o.signal:
                    o.sigval = k
                    k += 1
        dcap = SEM_CAP // 16
        nw = 0
        ni = 0
        for e in self.ENGS:
            eng = self.eng_obj[e]
            waited = {}
            for o in self.ops[e]:
                need = {}
                for (p, kind, cnt) in o.deps:
                    if not self._needs_sem(p, o, kind):
                        continue
                    if p.is_dma:
                        tot = cnt
                        key = ("d", id(p.dsem), (tot - 1) // dcap)
                        v = ((tot - 1) % dcap + 1) * 16
                    else:
                        key = ("c", p.eng, p.sigval // SEM_CAP)
                        v = p.sigval % SEM_CAP + 1
                    if need.get(key, 0) < v:
                        need[key] = v
                for key, v in need.items():
                    if waited.get(key, 0) >= v:
                        continue
                    waited[key] = v
                    eng.wait_ge(getsem(key), v)
                    nw += 1
                if o.fn is None:
                    continue
                ins = o.fn()
                ni += 1
                if o.is_dma:
                    ins.then_inc(getsem(("d", id(o.dsem), (o.dtarget - 1) // dcap)), 16)
                elif o.signal:
                    ins.then_inc(getsem(("c", o.eng, o.sigval // SEM_CAP)), 1)
        self.stats = dict(waits=nw, insts=ni, sems=len(pool))
        return self.stats


IN_GROUPS = [
    ("cq", 0, 512, "F", "lat"), ("ckv", 512, 256, "F", "lat"), ("kr", 768, 64, "F", "lat"),
    ("g_mla", 832, 1024, "F", "silu"), ("dq", 1856, 1024, "F", "rope"), ("dk", 2880, 1024, "F", "rope"),
    ("dv", 3904, 1024, "T", "copy"), ("g_diff", 4928, 1024, "F", "silu"), ("z", 5952, 1024, "T", "silu"),
    ("xbc", 6976, 1536, "F", "copy"), ("dt", 8512, 32, "T", "copyf"), ("u", 8544, 1024, "F", "copy"),
    ("g_pool", 9568, 1024, "F", "silu"), ("mg", 10592, 8192, "F", "sigmoid"),
]


class Builder:
    def __init__(self, T, debug=False, nlayers=DEPTH, phases=None):
        self.T = T
        self.SEG = T // 2
        self.NT = T // 128
        self.NB = T // 512
        self.debug = debug
        self.nlayers = nlayers
        self.phases = phases
        self.nc = bass.Bass("TRN2", target_bir_lowering=False)
        self.P = Prog(self.nc)
        self.es = contextlib.ExitStack()

    def dram_in(self, name, shape, dt=F32):
        return self.nc.dram_tensor(name, list(shape), dt, kind="ExternalInput").ap()

    def dram_out(self, name, shape, dt=F32):
        return self.nc.dram_tensor(name, list(shape), dt, kind="ExternalOutput").ap()

    def dram_scr(self, name, shape, dt):
        kind = "ExternalOutput" if self.debug else "Internal"
        return self.nc.dram_tensor(name, list(shape), dt, kind=kind).ap()

    def reset_arena(self):
        self.off = 0

    def alloc(self, shape, dt, name="", const=False):
        n = int(np.prod(shape))
        nbytes = n * (4 if dt == F32 else 2)
        nw = (nbytes + 3) // 4
        nw = (nw + 7) // 8 * 8
        assert self.off + nw <= self.BIGW, f"SBUF arena overflow {name} {self.off + nw}"
        ap = self.big[:, self.off:self.off + nw]
        self.off += nw
        if dt == BF16:
            ap = ap.bitcast(BF16)[:, 0:n]
        else:
            ap = ap[:, 0:n]
        if len(shape) > 1:
            names = " ".join(f"a{i}" for i in range(len(shape)))
            kw = {f"a{i}": int(s) for i, s in enumerate(shape)}
            ap = ap.rearrange(f"p ({names}) -> p {names}", **kw)
        return TL(ap, Buf(name, const=const))

    def ring(self, n, shape, dt, name=""):
        tl = [self.alloc(shape, dt, f"{name}{i}") for i in range(n)]
        ds = [self.P.dsem() for _ in range(n)]
        return _Ring(tl, ds)

    def psum_tl(self, i, dt=F32):
        ap = self.banks[i][:]
        if dt == BF16:
            ap = ap.bitcast(BF16)
        return TL(ap, self.bank_bufs[i])

    def mm(self, out, lhsT, rhs, start=True, stop=True, skip=False):
        nc = self.nc
        o, a, b = out.ap, lhsT.ap, rhs.ap
        if skip:
            return self.P.op("pe", lambda: nc.tensor.matmul(o, a, b, start=start, stop=stop, skip_group_check=True),
                             reads=[lhsT, rhs], writes=[out])
        return self.P.op("pe", lambda: nc.tensor.matmul(o, a, b, start=start, stop=stop),
                         reads=[lhsT, rhs], writes=[out])

    def tr(self, out, in_, ident):
        nc = self.nc
        o, a, b = out.ap, in_.ap, ident.ap
        return self.P.op("pe", lambda: nc.tensor.transpose(o, a, b), reads=[in_, ident], writes=[out])

    def act(self, out, in_, func, bias=None, scale=1.0, accum=None, eng="act"):
        nc = self.nc
        o, a = out.ap, in_.ap
        kw = {}
        rd = [in_]
        wr = [out]
        if bias is not None:
            if isinstance(bias, TL):
                kw["bias"] = bias.ap
                rd.append(bias)
            else:
                kw["bias"] = float(bias)
        if isinstance(scale, TL):
            kw["scale"] = scale.ap
            rd.append(scale)
        else:
            kw["scale"] = float(scale)
        if accum is not None:
            kw["accum_out"] = accum.ap
            wr.append(accum)
        return self.P.op("act", lambda: nc.scalar.activation(out=o, in_=a, func=func, **kw), reads=rd, writes=wr)

    def _e(self, eng):
        return self.P.eng_obj[eng]

    def tt(self, eng, out, in0, in1, op):
        e = self._e(eng)
        o, a, b = out.ap, in0.ap, in1.ap
        return self.P.op(eng, lambda: e.tensor_tensor(out=o, in0=a, in1=b, op=op), reads=[in0, in1], writes=[out])

    def ts(self, eng, out, in0, s1, s2=None, op0=ALU.mult, op1=None):
        e = self._e(eng)
        o, a = out.ap, in0.ap
        rd = [in0]
        v1 = s1.ap if isinstance(s1, TL) else float(s1)
        if isinstance(s1, TL):
            rd.append(s1)
        v2 = None
        if s2 is not None:
            v2 = s2.ap if isinstance(s2, TL) else float(s2)
            if isinstance(s2, TL):
                rd.append(s2)
        if op1 is None:
            return self.P.op(eng, lambda: e.tensor_scalar(out=o, in0=a, scalar1=v1, scalar2=None, op0=op0),
                             reads=rd, writes=[out])
        return self.P.op(eng, lambda: e.tensor_scalar(out=o, in0=a, scalar1=v1, scalar2=v2, op0=op0, op1=op1),
                         reads=rd, writes=[out])

    def stt(self, eng, out, in0, scalar, in1, op0, op1):
        eng = "dve"
        e = self._e(eng)
        o, a, b = out.ap, in0.ap, in1.ap
        rd = [in0, in1]
        sv = scalar.ap if isinstance(scalar, TL) else float(scalar)
        if isinstance(scalar, TL):
            rd.append(scalar)
        return self.P.op(eng, lambda: e.scalar_tensor_tensor(out=o, in0=a, scalar=sv, in1=b, op0=op0, op1=op1),
                         reads=rd, writes=[out])

    def copy(self, eng, out, in_):
        if eng == "act":
            return self.act(out, in_, AF.Copy)
        e = self._e(eng)
        o, a = out.ap, in_.ap
        return self.P.op(eng, lambda: e.tensor_copy(o, a), reads=[in_], writes=[out])

    def recip(self, out, in_):
        nc = self.nc
        o, a = out.ap, in_.ap
        return self.P.op("dve", lambda: nc.vector.reciprocal(o, a), reads=[in_], writes=[out])

    def memset(self, eng, out, val):
        e = self._e(eng)
        o = out.ap
        return self.P.op(eng, lambda: e.memset(o, float(val)), writes=[out])

    def load(self, eng, dst, src_ap, dsem):
        return self.P.dma(eng, dst, src_ap, dsem)

    def store(self, eng, dst_ap, src, dsem):
        return self.P.dma(eng, dst_ap, src, dsem)

    def dbg(self, name, tl, parts=128):
        if not self.debug:
            return
        shp = [parts] + list(tl.ap.shape[1:])
        d = self.nc.dram_tensor("dbg_" + name, shp, tl.ap.dtype, kind="ExternalOutput").ap()
        self.P.dma("sp", d, tl[0:parts], self.P.dsem())

    def rsqrt_act(self, out, in_, scale, eps):
        self.act(out, in_, AF.Ln, bias=self.eps_col[:, 0:1] if eps == EPS else eps, scale=scale)
        self.act(out, out, AF.Exp, scale=-0.5)


class _Ring:
    def __init__(self, tl, ds):
        self.tl = tl
        self.ds = ds
        self.i = -1

    def next(self):
        self.i = (self.i + 1) % len(self.tl)
        return self.tl[self.i], self.ds[self.i]


W_NAMES = [
    ("norm_w", (DEPTH, D)), ("w_in", (DEPTH, D, IN_DIM)), ("mla_q_norm", (DEPTH, 512)),
    ("mla_w_uq", (DEPTH, 512, 1536)), ("mla_kv_norm", (DEPTH, 256)), ("mla_w_ukv", (DEPTH, 256, 2048)),
    ("diff_lambda", (DEPTH, 4, 128)), ("diff_subln", (DEPTH, 256)), ("ssd_conv_w", (DEPTH, 4, 1536)),
    ("ssd_conv_b", (DEPTH, 1536)), ("ssd_dt_bias", (DEPTH, 2, 16)), ("ssd_a_log", (DEPTH, 2, 16)),
    ("ssd_d", (DEPTH, 16)), ("ssd_norm", (DEPTH, 1024)), ("pool_w", (DEPTH, 4, 256, 256)),
    ("pool_scale", (DEPTH, 1024)), ("w_branch", (DEPTH, 4, 1024, D)), ("w_out", (DEPTH, D, D)),
    ("final_norm", (D,)),
]


def build(T, debug=False, nlayers=DEPTH, phases=None):
    B = Builder(T, debug, nlayers, phases)
    nc, P, es = B.nc, B.P, B.es
    NT, NB, SEG = B.NT, B.NB, B.SEG
    x_in = B.dram_in("x", (T, D))
    W = {n: B.dram_in(n, s) for n, s in W_NAMES}
    c_link = B.dram_in("c_link", (128, 1))
    c_cbias = B.dram_in("c_cbias", (128, 1))
    c_ident = B.dram_in("c_ident", (128, 128))
    c_tri = B.dram_in("c_tri", (5, 128, 128))
    c_rot64 = B.dram_in("c_rot64", (64, 64))
    c_rot32 = B.dram_in("c_rot32", (32, 32))
    c_cs64 = B.dram_in("c_cs64", (2, 64, T))
    c_cs32 = B.dram_in("c_cs32", (2, 32, T))
    c_rcnt = B.dram_in("c_rcnt", (4, T))
    y_out = B.dram_out("y", (T, D))
    S = {}
    S["x1"] = B.dram_scr("s_x1", (T, D), F32)
    for nm, f in [("cqn", 512), ("ckvn", 256), ("kpe", 64), ("sg_mla", 1024), ("dq", 1024), ("dk", 1024),
                  ("sg_diff", 1024), ("xbc", 1536), ("u", 1024), ("sg_pool", 1024), ("mg", 8192),
                  ("br_mla", 1024), ("br_diff", 1024), ("br_ssd", 1024), ("br_pool", 1024)]:
        S[nm] = B.dram_scr("s_" + nm, (f, T), BF16)
    S["dv"] = B.dram_scr("s_dv", (T, 1024), BF16)
    S["sz"] = B.dram_scr("s_sz", (T, 1024), BF16)
    S["dt"] = B.dram_scr("s_dt", (T, 32), F32)
    S["xsb"] = B.dram_scr("s_xsb", (T, 1280), BF16)
    S["bcT"] = B.dram_scr("s_bcT", (512, T), BF16)
    S["H"] = B.dram_scr("s_H", (2, T // 128, 128, 1024), BF16)
    S["mT"] = B.dram_scr("s_mT", (D, T), BF16)
    B.S = S

    B.BIGW = 49152 - 1024
    B.big = es.enter_context(nc.sbuf_tensor("big", [128, B.BIGW + 64], F32))
    B.banks = [es.enter_context(nc.psum_tensor(f"ps{i}", [128, 512], F32)) for i in range(8)]
    B.bank_bufs = [Buf(f"ps{i}") for i in range(8)]
    cbase = B.BIGW
    B.eps_col = TL(B.big[:, cbase:cbase + 1], Buf("eps", const=True))
    B.link = TL(B.big[:, cbase + 1:cbase + 2], Buf("link", const=True))
    B.cbias = TL(B.big[:, cbase + 2:cbase + 3], Buf("cbias", const=True))
    B.zero_col = TL(B.big[:, cbase + 3:cbase + 4], Buf("zero", const=True))
    B.one_col = TL(B.big[:, cbase + 4:cbase + 5], Buf("one", const=True))
    B.memset("dve", B.eps_col, EPS)
    B.memset("dve", B.zero_col, 0.0)
    B.memset("dve", B.one_col, 1.0)
    ds0 = DmaSem()
    B.load("sp", B.link, c_link[:, :], ds0)
    B.load("sp", B.cbias, c_cbias[:, :], ds0)
    P.barrier()

    consts = dict(ident=c_ident, tri=c_tri, rot64=c_rot64, rot32=c_rot32, cs64=c_cs64, cs32=c_cs32, rcnt=c_rcnt)
    for l in range(nlayers):
        x_src = x_in if l == 0 else S["x1"]
        last = (l == nlayers - 1)
        if phases is None or "A" in phases:
            phase_A(B, l, x_src, W, consts)
            P.barrier()
        if phases is None or "B" in phases:
            phase_B(B, l, W, consts)
            P.barrier()
        if phases is None or "C" in phases:
            phase_C(B, l, W, consts)
            P.barrier()
        if phases is None or "D" in phases:
            phase_D(B, l, W, consts)
            P.barrier()
        if phases is None or "E" in phases:
            phase_E(B, l, W, consts)
            P.barrier()
        if phases is None or "F" in phases:
            phase_F(B, l, x_src, y_out if last else S["x1"], W, consts, last)
            P.barrier()
    P.barrier()
    st = P.emit(es)
    B.stats = st
    return B


def phase_A(B, l, x_src, W, C):
    nc, P, S, T = B.nc, B.P, B.S, B.T
    B.reset_arena()
    TBA = min(1024, T)
    nblk = T // TBA
    nsub = TBA // 512
    ntile = TBA // 128
    CWMAX = 544
    normw = B.alloc([D], F32, "normw", const=True)
    identb = B.alloc([128], BF16, "identb", const=True)
    ones_b = B.alloc([128], BF16, "ones_b", const=True)
    rot64 = B.alloc([64], F32, "rot64", const=True)
    rot32 = B.alloc([32], F32, "rot32", const=True)
    qnw = B.alloc([4], F32, "qnw", const=True)
    kvnw = B.alloc([2], F32, "kvnw", const=True)
    hT = B.alloc([16, TBA], BF16, "hT")
    xring = B.ring(2, [D], F32, "x")
    hb = B.alloc([D], BF16, "hb")
    sqj = B.alloc([D], BF16, "sqj")
    stat = B.alloc([4], F32, "stat")
    wring = B.ring(3, [16, CWMAX], BF16, "w")
    ost = B.ring(4, [TBA], BF16, "ost")
    ostT = B.ring(3, [CWMAX], BF16, "ostT")
    ostF = B.ring(2, [32], F32, "ostF")
    lat = B.alloc([7, TBA], F32, "lat")
    rf = B.ring(2, [512], F32, "rf")
    cs32 = B.alloc([2, TBA], F32, "cs32")
    cs64 = B.alloc([2, TBA], F32, "cs64")
    tmpf = B.ring(2, [512], F32, "tmpf")
    tmpb = B.ring(2, [512], BF16, "tmpb")
    cd = P.dsem()
    cd32 = P.dsem()
    cd64 = P.dsem()
    B.load("sp", normw, W["norm_w"][l].partition_broadcast(128), cd)
    B.load("pool", identb, C["ident"][:, :], cd)
    B.load("sp", rot64[0:64], C["rot64"][:, :], cd)
    B.load("sp", rot32[0:32], C["rot32"][:, :], cd)
    B.load("sp", qnw, W["mla_q_norm"][l].rearrange("(c p) -> p c", p=128), cd)
    B.load("sp", kvnw, W["mla_kv_norm"][l].rearrange("(c p) -> p c", p=128), cd)
    B.memset("dve", ones_b, 1.0)
    pb = [B.psum_tl(i) for i in range(8)]
    pbb = [B.psum_tl(i, BF16) for i in range(8)]
    mmbank = _Cycle([0, 1, 2, 3])
    auxbank = _Cycle([4, 5])
    rotbank = _Cycle([6, 7])
    w_in = W["w_in"][l].rearrange("(k p) c -> p k c", p=128)

    wtiles = []
    for (nm, c0, ncol, orient, epi) in IN_GROUPS:
        if nm in ("ckv", "kr", "dt"):
            continue
        if nm == "cq":
            wtiles.append((0, 512, [("cq", 0, 512)]))
            wtiles.append((512, 320, [("ckv", 0, 256), ("kr", 256, 64)]))
            continue
        nt_ = ncol // 512
        for j in range(nt_):
            if nm == "xbc" and j == nt_ - 1:
                wtiles.append((c0 + j * 512, 544, [("xbc", 0, 512), ("dt", 512, 32)]))
            else:
                wtiles.append((c0 + j * 512, 512, [(nm, 0, 512)]))
    ginfo = {g[0]: g for g in IN_GROUPS}
    act_toggle = [0]

    for blk in range(nblk):
        t0 = blk * TBA
        B.load("sp", cs32[0:32], C["cs32"][:, :, t0:t0 + TBA].rearrange("a p t -> p a t"), cd32)
        B.load("sp", cs64[0:64], C["cs64"][:, :, t0:t0 + TBA].rearrange("a p t -> p a t"), cd64)
        for i in range(ntile):
            xt, xd = xring.next()
            B.load("sp", xt, x_src[t0 + i * 128:t0 + (i + 1) * 128, :], xd)
            B.act(sqj, xt, AF.Square, accum=stat[:, 0:1])
            B.rsqrt_act(stat[:, 1:2], stat[:, 0:1], 1.0 / D, EPS)
            B.stt("dve", hb, xt, stat[:, 1:2], normw, ALU.mult, ALU.mult)
            for half in range(2):
                bk = auxbank.next()
                for j in range(8):
                    k = half * 8 + j
                    B.tr(pbb[bk][:, j * 128:(j + 1) * 128], hb[:, k * 128:(k + 1) * 128], identb)
                B.copy("dve" if half == 0 else "act",
                       hT[:, half * 8:(half + 1) * 8, i * 128:(i + 1) * 128],
                       pbb[bk].v(lambda a: a.rearrange("p (j c) -> p j c", j=8)))
        for (c0, cw, parts) in wtiles:
            wt, wd = wring.next()
            B.load("pool", wt[:, :, 0:cw], w_in[:, :, c0:c0 + cw], wd)
            for (nm, po, pn) in parts:
                _, gc0, gn, orient, epi = ginfo[nm]
                gcol = c0 + po - gc0
                if orient == "F":
                    for cc in range(0, pn, 128):
                        m = min(128, pn - cc)
                        feat = gcol + cc
                        if epi != "lat":
                            og, od = ost.next()
                        for sb in range(nsub):
                            bk = mmbank.next()
                            for k in range(16):
                                B.mm(pb[bk][0:m, :], wt[:, k, po + cc:po + cc + m], hT[:, k, sb * 512:(sb + 1) * 512],
                                     start=(k == 0), stop=(k == 15))
                            src = pb[bk][0:m, :]
                            if epi == "lat":
                                li = {"cq": 0, "ckv": 4, "kr": 6}[nm] + cc // 128
                                B.copy("dve", lat[0:m, li, sb * 512:(sb + 1) * 512], src)
                            elif epi == "silu":
                                B.act(og[0:m, sb * 512:(sb + 1) * 512], src, AF.Silu)
                            elif epi == "sigmoid":
                                B.act(og[0:m, sb * 512:(sb + 1) * 512], src, AF.Sigmoid)
                            elif epi == "copy":
                                act_toggle[0] ^= 1
                                B.copy("dve", og[0:m, sb * 512:(sb + 1) * 512], src)
                            elif epi == "rope":
                                r, _ = rf.next()
                                B.copy("dve", r, src)
                                rb = rotbank.next()
                                B.mm(pb[rb][0:32, :], rot32[0:32, 0:32], r[0:32, :])
                                tf, _ = tmpf.next()
                                sl = slice(sb * 512, (sb + 1) * 512)
                                B.tt("dve", tf[0:32], r[0:32], cs32[0:32, 0, sl], ALU.mult)
                                tf2, _ = tmpf.next()
                                B.tt("dve", tf2[0:32], pb[rb][0:32, :], cs32[0:32, 1, sl], ALU.mult)
                                B.copy("act", og[:, sl], r)
                                B.tt("dve", og[0:32, sl], tf[0:32], tf2[0:32], ALU.add)
                        if epi != "lat":
                            sname = {"g_mla": "sg_mla", "g_diff": "sg_diff", "g_pool": "sg_pool"}.get(nm, nm)
                            B.store("sp", S[sname][feat:feat + m, t0:t0 + TBA], og[0:m, :], od)
                else:
                    for i in range(ntile):
                        bk = mmbank.next()
                        for k in range(16):
                            B.mm(pb[bk][:, 0:pn], hT[:, k, i * 128:(i + 1) * 128], wt[:, k, po:po + pn],
                                 start=(k == 0), stop=(k == 15))
                        src = pb[bk][:, 0:pn]
                        rows = slice(t0 + i * 128, t0 + (i + 1) * 128)
                        if epi == "copyf":
                            og, od = ostF.next()
                            B.copy("dve", og[:, 0:pn], src)
                            B.store("sp", S["dt"][rows, :], og[:, 0:pn], od)
                        else:
                            og, od = ostT.next()
                            if epi == "silu":
                                B.act(og[:, 0:pn], src, AF.Silu)
                            else:
                                B.copy("dve", og[:, 0:pn], src)
                            sname = {"z": "sz"}.get(nm, nm)
                            B.store("sp", S[sname][rows, gcol:gcol + pn], og[:, 0:pn], od)
            if parts[0][0] == "ckv":
                for sb in range(nsub):
                    sl = slice(sb * 512, (sb + 1) * 512)
                    for (nm, li0, nch, wcol, dim) in (("cqn", 0, 4, qnw, 512), ("ckvn", 4, 2, kvnw, 256)):
                        bk = auxbank.next()
                        for c in range(nch):
                            tb_, _ = tmpb.next()
                            B.act(tb_, lat[:, li0 + c, sl], AF.Square)
                            B.mm(pb[bk], ones_b, tb_, start=(c == 0), stop=(c == nch - 1))
                        tf, _ = tmpf.next()
                        B.rsqrt_act(tf, pb[bk], 1.0 / dim, EPS)
                        for c in range(nch):
                            og, od = ost.next()
                            B.stt("dve", og[:, 0:512], lat[:, li0 + c, sl], wcol[:, c:c + 1], tf, ALU.mult, ALU.mult)
                            B.store("sp", S[nm][c * 128:(c + 1) * 128, t0 + sb * 512:t0 + (sb + 1) * 512], og[:, 0:512], od)
                    rb = rotbank.next()
                    B.mm(pb[rb][0:64, :], rot64[0:64, 0:64], lat[0:64, 6, sl])
                    tf, _ = tmpf.next()
                    B.tt("dve", tf[0:64], lat[0:64, 6, sl], cs64[0:64, 0, sl], ALU.mult)
                    tf2, _ = tmpf.next()
                    B.tt("dve", tf2[0:64], pb[rb][0:64, :], cs64[0:64, 1, sl], ALU.mult)
                    og, od = ost.next()
                    B.tt("dve", og[0:64, 0:512], tf[0:64], tf2[0:64], ALU.add)
                    B.store("sp", S["kpe"][0:64, t0 + sb * 512:t0 + (sb + 1) * 512], og[0:64, 0:512], od)


def attn_qblock(B, qsl, s_terms, v_list, acc_banks, den_bank, Sring, Pring, pb, ones_b, scale, NT, SEG, q0):
    LA = 2
    pend = []
    for step in range(NT + LA):
        if step < NT:
            kb = step
            sb = Sring.next()
            ksl = slice(kb * 128, (kb + 1) * 128)
            for i, (kT, qT) in enumerate(s_terms):
                B.mm(pb[sb], kT[:, ksl], qT[:, qsl], start=(i == 0), stop=(i == len(s_terms) - 1))
            cross = (q0 // SEG) != ((kb * 128) // SEG)
            pt, _ = Pring.next()
            B.act(pt, pb[sb], AF.Exp, scale=scale, bias=(B.cbias if cross else None))
            pend.append((kb, pt))
        if step >= LA:
            kb, pt = pend.pop(0)
            for v, ab in zip(v_list, acc_banks):
                B.mm(pb[ab], v[:, kb, :], pt, start=(kb == 0), stop=(kb == NT - 1))
            B.mm(pb[den_bank], ones_b, pt, start=(kb == 0), stop=(kb == NT - 1))


def phase_B(B, l, W, C):
    nc, P, S, T, NT, NB, SEG = B.nc, B.P, B.S, B.T, B.NT, B.NB, B.SEG
    B.reset_arena()
    cqn = B.alloc([4, T], BF16, "cqn")
    ckvn = B.alloc([2, T], BF16, "ckvn")
    kpe = B.alloc([T], BF16, "kpe")
    wuq = B.alloc([4, 1536], BF16, "wuq", const=True)
    wukv = B.alloc([2, 2048], BF16, "wukv", const=True)
    ones_b = B.alloc([128], BF16, "ones_b", const=True)
    rot64 = B.alloc([64], F32, "rot64", const=True)
    qn = [B.alloc([T], BF16, f"qn{i}") for i in range(2)]
    qp = [B.alloc([T], BF16, f"qp{i}") for i in range(2)]
    kn = [B.alloc([T], BF16, f"kn{i}") for i in range(2)]
    vv = [B.alloc([NT, 128], BF16, f"v{i}") for i in range(2)]
    csr = B.ring(2, [2, 512], F32, "cs")
    rr = B.ring(2, [512], F32, "rr")
    tmpf = B.ring(4, [512], F32, "tmpf")
    Pring = B.ring(4, [512], BF16, "pt")
    sgr = B.ring(2, [512], BF16, "sg")
    ost = B.ring(2, [512], BF16, "ost")
    cd = P.dsem()
    B.load("sp", cqn, S["cqn"].rearrange("(c p) t -> p c t", p=128), cd)
    B.load("sp", ckvn, S["ckvn"].rearrange("(c p) t -> p c t", p=128), cd)
    B.load("sp", kpe[0:64], S["kpe"][:, :], cd)
    B.memset("pool", kpe[64:128], 0.0)
    B.memset("pool", qp[0][64:128], 0.0)
    B.memset("pool", qp[1][64:128], 0.0)
    B.load("pool", wuq, W["mla_w_uq"][l].rearrange("(c p) n -> p c n", p=128), cd)
    B.load("pool", wukv, W["mla_w_ukv"][l].rearrange("(c p) n -> p c n", p=128), cd)
    B.load("sp", rot64[0:64], C["rot64"][:, :], cd)
    B.memset("dve", ones_b, 1.0)
    pb = [B.psum_tl(i) for i in range(8)]
    Sring = _Cycle([0, 1, 2])
    accs = _Cycle([(3, 4), (5, 6)])
    scale = float(192 ** -0.5)

    def prologue(h):
        s = h % 2
        for sb in range(NB):
            sl = slice(sb * 512, (sb + 1) * 512)
            for c in range(4):
                B.mm(pb[7], wuq[:, c, h * 192:h * 192 + 128], cqn[:, c, sl], start=(c == 0), stop=(c == 3))
            B.copy("dve", qn[s][:, sl], pb[7])
            for c in range(4):
                B.mm(pb[7][0:64], wuq[:, c, h * 192 + 128:h * 192 + 192], cqn[:, c, sl], start=(c == 0), stop=(c == 3))
            r, _ = rr.next()
            B.copy("dve", r[0:64], pb[7][0:64])
            cs, cdm = csr.next()
            B.load("sp", cs[0:64], C["cs64"][:, :, sl].rearrange("a p t -> p a t"), cdm)
            B.mm(pb[7][0:64], rot64[0:64, 0:64], r[0:64])
            tf, _ = tmpf.next()
            B.tt("dve", tf[0:64], r[0:64], cs[0:64, 0], ALU.mult)
            tf2, _ = tmpf.next()
            B.tt("dve", tf2[0:64], pb[7][0:64], cs[0:64, 1], ALU.mult)
            B.tt("dve", qp[s][0:64, sl], tf[0:64], tf2[0:64], ALU.add)
            for c in range(2):
                B.mm(pb[7], wukv[:, c, h * 256:h * 256 + 128], ckvn[:, c, sl], start=(c == 0), stop=(c == 1))
            B.copy("act", kn[s][:, sl], pb[7])
            for i in range(4):
                tsl = slice(sb * 512 + i * 128, sb * 512 + (i + 1) * 128)
                for c in range(2):
                    B.mm(pb[7][:, i * 128:(i + 1) * 128], ckvn[:, c, tsl], wukv[:, c, h * 256 + 128:h * 256 + 256],
                         start=(c == 0), stop=(c == 1))
            B.copy("act", vv[s][:, sb * 4:(sb + 1) * 4, :], pb[7].v(lambda a: a.rearrange("p (i d) -> p i d", i=4)))

    prologue(0)
    for h in range(8):
        s = h % 2
        for qb in range(NB):
            qsl = slice(qb * 512, (qb + 1) * 512)
            ab, db = accs.next()
            attn_qblock(B, qsl, [(kn[s], qn[s]), (kpe, qp[s])], [vv[s]], [ab], db,
                        Sring, Pring, pb, ones_b, scale, NT, SEG, qb * 512)
            if qb == 0 and h + 1 < 8:
                prologue(h + 1)
            rd, _ = tmpf.next()
            B.recip(rd, pb[db])
            o, _ = tmpf.next()
            B.tt("dve", o, pb[ab], rd, ALU.mult)
            sg, sgd = sgr.next()
            B.load("sp", sg, S["sg_mla"][h * 128:(h + 1) * 128, qsl], sgd)
            og, od = ost.next()
            B.tt("dve", og, o, sg, ALU.mult)
            B.store("sp", S["br_mla"][h * 128:(h + 1) * 128, qsl], og, od)
    B.dbg("qn", qn[1]); B.dbg("qp", qp[1], 64); B.dbg("kn", kn[1]); B.dbg("vv", vv[1]); B.dbg("kpe", kpe, 64)


def phase_C(B, l, W, C):
    nc, P, S, T, NT, NB, SEG = B.nc, B.P, B.S, B.T, B.NT, B.NB, B.SEG
    B.reset_arena()
    lam_init = 0.8 - 0.6 * math.exp(-0.3 * l)
    ones_b = B.alloc([128], BF16, "ones_b", const=True)
    ones_f = B.alloc([128], F32, "ones_f", const=True)
    subw = B.alloc([2], F32, "subw", const=True)
    lp = B.alloc([512], F32, "lp")
    lt = B.alloc([256], F32, "lt")
    ls = B.alloc([8], F32, "ls")
    nlam = B.alloc([1], F32, "nlam")
    qk = [[B.alloc([T], BF16, f"qk{i}{j}") for j in range(4)] for i in range(2)]
    vv = [B.alloc([NT, 256], BF16, f"v{i}") for i in range(2)]
    hd = [P.dsem() for _ in range(2)]
    Pring = B.ring(4, [512], BF16, "pt")
    on = [B.alloc([2, 512], F32, f"on{i}") for i in range(2)]
    oo = B.alloc([2, 512], F32, "oo")
    tmpf = B.ring(3, [512], F32, "tmpf")
    sqr = B.ring(2, [512], BF16, "sq")
    sgr = B.ring(2, [2, 512], BF16, "sg")
    ost = B.ring(2, [2, 512], BF16, "ost")
    cd = P.dsem()
    B.memset("dve", ones_b, 1.0)
    B.memset("dve", ones_f, 1.0)
    B.load("sp", subw, W["diff_subln"][l].rearrange("(c p) -> p c", p=128), cd)
    B.load("sp", lp[0:1], W["diff_lambda"][l].rearrange("(o a) d -> o (a d)", o=1), cd)
    pb = [B.psum_tl(i) for i in range(8)]
    B.tt("dve", lt[0:1, 0:128], lp[0:1, 0:128], lp[0:1, 128:256], ALU.mult)
    B.tt("dve", lt[0:1, 128:256], lp[0:1, 256:384], lp[0:1, 384:512], ALU.mult)
    e = nc.vector
    for j in range(2):
        o_, a_ = ls[0:1, j:j + 1], lt[0:1, j * 128:(j + 1) * 128]
        P.op("dve", (lambda o_=o_, a_=a_: e.reduce_sum(out=o_.ap, in_=a_.ap, axis=AX.X)), reads=[a_], writes=[o_])
    B.act(ls[0:1, 2:4], ls[0:1, 0:2], AF.Exp)
    B.tt("dve", ls[0:1, 4:5], ls[0:1, 3:4], ls[0:1, 2:3], ALU.subtract)
    B.ts("dve", ls[0:1, 5:6], ls[0:1, 4:5], -lam_init, None, op0=ALU.add)
    B.mm(pb[7][:, 0:1], ones_f[0:1, :], ls[0:1, 5:6])
    B.copy("dve", nlam, pb[7][:, 0:1])
    Sring = _Cycle([6, 7])
    accsets = _Cycle([(0, 1, 2), (3, 4, 5)])
    scale = float(128 ** -0.5)

    def loadhead(h):
        s = h % 2
        for j, (nm, r0) in enumerate((("dq", 2 * h), ("dq", 2 * h + 1), ("dk", 2 * h), ("dk", 2 * h + 1))):
            B.load("sp", qk[s][j], S[nm][r0 * 128:(r0 + 1) * 128, :], hd[s])
        B.load("sp", vv[s], S["dv"][:, h * 256:(h + 1) * 256].rearrange("(n p) c -> p n c", p=128), hd[s])

    loadhead(0)
    for h in range(4):
        s = h % 2
        for qb in range(NB):
            qsl = slice(qb * 512, (qb + 1) * 512)
            for sm in range(2):
                a0, a1, db = accsets.next()
                attn_qblock(B, qsl, [(qk[s][2 + sm], qk[s][sm])], [vv[s][:, :, 0:128], vv[s][:, :, 128:256]],
                            [a0, a1], db, Sring, Pring, pb, ones_b, scale, NT, SEG, qb * 512)
                if qb == 0 and sm == 0 and h + 1 < 4:
                    loadhead(h + 1)
                rd, _ = tmpf.next()
                B.recip(rd, pb[db])
                B.tt("dve", on[sm][:, 0], pb[a0], rd, ALU.mult)
                B.tt("dve", on[sm][:, 1], pb[a1], rd, ALU.mult)
            B.stt("dve", oo, on[1], nlam[:, 0:1], on[0], ALU.mult, ALU.add)
            sbk = Sring.next()
            for c in range(2):
                sq, _ = sqr.next()
                B.act(sq, oo[:, c], AF.Square)
                B.mm(pb[sbk], ones_b, sq, start=(c == 0), stop=(c == 1))
            rs, _ = tmpf.next()
            B.rsqrt_act(rs, pb[sbk], 1.0 / 256, EPS)
            sg, sgd = sgr.next()
            B.load("sp", sg, S["sg_diff"][h * 256:(h + 1) * 256, qsl].rearrange("(c p) t -> p c t", p=128), sgd)
            og, od = ost.next()
            for c in range(2):
                tf, _ = tmpf.next()
                B.stt("dve", tf, oo[:, c], subw[:, c:c + 1], rs, ALU.mult, ALU.mult)
                B.stt("dve", og[:, c], tf, 1.0 - lam_init, sg[:, c], ALU.mult, ALU.mult)
            B.store("sp", S["br_diff"][h * 256:(h + 1) * 256, qsl].rearrange("(c p) t -> p c t", p=128), og, od)


def phase_E(B, l, W, C):
    nc, P, S, T, NT, NB, SEG = B.nc, B.P, B.S, B.T, B.NT, B.NB, B.SEG
    B.reset_arena()
    PADW = SEG + 16
    ubr = B.ring(2, [2, SEG], BF16, "ub")
    bufs = [B.alloc([2, PADW], F32, f"pbuf{i}") for i in range(2)]
    rcb = B.alloc([2, SEG], F32, "rcb")
    pooled = [B.alloc([T], BF16, f"pooled{i}") for i in range(2)]
    mean = B.alloc([2, SEG], F32, "mean")
    pw = B.alloc([2, 256], BF16, "pw")
    psc = B.alloc([8], F32, "psc", const=True)
    sgr = B.ring(2, [512], BF16, "sg")
    ost = B.ring(2, [512], BF16, "ost")
    tmpf = B.ring(2, [512], F32, "tmpf")
    cd = P.dsem()
    rcd = P.dsem()
    pwd = P.dsem()
    B.load("sp", psc, W["pool_scale"][l].rearrange("(c p) -> p c", p=128), cd)
    pb = [B.psum_tl(i) for i in range(8)]
    banks = _Cycle(list(range(8)))
    shifts = [(-1, 0, 1, PADW), (-1, 1, 2, PADW - 1), (-2, 2, 4, PADW - 3), (-4, 4, 8, PADW - 7)]
    for g in range(4):
        B.load("sp", rcb, C["rcnt"][g].rearrange("(s t) -> s t", s=2).partition_broadcast(128), rcd)
        B.load("pool", pw, W["pool_w"][l, g].rearrange("(c p) d -> p c d", p=128), pwd)
        for j in range(2):
            c = 2 * g + j
            ub, ud = ubr.next()
            B.load("sp", ub, S["u"][c * 128:(c + 1) * 128, :].rearrange("p (s t) -> p s t", s=2), ud)
            a = bufs[0]
            B.memset("pool", a[:, 0, 0:8], 0.0)
            B.memset("pool", a[:, 1, 8 + SEG:PADW], 0.0)
            B.copy("pool", a[:, :, 8:8 + SEG], ub)
            B.ts("pool", a[:, 0, 8 + SEG:PADW], ub[:, 1, 0:8], B.link[:, 0:1], None, op0=ALU.mult)
            B.ts("pool", a[:, 1, 0:8], ub[:, 0, SEG - 8:SEG], B.link[:, 0:1], None, op0=ALU.mult)
            cur = 0
            for st in range(g + 1):
                s0, s1, lo, hi = shifts[st]
                src, dst = bufs[cur], bufs[1 - cur]
                B.tt("dve" if st % 2 == 0 else "pool", dst[:, :, lo:hi], src[:, :, lo + s0:hi + s0], src[:, :, lo + s1:hi + s1], ALU.add)
                cur = 1 - cur
            B.tt("dve", mean, bufs[cur][:, :, 8:8 + SEG], rcb, ALU.mult)
            B.tt("dve", pooled[j].v(lambda a_: a_.rearrange("p (s t) -> p s t", s=2)), mean, ub, ALU.subtract)
        for dc in range(2):
            co = 2 * g + dc
            for sb in range(NB):
                sl = slice(sb * 512, (sb + 1) * 512)
                bk = banks.next()
                for j in range(2):
                    B.mm(pb[bk], pw[:, j, dc * 128:(dc + 1) * 128], pooled[j][:, sl], start=(j == 0), stop=(j == 1))
                sg, sgd = sgr.next()
                B.load("sp", sg, S["sg_pool"][co * 128:(co + 1) * 128, sl], sgd)
                og, od = ost.next()
                B.stt("dve", og, pb[bk], psc[:, co:co + 1], sg, ALU.mult, ALU.mult)
                B.store("sp", S["br_pool"][co * 128:(co + 1) * 128, sl], og, od)


def phase_F(B, l, x_src, x_dst, W, C, last):
    nc, P, S, T, NT, NB, SEG = B.nc, B.P, B.S, B.T, B.NT, B.NB, B.SEG
    pb = [B.psum_tl(i) for i in range(8)]
    B.reset_arena()
    TB1 = min(2048, T)
    nsb = TB1 // 512
    brT = B.alloc([32, TB1], BF16, "brT")
    wbr = B.ring(2, [32, 128], BF16, "wbr")
    mgr = B.ring(3, [4, 512], BF16, "mg")
    tmpf = B.ring(8, [512], F32, "tmpf")
    ost = B.ring(3, [512], BF16, "ost")
    brd = P.dsem()
    banks = _Cycle(list(range(8)))
    wb_src = W["w_branch"][l].rearrange("i (k p) n -> p i k n", p=128)
    mg_src = S["mg"].rearrange("(i d) t -> d i t", i=4)
    brs = [S["br_mla"], S["br_diff"], S["br_ssd"], S["br_pool"]]
    for blk in range(T // TB1):
        t0 = blk * TB1
        for i in range(4):
            B.load("sp", brT[:, i * 8:(i + 1) * 8, :], brs[i][:, t0:t0 + TB1].rearrange("(k p) t -> p k t", p=128), brd)
        for dmc in range(16):
            wb, wd = wbr.next()
            for i in range(4):
                B.load("pool", wb[:, i * 8:(i + 1) * 8, :], wb_src[:, i, :, dmc * 128:(dmc + 1) * 128], wd)
            for sb in range(nsb):
                sl = slice(t0 + sb * 512, t0 + (sb + 1) * 512)
                lsl = slice(sb * 512, (sb + 1) * 512)
                mg, mgd = mgr.next()
                B.load("sp", mg, mg_src[dmc * 128:(dmc + 1) * 128, :, sl], mgd)
                ts_ = []
                for i in range(4):
                    bk = banks.next()
                    for k in range(8):
                        B.mm(pb[bk], wb[:, i * 8 + k, :], brT[:, i * 8 + k, lsl], start=(k == 0), stop=(k == 7))
                    tf, _ = tmpf.next()
                    B.tt("dve", tf, pb[bk], mg[:, i], ALU.mult)
                    ts_.append(tf)
                B.tt("pool", ts_[0], ts_[0], ts_[1], ALU.add)
                B.tt("pool", ts_[2], ts_[2], ts_[3], ALU.add)
                og, od = ost.next()
                B.tt("pool", og, ts_[0], ts_[2], ALU.add)
                B.store("sp", S["mT"][dmc * 128:(dmc + 1) * 128, sl], og, od)
    P.barrier()
    B.reset_arena()
    wout = B.alloc([16, D], BF16, "wout", const=True)
    mTr = B.ring(2, [16, 512], BF16, "mT")
    xring = B.ring(3, [D], F32, "x")
    stat = B.ring(2, [4], F32, "stat")
    sqj = B.alloc([D], BF16, "sqj")
    if last:
        fnw = B.alloc([D], F32, "fnw", const=True)
    cd = P.dsem()
    for k4 in range(4):
        B.load("pool", wout[:, k4 * 4:(k4 + 1) * 4, :],
               W["w_out"][l].rearrange("(k p) n -> p k n", p=128)[:, k4 * 4:(k4 + 1) * 4, :], cd)
    if last:
        B.load("sp", fnw, W["final_norm"].partition_broadcast(128), cd)
    banks = _Cycle(list(range(8)))
    for tb in range(NB):
        sl = slice(tb * 512, (tb + 1) * 512)
        mT, mTd = mTr.next()
        B.load("sp", mT, S["mT"][:, sl].rearrange("(k p) t -> p k t", p=128), mTd)
        for i in range(4):
            rows = slice(tb * 512 + i * 128, tb * 512 + (i + 1) * 128)
            xt, xd = xring.next()
            B.load("sp", xt, x_src[rows, :], xd)
            for nb in range(4):
                bk = banks.next()
                for k in range(16):
                    B.mm(pb[bk], mT[:, k, i * 128:(i + 1) * 128], wout[:, k, nb * 512:(nb + 1) * 512],
                         start=(k == 0), stop=(k == 15))
                B.tt("dve", xt[:, nb * 512:(nb + 1) * 512], xt[:, nb * 512:(nb + 1) * 512], pb[bk], ALU.add)
            if last:
                st, _ = stat.next()
                B.act(sqj, xt, AF.Square, accum=st[:, 0:1])
                B.rsqrt_act(st[:, 1:2], st[:, 0:1], 1.0 / D, EPS)
                B.stt("dve", xt, xt, st[:, 1:2], fnw, ALU.mult, ALU.mult)
            B.store("sp", x_dst[rows, :], xt, xd)


def phase_D(B, l, W, C):
    nc, P, S, T, NT, NB, SEG = B.nc, B.P, B.S, B.T, B.NT, B.NB, B.SEG
    pb = [B.psum_tl(i) for i in range(8)]
    pbb = [B.psum_tl(i, BF16) for i in range(8)]
    bc3 = lambda n: (lambda a: a.unsqueeze(2).broadcast_to([a.shape[0], a.shape[1], n]))
    B.reset_arena()
    cw = B.alloc([4, 12], F32, "cw", const=True)
    cbv = B.alloc([12], F32, "cbv", const=True)
    identb = B.alloc([128], BF16, "identb", const=True)
    xraw = B.ring(2, [2, SEG], BF16, "xraw")
    xpad = [B.alloc([2, SEG + 3], F32, f"xpad{i}") for i in range(2)]
    acc = [B.alloc([2, SEG], F32, f"acc{i}") for i in range(2)]
    xc = B.ring(2, [T], BF16, "xc")
    stg = B.ring(2, [8, 128], BF16, "stg")
    cd = P.dsem()
    for j in range(4):
        B.load("sp", cw[:, j, :], W["ssd_conv_w"][l, j].rearrange("(c p) -> p c", p=128), cd)
    B.load("sp", cbv, W["ssd_conv_b"][l].rearrange("(c p) -> p c", p=128), cd)
    B.load("pool", identb, C["ident"][:, :], cd)
    trb = _Cycle([0, 1, 2, 3])
    for c in range(12):
        eng = "dve" if c % 2 == 0 else "pool"
        xr, xrd = xraw.next()
        B.load("sp", xr, S["xbc"][c * 128:(c + 1) * 128, :].rearrange("p (s t) -> p s t", s=2), xrd)
        xp, ac = xpad[c % 2], acc[c % 2]
        B.memset(eng, xp[:, 0, 0:2], 0.0)
        B.memset(eng, xp[:, 1, SEG + 2:SEG + 3], 0.0)
        B.copy(eng, xp[:, :, 2:2 + SEG], xr)
        B.ts(eng, xp[:, 0, SEG + 2:SEG + 3], xr[:, 1, 0:1], B.link[:, 0:1], None, op0=ALU.mult)
        B.ts(eng, xp[:, 1, 0:2], xr[:, 0, SEG - 2:SEG], B.link[:, 0:1], None, op0=ALU.mult)
        B.ts(eng, ac, xp[:, :, 0:SEG], cw[:, 0, c:c + 1], cbv[:, c:c + 1], op0=ALU.mult, op1=ALU.add)
        for j in range(1, 4):
            B.stt(eng, ac, xp[:, :, j:j + SEG], cw[:, j, c:c + 1], ac, ALU.mult, ALU.add)
        xo, xod = xc.next()
        B.act(xo.v(lambda a: a.rearrange("p (s t) -> p s t", s=2)), ac, AF.Silu)
        if c >= 8:
            B.store("sp", S["bcT"][(c - 8) * 128:(c - 7) * 128, :], xo, xod)
        if c < 10:
            for i0 in range(0, NT, 8):
                n = min(8, NT - i0)
                bk = trb.next()
                for i in range(n):
                    B.tr(pbb[bk][:, i * 128:(i + 1) * 128], xo[:, (i0 + i) * 128:(i0 + i + 1) * 128], identb)
                sg_, sgd = stg.next()
                B.copy("act" if (i0 // 8) % 2 == 0 else "dve", sg_[:, 0:n, :],
                       pbb[bk][:, 0:n * 128].v(lambda a: a.rearrange("p (i c) -> p i c", c=128)))
                B.store("sp", S["xsb"][i0 * 128:(i0 + n) * 128, c * 128:(c + 1) * 128].rearrange("(n p) c -> p n c", p=128),
                        sg_[:, 0:n, :], sgd)
    P.barrier()
    B.reset_arena()
    tri = B.alloc([5, 128], F32, "tri", const=True)
    trib = B.alloc([2, 128], BF16, "trib", const=True)
    identb = B.alloc([128], BF16, "identb", const=True)
    dt = B.alloc([NT, 32], F32, "dt")
    da = B.alloc([NT, 32], F32, "da")
    dtb = B.alloc([32], F32, "dtb", const=True)
    av = B.alloc([32], F32, "av")
    dvec = B.alloc([16], F32, "dvec", const=True)
    nrmw = B.alloc([1024], F32, "nrmw", const=True)
    stats = B.alloc([NT, 5, 32], F32, "stats")
    est = B.alloc([NT, 5, 32], F32, "est")
    coef = B.alloc([NT, 32], F32, "coef")
    cd = P.dsem()
    B.load("sp", tri, C["tri"].rearrange("j p c -> p j c"), cd)
    B.load("pool", trib, C["tri"][0:2].rearrange("j p c -> p j c"), cd)
    B.load("pool", identb, C["ident"][:, :], cd)
    B.load("sp", dt, S["dt"].rearrange("(n p) c -> p n c", p=128), cd)
    B.load("sp", dtb, W["ssd_dt_bias"][l].rearrange("a b -> (a b)").partition_broadcast(128), cd)
    B.load("sp", av, W["ssd_a_log"][l].rearrange("a b -> (a b)").partition_broadcast(128), cd)
    B.load("sp", dvec, W["ssd_d"][l].partition_broadcast(128), cd)
    B.load("sp", nrmw, W["ssd_norm"][l].partition_broadcast(128), cd)
    bc_nt = lambda a: a.unsqueeze(1).broadcast_to([128, NT, 32])
    B.tt("dve", dt, dt, dtb.v(bc_nt), ALU.add)
    B.act(dt, dt, AF.Exp)
    B.act(dt, dt, AF.Ln, bias=B.one_col[:, 0:1])
    B.act(av, av, AF.Exp)
    B.ts("dve", av, av, -1.0, None, op0=ALU.mult)
    B.tt("dve", da, dt, av.v(bc_nt), ALU.mult)
    sbk = _Cycle([0, 1, 2, 3])
    for c in range(NT):
        bk = sbk.next()
        for j in range(5):
            B.mm(pb[bk][:, j * 32:(j + 1) * 32], tri[:, j, :], da[:, c, :])
        B.copy("dve" if c % 2 == 0 else "act", stats[:, c].v(lambda a: a.rearrange("p j h -> p (j h)")), pb[bk][:, 0:160])
    B.act(est, stats, AF.Exp)
    B.tt("dve", coef[:, :, 0:16], dt[:, :, 0:16], est[:, :, 2, 0:16], ALU.mult)
    B.tt("dve", coef[:, :, 16:32], dt[:, :, 16:32], est[:, :, 3, 16:32], ALU.mult)
    B.dbg("est", est); B.dbg("dt", dt); B.dbg("da", da); B.dbg("coef", coef)
    mark = B.off
    xsr = B.ring(3, [1280], BF16, "xs")
    xdw = B.ring(2, [1024], BF16, "xdw")
    Hst = [B.alloc([1024], F32, f"H{i}") for i in range(2)]
    Hsv = B.ring(3, [1024], BF16, "Hsv")
    B.memset("dve", Hst[0], 0.0)
    B.memset("dve", Hst[1], 0.0)
    stb = _Cycle([4, 5, 6, 7])
    for i in range(NT):
        for d in range(2):
            c = i if d == 0 else NT - 1 - i
            xs, xsd = xsr.next()
            B.load("sp", xs, S["xsb"][c * 128:(c + 1) * 128, :], xsd)
            xw, _ = xdw.next()
            B.tt("dve", xw.v(lambda a: a.rearrange("p (h q) -> p h q", q=64)),
                 xs[:, 0:1024].v(lambda a: a.rearrange("p (h q) -> p h q", q=64)),
                 coef[:, c, d * 16:(d + 1) * 16].v(bc3(64)), ALU.mult)
            H = Hst[d]
            if (d == 0 and c == NT // 2) or (d == 1 and c == NT // 2 - 1):
                B.ts("dve", H, H, B.link[:, 0:1], None, op0=ALU.mult)
            hs, hsd = Hsv.next()
            B.copy("act", hs, H)
            B.store("sp", S["H"][d, c], hs, hsd)
            B.tt("dve", H.v(lambda a: a.rearrange("p (h q) -> p h q", q=64)),
                 H.v(lambda a: a.rearrange("p (h q) -> p h q", q=64)),
                 est[:, c, 4, d * 16:(d + 1) * 16].v(bc3(64)), ALU.mult)
            for g in range(2):
                bk = stb.next()
                B.mm(pb[bk], xs[:, 1024 + g * 128:1024 + (g + 1) * 128], xw[:, g * 512:(g + 1) * 512])
                B.tt("dve", H[:, g * 512:(g + 1) * 512], H[:, g * 512:(g + 1) * 512], pb[bk], ALU.add)
    P.barrier()
    B.off = mark
    xsr = B.ring(2, [1024], BF16, "xs")
    bct = B.ring(2, [4, 128], BF16, "bct")
    Hr = B.ring(2, [2, 1024], BF16, "Hr")
    szr = B.ring(2, [1024], BF16, "sz")
    Dm = B.ring(3, [16, 128], F32, "Dm")
    cbm = B.ring(2, [2, 2, 128], BF16, "cbm")
    Lx = B.ring(3, [4, 128], BF16, "Lx")
    Mt = B.ring(3, [4, 128], BF16, "Mt")
    xdr = B.ring(3, [1024], BF16, "xd")
    yo = B.ring(2, [1024], F32, "yo")
    yo2 = B.alloc([1024], F32, "yo2")
    t3 = B.alloc([1024], F32, "t3")
    sqj = B.alloc([512], BF16, "sqj")
    ss = B.ring(2, [4], F32, "ss")
    yn = B.ring(2, [1024], BF16, "yn")
    stg = B.ring(2, [8, 128], BF16, "stg")
    cbk = _Cycle([0, 1])
    sgk = _Cycle([2, 3])
    ydk = [4, 5]
    yfk = _Cycle([6, 7])
    hq = lambda a: a.rearrange("p (h q) -> p h q", q=64)
    for c in range(NT):
        rows = slice(c * 128, (c + 1) * 128)
        xs, xsd = xsr.next()
        B.load("sp", xs, S["xsb"][rows, 0:1024], xsd)
        bt, btd = bct.next()
        B.load("sp", bt, S["bcT"][:, rows].rearrange("(j p) t -> p j t", p=128), btd)
        Hc, Hd = Hr.next()
        B.load("sp", Hc[:, 0], S["H"][0, c], Hd)
        B.load("sp", Hc[:, 1], S["H"][1, c], Hd)
        sz, szd = szr.next()
        B.load("sp", sz, S["sz"][rows, :], szd)
        bk = cbk.next()
        for g in range(2):
            B.mm(pb[bk][:, g * 128:(g + 1) * 128], bt[:, g, :], bt[:, 2 + g, :])
        cm, _ = cbm.next()
        for d in range(2):
            B.tt("dve", cm[:, d], pb[bk][:, 0:256].v(lambda a: a.rearrange("p (g l) -> p g l", g=2)),
                 trib[:, d, :].v(lambda a: a.unsqueeze(1).broadcast_to([128, 2, 128])), ALU.mult)
        yoc, _ = yo.next()
        for d in range(2):
            dm, _ = Dm.next()
            B.tt("pool", dm, tri[:, d, :].v(lambda a: a.unsqueeze(1).broadcast_to([128, 16, 128])),
                 da[:, c, d * 16:(d + 1) * 16].v(bc3(128)), ALU.mult)
            xd, _ = xdr.next()
            B.tt("dve", xd.v(hq), xs.v(hq), dt[:, c, d * 16:(d + 1) * 16].v(bc3(64)), ALU.mult)
            for q in range(4):
                g = q // 2
                sk = sgk.next()
                B.mm(pb[sk], tri[:, 2 + d, :], dm[:, q * 4:(q + 1) * 4, :].v(lambda a: a.rearrange("p h l -> p (h l)")))
                lx, _ = Lx.next()
                B.act(lx.v(lambda a: a.rearrange("p h l -> p (h l)")), pb[sk], AF.Exp)
                mt, _ = Mt.next()
                B.tt("dve", mt, lx, cm[:, d, g, :].v(lambda a: a.unsqueeze(1).broadcast_to([128, 4, 128])), ALU.mult)
                for hh in range(4):
                    hd = q * 4 + hh
                    B.mm(pb[ydk[g]][:, (hd % 8) * 64:(hd % 8 + 1) * 64], mt[:, hh, :], xd[:, hd * 64:(hd + 1) * 64],
                         start=(d == 0 and hd % 8 == 0), stop=(d == 1 and hd % 8 == 7), skip=True)
            for g in range(2):
                fk = yfk.next()
                B.mm(pb[fk], bt[:, 2 + g, :], Hc[:, d, g * 512:(g + 1) * 512])
                dst = (yoc if d == 0 else yo2)[:, g * 512:(g + 1) * 512]
                eai = est[:, c, d, d * 16 + g * 8:d * 16 + (g + 1) * 8]
                B.tt("dve", dst.v(hq), pb[fk].v(hq), eai.v(bc3(64)), ALU.mult)
        B.tt("pool", t3.v(hq), xs.v(hq), dvec.v(bc3(64)), ALU.mult)
        B.tt("dve", yoc, yoc, yo2, ALU.add)
        for g in range(2):
            B.tt("dve", yoc[:, g * 512:(g + 1) * 512], yoc[:, g * 512:(g + 1) * 512], pb[ydk[g]], ALU.add)
        B.tt("dve", yoc, yoc, t3, ALU.add)
        B.tt("dve", yoc, yoc, sz, ALU.mult)
        st, _ = ss.next()
        for g in range(2):
            B.act(sqj, yoc[:, g * 512:(g + 1) * 512], AF.Square, accum=st[:, g:g + 1])
        B.rsqrt_act(st[:, 2:4], st[:, 0:2], 1.0 / 512, EPS)
        ynt, _ = yn.next()
        for g in range(2):
            B.stt("dve" if g == 0 else "pool", ynt[:, g * 512:(g + 1) * 512], yoc[:, g * 512:(g + 1) * 512],
                  st[:, 2 + g:3 + g], nrmw[:, g * 512:(g + 1) * 512], ALU.mult, ALU.mult)
        bk = cbk.next()
        for k in range(8):
            B.tr(pbb[bk][:, k * 128:(k + 1) * 128], ynt[:, k * 128:(k + 1) * 128], identb)
        sg_, sgd = stg.next()
        B.copy("act", sg_, pbb[bk].v(lambda a: a.rearrange("p (k t) -> p k t", k=8)))
        B.store("sp", S["br_ssd"][:, rows].rearrange("(k p) t -> p k t", p=128), sg_, sgd)


class _Cycle:
    def __init__(self, items):
        self.items = items
        self.i = -1

    def next(self):
        self.i = (self.i + 1) % len(self.items)
        return self.items[self.i]


def host_consts(T, link):
    SEG = T // 2
    seqlen = T if link else SEG
    pos = np.arange(T) if link else np.concatenate([np.arange(SEG), np.arange(SEG)])
    out = {}

    def tables(rot_dim):
        half = rot_dim // 2
        inv = np.power(np.float32(500000.0), -np.arange(half, dtype=np.float32) * np.float32(2.0) / np.float32(rot_dim)).astype(np.float32)
        ang = pos.astype(np.float32)[:, None] * inv[None, :]
        c = np.cos(ang).astype(np.float32).T
        s = np.sin(ang).astype(np.float32).T
        return np.stack([np.concatenate([c, c], 0), np.concatenate([s, s], 0)], 0).astype(np.float32)

    out["c_cs64"] = np.ascontiguousarray(tables(64))
    out["c_cs32"] = np.ascontiguousarray(tables(32))
    rc = np.zeros((4, T), np.float32)
    for i, w in enumerate((2, 4, 8, 16)):
        lo = w // 2
        hi = w - 1 - lo
        start = np.clip(pos - lo, 0, seqlen)
        end = np.clip(pos + hi + 1, 0, seqlen)
        rc[i] = 1.0 / (end - start).astype(np.float32)
    out["c_rcnt"] = rc
    t = np.arange(128)[:, None]
    l_ = np.arange(128)[None, :]
    out["c_tri"] = np.stack([(t <= l_), (t >= l_), (t > l_), (t < l_), np.ones((128, 128), bool)], 0).astype(np.float32)

    def rot(n):
        h = n // 2
        m = np.zeros((n, n), np.float32)
        for i in range(h):
            m[i + h, i] = -1.0
            m[i, i + h] = 1.0
        return m

    out["c_rot64"] = rot(64)
    out["c_rot32"] = rot(32)
    out["c_ident"] = np.eye(128, dtype=np.float32)
    out["c_link"] = np.full((128, 1), 1.0 if link else 0.0, np.float32)
    out["c_cbias"] = np.full((128, 1), 0.0 if link else NEG, np.float32)
    return out


_CACHE = {}


def kernel(**inputs):
    T = 4096
    xp = np.asarray(inputs["x_prompt"], dtype=np.float32)
    xs = np.asarray(inputs["x_sample"], dtype=np.float32)
    slots = []
    for c in range(8):
        if c < 2:
            slots.append((np.ascontiguousarray(xs[c]), 1))
        elif c < 6:
            i = (c - 2) * 2
            slots.append((np.ascontiguousarray(np.concatenate([xp[i], xp[i + 1]], axis=0)), 0))
        else:
            slots.append((np.zeros((T, D), np.float32), 0))
    if T not in _CACHE:
        _CACHE[T] = build(T)
    B = _CACHE[T]
    wts = {n: np.ascontiguousarray(np.asarray(inputs[n], dtype=np.float32)) for n, _ in W_NAMES}
    hc = {1: host_consts(T, 1), 0: host_consts(T, 0)}
    in_maps = []
    for x, link in slots:
        m = {"x": x}
        m.update(wts)
        m.update(hc[link])
        in_maps.append(m)
    res = run_bass_kernel_spmd(B.nc, in_maps, core_ids=list(range(8)))
    ys = [np.asarray(r["y"], dtype=np.float32) for r in res.results]
    y_sample = np.stack([ys[0], ys[1]], axis=0)
    yp = []
    for c in range(2, 6):
        yp.append(ys[c][:2048])
        yp.append(ys[c][2048:])
    y_prompt = np.stack(yp, axis=0)
    return (y_prompt, y_sample)
```

```python
import contextlib
import math
import numpy as np
import concourse.bass as bass
import concourse.mybir as mybir
from concourse.bass_utils import run_bass_kernel_spmd

F32 = mybir.dt.float32
BF16 = mybir.dt.bfloat16
AF = mybir.ActivationFunctionType
ALU = mybir.AluOpType
AX = mybir.AxisListType

D = 2048
BW = 1024
IN_DIM = 18784
DEPTH = 2
EPS = 1e-6
SEM_CAP = 32000
NEG = -30000.0


class Buf:
    __slots__ = ("name", "last_w", "readers", "const")

    def __init__(self, name="", const=False):
        self.name = name
        self.last_w = None
        self.readers = []
        self.const = const


class TL:
    __slots__ = ("ap", "buf")

    def __init__(self, ap, buf):
        self.ap = ap
        self.buf = buf

    def __getitem__(self, k):
        return TL(self.ap[k], self.buf)

    def v(self, fn):
        return TL(fn(self.ap), self.buf)


class Op:
    __slots__ = ("eng", "fn", "deps", "signal", "sigval", "is_dma", "dsem", "dtarget")

    def __init__(self, eng, fn, is_dma=False):
        self.eng = eng
        self.fn = fn
        self.deps = []
        self.signal = False
        self.sigval = None
        self.is_dma = is_dma
        self.dsem = None
        self.dtarget = None


class DmaSem:
    def __init__(self):
        self.count = 0


class Prog:
    ENGS = ("pe", "act", "dve", "pool", "sp")

    def __init__(self, nc):
        self.nc = nc
        self.eng_obj = {"pe": nc.tensor, "act": nc.scalar, "dve": nc.vector,
                        "pool": nc.gpsimd, "sp": nc.sync}
        self.ops = {e: [] for e in self.ENGS}
        self.dma_last = {}
        self.dsem_pool = []
        self.dsem_i = 0

    def dsem(self):
        if self.dsem_i >= len(self.dsem_pool):
            self.dsem_pool.append(DmaSem())
        s = self.dsem_pool[self.dsem_i]
        self.dsem_i += 1
        return s

    def _track(self, o, reads, writes, nowaw=False):
        seen = set()
        for b in reads:
            w = b.last_w
            if w is not None and id(w) not in seen:
                o.deps.append((w, "raw", w.dsem.count if w.is_dma else 0))
                seen.add(id(w))
        for b in writes:
            w = b.last_w
            if w is not None and id(w) not in seen:
                if not (nowaw and w.is_dma and w.dsem is o.dsem):
                    o.deps.append((w, "waw", w.dsem.count if w.is_dma else 0))
                    seen.add(id(w))
            for r in b.readers:
                if id(r) not in seen:
                    o.deps.append((r, "war", r.dsem.count if r.is_dma else 0))
                    seen.add(id(r))
        for b in reads:
            if not b.const:
                b.readers.append(o)
        for b in writes:
            b.last_w = o
            b.readers = []
        self.ops[o.eng].append(o)

    def op(self, eng, fn, reads=(), writes=()):
        o = Op(eng, fn)
        self._track(o, [t.buf for t in reads], [t.buf for t in writes])
        return o

    def dma(self, eng, out, in_, dsem, reads=(), writes=()):
        nc = self.nc
        eo = self.eng_obj[eng]
        oa = out.ap if isinstance(out, TL) else out
        ia = in_.ap if isinstance(in_, TL) else in_
        o = Op(eng, lambda: eo.dma_start(out=oa, in_=ia, allow_slow_non_contiguous=True), is_dma=True)
        o.dsem = dsem
        o.dtarget = dsem.count + 1
        rd = [t.buf for t in reads] + ([in_.buf] if isinstance(in_, TL) else [])
        wr = [t.buf for t in writes] + ([out.buf] if isinstance(out, TL) else [])
        self._track(o, rd, wr, nowaw=True)
        dsem.count += 1
        self.dma_last[id(dsem)] = o
        return o

    def barrier(self):
        lasts = []
        for e in self.ENGS:
            for o in reversed(self.ops[e]):
                if not o.is_dma and o.fn is not None:
                    lasts.append(o)
                    break
        lasts += list(self.dma_last.values())
        for e in self.ENGS:
            o = Op(e, None)
            for l in lasts:
                o.deps.append((l, "raw", l.dsem.count if l.is_dma else 0))
            self.ops[e].append(o)
        self.dsem_i = 0

    @staticmethod
    def _needs_sem(p, c, kind):
        if p.is_dma:
            return True
        if p.eng == c.eng:
            if p.eng in ("pe", "sp"):
                return False
            return kind == "raw"
        return True

    def emit(self, es):
        nc = self.nc
        for e in self.ENGS:
            for c in self.ops[e]:
                for (p, kind, cnt) in c.deps:
                    if not p.is_dma and self._needs_sem(p, c, kind):
                        p.signal = True
        pool = {}

        def getsem(key):
            if key not in pool:
                pool[key] = es.enter_context(nc.semaphore(f"s{len(pool)}"))
            return pool[key]

        for e in self.ENGS:
            k = 0
            for o in self.ops[e]:
                if not o.is_dma and o.signal:
                    o.sigval = k
                    k += 1
        dcap = SEM_CAP // 16
        nw = 0
        ni = 0
        for e in self.ENGS:
            eng = self.eng_obj[e]
            waited = {}
            for o in self.ops[e]:
                need = {}
                for (p, kind, cnt) in o.deps:
                    if not self._needs_sem(p, o, kind):
                        continue
                    if p.is_dma:
                        tot = cnt
                        key = ("d", id(p.dsem), (tot - 1) // dcap)
                        v = ((tot - 1) % dcap + 1) * 16
                    else:
                        key = ("c", p.eng, p.sigval // SEM_CAP)
                        v = p.sigval % SEM_CAP + 1
                    if need.get(key, 0) < v:
                        need[key] = v
                for key, v in need.items():
                    if waited.get(key, 0) >= v:
                        continue
                    waited[key] = v
                    eng.wait_ge(getsem(key), v)
                    nw += 1
                if o.fn is None:
                    continue
                ins = o.fn()
                ni += 1
                if o.is_dma:
                    ins.then_inc(getsem(("d", id(o.dsem), (o.dtarget - 1) // dcap)), 16)
                elif o.signal:
                    ins.then_inc(getsem(("c", o.eng, o.sigval // SEM_CAP)), 1)
        self.stats = dict(waits=nw, insts=ni, sems=len(pool))
        return self.stats


IN_GROUPS = [
    ("cq", 0, 512, "F", "lat"), ("ckv", 512, 256, "F", "lat"), ("kr", 768, 64, "F", "lat"),
    ("g_mla", 832, 1024, "F", "silu"), ("dq", 1856, 1024, "F", "rope"), ("dk", 2880, 1024, "F", "rope"),
    ("dv", 3904, 1024, "T", "copy"), ("g_diff", 4928, 1024, "F", "silu"), ("z", 5952, 1024, "T", "silu"),
    ("xbc", 6976, 1536, "F", "copy"), ("dt", 8512, 32, "T", "copyf"), ("u", 8544, 1024, "F", "copy"),
    ("g_pool", 9568, 1024, "F", "silu"), ("mg", 10592, 8192, "F", "sigmoid"),
]


class Builder:
    def __init__(self, T, debug=False, nlayers=DEPTH, phases=None):
        self.T = T
        self.SEG = T // 2
        self.NT = T // 128
        self.NB = T // 512
        self.debug = debug
        self.nlayers = nlayers
        self.phases = phases
        self.nc = bass.Bass("TRN2", target_bir_lowering=False)
        self.P = Prog(self.nc)
        self.es = contextlib.ExitStack()

    def dram_in(self, name, shape, dt=F32):
        return self.nc.dram_tensor(name, list(shape), dt, kind="ExternalInput").ap()

    def dram_out(self, name, shape, dt=F32):
        return self.nc.dram_tensor(name, list(shape), dt, kind="ExternalOutput").ap()

    def dram_scr(self, name, shape, dt):
        kind = "ExternalOutput" if self.debug else "Internal"
        return self.nc.dram_tensor(name, list(shape), dt, kind=kind).ap()

    def reset_arena(self):
        self.off = 0

    def alloc(self, shape, dt, name="", const=False):
        n = int(np.prod(shape))
        nbytes = n * (4 if dt == F32 else 2)
        nw = (nbytes + 3) // 4
        nw = (nw + 7) // 8 * 8
        assert self.off + nw <= self.BIGW, f"SBUF arena overflow {name} {self.off + nw}"
        ap = self.big[:, self.off:self.off + nw]
        self.off += nw
        if dt == BF16:
            ap = ap.bitcast(BF16)[:, 0:n]
        else:
            ap = ap[:, 0:n]
        if len(shape) > 1:
            names = " ".join(f"a{i}" for i in range(len(shape)))
            kw = {f"a{i}": int(s) for i, s in enumerate(shape)}
            ap = ap.rearrange(f"p ({names}) -> p {names}", **kw)
        return TL(ap, Buf(name, const=const))

    def ring(self, n, shape, dt, name=""):
        tl = [self.alloc(shape, dt, f"{name}{i}") for i in range(n)]
        ds = [self.P.dsem() for _ in range(n)]
        return _Ring(tl, ds)

    def psum_tl(self, i, dt=F32):
        ap = self.banks[i][:]
        if dt == BF16:
            ap = ap.bitcast(BF16)
        return TL(ap, self.bank_bufs[i])

    def mm(self, out, lhsT, rhs, start=True, stop=True, skip=False):
        nc = self.nc
        o, a, b = out.ap, lhsT.ap, rhs.ap
        if skip:
            return self.P.op("pe", lambda: nc.tensor.matmul(o, a, b, start=start, stop=stop, skip_group_check=True),
                             reads=[lhsT, rhs], writes=[out])
        return self.P.op("pe", lambda: nc.tensor.matmul(o, a, b, start=start, stop=stop),
                         reads=[lhsT, rhs], writes=[out])

    def tr(self, out, in_, ident):
        nc = self.nc
        o, a, b = out.ap, in_.ap, ident.ap
        return self.P.op("pe", lambda: nc.tensor.transpose(o, a, b), reads=[in_, ident], writes=[out])

    def act(self, out, in_, func, bias=None, scale=1.0, accum=None, eng="act"):
        nc = self.nc
        o, a = out.ap, in_.ap
        kw = {}
        rd = [in_]
        wr = [out]
        if bias is not None:
            if isinstance(bias, TL):
                kw["bias"] = bias.ap
                rd.append(bias)
            else:
                kw["bias"] = float(bias)
        if isinstance(scale, TL):
            kw["scale"] = scale.ap
            rd.append(scale)
        else:
            kw["scale"] = float(scale)
        if accum is not None:
            kw["accum_out"] = accum.ap
            wr.append(accum)
        return self.P.op("act", lambda: nc.scalar.activation(out=o, in_=a, func=func, **kw), reads=rd, writes=wr)

    def _e(self, eng):
        return self.P.eng_obj[eng]

    def tt(self, eng, out, in0, in1, op):
        e = self._e(eng)
        o, a, b = out.ap, in0.ap, in1.ap
        return self.P.op(eng, lambda: e.tensor_tensor(out=o, in0=a, in1=b, op=op), reads=[in0, in1], writes=[out])

    def ts(self, eng, out, in0, s1, s2=None, op0=ALU.mult, op1=None):
        e = self._e(eng)
        o, a = out.ap, in0.ap
        rd = [in0]
        v1 = s1.ap if isinstance(s1, TL) else float(s1)
        if isinstance(s1, TL):
            rd.append(s1)
        v2 = None
        if s2 is not None:
            v2 = s2.ap if isinstance(s2, TL) else float(s2)
            if isinstance(s2, TL):
                rd.append(s2)
        if op1 is None:
            return self.P.op(eng, lambda: e.tensor_scalar(out=o, in0=a, scalar1=v1, scalar2=None, op0=op0),
                             reads=rd, writes=[out])
        return self.P.op(eng, lambda: e.tensor_scalar(out=o, in0=a, scalar1=v1, scalar2=v2, op0=op0, op1=op1),
                         reads=rd, writes=[out])

    def stt(self, eng, out, in0, scalar, in1, op0, op1):
        eng = "dve"
        e = self._e(eng)
        o, a, b = out.ap, in0.ap, in1.ap
        rd = [in0, in1]
        sv = scalar.ap if isinstance(scalar, TL) else float(scalar)
        if isinstance(scalar, TL):
            rd.append(scalar)
        return self.P.op(eng, lambda: e.scalar_tensor_tensor(out=o, in0=a, scalar=sv, in1=b, op0=op0, op1=op1),
                         reads=rd, writes=[out])

    def copy(self, eng, out, in_):
        if eng == "act":
            return self.act(out, in_, AF.Copy)
        e = self._e(eng)
        o, a = out.ap, in_.ap
        return self.P.op(eng, lambda: e.tensor_copy(o, a), reads=[in_], writes=[out])

    def recip(self, out, in_):
        nc = self.nc
        o, a = out.ap, in_.ap
        return self.P.op("dve", lambda: nc.vector.reciprocal(o, a), reads=[in_], writes=[out])

    def memset(self, eng, out, val):
        e = self._e(eng)
        o = out.ap
        return self.P.op(eng, lambda: e.memset(o, float(val)), writes=[out])

    def load(self, eng, dst, src_ap, dsem):
        return self.P.dma(eng, dst, src_ap, dsem)

    def store(self, eng, dst_ap, src, dsem):
        return self.P.dma(eng, dst_ap, src, dsem)

    def dbg(self, name, tl, parts=128):
        if not self.debug:
            return
        shp = [parts] + list(tl.ap.shape[1:])
        d = self.nc.dram_tensor("dbg_" + name, shp, tl.ap.dtype, kind="ExternalOutput").ap()
        self.P.dma("sp", d, tl[0:parts], self.P.dsem())

    def rsqrt_act(self, out, in_, scale, eps):
        self.act(out, in_, AF.Ln, bias=self.eps_col[:, 0:1] if eps == EPS else eps, scale=scale)
        self.act(out, out, AF.Exp, scale=-0.5)


class _Ring:
    def __init__(self, tl, ds):
        self.tl = tl
        self.ds = ds
        self.i = -1

    def next(self):
        self.i = (self.i + 1) % len(self.tl)
        return self.tl[self.i], self.ds[self.i]


W_NAMES = [
    ("norm_w", (DEPTH, D)), ("w_in", (DEPTH, D, IN_DIM)), ("mla_q_norm", (DEPTH, 512)),
    ("mla_w_uq", (DEPTH, 512, 1536)), ("mla_kv_norm", (DEPTH, 256)), ("mla_w_ukv", (DEPTH, 256, 2048)),
    ("diff_lambda", (DEPTH, 4, 128)), ("diff_subln", (DEPTH, 256)), ("ssd_conv_w", (DEPTH, 4, 1536)),
    ("ssd_conv_b", (DEPTH, 1536)), ("ssd_dt_bias", (DEPTH, 2, 16)), ("ssd_a_log", (DEPTH, 2, 16)),
    ("ssd_d", (DEPTH, 16)), ("ssd_norm", (DEPTH, 1024)), ("pool_w", (DEPTH, 4, 256, 256)),
    ("pool_scale", (DEPTH, 1024)), ("w_branch", (DEPTH, 4, 1024, D)), ("w_out", (DEPTH, D, D)),
    ("final_norm", (D,)),
]


def build(T, debug=False, nlayers=DEPTH, phases=None):
    B = Builder(T, debug, nlayers, phases)
    nc, P, es = B.nc, B.P, B.es
    NT, NB, SEG = B.NT, B.NB, B.SEG
    x_in = B.dram_in("x", (T, D))
    W = {n: B.dram_in(n, s) for n, s in W_NAMES}
    c_link = B.dram_in("c_link", (128, 1))
    c_cbias = B.dram_in("c_cbias", (128, 1))
    c_ident = B.dram_in("c_ident", (128, 128))
    c_tri = B.dram_in("c_tri", (5, 128, 128))
    c_rot64 = B.dram_in("c_rot64", (64, 64))
    c_rot32 = B.dram_in("c_rot32", (32, 32))
    c_cs64 = B.dram_in("c_cs64", (2, 64, T))
    c_cs32 = B.dram_in("c_cs32", (2, 32, T))
    c_rcnt = B.dram_in("c_rcnt", (4, T))
    y_out = B.dram_out("y", (T, D))
    S = {}
    S["x1"] = B.dram_scr("s_x1", (T, D), F32)
    for nm, f in [("cqn", 512), ("ckvn", 256), ("kpe", 64), ("sg_mla", 1024), ("dq", 1024), ("dk", 1024),
                  ("sg_diff", 1024), ("xbc", 1536), ("u", 1024), ("sg_pool", 1024), ("mg", 8192),
                  ("br_mla", 1024), ("br_diff", 1024), ("br_ssd", 1024), ("br_pool", 1024)]:
        S[nm] = B.dram_scr("s_" + nm, (f, T), BF16)
    S["dv"] = B.dram_scr("s_dv", (T, 1024), BF16)
    S["sz"] = B.dram_scr("s_sz", (T, 1024), BF16)
    S["dt"] = B.dram_scr("s_dt", (T, 32), F32)
    S["xsb"] = B.dram_scr("s_xsb", (T, 1280), BF16)
    S["bcT"] = B.dram_scr("s_bcT", (512, T), BF16)
    S["H"] = B.dram_scr("s_H", (2, T // 128, 128, 1024), BF16)
    S["mT"] = B.dram_scr("s_mT", (D, T), BF16)
    B.S = S

    B.BIGW = 49152 - 1024
    B.big = es.enter_context(nc.sbuf_tensor("big", [128, B.BIGW + 64], F32))
    B.banks = [es.enter_context(nc.psum_tensor(f"ps{i}", [128, 512], F32)) for i in range(8)]
    B.bank_bufs = [Buf(f"ps{i}") for i in range(8)]
    cbase = B.BIGW
    B.eps_col = TL(B.big[:, cbase:cbase + 1], Buf("eps", const=True))
    B.link = TL(B.big[:, cbase + 1:cbase + 2], Buf("link", const=True))
    B.cbias = TL(B.big[:, cbase + 2:cbase + 3], Buf("cbias", const=True))
    B.zero_col = TL(B.big[:, cbase + 3:cbase + 4], Buf("zero", const=True))
    B.one_col = TL(B.big[:, cbase + 4:cbase + 5], Buf("one", const=True))
    B.memset("dve", B.eps_col, EPS)
    B.memset("dve", B.zero_col, 0.0)
    B.memset("dve", B.one_col, 1.0)
    ds0 = DmaSem()
    B.load("sp", B.link, c_link[:, :], ds0)
    B.load("sp", B.cbias, c_cbias[:, :], ds0)
    P.barrier()

    consts = dict(ident=c_ident, tri=c_tri, rot64=c_rot64, rot32=c_rot32, cs64=c_cs64, cs32=c_cs32, rcnt=c_rcnt)
    for l in range(nlayers):
        x_src = x_in if l == 0 else S["x1"]
        last = (l == nlayers - 1)
        if phases is None or "A" in phases:
            phase_A(B, l, x_src, W, consts)
            P.barrier()
        if phases is None or "B" in phases:
            phase_B(B, l, W, consts)
            P.barrier()
        if phases is None or "C" in phases:
            phase_C(B, l, W, consts)
            P.barrier()
        if phases is None or "D" in phases:
            phase_D(B, l, W, consts)
            P.barrier()
        if phases is None or "E" in phases:
            phase_E(B, l, W, consts)
            P.barrier()
        if phases is None or "F" in phases:
            phase_F(B, l, x_src, y_out if last else S["x1"], W, consts, last)
            P.barrier()
    P.barrier()
    st = P.emit(es)
    B.stats = st
    return B


def phase_A(B, l, x_src, W, C):
    nc, P, S, T = B.nc, B.P, B.S, B.T
    B.reset_arena()
    TBA = min(1024, T)
    nblk = T // TBA
    nsub = TBA // 512
    ntile = TBA // 128
    CWMAX = 544
    normw = B.alloc([D], F32, "normw", const=True)
    identb = B.alloc([128], BF16, "identb", const=True)
    ones_b = B.alloc([128], BF16, "ones_b", const=True)
    rot64 = B.alloc([64], F32, "rot64", const=True)
    rot32 = B.alloc([32], F32, "rot32", const=True)
    qnw = B.alloc([4], F32, "qnw", const=True)
    kvnw = B.alloc([2], F32, "kvnw", const=True)
    hT = B.alloc([16, TBA], BF16, "hT")
    xring = B.ring(2, [D], F32, "x")
    hb = B.alloc([D], BF16, "hb")
    sqj = B.alloc([D], BF16, "sqj")
    stat = B.alloc([4], F32, "stat")
    wring = B.ring(3, [16, CWMAX], BF16, "w")
    ost = B.ring(4, [TBA], BF16, "ost")
    ostT = B.ring(3, [CWMAX], BF16, "ostT")
    ostF = B.ring(2, [32], F32, "ostF")
    lat = B.alloc([7, TBA], F32, "lat")
    rf = B.ring(2, [512], F32, "rf")
    cs32 = B.alloc([2, TBA], F32, "cs32")
    cs64 = B.alloc([2, TBA], F32, "cs64")
    tmpf = B.ring(2, [512], F32, "tmpf")
    tmpb = B.ring(2, [512], BF16, "tmpb")
    cd = P.dsem()
    cd32 = P.dsem()
    cd64 = P.dsem()
    B.load("sp", normw, W["norm_w"][l].partition_broadcast(128), cd)
    B.load("pool", identb, C["ident"][:, :], cd)
    B.load("sp", rot64[0:64], C["rot64"][:, :], cd)
    B.load("sp", rot32[0:32], C["rot32"][:, :], cd)
    B.load("sp", qnw, W["mla_q_norm"][l].rearrange("(c p) -> p c", p=128), cd)
    B.load("sp", kvnw, W["mla_kv_norm"][l].rearrange("(c p) -> p c", p=128), cd)
    B.memset("dve", ones_b, 1.0)
    pb = [B.psum_tl(i) for i in range(8)]
    pbb = [B.psum_tl(i, BF16) for i in range(8)]
    mmbank = _Cycle([0, 1, 2, 3])
    auxbank = _Cycle([4, 5])
    rotbank = _Cycle([6, 7])
    w_in = W["w_in"][l].rearrange("(k p) c -> p k c", p=128)

    wtiles = []
    for (nm, c0, ncol, orient, epi) in IN_GROUPS:
        if nm in ("ckv", "kr", "dt"):
            continue
        if nm == "cq":
            wtiles.append((0, 512, [("cq", 0, 512)]))
            wtiles.append((512, 320, [("ckv", 0, 256), ("kr", 256, 64)]))
            continue
        nt_ = ncol // 512
        for j in range(nt_):
            if nm == "xbc" and j == nt_ - 1:
                wtiles.append((c0 + j * 512, 544, [("xbc", 0, 512), ("dt", 512, 32)]))
            else:
                wtiles.append((c0 + j * 512, 512, [(nm, 0, 512)]))
    ginfo = {g[0]: g for g in IN_GROUPS}
    act_toggle = [0]

    for blk in range(nblk):
        t0 = blk * TBA
        B.load("sp", cs32[0:32], C["cs32"][:, :, t0:t0 + TBA].rearrange("a p t -> p a t"), cd32)
        B.load("sp", cs64[0:64], C["cs64"][:, :, t0:t0 + TBA].rearrange("a p t -> p a t"), cd64)
        for i in range(ntile):
            xt, xd = xring.next()
            B.load("sp", xt, x_src[t0 + i * 128:t0 + (i + 1) * 128, :], xd)
            B.act(sqj, xt, AF.Square, accum=stat[:, 0:1])
            B.rsqrt_act(stat[:, 1:2], stat[:, 0:1], 1.0 / D, EPS)
            B.stt("dve", hb, xt, stat[:, 1:2], normw, ALU.mult, ALU.mult)
            for half in range(2):
                bk = auxbank.next()
                for j in range(8):
                    k = half * 8 + j
                    B.tr(pbb[bk][:, j * 128:(j + 1) * 128], hb[:, k * 128:(k + 1) * 128], identb)
                B.copy("dve" if half == 0 else "act",
                       hT[:, half * 8:(half + 1) * 8, i * 128:(i + 1) * 128],
                       pbb[bk].v(lambda a: a.rearrange("p (j c) -> p j c", j=8)))
        for (c0, cw, parts) in wtiles:
            wt, wd = wring.next()
            B.load("pool", wt[:, :, 0:cw], w_in[:, :, c0:c0 + cw], wd)
            for (nm, po, pn) in parts:
                _, gc0, gn, orient, epi = ginfo[nm]
                gcol = c0 + po - gc0
                if orient == "F":
                    for cc in range(0, pn, 128):
                        m = min(128, pn - cc)
                        feat = gcol + cc
                        if epi != "lat":
                            og, od = ost.next()
                        for sb in range(nsub):
                            bk = mmbank.next()
                            for k in range(16):
                                B.mm(pb[bk][0:m, :], wt[:, k, po + cc:po + cc + m], hT[:, k, sb * 512:(sb + 1) * 512],
                                     start=(k == 0), stop=(k == 15))
                            src = pb[bk][0:m, :]
                            if epi == "lat":
                                li = {"cq": 0, "ckv": 4, "kr": 6}[nm] + cc // 128
                                B.copy("dve", lat[0:m, li, sb * 512:(sb + 1) * 512], src)
                            elif epi == "silu":
                                B.act(og[0:m, sb * 512:(sb + 1) * 512], src, AF.Silu)
                            elif epi == "sigmoid":
                                B.act(og[0:m, sb * 512:(sb + 1) * 512], src, AF.Sigmoid)
                            elif epi == "copy":
                                act_toggle[0] ^= 1
                                B.copy("dve", og[0:m, sb * 512:(sb + 1) * 512], src)
                            elif epi == "rope":
                                r, _ = rf.next()
                                B.copy("dve", r, src)
                                rb = rotbank.next()
                                B.mm(pb[rb][0:32, :], rot32[0:32, 0:32], r[0:32, :])
                                tf, _ = tmpf.next()
                                sl = slice(sb * 512, (sb + 1) * 512)
                                B.tt("dve", tf[0:32], r[0:32], cs32[0:32, 0, sl], ALU.mult)
                                tf2, _ = tmpf.next()
                                B.tt("dve", tf2[0:32], pb[rb][0:32, :], cs32[0:32, 1, sl], ALU.mult)
                                B.copy("act", og[:, sl], r)
                                B.tt("dve", og[0:32, sl], tf[0:32], tf2[0:32], ALU.add)
                        if epi != "lat":
                            sname = {"g_mla": "sg_mla", "g_diff": "sg_diff", "g_pool": "sg_pool"}.get(nm, nm)
                            B.store("sp", S[sname][feat:feat + m, t0:t0 + TBA], og[0:m, :], od)
                else:
                    for i in range(ntile):
                        bk = mmbank.next()
                        for k in range(16):
                            B.mm(pb[bk][:, 0:pn], hT[:, k, i * 128:(i + 1) * 128], wt[:, k, po:po + pn],
                                 start=(k == 0), stop=(k == 15))
                        src = pb[bk][:, 0:pn]
                        rows = slice(t0 + i * 128, t0 + (i + 1) * 128)
                        if epi == "copyf":
                            og, od = ostF.next()
                            B.copy("dve", og[:, 0:pn], src)
                            B.store("sp", S["dt"][rows, :], og[:, 0:pn], od)
                        else:
                            og, od = ostT.next()
                            if epi == "silu":
                                B.act(og[:, 0:pn], src, AF.Silu)
                            else:
                                B.copy("dve", og[:, 0:pn], src)
                            sname = {"z": "sz"}.get(nm, nm)
                            B.store("sp", S[sname][rows, gcol:gcol + pn], og[:, 0:pn], od)
            if parts[0][0] == "ckv":
                for sb in range(nsub):
                    sl = slice(sb * 512, (sb + 1) * 512)
                    for (nm, li0, nch, wcol, dim) in (("cqn", 0, 4, qnw, 512), ("ckvn", 4, 2, kvnw, 256)):
                        bk = auxbank.next()
                        for c in range(nch):
                            tb_, _ = tmpb.next()
                            B.act(tb_, lat[:, li0 + c, sl], AF.Square)
                            B.mm(pb[bk], ones_b, tb_, start=(c == 0), stop=(c == nch - 1))
                        tf, _ = tmpf.next()
                        B.rsqrt_act(tf, pb[bk], 1.0 / dim, EPS)
                        for c in range(nch):
                            og, od = ost.next()
                            B.stt("dve", og[:, 0:512], lat[:, li0 + c, sl], wcol[:, c:c + 1], tf, ALU.mult, ALU.mult)
                            B.store("sp", S[nm][c * 128:(c + 1) * 128, t0 + sb * 512:t0 + (sb + 1) * 512], og[:, 0:512], od)
                    rb = rotbank.next()
                    B.mm(pb[rb][0:64, :], rot64[0:64, 0:64], lat[0:64, 6, sl])
                    tf, _ = tmpf.next()
                    B.tt("dve", tf[0:64], lat[0:64, 6, sl], cs64[0:64, 0, sl], ALU.mult)
                    tf2, _ = tmpf.next()
                    B.tt("dve", tf2[0:64], pb[rb][0:64, :], cs64[0:64, 1, sl], ALU.mult)
                    og, od = ost.next()
                    B.tt("dve", og[0:64, 0:512], tf[0:64], tf2[0:64], ALU.add)
                    B.store("sp", S["kpe"][0:64, t0 + sb * 512:t0 + (sb + 1) * 512], og[0:64, 0:512], od)


def attn_qblock(B, qsl, s_terms, v_list, acc_banks, den_bank, Sring, Pring, pb, ones_b, scale, NT, SEG, q0, P4ring):
    LA = 2
    DB = 4
    pend = []
    denq = []
    first = None
    p4 = None
    ng = NT // DB
    for step in range(NT + LA):
        if step < NT:
            kb = step
            sb = Sring.next()
            ksl = slice(kb * 128, (kb + 1) * 128)
            for i, (kT, qT) in enumerate(s_terms):
                B.mm(pb[sb], kT[:, ksl], qT[:, qsl], start=(i == 0), stop=(i == len(s_terms) - 1))
            cross = (q0 // SEG) != ((kb * 128) // SEG)
            pt, _ = Pring.next()
            B.act(pt, pb[sb], AF.Exp, scale=scale, bias=(B.cbias if cross else None))
            pend.append((kb, pt))
            j = kb % DB
            if j == 0:
                first = pt
            elif j == 1:
                p4, _ = P4ring.next()
                B.tt("dve", p4, first, pt, ALU.add)
            else:
                B.tt("dve", p4, p4, pt, ALU.add)
            if j == DB - 1:
                denq.append((kb // DB, p4))
        if step >= LA:
            kb, pt = pend.pop(0)
            for v, ab in zip(v_list, acc_banks):
                B.mm(pb[ab], v[:, kb, :], pt, start=(kb == 0), stop=(kb == NT - 1))
            if kb % DB == DB - 1:
                g, pp = denq.pop(0)
                B.mm(pb[den_bank], ones_b, pp, start=(g == 0), stop=(g == ng - 1))


def phase_B(B, l, W, C):
    nc, P, S, T, NT, NB, SEG = B.nc, B.P, B.S, B.T, B.NT, B.NB, B.SEG
    B.reset_arena()
    cqn = B.alloc([4, T], BF16, "cqn")
    ckvn = B.alloc([2, T], BF16, "ckvn")
    kpe = B.alloc([T], BF16, "kpe")
    wuq = B.alloc([4, 1536], BF16, "wuq", const=True)
    wukv = B.alloc([2, 2048], BF16, "wukv", const=True)
    ones_b = B.alloc([128], BF16, "ones_b", const=True)
    rot64 = B.alloc([64], F32, "rot64", const=True)
    qn = [B.alloc([T], BF16, f"qn{i}") for i in range(2)]
    qp = [B.alloc([T], BF16, f"qp{i}") for i in range(2)]
    kn = [B.alloc([T], BF16, f"kn{i}") for i in range(2)]
    vv = [B.alloc([NT, 128], BF16, f"v{i}") for i in range(2)]
    csr = B.ring(2, [2, 512], F32, "cs")
    rr = B.ring(2, [512], F32, "rr")
    tmpf = B.ring(4, [512], F32, "tmpf")
    Pring = B.ring(5, [512], BF16, "pt")
    P4ring = B.ring(3, [512], BF16, "p4")
    sgr = B.ring(2, [512], BF16, "sg")
    ost = B.ring(2, [512], BF16, "ost")
    cd = P.dsem()
    B.load("sp", cqn, S["cqn"].rearrange("(c p) t -> p c t", p=128), cd)
    B.load("sp", ckvn, S["ckvn"].rearrange("(c p) t -> p c t", p=128), cd)
    B.load("sp", kpe[0:64], S["kpe"][:, :], cd)
    B.memset("pool", kpe[64:128], 0.0)
    B.memset("pool", qp[0][64:128], 0.0)
    B.memset("pool", qp[1][64:128], 0.0)
    B.load("pool", wuq, W["mla_w_uq"][l].rearrange("(c p) n -> p c n", p=128), cd)
    B.load("pool", wukv, W["mla_w_ukv"][l].rearrange("(c p) n -> p c n", p=128), cd)
    B.load("sp", rot64[0:64], C["rot64"][:, :], cd)
    B.memset("dve", ones_b, 1.0)
    pb = [B.psum_tl(i) for i in range(8)]
    Sring = _Cycle([0, 1, 2])
    accs = _Cycle([(3, 4), (5, 6)])
    scale = float(192 ** -0.5)

    def prologue(h):
        s = h % 2
        for sb in range(NB):
            sl = slice(sb * 512, (sb + 1) * 512)
            for c in range(4):
                B.mm(pb[7], wuq[:, c, h * 192:h * 192 + 128], cqn[:, c, sl], start=(c == 0), stop=(c == 3))
            B.copy("dve", qn[s][:, sl], pb[7])
            for c in range(4):
                B.mm(pb[7][0:64], wuq[:, c, h * 192 + 128:h * 192 + 192], cqn[:, c, sl], start=(c == 0), stop=(c == 3))
            r, _ = rr.next()
            B.copy("dve", r[0:64], pb[7][0:64])
            cs, cdm = csr.next()
            B.load("sp", cs[0:64], C["cs64"][:, :, sl].rearrange("a p t -> p a t"), cdm)
            B.mm(pb[7][0:64], rot64[0:64, 0:64], r[0:64])
            tf, _ = tmpf.next()
            B.tt("dve", tf[0:64], r[0:64], cs[0:64, 0], ALU.mult)
            tf2, _ = tmpf.next()
            B.tt("dve", tf2[0:64], pb[7][0:64], cs[0:64, 1], ALU.mult)
            B.tt("dve", qp[s][0:64, sl], tf[0:64], tf2[0:64], ALU.add)
            for c in range(2):
                B.mm(pb[7], wukv[:, c, h * 256:h * 256 + 128], ckvn[:, c, sl], start=(c == 0), stop=(c == 1))
            B.copy("act", kn[s][:, sl], pb[7])
            for i in range(4):
                tsl = slice(sb * 512 + i * 128, sb * 512 + (i + 1) * 128)
                for c in range(2):
                    B.mm(pb[7][:, i * 128:(i + 1) * 128], ckvn[:, c, tsl], wukv[:, c, h * 256 + 128:h * 256 + 256],
                         start=(c == 0), stop=(c == 1))
            B.copy("act", vv[s][:, sb * 4:(sb + 1) * 4, :], pb[7].v(lambda a: a.rearrange("p (i d) -> p i d", i=4)))

    prologue(0)
    for h in range(8):
        s = h % 2
        for qb in range(NB):
            qsl = slice(qb * 512, (qb + 1) * 512)
            ab, db = accs.next()
            sg, sgd = sgr.next()
            B.load("sp", sg, S["sg_mla"][h * 128:(h + 1) * 128, qsl], sgd)
            attn_qblock(B, qsl, [(kn[s], qn[s]), (kpe, qp[s])], [vv[s]], [ab], db,
                        Sring, Pring, pb, ones_b, scale, NT, SEG, qb * 512, P4ring)
            if qb == 0 and h + 1 < 8:
                prologue(h + 1)
            rd, _ = tmpf.next()
            B.recip(rd, pb[db])
            o, _ = tmpf.next()
            B.tt("dve", o, pb[ab], rd, ALU.mult)
            og, od = ost.next()
            B.tt("dve", og, o, sg, ALU.mult)
            B.store("sp", S["br_mla"][h * 128:(h + 1) * 128, qsl], og, od)
    B.dbg("qn", qn[1]); B.dbg("qp", qp[1], 64); B.dbg("kn", kn[1]); B.dbg("vv", vv[1]); B.dbg("kpe", kpe, 64)


def phase_C(B, l, W, C):
    nc, P, S, T, NT, NB, SEG = B.nc, B.P, B.S, B.T, B.NT, B.NB, B.SEG
    B.reset_arena()
    lam_init = 0.8 - 0.6 * math.exp(-0.3 * l)
    ones_b = B.alloc([128], BF16, "ones_b", const=True)
    ones_f = B.alloc([128], F32, "ones_f", const=True)
    subw = B.alloc([2], F32, "subw", const=True)
    lp = B.alloc([512], F32, "lp")
    lt = B.alloc([256], F32, "lt")
    ls = B.alloc([8], F32, "ls")
    nlam = B.alloc([1], F32, "nlam")
    qk = [[B.alloc([T], BF16, f"qk{i}{j}") for j in range(4)] for i in range(2)]
    vv = [B.alloc([NT, 256], BF16, f"v{i}") for i in range(2)]
    hd = [P.dsem() for _ in range(2)]
    Pring = B.ring(5, [512], BF16, "pt")
    P4ring = B.ring(3, [512], BF16, "p4")
    on = [B.alloc([2, 512], F32, f"on{i}") for i in range(2)]
    oo = B.alloc([2, 512], F32, "oo")
    tmpf = B.ring(3, [512], F32, "tmpf")
    sqr = B.ring(2, [512], BF16, "sq")
    sgr = B.ring(2, [2, 512], BF16, "sg")
    ost = B.ring(2, [2, 512], BF16, "ost")
    cd = P.dsem()
    B.memset("dve", ones_b, 1.0)
    B.memset("dve", ones_f, 1.0)
    B.load("sp", subw, W["diff_subln"][l].rearrange("(c p) -> p c", p=128), cd)
    B.load("sp", lp[0:1], W["diff_lambda"][l].rearrange("(o a) d -> o (a d)", o=1), cd)
    pb = [B.psum_tl(i) for i in range(8)]
    B.tt("dve", lt[0:1, 0:128], lp[0:1, 0:128], lp[0:1, 128:256], ALU.mult)
    B.tt("dve", lt[0:1, 128:256], lp[0:1, 256:384], lp[0:1, 384:512], ALU.mult)
    e = nc.vector
    for j in range(2):
        o_, a_ = ls[0:1, j:j + 1], lt[0:1, j * 128:(j + 1) * 128]
        P.op("dve", (lambda o_=o_, a_=a_: e.reduce_sum(out=o_.ap, in_=a_.ap, axis=AX.X)), reads=[a_], writes=[o_])
    B.act(ls[0:1, 2:4], ls[0:1, 0:2], AF.Exp)
    B.tt("dve", ls[0:1, 4:5], ls[0:1, 3:4], ls[0:1, 2:3], ALU.subtract)
    B.ts("dve", ls[0:1, 5:6], ls[0:1, 4:5], -lam_init, None, op0=ALU.add)
    B.mm(pb[7][:, 0:1], ones_f[0:1, :], ls[0:1, 5:6])
    B.copy("dve", nlam, pb[7][:, 0:1])
    Sring = _Cycle([6, 7])
    accsets = _Cycle([(0, 1, 2), (3, 4, 5)])
    scale = float(128 ** -0.5)

    def loadhead(h):
        s = h % 2
        for j, (nm, r0) in enumerate((("dq", 2 * h), ("dq", 2 * h + 1), ("dk", 2 * h), ("dk", 2 * h + 1))):
            B.load("sp", qk[s][j], S[nm][r0 * 128:(r0 + 1) * 128, :], hd[s])
        B.load("sp", vv[s], S["dv"][:, h * 256:(h + 1) * 256].rearrange("(n p) c -> p n c", p=128), hd[s])

    loadhead(0)
    for h in range(4):
        s = h % 2
        for qb in range(NB):
            qsl = slice(qb * 512, (qb + 1) * 512)
            sg, sgd = sgr.next()
            B.load("sp", sg, S["sg_diff"][h * 256:(h + 1) * 256, qsl].rearrange("(c p) t -> p c t", p=128), sgd)
            for sm in range(2):
                a0, a1, db = accsets.next()
                attn_qblock(B, qsl, [(qk[s][2 + sm], qk[s][sm])], [vv[s][:, :, 0:128], vv[s][:, :, 128:256]],
                            [a0, a1], db, Sring, Pring, pb, ones_b, scale, NT, SEG, qb * 512, P4ring)
                if qb == 0 and sm == 0 and h + 1 < 4:
                    loadhead(h + 1)
                rd, _ = tmpf.next()
                B.recip(rd, pb[db])
                B.tt("dve", on[sm][:, 0], pb[a0], rd, ALU.mult)
                B.tt("dve", on[sm][:, 1], pb[a1], rd, ALU.mult)
            B.stt("dve", oo, on[1], nlam[:, 0:1], on[0], ALU.mult, ALU.add)
            sbk = Sring.next()
            for c in range(2):
                sq, _ = sqr.next()
                B.act(sq, oo[:, c], AF.Square)
                B.mm(pb[sbk], ones_b, sq, start=(c == 0), stop=(c == 1))
            rs, _ = tmpf.next()
            B.rsqrt_act(rs, pb[sbk], 1.0 / 256, EPS)
            og, od = ost.next()
            for c in range(2):
                tf, _ = tmpf.next()
                B.stt("dve", tf, oo[:, c], subw[:, c:c + 1], rs, ALU.mult, ALU.mult)
                B.stt("dve", og[:, c], tf, 1.0 - lam_init, sg[:, c], ALU.mult, ALU.mult)
            B.store("sp", S["br_diff"][h * 256:(h + 1) * 256, qsl].rearrange("(c p) t -> p c t", p=128), og, od)


def phase_E(B, l, W, C):
    nc, P, S, T, NT, NB, SEG = B.nc, B.P, B.S, B.T, B.NT, B.NB, B.SEG
    B.reset_arena()
    PADW = SEG + 16
    ubr = B.ring(2, [2, SEG], BF16, "ub")
    bufs = [B.alloc([2, PADW], F32, f"pbuf{i}") for i in range(2)]
    rcb = B.alloc([2, SEG], F32, "rcb")
    pooled = [B.alloc([T], BF16, f"pooled{i}") for i in range(2)]
    mean = B.alloc([2, SEG], F32, "mean")
    pw = B.alloc([2, 256], BF16, "pw")
    psc = B.alloc([8], F32, "psc", const=True)
    sgr = B.ring(2, [512], BF16, "sg")
    ost = B.ring(2, [512], BF16, "ost")
    tmpf = B.ring(2, [512], F32, "tmpf")
    cd = P.dsem()
    rcd = P.dsem()
    pwd = P.dsem()
    B.load("sp", psc, W["pool_scale"][l].rearrange("(c p) -> p c", p=128), cd)
    pb = [B.psum_tl(i) for i in range(8)]
    banks = _Cycle(list(range(8)))
    shifts = [(-1, 0, 1, PADW), (-1, 1, 2, PADW - 1), (-2, 2, 4, PADW - 3), (-4, 4, 8, PADW - 7)]
    for g in range(4):
        B.load("sp", rcb, C["rcnt"][g].rearrange("(s t) -> s t", s=2).partition_broadcast(128), rcd)
        B.load("pool", pw, W["pool_w"][l, g].rearrange("(c p) d -> p c d", p=128), pwd)
        for j in range(2):
            c = 2 * g + j
            ub, ud = ubr.next()
            B.load("sp", ub, S["u"][c * 128:(c + 1) * 128, :].rearrange("p (s t) -> p s t", s=2), ud)
            a = bufs[0]
            B.memset("pool", a[:, 0, 0:8], 0.0)
            B.memset("pool", a[:, 1, 8 + SEG:PADW], 0.0)
            B.copy("pool", a[:, :, 8:8 + SEG], ub)
            B.ts("pool", a[:, 0, 8 + SEG:PADW], ub[:, 1, 0:8], B.link[:, 0:1], None, op0=ALU.mult)
            B.ts("pool", a[:, 1, 0:8], ub[:, 0, SEG - 8:SEG], B.link[:, 0:1], None, op0=ALU.mult)
            cur = 0
            for st in range(g + 1):
                s0, s1, lo, hi = shifts[st]
                src, dst = bufs[cur], bufs[1 - cur]
                B.tt("dve" if st % 2 == 0 else "pool", dst[:, :, lo:hi], src[:, :, lo + s0:hi + s0], src[:, :, lo + s1:hi + s1], ALU.add)
                cur = 1 - cur
            B.tt("dve", mean, bufs[cur][:, :, 8:8 + SEG], rcb, ALU.mult)
            B.tt("dve", pooled[j].v(lambda a_: a_.rearrange("p (s t) -> p s t", s=2)), mean, ub, ALU.subtract)
        for dc in range(2):
            co = 2 * g + dc
            for sb in range(NB):
                sl = slice(sb * 512, (sb + 1) * 512)
                bk = banks.next()
                for j in range(2):
                    B.mm(pb[bk], pw[:, j, dc * 128:(dc + 1) * 128], pooled[j][:, sl], start=(j == 0), stop=(j == 1))
                sg, sgd = sgr.next()
                B.load("sp", sg, S["sg_pool"][co * 128:(co + 1) * 128, sl], sgd)
                og, od = ost.next()
                B.stt("dve", og, pb[bk], psc[:, co:co + 1], sg, ALU.mult, ALU.mult)
                B.store("sp", S["br_pool"][co * 128:(co + 1) * 128, sl], og, od)


def phase_F(B, l, x_src, x_dst, W, C, last):
    nc, P, S, T, NT, NB, SEG = B.nc, B.P, B.S, B.T, B.NT, B.NB, B.SEG
    pb = [B.psum_tl(i) for i in range(8)]
    B.reset_arena()
    TB1 = min(2048, T)
    nsb = TB1 // 512
    brT = B.alloc([32, TB1], BF16, "brT")
    wbr = B.ring(2, [32, 128], BF16, "wbr")
    mgr = B.ring(4, [4, 512], BF16, "mg")
    tmpf = B.ring(8, [512], F32, "tmpf")
    ost = B.ring(3, [512], BF16, "ost")
    brd = P.dsem()
    banks = _Cycle(list(range(8)))
    wb_src = W["w_branch"][l].rearrange("i (k p) n -> p i k n", p=128)
    mg_src = S["mg"].rearrange("(i d) t -> d i t", i=4)
    brs = [S["br_mla"], S["br_diff"], S["br_ssd"], S["br_pool"]]
    for blk in range(T // TB1):
        t0 = blk * TB1
        for i in range(4):
            B.load("sp", brT[:, i * 8:(i + 1) * 8, :], brs[i][:, t0:t0 + TB1].rearrange("(k p) t -> p k t", p=128), brd)
        wtiles = {}
        mgt = {}

        def f1_load(it, t0=t0, wtiles=wtiles, mgt=mgt):
            dmc, sb = divmod(it, nsb)
            if sb == 0:
                wb, wd = wbr.next()
                for i in range(4):
                    B.load("pool", wb[:, i * 8:(i + 1) * 8, :], wb_src[:, i, :, dmc * 128:(dmc + 1) * 128], wd)
                wtiles[dmc] = wb
            sl = slice(t0 + sb * 512, t0 + (sb + 1) * 512)
            mg, mgd = mgr.next()
            B.load("sp", mg, mg_src[dmc * 128:(dmc + 1) * 128, :, sl], mgd)
            mgt[it] = mg

        def f1_compute(it, t0=t0, wtiles=wtiles, mgt=mgt):
            dmc, sb = divmod(it, nsb)
            wb = wtiles[dmc]
            mg = mgt.pop(it)
            sl = slice(t0 + sb * 512, t0 + (sb + 1) * 512)
            lsl = slice(sb * 512, (sb + 1) * 512)
            ts_ = []
            for i in range(4):
                bk = banks.next()
                for k in range(8):
                    B.mm(pb[bk], wb[:, i * 8 + k, :], brT[:, i * 8 + k, lsl], start=(k == 0), stop=(k == 7))
                tf, _ = tmpf.next()
                B.tt("dve", tf, pb[bk], mg[:, i], ALU.mult)
                ts_.append(tf)
            B.tt("dve", ts_[0], ts_[0], ts_[1], ALU.add)
            B.tt("dve", ts_[2], ts_[2], ts_[3], ALU.add)
            og, od = ost.next()
            B.tt("dve", og, ts_[0], ts_[2], ALU.add)
            B.store("sp", S["mT"][dmc * 128:(dmc + 1) * 128, sl], og, od)

        prefetch_loop(16 * nsb, 2, f1_load, f1_compute)
    P.barrier()
    B.reset_arena()
    wout = B.alloc([16, D], BF16, "wout", const=True)
    mTr = B.ring(2, [16, 512], BF16, "mT")
    xring = B.ring(4, [D], F32, "x")
    stat = B.ring(2, [4], F32, "stat")
    sqj = B.alloc([D], BF16, "sqj")
    if last:
        fnw = B.alloc([D], F32, "fnw", const=True)
    cd = P.dsem()
    for k4 in range(4):
        B.load("pool", wout[:, k4 * 4:(k4 + 1) * 4, :],
               W["w_out"][l].rearrange("(k p) n -> p k n", p=128)[:, k4 * 4:(k4 + 1) * 4, :], cd)
    if last:
        B.load("sp", fnw, W["final_norm"].partition_broadcast(128), cd)
    banks = _Cycle(list(range(8)))
    mts = {}
    xts = {}

    def f2_load(it):
        tb, i = divmod(it, 4)
        if i == 0:
            sl = slice(tb * 512, (tb + 1) * 512)
            mT, mTd = mTr.next()
            B.load("sp", mT, S["mT"][:, sl].rearrange("(k p) t -> p k t", p=128), mTd)
            mts[tb] = mT
        rows = slice(tb * 512 + i * 128, tb * 512 + (i + 1) * 128)
        xt, xd = xring.next()
        B.load("sp", xt, x_src[rows, :], xd)
        xts[it] = (xt, xd)

    def f2_compute(it):
        tb, i = divmod(it, 4)
        mT = mts[tb]
        xt, xd = xts.pop(it)
        rows = slice(tb * 512 + i * 128, tb * 512 + (i + 1) * 128)
        for nb in range(4):
            bk = banks.next()
            for k in range(16):
                B.mm(pb[bk], mT[:, k, i * 128:(i + 1) * 128], wout[:, k, nb * 512:(nb + 1) * 512],
                     start=(k == 0), stop=(k == 15))
            B.tt("dve", xt[:, nb * 512:(nb + 1) * 512], xt[:, nb * 512:(nb + 1) * 512], pb[bk], ALU.add)
        if last:
            st, _ = stat.next()
            B.act(sqj, xt, AF.Square, accum=st[:, 0:1])
            B.rsqrt_act(st[:, 1:2], st[:, 0:1], 1.0 / D, EPS)
            B.stt("dve", xt, xt, st[:, 1:2], fnw, ALU.mult, ALU.mult)
        B.store("sp", x_dst[rows, :], xt, xd)

    prefetch_loop(NB * 4, 2, f2_load, f2_compute)


def phase_D(B, l, W, C):
    nc, P, S, T, NT, NB, SEG = B.nc, B.P, B.S, B.T, B.NT, B.NB, B.SEG
    pb = [B.psum_tl(i) for i in range(8)]
    pbb = [B.psum_tl(i, BF16) for i in range(8)]
    bc3 = lambda n: (lambda a: a.unsqueeze(2).broadcast_to([a.shape[0], a.shape[1], n]))
    B.reset_arena()
    cw = B.alloc([4, 12], F32, "cw", const=True)
    cbv = B.alloc([12], F32, "cbv", const=True)
    identb = B.alloc([128], BF16, "identb", const=True)
    xraw = B.ring(2, [2, SEG], BF16, "xraw")
    xpad = [B.alloc([2, SEG + 3], F32, f"xpad{i}") for i in range(2)]
    acc = [B.alloc([2, SEG], F32, f"acc{i}") for i in range(2)]
    xc = B.ring(2, [T], BF16, "xc")
    stg = B.ring(2, [8, 128], BF16, "stg")
    cd = P.dsem()
    for j in range(4):
        B.load("sp", cw[:, j, :], W["ssd_conv_w"][l, j].rearrange("(c p) -> p c", p=128), cd)
    B.load("sp", cbv, W["ssd_conv_b"][l].rearrange("(c p) -> p c", p=128), cd)
    B.load("pool", identb, C["ident"][:, :], cd)
    trb = _Cycle([0, 1, 2, 3])
    for c in range(12):
        eng = "dve" if c % 2 == 0 else "pool"
        xr, xrd = xraw.next()
        B.load("sp", xr, S["xbc"][c * 128:(c + 1) * 128, :].rearrange("p (s t) -> p s t", s=2), xrd)
        xp, ac = xpad[c % 2], acc[c % 2]
        B.memset(eng, xp[:, 0, 0:2], 0.0)
        B.memset(eng, xp[:, 1, SEG + 2:SEG + 3], 0.0)
        B.copy(eng, xp[:, :, 2:2 + SEG], xr)
        B.ts(eng, xp[:, 0, SEG + 2:SEG + 3], xr[:, 1, 0:1], B.link[:, 0:1], None, op0=ALU.mult)
        B.ts(eng, xp[:, 1, 0:2], xr[:, 0, SEG - 2:SEG], B.link[:, 0:1], None, op0=ALU.mult)
        B.ts(eng, ac, xp[:, :, 0:SEG], cw[:, 0, c:c + 1], cbv[:, c:c + 1], op0=ALU.mult, op1=ALU.add)
        for j in range(1, 4):
            B.stt(eng, ac, xp[:, :, j:j + SEG], cw[:, j, c:c + 1], ac, ALU.mult, ALU.add)
        xo, xod = xc.next()
        B.act(xo.v(lambda a: a.rearrange("p (s t) -> p s t", s=2)), ac, AF.Silu)
        if c >= 8:
            B.store("sp", S["bcT"][(c - 8) * 128:(c - 7) * 128, :], xo, xod)
        if c < 10:
            for i0 in range(0, NT, 8):
                n = min(8, NT - i0)
                bk = trb.next()
                for i in range(n):
                    B.tr(pbb[bk][:, i * 128:(i + 1) * 128], xo[:, (i0 + i) * 128:(i0 + i + 1) * 128], identb)
                sg_, sgd = stg.next()
                B.copy("act" if (i0 // 8) % 2 == 0 else "dve", sg_[:, 0:n, :],
                       pbb[bk][:, 0:n * 128].v(lambda a: a.rearrange("p (i c) -> p i c", c=128)))
                B.store("sp", S["xsb"][i0 * 128:(i0 + n) * 128, c * 128:(c + 1) * 128].rearrange("(n p) c -> p n c", p=128),
                        sg_[:, 0:n, :], sgd)
    P.barrier()
    B.reset_arena()
    tri = B.alloc([5, 128], F32, "tri", const=True)
    trib = B.alloc([2, 128], BF16, "trib", const=True)
    identb = B.alloc([128], BF16, "identb", const=True)
    dt = B.alloc([NT, 32], F32, "dt")
    da = B.alloc([NT, 32], F32, "da")
    dtb = B.alloc([32], F32, "dtb", const=True)
    av = B.alloc([32], F32, "av")
    dvec = B.alloc([16], F32, "dvec", const=True)
    nrmw = B.alloc([1024], F32, "nrmw", const=True)
    stats = B.alloc([NT, 5, 32], F32, "stats")
    est = B.alloc([NT, 5, 32], F32, "est")
    coef = B.alloc([NT, 32], F32, "coef")
    cd = P.dsem()
    B.load("sp", tri, C["tri"].rearrange("j p c -> p j c"), cd)
    B.load("pool", trib, C["tri"][0:2].rearrange("j p c -> p j c"), cd)
    B.load("pool", identb, C["ident"][:, :], cd)
    B.load("sp", dt, S["dt"].rearrange("(n p) c -> p n c", p=128), cd)
    B.load("sp", dtb, W["ssd_dt_bias"][l].rearrange("a b -> (a b)").partition_broadcast(128), cd)
    B.load("sp", av, W["ssd_a_log"][l].rearrange("a b -> (a b)").partition_broadcast(128), cd)
    B.load("sp", dvec, W["ssd_d"][l].partition_broadcast(128), cd)
    B.load("sp", nrmw, W["ssd_norm"][l].partition_broadcast(128), cd)
    bc_nt = lambda a: a.unsqueeze(1).broadcast_to([128, NT, 32])
    B.tt("dve", dt, dt, dtb.v(bc_nt), ALU.add)
    B.act(dt, dt, AF.Exp)
    B.act(dt, dt, AF.Ln, bias=B.one_col[:, 0:1])
    B.act(av, av, AF.Exp)
    B.ts("dve", av, av, -1.0, None, op0=ALU.mult)
    B.tt("dve", da, dt, av.v(bc_nt), ALU.mult)
    sbk = _Cycle([0, 1, 2, 3])
    for c in range(NT):
        bk = sbk.next()
        for j in range(5):
            B.mm(pb[bk][:, j * 32:(j + 1) * 32], tri[:, j, :], da[:, c, :])
        B.copy("dve" if c % 2 == 0 else "act", stats[:, c].v(lambda a: a.rearrange("p j h -> p (j h)")), pb[bk][:, 0:160])
    B.act(est, stats, AF.Exp)
    B.tt("dve", coef[:, :, 0:16], dt[:, :, 0:16], est[:, :, 2, 0:16], ALU.mult)
    B.tt("dve", coef[:, :, 16:32], dt[:, :, 16:32], est[:, :, 3, 16:32], ALU.mult)
    B.dbg("est", est); B.dbg("dt", dt); B.dbg("da", da); B.dbg("coef", coef)
    mark = B.off
    xsr = B.ring(4, [1280], BF16, "xs")
    xdw = B.ring(2, [1024], BF16, "xdw")
    Hst = [B.alloc([1024], F32, f"H{i}") for i in range(2)]
    Hsv = B.ring(3, [1024], BF16, "Hsv")
    B.memset("dve", Hst[0], 0.0)
    B.memset("dve", Hst[1], 0.0)
    stb = _Cycle([4, 5, 6, 7])
    d2x = {}

    def d2_load(it):
        i, d = divmod(it, 2)
        c = i if d == 0 else NT - 1 - i
        xs, xsd = xsr.next()
        B.load("sp", xs, S["xsb"][c * 128:(c + 1) * 128, :], xsd)
        d2x[it] = xs

    def d2_compute(it):
            i, d = divmod(it, 2)
            c = i if d == 0 else NT - 1 - i
            xs = d2x.pop(it)
            xw, _ = xdw.next()
            B.tt("dve", xw.v(lambda a: a.rearrange("p (h q) -> p h q", q=64)),
                 xs[:, 0:1024].v(lambda a: a.rearrange("p (h q) -> p h q", q=64)),
                 coef[:, c, d * 16:(d + 1) * 16].v(bc3(64)), ALU.mult)
            H = Hst[d]
            if (d == 0 and c == NT // 2) or (d == 1 and c == NT // 2 - 1):
                B.ts("dve", H, H, B.link[:, 0:1], None, op0=ALU.mult)
            hs, hsd = Hsv.next()
            B.copy("act", hs, H)
            B.store("sp", S["H"][d, c], hs, hsd)
            B.tt("dve", H.v(lambda a: a.rearrange("p (h q) -> p h q", q=64)),
                 H.v(lambda a: a.rearrange("p (h q) -> p h q", q=64)),
                 est[:, c, 4, d * 16:(d + 1) * 16].v(bc3(64)), ALU.mult)
            for g in range(2):
                bk = stb.next()
                B.mm(pb[bk], xs[:, 1024 + g * 128:1024 + (g + 1) * 128], xw[:, g * 512:(g + 1) * 512])
                B.tt("dve", H[:, g * 512:(g + 1) * 512], H[:, g * 512:(g + 1) * 512], pb[bk], ALU.add)

    prefetch_loop(NT * 2, 2, d2_load, d2_compute)
    P.barrier()
    B.off = mark
    xsr = B.ring(3, [1024], BF16, "xs")
    bct = B.ring(3, [4, 128], BF16, "bct")
    Hr = B.ring(3, [2, 1024], BF16, "Hr")
    szr = B.ring(3, [1024], BF16, "sz")
    Dm = B.ring(3, [16, 128], F32, "Dm")
    cbm = B.ring(2, [2, 2, 128], BF16, "cbm")
    Lx = B.ring(3, [4, 128], BF16, "Lx")
    Mt = B.ring(3, [4, 128], BF16, "Mt")
    xdr = B.ring(3, [1024], BF16, "xd")
    yo = B.ring(2, [1024], F32, "yo")
    yo2 = B.alloc([1024], F32, "yo2")
    t3 = B.alloc([1024], F32, "t3")
    sqj = B.alloc([512], BF16, "sqj")
    ss = B.ring(2, [4], F32, "ss")
    yn = B.ring(2, [1024], BF16, "yn")
    stg = B.ring(2, [8, 128], BF16, "stg")
    cbk = _Cycle([0, 1])
    sgk = _Cycle([2, 3])
    ydk = [4, 5]
    yfk = _Cycle([6, 7])
    hq = lambda a: a.rearrange("p (h q) -> p h q", q=64)
    d3t = {}

    def d3_load(c):
        rows = slice(c * 128, (c + 1) * 128)
        xs, xsd = xsr.next()
        B.load("sp", xs, S["xsb"][rows, 0:1024], xsd)
        bt, btd = bct.next()
        B.load("sp", bt, S["bcT"][:, rows].rearrange("(j p) t -> p j t", p=128), btd)
        Hc, Hd = Hr.next()
        B.load("sp", Hc[:, 0], S["H"][0, c], Hd)
        B.load("sp", Hc[:, 1], S["H"][1, c], Hd)
        sz, szd = szr.next()
        B.load("sp", sz, S["sz"][rows, :], szd)
        d3t[c] = (xs, bt, Hc, sz)

    def d3_compute(c):
        rows = slice(c * 128, (c + 1) * 128)
        xs, bt, Hc, sz = d3t.pop(c)
        bk = cbk.next()
        for g in range(2):
            B.mm(pb[bk][:, g * 128:(g + 1) * 128], bt[:, g, :], bt[:, 2 + g, :])
        cm, _ = cbm.next()
        for d in range(2):
            B.tt("dve", cm[:, d], pb[bk][:, 0:256].v(lambda a: a.rearrange("p (g l) -> p g l", g=2)),
                 trib[:, d, :].v(lambda a: a.unsqueeze(1).broadcast_to([128, 2, 128])), ALU.mult)
        yoc, _ = yo.next()
        for d in range(2):
            dm, _ = Dm.next()
            B.tt("pool", dm, tri[:, d, :].v(lambda a: a.unsqueeze(1).broadcast_to([128, 16, 128])),
                 da[:, c, d * 16:(d + 1) * 16].v(bc3(128)), ALU.mult)
            xd, _ = xdr.next()
            B.tt("dve", xd.v(hq), xs.v(hq), dt[:, c, d * 16:(d + 1) * 16].v(bc3(64)), ALU.mult)
            for q in range(4):
                g = q // 2
                sk = sgk.next()
                B.mm(pb[sk], tri[:, 2 + d, :], dm[:, q * 4:(q + 1) * 4, :].v(lambda a: a.rearrange("p h l -> p (h l)")))
                lx, _ = Lx.next()
                B.act(lx.v(lambda a: a.rearrange("p h l -> p (h l)")), pb[sk], AF.Exp)
                mt, _ = Mt.next()
                B.tt("dve", mt, lx, cm[:, d, g, :].v(lambda a: a.unsqueeze(1).broadcast_to([128, 4, 128])), ALU.mult)
                for hh in range(4):
                    hd = q * 4 + hh
                    B.mm(pb[ydk[g]][:, (hd % 8) * 64:(hd % 8 + 1) * 64], mt[:, hh, :], xd[:, hd * 64:(hd + 1) * 64],
                         start=(d == 0 and hd % 8 == 0), stop=(d == 1 and hd % 8 == 7), skip=True)
            for g in range(2):
                fk = yfk.next()
                B.mm(pb[fk], bt[:, 2 + g, :], Hc[:, d, g * 512:(g + 1) * 512])
                dst = (yoc if d == 0 else yo2)[:, g * 512:(g + 1) * 512]
                eai = est[:, c, d, d * 16 + g * 8:d * 16 + (g + 1) * 8]
                B.tt("dve", dst.v(hq), pb[fk].v(hq), eai.v(bc3(64)), ALU.mult)
        B.tt("pool", t3.v(hq), xs.v(hq), dvec.v(bc3(64)), ALU.mult)
        B.tt("dve", yoc, yoc, yo2, ALU.add)
        for g in range(2):
            B.tt("dve", yoc[:, g * 512:(g + 1) * 512], yoc[:, g * 512:(g + 1) * 512], pb[ydk[g]], ALU.add)
        B.tt("dve", yoc, yoc, t3, ALU.add)
        B.tt("dve", yoc, yoc, sz, ALU.mult)
        st, _ = ss.next()
        for g in range(2):
            B.act(sqj, yoc[:, g * 512:(g + 1) * 512], AF.Square, accum=st[:, g:g + 1])
        B.rsqrt_act(st[:, 2:4], st[:, 0:2], 1.0 / 512, EPS)
        ynt, _ = yn.next()
        for g in range(2):
            B.stt("dve" if g == 0 else "pool", ynt[:, g * 512:(g + 1) * 512], yoc[:, g * 512:(g + 1) * 512],
                  st[:, 2 + g:3 + g], nrmw[:, g * 512:(g + 1) * 512], ALU.mult, ALU.mult)
        bk = cbk.next()
        for k in range(8):
            B.tr(pbb[bk][:, k * 128:(k + 1) * 128], ynt[:, k * 128:(k + 1) * 128], identb)
        sg_, sgd = stg.next()
        B.copy("act", sg_, pbb[bk].v(lambda a: a.rearrange("p (k t) -> p k t", k=8)))
        B.store("sp", S["br_ssd"][:, rows].rearrange("(k p) t -> p k t", p=128), sg_, sgd)

    prefetch_loop(NT, 1, d3_load, d3_compute)


def prefetch_loop(n, depth, load, compute):
    for i in range(n + depth):
        if i < n:
            load(i)
        if i >= depth:
            compute(i - depth)


class _Cycle:
    def __init__(self, items):
        self.items = items
        self.i = -1

    def next(self):
        self.i = (self.i + 1) % len(self.items)
        return self.items[self.i]


def host_consts(T, link):
    SEG = T // 2
    seqlen = T if link else SEG
    pos = np.arange(T) if link else np.concatenate([np.arange(SEG), np.arange(SEG)])
    out = {}

    def tables(rot_dim):
        half = rot_dim // 2
        inv = np.power(np.float32(500000.0), -np.arange(half, dtype=np.float32) * np.float32(2.0) / np.float32(rot_dim)).astype(np.float32)
        ang = pos.astype(np.float32)[:, None] * inv[None, :]
        c = np.cos(ang).astype(np.float32).T
        s = np.sin(ang).astype(np.float32).T
        return np.stack([np.concatenate([c, c], 0), np.concatenate([s, s], 0)], 0).astype(np.float32)

    out["c_cs64"] = np.ascontiguousarray(tables(64))
    out["c_cs32"] = np.ascontiguousarray(tables(32))
    rc = np.zeros((4, T), np.float32)
    for i, w in enumerate((2, 4, 8, 16)):
        lo = w // 2
        hi = w - 1 - lo
        start = np.clip(pos - lo, 0, seqlen)
        end = np.clip(pos + hi + 1, 0, seqlen)
        rc[i] = 1.0 / (end - start).astype(np.float32)
    out["c_rcnt"] = rc
    t = np.arange(128)[:, None]
    l_ = np.arange(128)[None, :]
    out["c_tri"] = np.stack([(t <= l_), (t >= l_), (t > l_), (t < l_), np.ones((128, 128), bool)], 0).astype(np.float32)

    def rot(n):
        h = n // 2
        m = np.zeros((n, n), np.float32)
        for i in range(h):
            m[i + h, i] = -1.0
            m[i, i + h] = 1.0
        return m

    out["c_rot64"] = rot(64)
    out["c_rot32"] = rot(32)
    out["c_ident"] = np.eye(128, dtype=np.float32)
    out["c_link"] = np.full((128, 1), 1.0 if link else 0.0, np.float32)
    out["c_cbias"] = np.full((128, 1), 0.0 if link else NEG, np.float32)
    return out


_CACHE = {}


def kernel(**inputs):
    T = 4096
    xp = np.asarray(inputs["x_prompt"], dtype=np.float32)
    xs = np.asarray(inputs["x_sample"], dtype=np.float32)
    slots = []
    for c in range(8):
        if c < 2:
            slots.append((np.ascontiguousarray(xs[c]), 1))
        elif c < 6:
            i = (c - 2) * 2
            slots.append((np.ascontiguousarray(np.concatenate([xp[i], xp[i + 1]], axis=0)), 0))
        else:
            slots.append((np.zeros((T, D), np.float32), 0))
    if T not in _CACHE:
        _CACHE[T] = build(T)
    B = _CACHE[T]
    wts = {n: np.ascontiguousarray(np.asarray(inputs[n], dtype=np.float32)) for n, _ in W_NAMES}
    hc = {1: host_consts(T, 1), 0: host_consts(T, 0)}
    in_maps = []
    for x, link in slots:
        m = {"x": x}
        m.update(wts)
        m.update(hc[link])
        in_maps.append(m)
    res = run_bass_kernel_spmd(B.nc, in_maps, core_ids=list(range(8)))
    ys = [np.asarray(r["y"], dtype=np.float32) for r in res.results]
    y_sample = np.stack([ys[0], ys[1]], axis=0)
    yp = []
    for c in range(2, 6):
        yp.append(ys[c][:2048])
        yp.append(ys[c][2048:])
    y_prompt = np.stack(yp, axis=0)
    return (y_prompt, y_sample)
```

```python
import contextlib
import math
import numpy as np
import concourse.bass as bass
import concourse.mybir as mybir
from concourse.bass_utils import run_bass_kernel_spmd

F32 = mybir.dt.float32
BF16 = mybir.dt.bfloat16
AF = mybir.ActivationFunctionType
ALU = mybir.AluOpType
AX = mybir.AxisListType

D = 2048
BW = 1024
IN_DIM = 18784
DEPTH = 2
EPS = 1e-6
SEM_CAP = 32000
NEG = -30000.0


class Buf:
    __slots__ = ("name", "last_w", "readers", "const")

    def __init__(self, name="", const=False):
        self.name = name
        self.last_w = None
        self.readers = []
        self.const = const


class TL:
    __slots__ = ("ap", "buf")

    def __init__(self, ap, buf):
        self.ap = ap
        self.buf = buf

    def __getitem__(self, k):
        return TL(self.ap[k], self.buf)

    def v(self, fn):
        return TL(fn(self.ap), self.buf)


class Op:
    __slots__ = ("eng", "fn", "deps", "signal", "sigval", "is_dma", "dsem", "dtarget")

    def __init__(self, eng, fn, is_dma=False):
        self.eng = eng
        self.fn = fn
        self.deps = []
        self.signal = False
        self.sigval = None
        self.is_dma = is_dma
        self.dsem = None
        self.dtarget = None


class DmaSem:
    def __init__(self):
        self.count = 0
        self.eng = None


class Prog:
    ENGS = ("pe", "act", "dve", "pool", "sp")

    def __init__(self, nc):
        self.nc = nc
        self.eng_obj = {"pe": nc.tensor, "act": nc.scalar, "dve": nc.vector,
                        "pool": nc.gpsimd, "sp": nc.sync}
        self.ops = {e: [] for e in self.ENGS}
        self.dma_last = {}
        self.dsem_pool = []
        self.dsem_i = 0

    def dsem(self):
        if self.dsem_i >= len(self.dsem_pool):
            self.dsem_pool.append(DmaSem())
        s = self.dsem_pool[self.dsem_i]
        self.dsem_i += 1
        return s

    def _track(self, o, reads, writes, nowaw=False):
        seen = set()
        for b in reads:
            w = b.last_w
            if w is not None and id(w) not in seen:
                o.deps.append((w, "raw", w.dsem.count if w.is_dma else 0))
                seen.add(id(w))
        for b in writes:
            w = b.last_w
            if w is not None and id(w) not in seen:
                if not (nowaw and w.is_dma and w.dsem is o.dsem):
                    o.deps.append((w, "waw", w.dsem.count if w.is_dma else 0))
                    seen.add(id(w))
            for r in b.readers:
                if id(r) not in seen:
                    o.deps.append((r, "war", r.dsem.count if r.is_dma else 0))
                    seen.add(id(r))
        for b in reads:
            if not b.const:
                b.readers.append(o)
        for b in writes:
            b.last_w = o
            b.readers = []
        self.ops[o.eng].append(o)

    def op(self, eng, fn, reads=(), writes=()):
        o = Op(eng, fn)
        self._track(o, [t.buf for t in reads], [t.buf for t in writes])
        return o

    def dma(self, eng, out, in_, dsem, reads=(), writes=()):
        nc = self.nc
        eo = self.eng_obj[eng]
        oa = out.ap if isinstance(out, TL) else out
        ia = in_.ap if isinstance(in_, TL) else in_
        o = Op(eng, lambda: eo.dma_start(out=oa, in_=ia, allow_slow_non_contiguous=True), is_dma=True)
        o.dsem = dsem
        assert dsem.eng in (None, eng), "DMA semaphore shared between queues"
        dsem.eng = eng
        o.dtarget = dsem.count + 1
        rd = [t.buf for t in reads] + ([in_.buf] if isinstance(in_, TL) else [])
        wr = [t.buf for t in writes] + ([out.buf] if isinstance(out, TL) else [])
        self._track(o, rd, wr, nowaw=True)
        dsem.count += 1
        self.dma_last[id(dsem)] = o
        return o

    def barrier(self):
        lasts = []
        for e in self.ENGS:
            for o in reversed(self.ops[e]):
                if not o.is_dma and o.fn is not None:
                    lasts.append(o)
                    break
        lasts += list(self.dma_last.values())
        for e in self.ENGS:
            o = Op(e, None)
            for l in lasts:
                o.deps.append((l, "raw", l.dsem.count if l.is_dma else 0))
            self.ops[e].append(o)
        self.dsem_i = 0
        for d in self.dsem_pool:
            d.eng = None

    @staticmethod
    def _needs_sem(p, c, kind):
        if p.is_dma:
            return True
        if p.eng == c.eng:
            if p.eng in ("pe", "sp"):
                return False
            return kind == "raw"
        return True

    def emit(self, es):
        nc = self.nc
        for e in self.ENGS:
            for c in self.ops[e]:
                for (p, kind, cnt) in c.deps:
                    if not p.is_dma and self._needs_sem(p, c, kind):
                        p.signal = True
        pool = {}

        def getsem(key):
            if key not in pool:
                pool[key] = es.enter_context(nc.semaphore(f"s{len(pool)}"))
            return pool[key]

        for e in self.ENGS:
            k = 0
            for o in self.ops[e]:
                if not o.is_dma and o.signal:
                    o.sigval = k
                    k += 1
        dcap = SEM_CAP // 16
        nw = 0
        ni = 0
        for e in self.ENGS:
            eng = self.eng_obj[e]
            waited = {}
            for o in self.ops[e]:
                need = {}
                for (p, kind, cnt) in o.deps:
                    if not self._needs_sem(p, o, kind):
                        continue
                    if p.is_dma:
                        tot = cnt
                        key = ("d", id(p.dsem), (tot - 1) // dcap)
                        v = ((tot - 1) % dcap + 1) * 16
                    else:
                        key = ("c", p.eng, p.sigval // SEM_CAP)
                        v = p.sigval % SEM_CAP + 1
                    if need.get(key, 0) < v:
                        need[key] = v
                for key, v in need.items():
                    if waited.get(key, 0) >= v:
                        continue
                    waited[key] = v
                    eng.wait_ge(getsem(key), v)
                    nw += 1
                if o.fn is None:
                    continue
                ins = o.fn()
                ni += 1
                if o.is_dma:
                    ins.then_inc(getsem(("d", id(o.dsem), (o.dtarget - 1) // dcap)), 16)
                elif o.signal:
                    ins.then_inc(getsem(("c", o.eng, o.sigval // SEM_CAP)), 1)
        self.stats = dict(waits=nw, insts=ni, sems=len(pool))
        return self.stats


IN_GROUPS = [
    ("cq", 0, 512, "F", "lat"), ("ckv", 512, 256, "F", "lat"), ("kr", 768, 64, "F", "lat"),
    ("g_mla", 832, 1024, "F", "silu"), ("dq", 1856, 1024, "F", "rope"), ("dk", 2880, 1024, "F", "rope"),
    ("dv", 3904, 1024, "T", "copy"), ("g_diff", 4928, 1024, "F", "silu"), ("z", 5952, 1024, "T", "silu"),
    ("xbc", 6976, 1536, "F", "copy"), ("dt", 8512, 32, "T", "copyf"), ("u", 8544, 1024, "F", "copy"),
    ("g_pool", 9568, 1024, "F", "silu"), ("mg", 10592, 8192, "F", "sigmoid"),
]


class Builder:
    def __init__(self, T, debug=False, nlayers=DEPTH, phases=None):
        self.T = T
        self.SEG = T // 2
        self.NT = T // 128
        self.NB = T // 512
        self.debug = debug
        self.nlayers = nlayers
        self.phases = phases
        self.nc = bass.Bass("TRN2", target_bir_lowering=False)
        self.P = Prog(self.nc)
        self.es = contextlib.ExitStack()

    def dram_in(self, name, shape, dt=F32):
        return self.nc.dram_tensor(name, list(shape), dt, kind="ExternalInput").ap()

    def dram_out(self, name, shape, dt=F32):
        return self.nc.dram_tensor(name, list(shape), dt, kind="ExternalOutput").ap()

    def dram_scr(self, name, shape, dt):
        kind = "ExternalOutput" if self.debug else "Internal"
        return self.nc.dram_tensor(name, list(shape), dt, kind=kind).ap()

    def reset_arena(self):
        self.off = 0

    def alloc(self, shape, dt, name="", const=False):
        n = int(np.prod(shape))
        nbytes = n * (4 if dt == F32 else 2)
        nw = (nbytes + 3) // 4
        nw = (nw + 7) // 8 * 8
        assert self.off + nw <= self.BIGW, f"SBUF arena overflow {name} {self.off + nw}"
        ap = self.big[:, self.off:self.off + nw]
        self.off += nw
        if dt == BF16:
            ap = ap.bitcast(BF16)[:, 0:n]
        else:
            ap = ap[:, 0:n]
        if len(shape) > 1:
            names = " ".join(f"a{i}" for i in range(len(shape)))
            kw = {f"a{i}": int(s) for i, s in enumerate(shape)}
            ap = ap.rearrange(f"p ({names}) -> p {names}", **kw)
        return TL(ap, Buf(name, const=const))

    def ring(self, n, shape, dt, name=""):
        tl = [self.alloc(shape, dt, f"{name}{i}") for i in range(n)]
        ds = [self.P.dsem() for _ in range(n)]
        return _Ring(tl, ds)

    def psum_tl(self, i, dt=F32):
        ap = self.banks[i][:]
        if dt == BF16:
            ap = ap.bitcast(BF16)
        return TL(ap, self.bank_bufs[i])

    def mm(self, out, lhsT, rhs, start=True, stop=True, skip=False):
        nc = self.nc
        o, a, b = out.ap, lhsT.ap, rhs.ap
        if skip:
            return self.P.op("pe", lambda: nc.tensor.matmul(o, a, b, start=start, stop=stop, skip_group_check=True),
                             reads=[lhsT, rhs], writes=[out])
        return self.P.op("pe", lambda: nc.tensor.matmul(o, a, b, start=start, stop=stop),
                         reads=[lhsT, rhs], writes=[out])

    def tr(self, out, in_, ident):
        nc = self.nc
        o, a, b = out.ap, in_.ap, ident.ap
        return self.P.op("pe", lambda: nc.tensor.transpose(o, a, b), reads=[in_, ident], writes=[out])

    def act(self, out, in_, func, bias=None, scale=1.0, accum=None, eng="act"):
        nc = self.nc
        o, a = out.ap, in_.ap
        kw = {}
        rd = [in_]
        wr = [out]
        if bias is not None:
            if isinstance(bias, TL):
                kw["bias"] = bias.ap
                rd.append(bias)
            else:
                kw["bias"] = float(bias)
        if isinstance(scale, TL):
            kw["scale"] = scale.ap
            rd.append(scale)
        else:
            kw["scale"] = float(scale)
        if accum is not None:
            kw["accum_out"] = accum.ap
            wr.append(accum)
        return self.P.op("act", lambda: nc.scalar.activation(out=o, in_=a, func=func, **kw), reads=rd, writes=wr)

    def _e(self, eng):
        return self.P.eng_obj[eng]

    def tt(self, eng, out, in0, in1, op):
        e = self._e(eng)
        o, a, b = out.ap, in0.ap, in1.ap
        return self.P.op(eng, lambda: e.tensor_tensor(out=o, in0=a, in1=b, op=op), reads=[in0, in1], writes=[out])

    def ts(self, eng, out, in0, s1, s2=None, op0=ALU.mult, op1=None):
        e = self._e(eng)
        o, a = out.ap, in0.ap
        rd = [in0]
        v1 = s1.ap if isinstance(s1, TL) else float(s1)
        if isinstance(s1, TL):
            rd.append(s1)
        v2 = None
        if s2 is not None:
            v2 = s2.ap if isinstance(s2, TL) else float(s2)
            if isinstance(s2, TL):
                rd.append(s2)
        if op1 is None:
            return self.P.op(eng, lambda: e.tensor_scalar(out=o, in0=a, scalar1=v1, scalar2=None, op0=op0),
                             reads=rd, writes=[out])
        return self.P.op(eng, lambda: e.tensor_scalar(out=o, in0=a, scalar1=v1, scalar2=v2, op0=op0, op1=op1),
                         reads=rd, writes=[out])

    def stt(self, eng, out, in0, scalar, in1, op0, op1):
        eng = "dve"
        e = self._e(eng)
        o, a, b = out.ap, in0.ap, in1.ap
        rd = [in0, in1]
        sv = scalar.ap if isinstance(scalar, TL) else float(scalar)
        if isinstance(scalar, TL):
            rd.append(scalar)
        return self.P.op(eng, lambda: e.scalar_tensor_tensor(out=o, in0=a, scalar=sv, in1=b, op0=op0, op1=op1),
                         reads=rd, writes=[out])

    def copy(self, eng, out, in_):
        if eng == "act":
            return self.act(out, in_, AF.Copy)
        e = self._e(eng)
        o, a = out.ap, in_.ap
        return self.P.op(eng, lambda: e.tensor_copy(o, a), reads=[in_], writes=[out])

    def recip(self, out, in_):
        nc = self.nc
        o, a = out.ap, in_.ap
        return self.P.op("dve", lambda: nc.vector.reciprocal(o, a), reads=[in_], writes=[out])

    def memset(self, eng, out, val):
        e = self._e(eng)
        o = out.ap
        return self.P.op(eng, lambda: e.memset(o, float(val)), writes=[out])

    def load(self, eng, dst, src_ap, dsem):
        return self.P.dma(eng, dst, src_ap, dsem)

    def store(self, eng, dst_ap, src, dsem):
        return self.P.dma(eng, dst_ap, src, dsem)

    def dbg(self, name, tl, parts=128):
        if not self.debug:
            return
        shp = [parts] + list(tl.ap.shape[1:])
        d = self.nc.dram_tensor("dbg_" + name, shp, tl.ap.dtype, kind="ExternalOutput").ap()
        self.P.dma("sp", d, tl[0:parts], self.P.dsem())

    def rsqrt_act(self, out, in_, scale, eps):
        self.act(out, in_, AF.Ln, bias=self.eps_col[:, 0:1] if eps == EPS else eps, scale=scale)
        self.act(out, out, AF.Exp, scale=-0.5)


class _Ring:
    def __init__(self, tl, ds):
        self.tl = tl
        self.ds = ds
        self.i = -1

    def next(self):
        self.i = (self.i + 1) % len(self.tl)
        return self.tl[self.i], self.ds[self.i]


W_NAMES = [
    ("norm_w", (DEPTH, D)), ("w_in", (DEPTH, D, IN_DIM)), ("mla_q_norm", (DEPTH, 512)),
    ("mla_w_uq", (DEPTH, 512, 1536)), ("mla_kv_norm", (DEPTH, 256)), ("mla_w_ukv", (DEPTH, 256, 2048)),
    ("diff_lambda", (DEPTH, 4, 128)), ("diff_subln", (DEPTH, 256)), ("ssd_conv_w", (DEPTH, 4, 1536)),
    ("ssd_conv_b", (DEPTH, 1536)), ("ssd_dt_bias", (DEPTH, 2, 16)), ("ssd_a_log", (DEPTH, 2, 16)),
    ("ssd_d", (DEPTH, 16)), ("ssd_norm", (DEPTH, 1024)), ("pool_w", (DEPTH, 4, 256, 256)),
    ("pool_scale", (DEPTH, 1024)), ("w_branch", (DEPTH, 4, 1024, D)), ("w_out", (DEPTH, D, D)),
    ("final_norm", (D,)),
]


def build(T, debug=False, nlayers=DEPTH, phases=None):
    B = Builder(T, debug, nlayers, phases)
    nc, P, es = B.nc, B.P, B.es
    NT, NB, SEG = B.NT, B.NB, B.SEG
    x_in = B.dram_in("x", (T, D))
    W = {n: B.dram_in(n, s) for n, s in W_NAMES}
    c_link = B.dram_in("c_link", (128, 1))
    c_cbias = B.dram_in("c_cbias", (128, 1))
    c_ident = B.dram_in("c_ident", (128, 128))
    c_tri = B.dram_in("c_tri", (5, 128, 128))
    c_rot64 = B.dram_in("c_rot64", (64, 64))
    c_rot32 = B.dram_in("c_rot32", (32, 32))
    c_cs64 = B.dram_in("c_cs64", (2, 64, T))
    c_cs32 = B.dram_in("c_cs32", (2, 32, T))
    c_rcnt = B.dram_in("c_rcnt", (4, T))
    y_out = B.dram_out("y", (T, D))
    S = {}
    S["x1"] = B.dram_scr("s_x1", (T, D), F32)
    for nm, f in [("cqn", 512), ("ckvn", 256), ("kpe", 64), ("sg_mla", 1024), ("dq", 1024), ("dk", 1024),
                  ("sg_diff", 1024), ("xbc", 1536), ("u", 1024), ("sg_pool", 1024), ("mg", 8192),
                  ("br_mla", 1024), ("br_diff", 1024), ("br_ssd", 1024), ("br_pool", 1024)]:
        S[nm] = B.dram_scr("s_" + nm, (f, T), BF16)
    S["dv"] = B.dram_scr("s_dv", (T, 1024), BF16)
    S["sz"] = B.dram_scr("s_sz", (T, 1024), BF16)
    S["dt"] = B.dram_scr("s_dt", (T, 32), F32)
    S["xsb"] = B.dram_scr("s_xsb", (T, 1280), BF16)
    S["bcT"] = B.dram_scr("s_bcT", (512, T), BF16)
    S["H"] = B.dram_scr("s_H", (2, T // 128, 128, 1024), BF16)
    S["mT"] = B.dram_scr("s_mT", (D, T), BF16)
    B.S = S

    B.BIGW = 49152 - 1024
    B.big = es.enter_context(nc.sbuf_tensor("big", [128, B.BIGW + 64], F32))
    B.banks = [es.enter_context(nc.psum_tensor(f"ps{i}", [128, 512], F32)) for i in range(8)]
    B.bank_bufs = [Buf(f"ps{i}") for i in range(8)]
    cbase = B.BIGW
    B.eps_col = TL(B.big[:, cbase:cbase + 1], Buf("eps", const=True))
    B.link = TL(B.big[:, cbase + 1:cbase + 2], Buf("link", const=True))
    B.cbias = TL(B.big[:, cbase + 2:cbase + 3], Buf("cbias", const=True))
    B.zero_col = TL(B.big[:, cbase + 3:cbase + 4], Buf("zero", const=True))
    B.one_col = TL(B.big[:, cbase + 4:cbase + 5], Buf("one", const=True))
    B.memset("dve", B.eps_col, EPS)
    B.memset("dve", B.zero_col, 0.0)
    B.memset("dve", B.one_col, 1.0)
    ds0 = DmaSem()
    B.load("sp", B.link, c_link[:, :], ds0)
    B.load("sp", B.cbias, c_cbias[:, :], ds0)
    P.barrier()

    consts = dict(ident=c_ident, tri=c_tri, rot64=c_rot64, rot32=c_rot32, cs64=c_cs64, cs32=c_cs32, rcnt=c_rcnt)
    for l in range(nlayers):
        x_src = x_in if l == 0 else S["x1"]
        last = (l == nlayers - 1)
        if phases is None or "A" in phases:
            phase_A(B, l, x_src, W, consts)
            P.barrier()
        if phases is None or "B" in phases:
            phase_B(B, l, W, consts)
            P.barrier()
        if phases is None or "C" in phases:
            phase_C(B, l, W, consts)
            P.barrier()
        if phases is None or "D" in phases:
            phase_D(B, l, W, consts)
            P.barrier()
        if phases is None or "E" in phases:
            phase_E(B, l, W, consts)
            P.barrier()
        if phases is None or "F" in phases:
            phase_F(B, l, x_src, y_out if last else S["x1"], W, consts, last)
            P.barrier()
    P.barrier()
    st = P.emit(es)
    B.stats = st
    return B


def phase_A(B, l, x_src, W, C):
    nc, P, S, T = B.nc, B.P, B.S, B.T
    B.reset_arena()
    TBA = min(1024, T)
    nblk = T // TBA
    nsub = TBA // 512
    ntile = TBA // 128
    CWMAX = 544
    normw = B.alloc([D], F32, "normw", const=True)
    identb = B.alloc([128], BF16, "identb", const=True)
    ones_b = B.alloc([128], BF16, "ones_b", const=True)
    rot64 = B.alloc([64], F32, "rot64", const=True)
    rot32 = B.alloc([32], F32, "rot32", const=True)
    qnw = B.alloc([4], F32, "qnw", const=True)
    kvnw = B.alloc([2], F32, "kvnw", const=True)
    hT = B.alloc([16, TBA], BF16, "hT")
    xring = B.ring(2, [D], F32, "x")
    hb = B.alloc([D], BF16, "hb")
    sqj = B.alloc([D], BF16, "sqj")
    stat = B.alloc([4], F32, "stat")
    wring = B.ring(3, [16, CWMAX], BF16, "w")
    ost = B.ring(4, [TBA], BF16, "ost")
    ostT = B.ring(3, [CWMAX], BF16, "ostT")
    ostF = B.ring(2, [32], F32, "ostF")
    lat = B.alloc([7, TBA], F32, "lat")
    rf = B.ring(2, [512], F32, "rf")
    cs32 = B.alloc([2, TBA], F32, "cs32")
    cs64 = B.alloc([2, TBA], F32, "cs64")
    tmpf = B.ring(2, [512], F32, "tmpf")
    tmpb = B.ring(2, [512], BF16, "tmpb")
    cd = P.dsem()
    cd32 = P.dsem()
    cd64 = P.dsem()
    B.load("sp", normw, W["norm_w"][l].partition_broadcast(128), cd)
    B.load("pool", identb, C["ident"][:, :], P.dsem())
    B.load("sp", rot64[0:64], C["rot64"][:, :], cd)
    B.load("sp", rot32[0:32], C["rot32"][:, :], cd)
    B.load("sp", qnw, W["mla_q_norm"][l].rearrange("(c p) -> p c", p=128), cd)
    B.load("sp", kvnw, W["mla_kv_norm"][l].rearrange("(c p) -> p c", p=128), cd)
    B.memset("dve", ones_b, 1.0)
    pb = [B.psum_tl(i) for i in range(8)]
    pbb = [B.psum_tl(i, BF16) for i in range(8)]
    mmbank = _Cycle([0, 1, 2, 3])
    auxbank = _Cycle([4, 5])
    rotbank = _Cycle([6, 7])
    w_in = W["w_in"][l].rearrange("(k p) c -> p k c", p=128)

    wtiles = []
    for (nm, c0, ncol, orient, epi) in IN_GROUPS:
        if nm in ("ckv", "kr", "dt"):
            continue
        if nm == "cq":
            wtiles.append((0, 512, [("cq", 0, 512)]))
            wtiles.append((512, 320, [("ckv", 0, 256), ("kr", 256, 64)]))
            continue
        nt_ = ncol // 512
        for j in range(nt_):
            if nm == "xbc" and j == nt_ - 1:
                wtiles.append((c0 + j * 512, 544, [("xbc", 0, 512), ("dt", 512, 32)]))
            else:
                wtiles.append((c0 + j * 512, 512, [(nm, 0, 512)]))
    ginfo = {g[0]: g for g in IN_GROUPS}
    act_toggle = [0]

    for blk in range(nblk):
        t0 = blk * TBA
        B.load("sp", cs32[0:32], C["cs32"][:, :, t0:t0 + TBA].rearrange("a p t -> p a t"), cd32)
        B.load("sp", cs64[0:64], C["cs64"][:, :, t0:t0 + TBA].rearrange("a p t -> p a t"), cd64)
        for i in range(ntile):
            xt, xd = xring.next()
            B.load("sp", xt, x_src[t0 + i * 128:t0 + (i + 1) * 128, :], xd)
            B.act(sqj, xt, AF.Square, accum=stat[:, 0:1])
            B.rsqrt_act(stat[:, 1:2], stat[:, 0:1], 1.0 / D, EPS)
            B.stt("dve", hb, xt, stat[:, 1:2], normw, ALU.mult, ALU.mult)
            for half in range(2):
                bk = auxbank.next()
                for j in range(8):
                    k = half * 8 + j
                    B.tr(pbb[bk][:, j * 128:(j + 1) * 128], hb[:, k * 128:(k + 1) * 128], identb)
                B.copy("dve" if half == 0 else "act",
                       hT[:, half * 8:(half + 1) * 8, i * 128:(i + 1) * 128],
                       pbb[bk].v(lambda a: a.rearrange("p (j c) -> p j c", j=8)))
        for (c0, cw, parts) in wtiles:
            wt, wd = wring.next()
            B.load("pool", wt[:, :, 0:cw], w_in[:, :, c0:c0 + cw], wd)
            for (nm, po, pn) in parts:
                _, gc0, gn, orient, epi = ginfo[nm]
                gcol = c0 + po - gc0
                if orient == "F":
                    for cc in range(0, pn, 128):
                        m = min(128, pn - cc)
                        feat = gcol + cc
                        if epi != "lat":
                            og, od = ost.next()
                        for sb in range(nsub):
                            bk = mmbank.next()
                            for k in range(16):
                                B.mm(pb[bk][0:m, :], wt[:, k, po + cc:po + cc + m], hT[:, k, sb * 512:(sb + 1) * 512],
                                     start=(k == 0), stop=(k == 15))
                            src = pb[bk][0:m, :]
                            if epi == "lat":
                                li = {"cq": 0, "ckv": 4, "kr": 6}[nm] + cc // 128
                                B.copy("dve", lat[0:m, li, sb * 512:(sb + 1) * 512], src)
                            elif epi == "silu":
                                B.act(og[0:m, sb * 512:(sb + 1) * 512], src, AF.Silu)
                            elif epi == "sigmoid":
                                B.act(og[0:m, sb * 512:(sb + 1) * 512], src, AF.Sigmoid)
                            elif epi == "copy":
                                act_toggle[0] ^= 1
                                B.copy("dve", og[0:m, sb * 512:(sb + 1) * 512], src)
                            elif epi == "rope":
                                r, _ = rf.next()
                                B.copy("dve", r, src)
                                rb = rotbank.next()
                                B.mm(pb[rb][0:32, :], rot32[0:32, 0:32], r[0:32, :])
                                tf, _ = tmpf.next()
                                sl = slice(sb * 512, (sb + 1) * 512)
                                B.tt("dve", tf[0:32], r[0:32], cs32[0:32, 0, sl], ALU.mult)
                                tf2, _ = tmpf.next()
                                B.tt("dve", tf2[0:32], pb[rb][0:32, :], cs32[0:32, 1, sl], ALU.mult)
                                B.copy("act", og[:, sl], r)
                                B.tt("dve", og[0:32, sl], tf[0:32], tf2[0:32], ALU.add)
                        if epi != "lat":
                            sname = {"g_mla": "sg_mla", "g_diff": "sg_diff", "g_pool": "sg_pool"}.get(nm, nm)
                            B.store("sp", S[sname][feat:feat + m, t0:t0 + TBA], og[0:m, :], od)
                else:
                    for i in range(ntile):
                        bk = mmbank.next()
                        for k in range(16):
                            B.mm(pb[bk][:, 0:pn], hT[:, k, i * 128:(i + 1) * 128], wt[:, k, po:po + pn],
                                 start=(k == 0), stop=(k == 15))
                        src = pb[bk][:, 0:pn]
                        rows = slice(t0 + i * 128, t0 + (i + 1) * 128)
                        if epi == "copyf":
                            og, od = ostF.next()
                            B.copy("dve", og[:, 0:pn], src)
                            B.store("sp", S["dt"][rows, :], og[:, 0:pn], od)
                        else:
                            og, od = ostT.next()
                            if epi == "silu":
                                B.act(og[:, 0:pn], src, AF.Silu)
                            else:
                                B.copy("dve", og[:, 0:pn], src)
                            sname = {"z": "sz"}.get(nm, nm)
                            B.store("sp", S[sname][rows, gcol:gcol + pn], og[:, 0:pn], od)
            if parts[0][0] == "ckv":
                for sb in range(nsub):
                    sl = slice(sb * 512, (sb + 1) * 512)
                    for (nm, li0, nch, wcol, dim) in (("cqn", 0, 4, qnw, 512), ("ckvn", 4, 2, kvnw, 256)):
                        bk = auxbank.next()
                        for c in range(nch):
                            tb_, _ = tmpb.next()
                            B.act(tb_, lat[:, li0 + c, sl], AF.Square)
                            B.mm(pb[bk], ones_b, tb_, start=(c == 0), stop=(c == nch - 1))
                        tf, _ = tmpf.next()
                        B.rsqrt_act(tf, pb[bk], 1.0 / dim, EPS)
                        for c in range(nch):
                            og, od = ost.next()
                            B.stt("dve", og[:, 0:512], lat[:, li0 + c, sl], wcol[:, c:c + 1], tf, ALU.mult, ALU.mult)
                            B.store("sp", S[nm][c * 128:(c + 1) * 128, t0 + sb * 512:t0 + (sb + 1) * 512], og[:, 0:512], od)
                    rb = rotbank.next()
                    B.mm(pb[rb][0:64, :], rot64[0:64, 0:64], lat[0:64, 6, sl])
                    tf, _ = tmpf.next()
                    B.tt("dve", tf[0:64], lat[0:64, 6, sl], cs64[0:64, 0, sl], ALU.mult)
                    tf2, _ = tmpf.next()
                    B.tt("dve", tf2[0:64], pb[rb][0:64, :], cs64[0:64, 1, sl], ALU.mult)
                    og, od = ost.next()
                    B.tt("dve", og[0:64, 0:512], tf[0:64], tf2[0:64], ALU.add)
                    B.store("sp", S["kpe"][0:64, t0 + sb * 512:t0 + (sb + 1) * 512], og[0:64, 0:512], od)


def attn_qblock(B, qsl, s_terms, v_list, acc_banks, den_bank, Sring, Pring, pb, ones_b, scale, NT, SEG, q0, P4ring):
    LA = 2
    DB = 4
    pend = []
    denq = []
    first = None
    p4 = None
    ng = NT // DB
    for step in range(NT + LA):
        if step < NT:
            kb = step
            sb = Sring.next()
            ksl = slice(kb * 128, (kb + 1) * 128)
            for i, (kT, qT) in enumerate(s_terms):
                B.mm(pb[sb], kT[:, ksl], qT[:, qsl], start=(i == 0), stop=(i == len(s_terms) - 1))
            cross = (q0 // SEG) != ((kb * 128) // SEG)
            pt, _ = Pring.next()
            B.act(pt, pb[sb], AF.Exp, scale=scale, bias=(B.cbias if cross else None))
            pend.append((kb, pt))
            j = kb % DB
            if j == 0:
                first = pt
            elif j == 1:
                p4, _ = P4ring.next()
                B.tt("dve", p4, first, pt, ALU.add)
            else:
                B.tt("dve", p4, p4, pt, ALU.add)
            if j == DB - 1:
                denq.append((kb // DB, p4))
        if step >= LA:
            kb, pt = pend.pop(0)
            for v, ab in zip(v_list, acc_banks):
                B.mm(pb[ab], v[:, kb, :], pt, start=(kb == 0), stop=(kb == NT - 1))
            if kb % DB == DB - 1:
                g, pp = denq.pop(0)
                B.mm(pb[den_bank], ones_b, pp, start=(g == 0), stop=(g == ng - 1))


def phase_B(B, l, W, C):
    nc, P, S, T, NT, NB, SEG = B.nc, B.P, B.S, B.T, B.NT, B.NB, B.SEG
    B.reset_arena()
    cqn = B.alloc([4, T], BF16, "cqn")
    ckvn = B.alloc([2, T], BF16, "ckvn")
    kpe = B.alloc([T], BF16, "kpe")
    wuq = B.alloc([4, 1536], BF16, "wuq", const=True)
    wukv = B.alloc([2, 2048], BF16, "wukv", const=True)
    ones_b = B.alloc([128], BF16, "ones_b", const=True)
    rot64 = B.alloc([64], F32, "rot64", const=True)
    qn = [B.alloc([T], BF16, f"qn{i}") for i in range(2)]
    qp = [B.alloc([T], BF16, f"qp{i}") for i in range(2)]
    kn = [B.alloc([T], BF16, f"kn{i}") for i in range(2)]
    vv = [B.alloc([NT, 128], BF16, f"v{i}") for i in range(2)]
    csr = B.ring(2, [2, 512], F32, "cs")
    rr = B.ring(2, [512], F32, "rr")
    tmpf = B.ring(4, [512], F32, "tmpf")
    Pring = B.ring(5, [512], BF16, "pt")
    P4ring = B.ring(3, [512], BF16, "p4")
    sgr = B.ring(2, [512], BF16, "sg")
    ost = B.ring(2, [512], BF16, "ost")
    cd = P.dsem()
    B.load("sp", cqn, S["cqn"].rearrange("(c p) t -> p c t", p=128), cd)
    B.load("sp", ckvn, S["ckvn"].rearrange("(c p) t -> p c t", p=128), cd)
    B.load("sp", kpe[0:64], S["kpe"][:, :], cd)
    B.memset("pool", kpe[64:128], 0.0)
    B.memset("pool", qp[0][64:128], 0.0)
    B.memset("pool", qp[1][64:128], 0.0)
    cdp = P.dsem()
    B.load("pool", wuq, W["mla_w_uq"][l].rearrange("(c p) n -> p c n", p=128), cdp)
    B.load("pool", wukv, W["mla_w_ukv"][l].rearrange("(c p) n -> p c n", p=128), cdp)
    B.load("sp", rot64[0:64], C["rot64"][:, :], cd)
    B.memset("dve", ones_b, 1.0)
    pb = [B.psum_tl(i) for i in range(8)]
    Sring = _Cycle([0, 1, 2])
    accs = _Cycle([(3, 4), (5, 6)])
    scale = float(192 ** -0.5)

    def prologue(h):
        s = h % 2
        for sb in range(NB):
            sl = slice(sb * 512, (sb + 1) * 512)
            for c in range(4):
                B.mm(pb[7], wuq[:, c, h * 192:h * 192 + 128], cqn[:, c, sl], start=(c == 0), stop=(c == 3))
            B.copy("dve", qn[s][:, sl], pb[7])
            for c in range(4):
                B.mm(pb[7][0:64], wuq[:, c, h * 192 + 128:h * 192 + 192], cqn[:, c, sl], start=(c == 0), stop=(c == 3))
            r, _ = rr.next()
            B.copy("dve", r[0:64], pb[7][0:64])
            cs, cdm = csr.next()
            B.load("sp", cs[0:64], C["cs64"][:, :, sl].rearrange("a p t -> p a t"), cdm)
            B.mm(pb[7][0:64], rot64[0:64, 0:64], r[0:64])
            tf, _ = tmpf.next()
            B.tt("dve", tf[0:64], r[0:64], cs[0:64, 0], ALU.mult)
            tf2, _ = tmpf.next()
            B.tt("dve", tf2[0:64], pb[7][0:64], cs[0:64, 1], ALU.mult)
            B.tt("dve", qp[s][0:64, sl], tf[0:64], tf2[0:64], ALU.add)
            for c in range(2):
                B.mm(pb[7], wukv[:, c, h * 256:h * 256 + 128], ckvn[:, c, sl], start=(c == 0), stop=(c == 1))
            B.copy("act", kn[s][:, sl], pb[7])
            for i in range(4):
                tsl = slice(sb * 512 + i * 128, sb * 512 + (i + 1) * 128)
                for c in range(2):
                    B.mm(pb[7][:, i * 128:(i + 1) * 128], ckvn[:, c, tsl], wukv[:, c, h * 256 + 128:h * 256 + 256],
                         start=(c == 0), stop=(c == 1))
            B.copy("act", vv[s][:, sb * 4:(sb + 1) * 4, :], pb[7].v(lambda a: a.rearrange("p (i d) -> p i d", i=4)))

    prologue(0)
    for h in range(8):
        s = h % 2
        for qb in range(NB):
            qsl = slice(qb * 512, (qb + 1) * 512)
            ab, db = accs.next()
            sg, sgd = sgr.next()
            B.load("sp", sg, S["sg_mla"][h * 128:(h + 1) * 128, qsl], sgd)
            attn_qblock(B, qsl, [(kn[s], qn[s]), (kpe, qp[s])], [vv[s]], [ab], db,
                        Sring, Pring, pb, ones_b, scale, NT, SEG, qb * 512, P4ring)
            if qb == 0 and h + 1 < 8:
                prologue(h + 1)
            rd, _ = tmpf.next()
            B.recip(rd, pb[db])
            o, _ = tmpf.next()
            B.tt("dve", o, pb[ab], rd, ALU.mult)
            og, od = ost.next()
            B.tt("dve", og, o, sg, ALU.mult)
            B.store("sp", S["br_mla"][h * 128:(h + 1) * 128, qsl], og, od)
    B.dbg("qn", qn[1]); B.dbg("qp", qp[1], 64); B.dbg("kn", kn[1]); B.dbg("vv", vv[1]); B.dbg("kpe", kpe, 64)


def phase_C(B, l, W, C):
    nc, P, S, T, NT, NB, SEG = B.nc, B.P, B.S, B.T, B.NT, B.NB, B.SEG
    B.reset_arena()
    lam_init = 0.8 - 0.6 * math.exp(-0.3 * l)
    ones_b = B.alloc([128], BF16, "ones_b", const=True)
    ones_f = B.alloc([128], F32, "ones_f", const=True)
    subw = B.alloc([2], F32, "subw", const=True)
    lp = B.alloc([512], F32, "lp")
    lt = B.alloc([256], F32, "lt")
    ls = B.alloc([8], F32, "ls")
    nlam = B.alloc([1], F32, "nlam")
    qk = [[B.alloc([T], BF16, f"qk{i}{j}") for j in range(4)] for i in range(2)]
    vv = [B.alloc([NT, 256], BF16, f"v{i}") for i in range(2)]
    hd = [P.dsem() for _ in range(2)]
    Pring = B.ring(5, [512], BF16, "pt")
    P4ring = B.ring(3, [512], BF16, "p4")
    on = [B.alloc([2, 512], F32, f"on{i}") for i in range(2)]
    oo = B.alloc([2, 512], F32, "oo")
    tmpf = B.ring(3, [512], F32, "tmpf")
    sqr = B.ring(2, [512], BF16, "sq")
    sgr = B.ring(2, [2, 512], BF16, "sg")
    ost = B.ring(2, [2, 512], BF16, "ost")
    cd = P.dsem()
    B.memset("dve", ones_b, 1.0)
    B.memset("dve", ones_f, 1.0)
    B.load("sp", subw, W["diff_subln"][l].rearrange("(c p) -> p c", p=128), cd)
    B.load("sp", lp[0:1], W["diff_lambda"][l].rearrange("(o a) d -> o (a d)", o=1), cd)
    pb = [B.psum_tl(i) for i in range(8)]
    B.tt("dve", lt[0:1, 0:128], lp[0:1, 0:128], lp[0:1, 128:256], ALU.mult)
    B.tt("dve", lt[0:1, 128:256], lp[0:1, 256:384], lp[0:1, 384:512], ALU.mult)
    e = nc.vector
    for j in range(2):
        o_, a_ = ls[0:1, j:j + 1], lt[0:1, j * 128:(j + 1) * 128]
        P.op("dve", (lambda o_=o_, a_=a_: e.reduce_sum(out=o_.ap, in_=a_.ap, axis=AX.X)), reads=[a_], writes=[o_])
    B.act(ls[0:1, 2:4], ls[0:1, 0:2], AF.Exp)
    B.tt("dve", ls[0:1, 4:5], ls[0:1, 3:4], ls[0:1, 2:3], ALU.subtract)
    B.ts("dve", ls[0:1, 5:6], ls[0:1, 4:5], -lam_init, None, op0=ALU.add)
    B.mm(pb[7][:, 0:1], ones_f[0:1, :], ls[0:1, 5:6])
    B.copy("dve", nlam, pb[7][:, 0:1])
    Sring = _Cycle([6, 7])
    accsets = _Cycle([(0, 1, 2), (3, 4, 5)])
    scale = float(128 ** -0.5)

    def loadhead(h):
        s = h % 2
        for j, (nm, r0) in enumerate((("dq", 2 * h), ("dq", 2 * h + 1), ("dk", 2 * h), ("dk", 2 * h + 1))):
            B.load("sp", qk[s][j], S[nm][r0 * 128:(r0 + 1) * 128, :], hd[s])
        B.load("sp", vv[s], S["dv"][:, h * 256:(h + 1) * 256].rearrange("(n p) c -> p n c", p=128), hd[s])

    loadhead(0)
    for h in range(4):
        s = h % 2
        for qb in range(NB):
            qsl = slice(qb * 512, (qb + 1) * 512)
            sg, sgd = sgr.next()
            B.load("sp", sg, S["sg_diff"][h * 256:(h + 1) * 256, qsl].rearrange("(c p) t -> p c t", p=128), sgd)
            for sm in range(2):
                a0, a1, db = accsets.next()
                attn_qblock(B, qsl, [(qk[s][2 + sm], qk[s][sm])], [vv[s][:, :, 0:128], vv[s][:, :, 128:256]],
                            [a0, a1], db, Sring, Pring, pb, ones_b, scale, NT, SEG, qb * 512, P4ring)
                if qb == 0 and sm == 0 and h + 1 < 4:
                    loadhead(h + 1)
                rd, _ = tmpf.next()
                B.recip(rd, pb[db])
                B.tt("dve", on[sm][:, 0], pb[a0], rd, ALU.mult)
                B.tt("dve", on[sm][:, 1], pb[a1], rd, ALU.mult)
            B.stt("dve", oo, on[1], nlam[:, 0:1], on[0], ALU.mult, ALU.add)
            sbk = Sring.next()
            for c in range(2):
                sq, _ = sqr.next()
                B.act(sq, oo[:, c], AF.Square)
                B.mm(pb[sbk], ones_b, sq, start=(c == 0), stop=(c == 1))
            rs, _ = tmpf.next()
            B.rsqrt_act(rs, pb[sbk], 1.0 / 256, EPS)
            og, od = ost.next()
            for c in range(2):
                tf, _ = tmpf.next()
                B.stt("dve", tf, oo[:, c], subw[:, c:c + 1], rs, ALU.mult, ALU.mult)
                B.stt("dve", og[:, c], tf, 1.0 - lam_init, sg[:, c], ALU.mult, ALU.mult)
            B.store("sp", S["br_diff"][h * 256:(h + 1) * 256, qsl].rearrange("(c p) t -> p c t", p=128), og, od)


def phase_E(B, l, W, C):
    nc, P, S, T, NT, NB, SEG = B.nc, B.P, B.S, B.T, B.NT, B.NB, B.SEG
    B.reset_arena()
    PADW = SEG + 16
    ubr = B.ring(2, [2, SEG], BF16, "ub")
    bufs = [B.alloc([2, PADW], F32, f"pbuf{i}") for i in range(2)]
    rcb = B.alloc([2, SEG], F32, "rcb")
    pooled = [B.alloc([T], BF16, f"pooled{i}") for i in range(2)]
    mean = B.alloc([2, SEG], F32, "mean")
    pw = B.alloc([2, 256], BF16, "pw")
    psc = B.alloc([8], F32, "psc", const=True)
    sgr = B.ring(2, [512], BF16, "sg")
    ost = B.ring(2, [512], BF16, "ost")
    tmpf = B.ring(2, [512], F32, "tmpf")
    cd = P.dsem()
    rcd = P.dsem()
    pwd = P.dsem()
    B.load("sp", psc, W["pool_scale"][l].rearrange("(c p) -> p c", p=128), cd)
    pb = [B.psum_tl(i) for i in range(8)]
    banks = _Cycle(list(range(8)))
    shifts = [(-1, 0, 1, PADW), (-1, 1, 2, PADW - 1), (-2, 2, 4, PADW - 3), (-4, 4, 8, PADW - 7)]
    for g in range(4):
        B.load("sp", rcb, C["rcnt"][g].rearrange("(s t) -> s t", s=2).partition_broadcast(128), rcd)
        B.load("pool", pw, W["pool_w"][l, g].rearrange("(c p) d -> p c d", p=128), pwd)
        for j in range(2):
            c = 2 * g + j
            ub, ud = ubr.next()
            B.load("sp", ub, S["u"][c * 128:(c + 1) * 128, :].rearrange("p (s t) -> p s t", s=2), ud)
            a = bufs[0]
            B.memset("pool", a[:, 0, 0:8], 0.0)
            B.memset("pool", a[:, 1, 8 + SEG:PADW], 0.0)
            B.copy("pool", a[:, :, 8:8 + SEG], ub)
            B.ts("pool", a[:, 0, 8 + SEG:PADW], ub[:, 1, 0:8], B.link[:, 0:1], None, op0=ALU.mult)
            B.ts("pool", a[:, 1, 0:8], ub[:, 0, SEG - 8:SEG], B.link[:, 0:1], None, op0=ALU.mult)
            cur = 0
            for st in range(g + 1):
                s0, s1, lo, hi = shifts[st]
                src, dst = bufs[cur], bufs[1 - cur]
                B.tt("dve" if st % 2 == 0 else "pool", dst[:, :, lo:hi], src[:, :, lo + s0:hi + s0], src[:, :, lo + s1:hi + s1], ALU.add)
                cur = 1 - cur
            B.tt("dve", mean, bufs[cur][:, :, 8:8 + SEG], rcb, ALU.mult)
            B.tt("dve", pooled[j].v(lambda a_: a_.rearrange("p (s t) -> p s t", s=2)), mean, ub, ALU.subtract)
        for dc in range(2):
            co = 2 * g + dc
            for sb in range(NB):
                sl = slice(sb * 512, (sb + 1) * 512)
                bk = banks.next()
                for j in range(2):
                    B.mm(pb[bk], pw[:, j, dc * 128:(dc + 1) * 128], pooled[j][:, sl], start=(j == 0), stop=(j == 1))
                sg, sgd = sgr.next()
                B.load("sp", sg, S["sg_pool"][co * 128:(co + 1) * 128, sl], sgd)
                og, od = ost.next()
                B.stt("dve", og, pb[bk], psc[:, co:co + 1], sg, ALU.mult, ALU.mult)
                B.store("sp", S["br_pool"][co * 128:(co + 1) * 128, sl], og, od)


def phase_F(B, l, x_src, x_dst, W, C, last):
    nc, P, S, T, NT, NB, SEG = B.nc, B.P, B.S, B.T, B.NT, B.NB, B.SEG
    pb = [B.psum_tl(i) for i in range(8)]
    B.reset_arena()
    TB1 = min(2048, T)
    nsb = TB1 // 512
    brT = B.alloc([32, TB1], BF16, "brT")
    wbr = B.ring(2, [32, 128], BF16, "wbr")
    mgr = B.ring(4, [4, 512], BF16, "mg")
    tmpf = B.ring(8, [512], F32, "tmpf")
    ost = B.ring(3, [512], BF16, "ost")
    brd = P.dsem()
    banks = _Cycle(list(range(8)))
    wb_src = W["w_branch"][l].rearrange("i (k p) n -> p i k n", p=128)
    mg_src = S["mg"].rearrange("(i d) t -> d i t", i=4)
    brs = [S["br_mla"], S["br_diff"], S["br_ssd"], S["br_pool"]]
    for blk in range(T // TB1):
        t0 = blk * TB1
        for i in range(4):
            B.load("sp", brT[:, i * 8:(i + 1) * 8, :], brs[i][:, t0:t0 + TB1].rearrange("(k p) t -> p k t", p=128), brd)
        wtiles = {}
        mgt = {}

        def f1_load(it, t0=t0, wtiles=wtiles, mgt=mgt):
            dmc, sb = divmod(it, nsb)
            if sb == 0:
                wb, wd = wbr.next()
                for i in range(4):
                    B.load("pool", wb[:, i * 8:(i + 1) * 8, :], wb_src[:, i, :, dmc * 128:(dmc + 1) * 128], wd)
                wtiles[dmc] = wb
            sl = slice(t0 + sb * 512, t0 + (sb + 1) * 512)
            mg, mgd = mgr.next()
            B.load("sp", mg, mg_src[dmc * 128:(dmc + 1) * 128, :, sl], mgd)
            mgt[it] = mg

        def f1_compute(it, t0=t0, wtiles=wtiles, mgt=mgt):
            dmc, sb = divmod(it, nsb)
            wb = wtiles[dmc]
            mg = mgt.pop(it)
            sl = slice(t0 + sb * 512, t0 + (sb + 1) * 512)
            lsl = slice(sb * 512, (sb + 1) * 512)
            ts_ = []
            for i in range(4):
                bk = banks.next()
                for k in range(8):
                    B.mm(pb[bk], wb[:, i * 8 + k, :], brT[:, i * 8 + k, lsl], start=(k == 0), stop=(k == 7))
                tf, _ = tmpf.next()
                B.tt("dve", tf, pb[bk], mg[:, i], ALU.mult)
                ts_.append(tf)
            B.tt("dve", ts_[0], ts_[0], ts_[1], ALU.add)
            B.tt("dve", ts_[2], ts_[2], ts_[3], ALU.add)
            og, od = ost.next()
            B.tt("dve", og, ts_[0], ts_[2], ALU.add)
            B.store("sp", S["mT"][dmc * 128:(dmc + 1) * 128, sl], og, od)

        prefetch_loop(16 * nsb, 2, f1_load, f1_compute)
    P.barrier()
    B.reset_arena()
    wout = B.alloc([16, D], BF16, "wout", const=True)
    mTr = B.ring(2, [16, 512], BF16, "mT")
    xring = B.ring(4, [D], F32, "x")
    stat = B.ring(2, [4], F32, "stat")
    sqj = B.alloc([D], BF16, "sqj")
    if last:
        fnw = B.alloc([D], F32, "fnw", const=True)
    cd = P.dsem()
    for k4 in range(4):
        B.load("pool", wout[:, k4 * 4:(k4 + 1) * 4, :],
               W["w_out"][l].rearrange("(k p) n -> p k n", p=128)[:, k4 * 4:(k4 + 1) * 4, :], cd)
    if last:
        B.load("sp", fnw, W["final_norm"].partition_broadcast(128), P.dsem())
    banks = _Cycle(list(range(8)))
    mts = {}
    xts = {}

    def f2_load(it):
        tb, i = divmod(it, 4)
        if i == 0:
            sl = slice(tb * 512, (tb + 1) * 512)
            mT, mTd = mTr.next()
            B.load("sp", mT, S["mT"][:, sl].rearrange("(k p) t -> p k t", p=128), mTd)
            mts[tb] = mT
        rows = slice(tb * 512 + i * 128, tb * 512 + (i + 1) * 128)
        xt, xd = xring.next()
        B.load("sp", xt, x_src[rows, :], xd)
        xts[it] = (xt, xd)

    def f2_compute(it):
        tb, i = divmod(it, 4)
        mT = mts[tb]
        xt, xd = xts.pop(it)
        rows = slice(tb * 512 + i * 128, tb * 512 + (i + 1) * 128)
        for nb in range(4):
            bk = banks.next()
            for k in range(16):
                B.mm(pb[bk], mT[:, k, i * 128:(i + 1) * 128], wout[:, k, nb * 512:(nb + 1) * 512],
                     start=(k == 0), stop=(k == 15))
            B.tt("dve", xt[:, nb * 512:(nb + 1) * 512], xt[:, nb * 512:(nb + 1) * 512], pb[bk], ALU.add)
        if last:
            st, _ = stat.next()
            B.act(sqj, xt, AF.Square, accum=st[:, 0:1])
            B.rsqrt_act(st[:, 1:2], st[:, 0:1], 1.0 / D, EPS)
            B.stt("dve", xt, xt, st[:, 1:2], fnw, ALU.mult, ALU.mult)
        B.store("sp", x_dst[rows, :], xt, xd)

    prefetch_loop(NB * 4, 2, f2_load, f2_compute)


def phase_D(B, l, W, C):
    nc, P, S, T, NT, NB, SEG = B.nc, B.P, B.S, B.T, B.NT, B.NB, B.SEG
    pb = [B.psum_tl(i) for i in range(8)]
    pbb = [B.psum_tl(i, BF16) for i in range(8)]
    bc3 = lambda n: (lambda a: a.unsqueeze(2).broadcast_to([a.shape[0], a.shape[1], n]))
    B.reset_arena()
    cw = B.alloc([4, 12], F32, "cw", const=True)
    cbv = B.alloc([12], F32, "cbv", const=True)
    identb = B.alloc([128], BF16, "identb", const=True)
    xpad = [B.alloc([2, SEG + 3], F32, f"xpad{i}") for i in range(2)]
    acc = [B.alloc([2, SEG], F32, f"acc{i}") for i in range(2)]
    xc = B.ring(3, [T], BF16, "xc")
    stg = B.ring(4, [8, 128], BF16, "stg")
    cd = P.dsem()
    for j in range(4):
        B.load("sp", cw[:, j, :], W["ssd_conv_w"][l, j].rearrange("(c p) -> p c", p=128), cd)
    B.load("sp", cbv, W["ssd_conv_b"][l].rearrange("(c p) -> p c", p=128), cd)
    B.load("pool", identb, C["ident"][:, :], P.dsem())
    trb = _Cycle([0, 1, 2, 3])
    xpd = [P.dsem() for _ in range(2)]
    for c in range(12):
        xp, ac = xpad[c % 2], acc[c % 2]
        B.load("pool", xp[:, :, 2:2 + SEG], S["xbc"][c * 128:(c + 1) * 128, :].rearrange("p (s t) -> p s t", s=2), xpd[c % 2])
        B.memset("pool", xp[:, 0, 0:2], 0.0)
        B.memset("pool", xp[:, 1, SEG + 2:SEG + 3], 0.0)
        B.ts("pool", xp[:, 0, SEG + 2:SEG + 3], xp[:, 1, 2:3], B.link[:, 0:1], None, op0=ALU.mult)
        B.ts("pool", xp[:, 1, 0:2], xp[:, 0, SEG:SEG + 2], B.link[:, 0:1], None, op0=ALU.mult)
        B.ts("dve", ac, xp[:, :, 0:SEG], cw[:, 0, c:c + 1], cbv[:, c:c + 1], op0=ALU.mult, op1=ALU.add)
        for j in range(1, 4):
            B.stt("dve", ac, xp[:, :, j:j + SEG], cw[:, j, c:c + 1], ac, ALU.mult, ALU.add)
        xo, xod = xc.next()
        B.act(xo.v(lambda a: a.rearrange("p (s t) -> p s t", s=2)), ac, AF.Silu)
        if c >= 8:
            B.store("sp", S["bcT"][(c - 8) * 128:(c - 7) * 128, :], xo, xod)
        if c < 10:
            for i0 in range(0, NT, 8):
                n = min(8, NT - i0)
                bk = trb.next()
                for i in range(n):
                    B.tr(pbb[bk][:, i * 128:(i + 1) * 128], xo[:, (i0 + i) * 128:(i0 + i + 1) * 128], identb)
                sg_, sgd = stg.next()
                B.copy("act", sg_[:, 0:n, :],
                       pbb[bk][:, 0:n * 128].v(lambda a: a.rearrange("p (i c) -> p i c", c=128)))
                B.store("sp", S["xsb"][i0 * 128:(i0 + n) * 128, c * 128:(c + 1) * 128].rearrange("(n p) c -> p n c", p=128),
                        sg_[:, 0:n, :], sgd)
    P.barrier()
    B.reset_arena()
    tri = B.alloc([5, 128], F32, "tri", const=True)
    trib = B.alloc([2, 128], BF16, "trib", const=True)
    identb = B.alloc([128], BF16, "identb", const=True)
    dt = B.alloc([NT, 32], F32, "dt")
    da = B.alloc([NT, 32], F32, "da")
    dtb = B.alloc([32], F32, "dtb", const=True)
    av = B.alloc([32], F32, "av")
    dvec = B.alloc([16], F32, "dvec", const=True)
    nrmw = B.alloc([1024], F32, "nrmw", const=True)
    stats = B.alloc([NT, 5, 32], F32, "stats")
    est = B.alloc([NT, 5, 32], F32, "est")
    coef = B.alloc([NT, 32], F32, "coef")
    cd = P.dsem()
    B.load("sp", tri, C["tri"].rearrange("j p c -> p j c"), cd)
    cdp = P.dsem()
    B.load("pool", trib, C["tri"][0:2].rearrange("j p c -> p j c"), cdp)
    B.load("pool", identb, C["ident"][:, :], cdp)
    B.load("sp", dt, S["dt"].rearrange("(n p) c -> p n c", p=128), cd)
    B.load("sp", dtb, W["ssd_dt_bias"][l].rearrange("a b -> (a b)").partition_broadcast(128), cd)
    B.load("sp", av, W["ssd_a_log"][l].rearrange("a b -> (a b)").partition_broadcast(128), cd)
    B.load("sp", dvec, W["ssd_d"][l].partition_broadcast(128), cd)
    B.load("sp", nrmw, W["ssd_norm"][l].partition_broadcast(128), cd)
    bc_nt = lambda a: a.unsqueeze(1).broadcast_to([128, NT, 32])
    B.tt("dve", dt, dt, dtb.v(bc_nt), ALU.add)
    B.act(dt, dt, AF.Exp)
    B.act(dt, dt, AF.Ln, bias=B.one_col[:, 0:1])
    B.act(av, av, AF.Exp)
    B.ts("dve", av, av, -1.0, None, op0=ALU.mult)
    B.tt("dve", da, dt, av.v(bc_nt), ALU.mult)
    sbk = _Cycle([0, 1, 2, 3])
    for c in range(NT):
        bk = sbk.next()
        for j in range(5):
            B.mm(pb[bk][:, j * 32:(j + 1) * 32], tri[:, j, :], da[:, c, :])
        B.copy("dve" if c % 2 == 0 else "act", stats[:, c].v(lambda a: a.rearrange("p j h -> p (j h)")), pb[bk][:, 0:160])
    B.act(est, stats, AF.Exp)
    B.tt("dve", coef[:, :, 0:16], dt[:, :, 0:16], est[:, :, 2, 0:16], ALU.mult)
    B.tt("dve", coef[:, :, 16:32], dt[:, :, 16:32], est[:, :, 3, 16:32], ALU.mult)
    B.dbg("est", est); B.dbg("dt", dt); B.dbg("da", da); B.dbg("coef", coef)
    mark = B.off
    xsr = B.ring(4, [1280], BF16, "xs")
    xdw = B.ring(2, [1024], BF16, "xdw")
    Hst = [B.alloc([1024], F32, f"H{i}") for i in range(2)]
    Hsv = B.ring(3, [1024], BF16, "Hsv")
    B.memset("dve", Hst[0], 0.0)
    B.memset("dve", Hst[1], 0.0)
    stb = _Cycle([4, 5, 6, 7])
    d2x = {}

    def d2_load(it):
        i, d = divmod(it, 2)
        c = i if d == 0 else NT - 1 - i
        xs, xsd = xsr.next()
        B.load("sp", xs, S["xsb"][c * 128:(c + 1) * 128, :], xsd)
        d2x[it] = xs

    def d2_compute(it):
            i, d = divmod(it, 2)
            c = i if d == 0 else NT - 1 - i
            xs = d2x.pop(it)
            xw, _ = xdw.next()
            B.tt("dve", xw.v(lambda a: a.rearrange("p (h q) -> p h q", q=64)),
                 xs[:, 0:1024].v(lambda a: a.rearrange("p (h q) -> p h q", q=64)),
                 coef[:, c, d * 16:(d + 1) * 16].v(bc3(64)), ALU.mult)
            H = Hst[d]
            if (d == 0 and c == NT // 2) or (d == 1 and c == NT // 2 - 1):
                B.ts("dve", H, H, B.link[:, 0:1], None, op0=ALU.mult)
            hs, hsd = Hsv.next()
            B.copy("act", hs, H)
            B.store("sp", S["H"][d, c], hs, hsd)
            B.tt("dve", H.v(lambda a: a.rearrange("p (h q) -> p h q", q=64)),
                 H.v(lambda a: a.rearrange("p (h q) -> p h q", q=64)),
                 est[:, c, 4, d * 16:(d + 1) * 16].v(bc3(64)), ALU.mult)
            for g in range(2):
                bk = stb.next()
                B.mm(pb[bk], xs[:, 1024 + g * 128:1024 + (g + 1) * 128], xw[:, g * 512:(g + 1) * 512])
                B.tt("dve", H[:, g * 512:(g + 1) * 512], H[:, g * 512:(g + 1) * 512], pb[bk], ALU.add)

    prefetch_loop(NT * 2, 2, d2_load, d2_compute)
    P.barrier()
    B.off = mark
    xsr = B.ring(3, [1024], BF16, "xs")
    bct = B.ring(3, [4, 128], BF16, "bct")
    Hr = B.ring(3, [2, 1024], BF16, "Hr")
    szr = B.ring(4, [1024], BF16, "sz")
    Dm = B.ring(3, [16, 128], F32, "Dm")
    cbm = B.ring(2, [2, 2, 128], BF16, "cbm")
    Lx = B.ring(3, [4, 128], BF16, "Lx")
    Mt = B.ring(3, [4, 128], BF16, "Mt")
    xdr = B.ring(3, [1024], BF16, "xd")
    yo = B.ring(3, [1024], F32, "yo")
    yo2 = B.alloc([1024], F32, "yo2")
    t3r = B.ring(3, [1024], F32, "t3")
    sqj = B.alloc([512], BF16, "sqj")
    ss = B.ring(2, [4], F32, "ss")
    yn = B.ring(2, [1024], BF16, "yn")
    stg = B.ring(2, [8, 128], BF16, "stg")
    cbk = _Cycle([0, 1])
    sgk = _Cycle([2, 3])
    ydk = [4, 5]
    yfk = _Cycle([6, 7])
    hq = lambda a: a.rearrange("p (h q) -> p h q", q=64)
    d3t = {}

    def d3_load(c):
        rows = slice(c * 128, (c + 1) * 128)
        xs, xsd = xsr.next()
        B.load("sp", xs, S["xsb"][rows, 0:1024], xsd)
        bt, btd = bct.next()
        B.load("sp", bt, S["bcT"][:, rows].rearrange("(j p) t -> p j t", p=128), btd)
        Hc, Hd = Hr.next()
        B.load("sp", Hc[:, 0], S["H"][0, c], Hd)
        B.load("sp", Hc[:, 1], S["H"][1, c], Hd)
        sz, szd = szr.next()
        B.load("sp", sz, S["sz"][rows, :], szd)
        d3t[c] = (xs, bt, Hc, sz)

    d3m = {}

    def d3_front(c):
        xs, bt, Hc, sz = d3t.pop(c)
        bk = 0
        for g in range(2):
            B.mm(pb[bk][:, g * 128:(g + 1) * 128], bt[:, g, :], bt[:, 2 + g, :])
        cm, _ = cbm.next()
        for d in range(2):
            B.tt("dve", cm[:, d], pb[bk][:, 0:256].v(lambda a: a.rearrange("p (g l) -> p g l", g=2)),
                 trib[:, d, :].v(lambda a: a.unsqueeze(1).broadcast_to([128, 2, 128])), ALU.mult)
        yoc, _ = yo.next()
        t3, _ = t3r.next()
        B.tt("pool", t3.v(hq), xs.v(hq), dvec.v(bc3(64)), ALU.mult)
        for d in range(2):
            dm, _ = Dm.next()
            B.tt("pool", dm, tri[:, d, :].v(lambda a: a.unsqueeze(1).broadcast_to([128, 16, 128])),
                 da[:, c, d * 16:(d + 1) * 16].v(bc3(128)), ALU.mult)
            xd, _ = xdr.next()
            B.tt("dve", xd.v(hq), xs.v(hq), dt[:, c, d * 16:(d + 1) * 16].v(bc3(64)), ALU.mult)
            for q in range(4):
                g = q // 2
                sk = sgk.next()
                B.mm(pb[sk], tri[:, 2 + d, :], dm[:, q * 4:(q + 1) * 4, :].v(lambda a: a.rearrange("p h l -> p (h l)")))
                lx, _ = Lx.next()
                B.act(lx.v(lambda a: a.rearrange("p h l -> p (h l)")), pb[sk], AF.Exp)
                mt, _ = Mt.next()
                B.tt("dve", mt, lx, cm[:, d, g, :].v(lambda a: a.unsqueeze(1).broadcast_to([128, 4, 128])), ALU.mult)
                for hh in range(4):
                    hd = q * 4 + hh
                    B.mm(pb[ydk[g]][:, (hd % 8) * 64:(hd % 8 + 1) * 64], mt[:, hh, :], xd[:, hd * 64:(hd + 1) * 64],
                         start=(d == 0 and hd % 8 == 0), stop=(d == 1 and hd % 8 == 7), skip=True)
            for g in range(2):
                fk = yfk.next()
                B.mm(pb[fk], bt[:, 2 + g, :], Hc[:, d, g * 512:(g + 1) * 512])
                dst = (yoc if d == 0 else yo2)[:, g * 512:(g + 1) * 512]
                eai = est[:, c, d, d * 16 + g * 8:d * 16 + (g + 1) * 8]
                B.tt("dve", dst.v(hq), pb[fk].v(hq), eai.v(bc3(64)), ALU.mult)
        B.tt("dve", yoc, yoc, yo2, ALU.add)
        for g in range(2):
            B.tt("dve", yoc[:, g * 512:(g + 1) * 512], yoc[:, g * 512:(g + 1) * 512], pb[ydk[g]], ALU.add)
        d3m[c] = (yoc, t3, sz)

    def d3_back(c):
        rows = slice(c * 128, (c + 1) * 128)
        yoc, t3, sz = d3m.pop(c)
        B.tt("dve", yoc, yoc, t3, ALU.add)
        B.tt("dve", yoc, yoc, sz, ALU.mult)
        st, _ = ss.next()
        for g in range(2):
            B.act(sqj, yoc[:, g * 512:(g + 1) * 512], AF.Square, accum=st[:, g:g + 1])
        B.rsqrt_act(st[:, 2:4], st[:, 0:2], 1.0 / 512, EPS)
        ynt, _ = yn.next()
        for g in range(2):
            B.stt("dve", ynt[:, g * 512:(g + 1) * 512], yoc[:, g * 512:(g + 1) * 512],
                  st[:, 2 + g:3 + g], nrmw[:, g * 512:(g + 1) * 512], ALU.mult, ALU.mult)
        bk = 1
        for k in range(8):
            B.tr(pbb[bk][:, k * 128:(k + 1) * 128], ynt[:, k * 128:(k + 1) * 128], identb)
        sg_, sgd = stg.next()
        B.copy("act", sg_, pbb[bk].v(lambda a: a.rearrange("p (k t) -> p k t", k=8)))
        B.store("sp", S["br_ssd"][:, rows].rearrange("(k p) t -> p k t", p=128), sg_, sgd)

    d3_load(0)
    for c in range(NT + 1):
        if c + 1 < NT:
            d3_load(c + 1)
        if c < NT:
            d3_front(c)
        if c >= 1:
            d3_back(c - 1)


def prefetch_loop(n, depth, load, compute):
    for i in range(n + depth):
        if i < n:
            load(i)
        if i >= depth:
            compute(i - depth)


class _Cycle:
    def __init__(self, items):
        self.items = items
        self.i = -1

    def next(self):
        self.i = (self.i + 1) % len(self.items)
        return self.items[self.i]


def host_consts(T, link):
    SEG = T // 2
    seqlen = T if link else SEG
    pos = np.arange(T) if link else np.concatenate([np.arange(SEG), np.arange(SEG)])
    out = {}

    def tables(rot_dim):
        half = rot_dim // 2
        inv = np.power(np.float32(500000.0), -np.arange(half, dtype=np.float32) * np.float32(2.0) / np.float32(rot_dim)).astype(np.float32)
        ang = pos.astype(np.float32)[:, None] * inv[None, :]
        c = np.cos(ang).astype(np.float32).T
        s = np.sin(ang).astype(np.float32).T
        return np.stack([np.concatenate([c, c], 0), np.concatenate([s, s], 0)], 0).astype(np.float32)

    out["c_cs64"] = np.ascontiguousarray(tables(64))
    out["c_cs32"] = np.ascontiguousarray(tables(32))
    rc = np.zeros((4, T), np.float32)
    for i, w in enumerate((2, 4, 8, 16)):
        lo = w // 2
        hi = w - 1 - lo
        start = np.clip(pos - lo, 0, seqlen)
        end = np.clip(pos + hi + 1, 0, seqlen)
        rc[i] = 1.0 / (end - start).astype(np.float32)
    out["c_rcnt"] = rc
    t = np.arange(128)[:, None]
    l_ = np.arange(128)[None, :]
    out["c_tri"] = np.stack([(t <= l_), (t >= l_), (t > l_), (t < l_), np.ones((128, 128), bool)], 0).astype(np.float32)

    def rot(n):
        h = n // 2
        m = np.zeros((n, n), np.float32)
        for i in range(h):
            m[i + h, i] = -1.0
            m[i, i + h] = 1.0
        return m

    out["c_rot64"] = rot(64)
    out["c_rot32"] = rot(32)
    out["c_ident"] = np.eye(128, dtype=np.float32)
    out["c_link"] = np.full((128, 1), 1.0 if link else 0.0, np.float32)
    out["c_cbias"] = np.full((128, 1), 0.0 if link else NEG, np.float32)
    return out


_CACHE = {}


def kernel(**inputs):
    T = 4096
    xp = np.asarray(inputs["x_prompt"], dtype=np.float32)
    xs = np.asarray(inputs["x_sample"], dtype=np.float32)
    slots = []
    for c in range(8):
        if c < 2:
            slots.append((np.ascontiguousarray(xs[c]), 1))
        elif c < 6:
            i = (c - 2) * 2
            slots.append((np.ascontiguousarray(np.concatenate([xp[i], xp[i + 1]], axis=0)), 0))
        else:
            slots.append((np.zeros((T, D), np.float32), 0))
    if T not in _CACHE:
        _CACHE[T] = build(T)
    B = _CACHE[T]
    wts = {n: np.ascontiguousarray(np.asarray(inputs[n], dtype=np.float32)) for n, _ in W_NAMES}
    hc = {1: host_consts(T, 1), 0: host_consts(T, 0)}
    in_maps = []
    for x, link in slots:
        m = {"x": x}
        m.update(wts)
        m.update(hc[link])
        in_maps.append(m)
    res = run_bass_kernel_spmd(B.nc, in_maps, core_ids=list(range(8)))
    ys = [np.asarray(r["y"], dtype=np.float32) for r in res.results]
    y_sample = np.stack([ys[0], ys[1]], axis=0)
    yp = []
    for c in range(2, 6):
        yp.append(ys[c][:2048])
        yp.append(ys[c][2048:])
    y_prompt = np.stack(yp, axis=0)
    return (y_prompt, y_sample)
```

```python
import contextlib
import math
import numpy as np
import concourse.bass as bass
import concourse.mybir as mybir
from concourse.bass_utils import run_bass_kernel_spmd

F32 = mybir.dt.float32
BF16 = mybir.dt.bfloat16
AF = mybir.ActivationFunctionType
ALU = mybir.AluOpType
AX = mybir.AxisListType

D = 2048
BW = 1024
IN_DIM = 18784
DEPTH = 2
EPS = 1e-6
SEM_CAP = 32000
NEG = -30000.0


class Buf:
    __slots__ = ("name", "last_w", "readers", "const")

    def __init__(self, name="", const=False):
        self.name = name
        self.last_w = None
        self.readers = []
        self.const = const


class TL:
    __slots__ = ("ap", "buf")

    def __init__(self, ap, buf):
        self.ap = ap
        self.buf = buf

    def __getitem__(self, k):
        return TL(self.ap[k], self.buf)

    def v(self, fn):
        return TL(fn(self.ap), self.buf)


class Op:
    __slots__ = ("eng", "fn", "deps", "signal", "sigval", "is_dma", "dsem", "dtarget")

    def __init__(self, eng, fn, is_dma=False):
        self.eng = eng
        self.fn = fn
        self.deps = []
        self.signal = False
        self.sigval = None
        self.is_dma = is_dma
        self.dsem = None
        self.dtarget = None


class DmaSem:
    def __init__(self):
        self.count = 0
        self.eng = None


class Prog:
    ENGS = ("pe", "act", "dve", "pool", "sp")

    def __init__(self, nc):
        self.nc = nc
        self.eng_obj = {"pe": nc.tensor, "act": nc.scalar, "dve": nc.vector,
                        "pool": nc.gpsimd, "sp": nc.sync}
        self.ops = {e: [] for e in self.ENGS}
        self.dma_last = {}
        self.dsem_pool = []
        self.dsem_i = 0

    def dsem(self):
        if self.dsem_i >= len(self.dsem_pool):
            self.dsem_pool.append(DmaSem())
        s = self.dsem_pool[self.dsem_i]
        self.dsem_i += 1
        return s

    def _track(self, o, reads, writes, nowaw=False):
        seen = set()
        for b in reads:
            w = b.last_w
            if w is not None and id(w) not in seen:
                o.deps.append((w, "raw", w.dsem.count if w.is_dma else 0))
                seen.add(id(w))
        for b in writes:
            w = b.last_w
            if w is not None and id(w) not in seen:
                if not (nowaw and w.is_dma and w.dsem is o.dsem):
                    o.deps.append((w, "waw", w.dsem.count if w.is_dma else 0))
                    seen.add(id(w))
            for r in b.readers:
                if id(r) not in seen:
                    o.deps.append((r, "war", r.dsem.count if r.is_dma else 0))
                    seen.add(id(r))
        for b in reads:
            if not b.const:
                b.readers.append(o)
        for b in writes:
            b.last_w = o
            b.readers = []
        self.ops[o.eng].append(o)

    def op(self, eng, fn, reads=(), writes=()):
        o = Op(eng, fn)
        self._track(o, [t.buf for t in reads], [t.buf for t in writes])
        return o

    def dma(self, eng, out, in_, dsem, reads=(), writes=()):
        nc = self.nc
        eo = self.eng_obj[eng]
        oa = out.ap if isinstance(out, TL) else out
        ia = in_.ap if isinstance(in_, TL) else in_
        o = Op(eng, lambda: eo.dma_start(out=oa, in_=ia, allow_slow_non_contiguous=True), is_dma=True)
        o.dsem = dsem
        assert dsem.eng in (None, eng), "DMA semaphore shared between queues"
        dsem.eng = eng
        o.dtarget = dsem.count + 1
        rd = [t.buf for t in reads] + ([in_.buf] if isinstance(in_, TL) else [])
        wr = [t.buf for t in writes] + ([out.buf] if isinstance(out, TL) else [])
        self._track(o, rd, wr, nowaw=True)
        dsem.count += 1
        self.dma_last[id(dsem)] = o
        return o

    def barrier(self):
        lasts = []
        for e in self.ENGS:
            for o in reversed(self.ops[e]):
                if not o.is_dma and o.fn is not None:
                    lasts.append(o)
                    break
        lasts += list(self.dma_last.values())
        for e in self.ENGS:
            o = Op(e, None)
            for l in lasts:
                o.deps.append((l, "raw", l.dsem.count if l.is_dma else 0))
            self.ops[e].append(o)
        self.dsem_i = 0
        for d in self.dsem_pool:
            d.eng = None

    @staticmethod
    def _needs_sem(p, c, kind):
        if p.is_dma:
            return True
        if p.eng == c.eng:
            if p.eng in ("pe", "sp"):
                return False
            return kind == "raw"
        return True

    def emit(self, es):
        nc = self.nc
        for e in self.ENGS:
            for c in self.ops[e]:
                for (p, kind, cnt) in c.deps:
                    if not p.is_dma and self._needs_sem(p, c, kind):
                        p.signal = True
        pool = {}

        def getsem(key):
            if key not in pool:
                pool[key] = es.enter_context(nc.semaphore(f"s{len(pool)}"))
            return pool[key]

        for e in self.ENGS:
            k = 0
            for o in self.ops[e]:
                if not o.is_dma and o.signal:
                    o.sigval = k
                    k += 1
        dcap = SEM_CAP // 16
        nw = 0
        ni = 0
        for e in self.ENGS:
            eng = self.eng_obj[e]
            waited = {}
            for o in self.ops[e]:
                need = {}
                for (p, kind, cnt) in o.deps:
                    if not self._needs_sem(p, o, kind):
                        continue
                    if p.is_dma:
                        tot = cnt
                        key = ("d", id(p.dsem), (tot - 1) // dcap)
                        v = ((tot - 1) % dcap + 1) * 16
                    else:
                        key = ("c", p.eng, p.sigval // SEM_CAP)
                        v = p.sigval % SEM_CAP + 1
                    if need.get(key, 0) < v:
                        need[key] = v
                for key, v in need.items():
                    if waited.get(key, 0) >= v:
                        continue
                    waited[key] = v
                    eng.wait_ge(getsem(key), v)
                    nw += 1
                if o.fn is None:
                    continue
                ins = o.fn()
                ni += 1
                if o.is_dma:
                    ins.then_inc(getsem(("d", id(o.dsem), (o.dtarget - 1) // dcap)), 16)
                elif o.signal:
                    ins.then_inc(getsem(("c", o.eng, o.sigval // SEM_CAP)), 1)
        self.stats = dict(waits=nw, insts=ni, sems=len(pool))
        return self.stats


IN_GROUPS = [
    ("cq", 0, 512, "F", "lat"), ("ckv", 512, 256, "F", "lat"), ("kr", 768, 64, "F", "lat"),
    ("g_mla", 832, 1024, "F", "silu"), ("dq", 1856, 1024, "F", "rope"), ("dk", 2880, 1024, "F", "rope"),
    ("dv", 3904, 1024, "T", "copy"), ("g_diff", 4928, 1024, "F", "silu"), ("z", 5952, 1024, "T", "silu"),
    ("xbc", 6976, 1536, "F", "copy"), ("dt", 8512, 32, "T", "copyf"), ("u", 8544, 1024, "F", "copy"),
    ("g_pool", 9568, 1024, "F", "silu"), ("mg", 10592, 8192, "F", "sigmoid"),
]


class Builder:
    def __init__(self, T, debug=False, nlayers=DEPTH, phases=None):
        self.T = T
        self.SEG = T // 2
        self.NT = T // 128
        self.NB = T // 512
        self.debug = debug
        self.nlayers = nlayers
        self.phases = phases
        self.nc = bass.Bass("TRN2", target_bir_lowering=False)
        self.P = Prog(self.nc)
        self.es = contextlib.ExitStack()

    def dram_in(self, name, shape, dt=F32):
        return self.nc.dram_tensor(name, list(shape), dt, kind="ExternalInput").ap()

    def dram_out(self, name, shape, dt=F32):
        return self.nc.dram_tensor(name, list(shape), dt, kind="ExternalOutput").ap()

    def dram_scr(self, name, shape, dt):
        kind = "ExternalOutput" if self.debug else "Internal"
        return self.nc.dram_tensor(name, list(shape), dt, kind=kind).ap()

    def reset_arena(self):
        self.off = 0

    def alloc(self, shape, dt, name="", const=False):
        n = int(np.prod(shape))
        nbytes = n * (4 if dt == F32 else 2)
        nw = (nbytes + 3) // 4
        nw = (nw + 7) // 8 * 8
        assert self.off + nw <= self.BIGW, f"SBUF arena overflow {name} {self.off + nw}"
        ap = self.big[:, self.off:self.off + nw]
        self.off += nw
        if dt == BF16:
            ap = ap.bitcast(BF16)[:, 0:n]
        else:
            ap = ap[:, 0:n]
        if len(shape) > 1:
            names = " ".join(f"a{i}" for i in range(len(shape)))
            kw = {f"a{i}": int(s) for i, s in enumerate(shape)}
            ap = ap.rearrange(f"p ({names}) -> p {names}", **kw)
        return TL(ap, Buf(name, const=const))

    def ring(self, n, shape, dt, name=""):
        tl = [self.alloc(shape, dt, f"{name}{i}") for i in range(n)]
        ds = [self.P.dsem() for _ in range(n)]
        return _Ring(tl, ds)

    def psum_tl(self, i, dt=F32):
        ap = self.banks[i][:]
        if dt == BF16:
            ap = ap.bitcast(BF16)
        return TL(ap, self.bank_bufs[i])

    def mm(self, out, lhsT, rhs, start=True, stop=True, skip=False):
        nc = self.nc
        o, a, b = out.ap, lhsT.ap, rhs.ap
        if skip:
            return self.P.op("pe", lambda: nc.tensor.matmul(o, a, b, start=start, stop=stop, skip_group_check=True),
                             reads=[lhsT, rhs], writes=[out])
        return self.P.op("pe", lambda: nc.tensor.matmul(o, a, b, start=start, stop=stop),
                         reads=[lhsT, rhs], writes=[out])

    def tr(self, out, in_, ident):
        nc = self.nc
        o, a, b = out.ap, in_.ap, ident.ap
        return self.P.op("pe", lambda: nc.tensor.transpose(o, a, b), reads=[in_, ident], writes=[out])

    def act(self, out, in_, func, bias=None, scale=1.0, accum=None, eng="act"):
        nc = self.nc
        o, a = out.ap, in_.ap
        kw = {}
        rd = [in_]
        wr = [out]
        if bias is not None:
            if isinstance(bias, TL):
                kw["bias"] = bias.ap
                rd.append(bias)
            else:
                kw["bias"] = float(bias)
        if isinstance(scale, TL):
            kw["scale"] = scale.ap
            rd.append(scale)
        else:
            kw["scale"] = float(scale)
        if accum is not None:
            kw["accum_out"] = accum.ap
            wr.append(accum)
        return self.P.op("act", lambda: nc.scalar.activation(out=o, in_=a, func=func, **kw), reads=rd, writes=wr)

    def _e(self, eng):
        return self.P.eng_obj[eng]

    def tt(self, eng, out, in0, in1, op):
        e = self._e(eng)
        o, a, b = out.ap, in0.ap, in1.ap
        return self.P.op(eng, lambda: e.tensor_tensor(out=o, in0=a, in1=b, op=op), reads=[in0, in1], writes=[out])

    def ts(self, eng, out, in0, s1, s2=None, op0=ALU.mult, op1=None):
        e = self._e(eng)
        o, a = out.ap, in0.ap
        rd = [in0]
        v1 = s1.ap if isinstance(s1, TL) else float(s1)
        if isinstance(s1, TL):
            rd.append(s1)
        v2 = None
        if s2 is not None:
            v2 = s2.ap if isinstance(s2, TL) else float(s2)
            if isinstance(s2, TL):
                rd.append(s2)
        if op1 is None:
            return self.P.op(eng, lambda: e.tensor_scalar(out=o, in0=a, scalar1=v1, scalar2=None, op0=op0),
                             reads=rd, writes=[out])
        return self.P.op(eng, lambda: e.tensor_scalar(out=o, in0=a, scalar1=v1, scalar2=v2, op0=op0, op1=op1),
                         reads=rd, writes=[out])

    def stt(self, eng, out, in0, scalar, in1, op0, op1):
        eng = "dve"
        e = self._e(eng)
        o, a, b = out.ap, in0.ap, in1.ap
        rd = [in0, in1]
        sv = scalar.ap if isinstance(scalar, TL) else float(scalar)
        if isinstance(scalar, TL):
            rd.append(scalar)
        return self.P.op(eng, lambda: e.scalar_tensor_tensor(out=o, in0=a, scalar=sv, in1=b, op0=op0, op1=op1),
                         reads=rd, writes=[out])

    def copy(self, eng, out, in_):
        if eng == "act":
            return self.act(out, in_, AF.Copy)
        e = self._e(eng)
        o, a = out.ap, in_.ap
        return self.P.op(eng, lambda: e.tensor_copy(o, a), reads=[in_], writes=[out])

    def recip(self, out, in_):
        nc = self.nc
        o, a = out.ap, in_.ap
        return self.P.op("dve", lambda: nc.vector.reciprocal(o, a), reads=[in_], writes=[out])

    def memset(self, eng, out, val):
        e = self._e(eng)
        o = out.ap
        return self.P.op(eng, lambda: e.memset(o, float(val)), writes=[out])

    def load(self, eng, dst, src_ap, dsem):
        return self.P.dma(eng, dst, src_ap, dsem)

    def store(self, eng, dst_ap, src, dsem):
        return self.P.dma(eng, dst_ap, src, dsem)

    def dbg(self, name, tl, parts=128):
        if not self.debug:
            return
        shp = [parts] + list(tl.ap.shape[1:])
        d = self.nc.dram_tensor("dbg_" + name, shp, tl.ap.dtype, kind="ExternalOutput").ap()
        self.P.dma("sp", d, tl[0:parts], self.P.dsem())

    def rsqrt_act(self, out, in_, scale, eps):
        self.act(out, in_, AF.Ln, bias=self.eps_col[:, 0:1] if eps == EPS else eps, scale=scale)
        self.act(out, out, AF.Exp, scale=-0.5)


class _Ring:
    def __init__(self, tl, ds):
        self.tl = tl
        self.ds = ds
        self.i = -1

    def next(self):
        self.i = (self.i + 1) % len(self.tl)
        return self.tl[self.i], self.ds[self.i]


W_NAMES = [
    ("norm_w", (DEPTH, D)), ("w_in", (DEPTH, D, IN_DIM)), ("mla_q_norm", (DEPTH, 512)),
    ("mla_w_uq", (DEPTH, 512, 1536)), ("mla_kv_norm", (DEPTH, 256)), ("mla_w_ukv", (DEPTH, 256, 2048)),
    ("diff_lambda", (DEPTH, 4, 128)), ("diff_subln", (DEPTH, 256)), ("ssd_conv_w", (DEPTH, 4, 1536)),
    ("ssd_conv_b", (DEPTH, 1536)), ("ssd_dt_bias", (DEPTH, 2, 16)), ("ssd_a_log", (DEPTH, 2, 16)),
    ("ssd_d", (DEPTH, 16)), ("ssd_norm", (DEPTH, 1024)), ("pool_w", (DEPTH, 4, 256, 256)),
    ("pool_scale", (DEPTH, 1024)), ("w_branch", (DEPTH, 4, 1024, D)), ("w_out", (DEPTH, D, D)),
    ("final_norm", (D,)),
]


def build(T, debug=False, nlayers=DEPTH, phases=None):
    B = Builder(T, debug, nlayers, phases)
    nc, P, es = B.nc, B.P, B.es
    NT, NB, SEG = B.NT, B.NB, B.SEG
    x_in = B.dram_in("x", (T, D))
    W = {n: B.dram_in(n, s) for n, s in W_NAMES}
    c_link = B.dram_in("c_link", (128, 1))
    c_cbias = B.dram_in("c_cbias", (128, 1))
    c_ident = B.dram_in("c_ident", (128, 128))
    c_tri = B.dram_in("c_tri", (5, 128, 128))
    c_rot64 = B.dram_in("c_rot64", (64, 64))
    c_rot32 = B.dram_in("c_rot32", (32, 32))
    c_cs64 = B.dram_in("c_cs64", (2, 64, T))
    c_cs32 = B.dram_in("c_cs32", (2, 32, T))
    c_rcnt = B.dram_in("c_rcnt", (4, T))
    y_out = B.dram_out("y", (T, D))
    S = {}
    S["x1"] = B.dram_scr("s_x1", (T, D), F32)
    for nm, f in [("cqn", 512), ("ckvn", 256), ("kpe", 64), ("sg_mla", 1024), ("dq", 1024), ("dk", 1024),
                  ("sg_diff", 1024), ("xbc", 1536), ("u", 1024), ("sg_pool", 1024), ("mg", 8192),
                  ("br_mla", 1024), ("br_diff", 1024), ("br_ssd", 1024), ("br_pool", 1024)]:
        S[nm] = B.dram_scr("s_" + nm, (f, T), BF16)
    S["dv"] = B.dram_scr("s_dv", (T, 1024), BF16)
    S["sz"] = B.dram_scr("s_sz", (T, 1024), BF16)
    S["dt"] = B.dram_scr("s_dt", (T, 32), F32)
    S["xsb"] = B.dram_scr("s_xsb", (T, 1280), BF16)
    S["bcT"] = B.dram_scr("s_bcT", (512, T), BF16)
    S["H"] = B.dram_scr("s_H", (2, T // 128, 128, 1024), BF16)
    S["mT"] = B.dram_scr("s_mT", (D, T), BF16)
    B.S = S

    B.BIGW = 49152 - 1024
    B.big = es.enter_context(nc.sbuf_tensor("big", [128, B.BIGW + 64], F32))
    B.banks = [es.enter_context(nc.psum_tensor(f"ps{i}", [128, 512], F32)) for i in range(8)]
    B.bank_bufs = [Buf(f"ps{i}") for i in range(8)]
    cbase = B.BIGW
    B.eps_col = TL(B.big[:, cbase:cbase + 1], Buf("eps", const=True))
    B.link = TL(B.big[:, cbase + 1:cbase + 2], Buf("link", const=True))
    B.cbias = TL(B.big[:, cbase + 2:cbase + 3], Buf("cbias", const=True))
    B.zero_col = TL(B.big[:, cbase + 3:cbase + 4], Buf("zero", const=True))
    B.one_col = TL(B.big[:, cbase + 4:cbase + 5], Buf("one", const=True))
    B.memset("dve", B.eps_col, EPS)
    B.memset("dve", B.zero_col, 0.0)
    B.memset("dve", B.one_col, 1.0)
    ds0 = DmaSem()
    B.load("sp", B.link, c_link[:, :], ds0)
    B.load("sp", B.cbias, c_cbias[:, :], ds0)
    P.barrier()

    consts = dict(ident=c_ident, tri=c_tri, rot64=c_rot64, rot32=c_rot32, cs64=c_cs64, cs32=c_cs32, rcnt=c_rcnt)
    for l in range(nlayers):
        x_src = x_in if l == 0 else S["x1"]
        last = (l == nlayers - 1)
        if phases is None or "A" in phases:
            phase_A(B, l, x_src, W, consts)
            P.barrier()
        if phases is None or "B" in phases:
            phase_B(B, l, W, consts)
            P.barrier()
        if phases is None or "C" in phases:
            phase_C(B, l, W, consts)
            P.barrier()
        if phases is None or "D" in phases:
            phase_D(B, l, W, consts)
            P.barrier()
        if phases is None or "E" in phases:
            phase_E(B, l, W, consts)
            P.barrier()
        if phases is None or "F" in phases:
            phase_F(B, l, x_src, y_out if last else S["x1"], W, consts, last)
            P.barrier()
    P.barrier()
    st = P.emit(es)
    B.stats = st
    return B


def phase_A(B, l, x_src, W, C):
    nc, P, S, T = B.nc, B.P, B.S, B.T
    B.reset_arena()
    TBA = min(1024, T)
    nblk = T // TBA
    nsub = TBA // 512
    ntile = TBA // 128
    CWMAX = 544
    normw = B.alloc([D], F32, "normw", const=True)
    identb = B.alloc([128], BF16, "identb", const=True)
    ones_b = B.alloc([128], BF16, "ones_b", const=True)
    rot64 = B.alloc([64], F32, "rot64", const=True)
    rot32 = B.alloc([32], F32, "rot32", const=True)
    qnw = B.alloc([4], F32, "qnw", const=True)
    kvnw = B.alloc([2], F32, "kvnw", const=True)
    hT = B.alloc([16, TBA], BF16, "hT")
    xring = B.ring(2, [D], F32, "x")
    hb = B.alloc([D], BF16, "hb")
    sqj = B.alloc([D], BF16, "sqj")
    stat = B.alloc([4], F32, "stat")
    wring = B.ring(3, [16, CWMAX], BF16, "w")
    ost = B.ring(4, [TBA], BF16, "ost")
    ostT = B.ring(3, [CWMAX], BF16, "ostT")
    ostF = B.ring(2, [32], F32, "ostF")
    lat = B.alloc([7, TBA], F32, "lat")
    rf = B.ring(2, [512], F32, "rf")
    cs32 = B.alloc([2, TBA], F32, "cs32")
    cs64 = B.alloc([2, TBA], F32, "cs64")
    tmpf = B.ring(2, [512], F32, "tmpf")
    tmpb = B.ring(2, [512], BF16, "tmpb")
    cd = P.dsem()
    cd32 = P.dsem()
    cd64 = P.dsem()
    B.load("sp", normw, W["norm_w"][l].partition_broadcast(128), cd)
    B.load("pool", identb, C["ident"][:, :], P.dsem())
    B.load("sp", rot64[0:64], C["rot64"][:, :], cd)
    B.load("sp", rot32[0:32], C["rot32"][:, :], cd)
    B.load("sp", qnw, W["mla_q_norm"][l].rearrange("(c p) -> p c", p=128), cd)
    B.load("sp", kvnw, W["mla_kv_norm"][l].rearrange("(c p) -> p c", p=128), cd)
    B.memset("dve", ones_b, 1.0)
    pb = [B.psum_tl(i) for i in range(8)]
    pbb = [B.psum_tl(i, BF16) for i in range(8)]
    mmbank = _Cycle([0, 1, 2, 3])
    auxbank = _Cycle([4, 5])
    rotbank = _Cycle([6, 7])
    w_in = W["w_in"][l].rearrange("(k p) c -> p k c", p=128)

    wtiles = []
    for (nm, c0, ncol, orient, epi) in IN_GROUPS:
        if nm in ("ckv", "kr", "dt"):
            continue
        if nm == "cq":
            wtiles.append((0, 512, [("cq", 0, 512)]))
            wtiles.append((512, 320, [("ckv", 0, 256), ("kr", 256, 64)]))
            continue
        nt_ = ncol // 512
        for j in range(nt_):
            if nm == "xbc" and j == nt_ - 1:
                wtiles.append((c0 + j * 512, 544, [("xbc", 0, 512), ("dt", 512, 32)]))
            else:
                wtiles.append((c0 + j * 512, 512, [(nm, 0, 512)]))
    ginfo = {g[0]: g for g in IN_GROUPS}
    act_toggle = [0]

    for blk in range(nblk):
        t0 = blk * TBA
        B.load("sp", cs32[0:32], C["cs32"][:, :, t0:t0 + TBA].rearrange("a p t -> p a t"), cd32)
        B.load("sp", cs64[0:64], C["cs64"][:, :, t0:t0 + TBA].rearrange("a p t -> p a t"), cd64)
        for i in range(ntile):
            xt, xd = xring.next()
            B.load("sp", xt, x_src[t0 + i * 128:t0 + (i + 1) * 128, :], xd)
            B.act(sqj, xt, AF.Square, accum=stat[:, 0:1])
            B.rsqrt_act(stat[:, 1:2], stat[:, 0:1], 1.0 / D, EPS)
            B.stt("dve", hb, xt, stat[:, 1:2], normw, ALU.mult, ALU.mult)
            for half in range(2):
                bk = auxbank.next()
                for j in range(8):
                    k = half * 8 + j
                    B.tr(pbb[bk][:, j * 128:(j + 1) * 128], hb[:, k * 128:(k + 1) * 128], identb)
                B.copy("dve" if half == 0 else "act",
                       hT[:, half * 8:(half + 1) * 8, i * 128:(i + 1) * 128],
                       pbb[bk].v(lambda a: a.rearrange("p (j c) -> p j c", j=8)))
        for (c0, cw, parts) in wtiles:
            wt, wd = wring.next()
            B.load("pool", wt[:, :, 0:cw], w_in[:, :, c0:c0 + cw], wd)
            for (nm, po, pn) in parts:
                _, gc0, gn, orient, epi = ginfo[nm]
                gcol = c0 + po - gc0
                if orient == "F":
                    for cc in range(0, pn, 128):
                        m = min(128, pn - cc)
                        feat = gcol + cc
                        if epi != "lat":
                            og, od = ost.next()
                        for sb in range(nsub):
                            bk = mmbank.next()
                            for k in range(16):
                                B.mm(pb[bk][0:m, :], wt[:, k, po + cc:po + cc + m], hT[:, k, sb * 512:(sb + 1) * 512],
                                     start=(k == 0), stop=(k == 15))
                            src = pb[bk][0:m, :]
                            if epi == "lat":
                                li = {"cq": 0, "ckv": 4, "kr": 6}[nm] + cc // 128
                                B.copy("dve", lat[0:m, li, sb * 512:(sb + 1) * 512], src)
                            elif epi == "silu":
                                B.act(og[0:m, sb * 512:(sb + 1) * 512], src, AF.Silu)
                            elif epi == "sigmoid":
                                B.act(og[0:m, sb * 512:(sb + 1) * 512], src, AF.Sigmoid)
                            elif epi == "copy":
                                act_toggle[0] ^= 1
                                B.copy("dve", og[0:m, sb * 512:(sb + 1) * 512], src)
                            elif epi == "rope":
                                r, _ = rf.next()
                                B.copy("dve", r, src)
                                rb = rotbank.next()
                                B.mm(pb[rb][0:32, :], rot32[0:32, 0:32], r[0:32, :])
                                tf, _ = tmpf.next()
                                sl = slice(sb * 512, (sb + 1) * 512)
                                B.tt("dve", tf[0:32], r[0:32], cs32[0:32, 0, sl], ALU.mult)
                                tf2, _ = tmpf.next()
                                B.tt("dve", tf2[0:32], pb[rb][0:32, :], cs32[0:32, 1, sl], ALU.mult)
                                B.copy("act", og[:, sl], r)
                                B.tt("dve", og[0:32, sl], tf[0:32], tf2[0:32], ALU.add)
                        if epi != "lat":
                            sname = {"g_mla": "sg_mla", "g_diff": "sg_diff", "g_pool": "sg_pool"}.get(nm, nm)
                            B.store("sp", S[sname][feat:feat + m, t0:t0 + TBA], og[0:m, :], od)
                else:
                    for i in range(ntile):
                        bk = mmbank.next()
                        for k in range(16):
                            B.mm(pb[bk][:, 0:pn], hT[:, k, i * 128:(i + 1) * 128], wt[:, k, po:po + pn],
                                 start=(k == 0), stop=(k == 15))
                        src = pb[bk][:, 0:pn]
                        rows = slice(t0 + i * 128, t0 + (i + 1) * 128)
                        if epi == "copyf":
                            og, od = ostF.next()
                            B.copy("dve", og[:, 0:pn], src)
                            B.store("sp", S["dt"][rows, :], og[:, 0:pn], od)
                        else:
                            og, od = ostT.next()
                            if epi == "silu":
                                B.act(og[:, 0:pn], src, AF.Silu)
                            else:
                                B.copy("dve", og[:, 0:pn], src)
                            sname = {"z": "sz"}.get(nm, nm)
                            B.store("sp", S[sname][rows, gcol:gcol + pn], og[:, 0:pn], od)
            if parts[0][0] == "ckv":
                for sb in range(nsub):
                    sl = slice(sb * 512, (sb + 1) * 512)
                    for (nm, li0, nch, wcol, dim) in (("cqn", 0, 4, qnw, 512), ("ckvn", 4, 2, kvnw, 256)):
                        bk = auxbank.next()
                        for c in range(nch):
                            tb_, _ = tmpb.next()
                            B.act(tb_, lat[:, li0 + c, sl], AF.Square)
                            B.mm(pb[bk], ones_b, tb_, start=(c == 0), stop=(c == nch - 1))
                        tf, _ = tmpf.next()
                        B.rsqrt_act(tf, pb[bk], 1.0 / dim, EPS)
                        for c in range(nch):
                            og, od = ost.next()
                            B.stt("dve", og[:, 0:512], lat[:, li0 + c, sl], wcol[:, c:c + 1], tf, ALU.mult, ALU.mult)
                            B.store("sp", S[nm][c * 128:(c + 1) * 128, t0 + sb * 512:t0 + (sb + 1) * 512], og[:, 0:512], od)
                    rb = rotbank.next()
                    B.mm(pb[rb][0:64, :], rot64[0:64, 0:64], lat[0:64, 6, sl])
                    tf, _ = tmpf.next()
                    B.tt("dve", tf[0:64], lat[0:64, 6, sl], cs64[0:64, 0, sl], ALU.mult)
                    tf2, _ = tmpf.next()
                    B.tt("dve", tf2[0:64], pb[rb][0:64, :], cs64[0:64, 1, sl], ALU.mult)
                    og, od = ost.next()
                    B.tt("dve", og[0:64, 0:512], tf[0:64], tf2[0:64], ALU.add)
                    B.store("sp", S["kpe"][0:64, t0 + sb * 512:t0 + (sb + 1) * 512], og[0:64, 0:512], od)


def attn_qblock(B, qsl, s_terms, v_list, acc_banks, Sring, Pring, pb, ones_f, scale, NT, SEG, q0, SPring,
                inject=None, every=6):
    LA = 3
    pend = []
    inject = list(inject) if inject else []
    sp, _ = SPring.next()
    for step in range(NT + LA):
        if inject and step % every == 2:
            inject.pop(0)()
        if step < NT:
            kb = step
            sb = Sring.next()
            ksl = slice(kb * 128, (kb + 1) * 128)
            for i, (kT, qT) in enumerate(s_terms):
                B.mm(pb[sb], kT[:, ksl], qT[:, qsl], start=(i == 0), stop=(i == len(s_terms) - 1))
            cross = (q0 // SEG) != ((kb * 128) // SEG)
            pt, _ = Pring.next()
            B.act(pt, pb[sb], AF.Exp, scale=scale, bias=(B.cbias if cross else None))
            pend.append((kb, pt))
            if kb == 0:
                B.copy("dve", sp, pt)
            else:
                B.tt("dve", sp, sp, pt, ALU.add)
        if step >= LA:
            kb, pt = pend.pop(0)
            for v, ab in zip(v_list, acc_banks):
                B.mm(pb[ab], v[:, kb, :], pt, start=(kb == 0), stop=(kb == NT - 1))
    dbk = Sring.next()
    B.mm(pb[dbk], ones_f, sp)
    for f in inject:
        f()
    return dbk


def phase_B(B, l, W, C):
    nc, P, S, T, NT, NB, SEG = B.nc, B.P, B.S, B.T, B.NT, B.NB, B.SEG
    B.reset_arena()
    cqn = B.alloc([4, T], BF16, "cqn")
    ckvn = B.alloc([2, T], BF16, "ckvn")
    kpe = B.alloc([T], BF16, "kpe")
    wuq = B.alloc([4, 1536], BF16, "wuq", const=True)
    wukv = B.alloc([2, 2048], BF16, "wukv", const=True)
    ones_b = B.alloc([128], BF16, "ones_b", const=True)
    rot64 = B.alloc([64], F32, "rot64", const=True)
    qn = [B.alloc([T], BF16, f"qn{i}") for i in range(2)]
    qp = [B.alloc([T], BF16, f"qp{i}") for i in range(2)]
    kn = [B.alloc([T], BF16, f"kn{i}") for i in range(2)]
    vv = [B.alloc([NT, 128], BF16, f"v{i}") for i in range(2)]
    csr = B.ring(2, [2, 512], F32, "cs")
    rr = B.ring(2, [512], F32, "rr")
    tmpf = B.ring(4, [512], F32, "tmpf")
    Pring = B.ring(6, [512], BF16, "pt")
    SPring = B.ring(3, [512], F32, "sp")
    ones_f = B.alloc([128], F32, "ones_f", const=True)
    B.memset("pool", ones_f, 1.0)
    sgr = B.ring(2, [512], BF16, "sg")
    ost = B.ring(2, [512], BF16, "ost")
    cd = P.dsem()
    B.load("sp", cqn, S["cqn"].rearrange("(c p) t -> p c t", p=128), cd)
    B.load("sp", ckvn, S["ckvn"].rearrange("(c p) t -> p c t", p=128), cd)
    B.load("sp", kpe[0:64], S["kpe"][:, :], cd)
    B.memset("pool", kpe[64:128], 0.0)
    B.memset("pool", qp[0][64:128], 0.0)
    B.memset("pool", qp[1][64:128], 0.0)
    cdp = P.dsem()
    B.load("pool", wuq, W["mla_w_uq"][l].rearrange("(c p) n -> p c n", p=128), cdp)
    B.load("pool", wukv, W["mla_w_ukv"][l].rearrange("(c p) n -> p c n", p=128), cdp)
    B.load("sp", rot64[0:64], C["rot64"][:, :], cd)
    B.memset("dve", ones_b, 1.0)
    pb = [B.psum_tl(i) for i in range(8)]
    Sring = _Cycle([0, 1, 2, 3, 6])
    accs = _Cycle([4, 5])
    scale = float(192 ** -0.5)

    def prologue(h):
        s = h % 2
        for sb in range(NB):
            sl = slice(sb * 512, (sb + 1) * 512)
            for c in range(4):
                B.mm(pb[7], wuq[:, c, h * 192:h * 192 + 128], cqn[:, c, sl], start=(c == 0), stop=(c == 3))
            B.copy("dve", qn[s][:, sl], pb[7])
            yield
            for c in range(4):
                B.mm(pb[7][0:64], wuq[:, c, h * 192 + 128:h * 192 + 192], cqn[:, c, sl], start=(c == 0), stop=(c == 3))
            r, _ = rr.next()
            B.copy("dve", r[0:64], pb[7][0:64])
            cs, cdm = csr.next()
            B.load("sp", cs[0:64], C["cs64"][:, :, sl].rearrange("a p t -> p a t"), cdm)
            yield
            B.mm(pb[7][0:64], rot64[0:64, 0:64], r[0:64])
            tf, _ = tmpf.next()
            B.tt("dve", tf[0:64], r[0:64], cs[0:64, 0], ALU.mult)
            tf2, _ = tmpf.next()
            B.tt("dve", tf2[0:64], pb[7][0:64], cs[0:64, 1], ALU.mult)
            B.tt("dve", qp[s][0:64, sl], tf[0:64], tf2[0:64], ALU.add)
            yield
            for c in range(2):
                B.mm(pb[7], wukv[:, c, h * 256:h * 256 + 128], ckvn[:, c, sl], start=(c == 0), stop=(c == 1))
            B.copy("act", kn[s][:, sl], pb[7])
            yield
            for i in range(4):
                tsl = slice(sb * 512 + i * 128, sb * 512 + (i + 1) * 128)
                for c in range(2):
                    B.mm(pb[7][:, i * 128:(i + 1) * 128], ckvn[:, c, tsl], wukv[:, c, h * 256 + 128:h * 256 + 256],
                         start=(c == 0), stop=(c == 1))
            B.copy("act", vv[s][:, sb * 4:(sb + 1) * 4, :], pb[7].v(lambda a: a.rearrange("p (i d) -> p i d", i=4)))
            yield

    for _ in prologue(0):
        pass
    npiece = 5 * NB
    per_qb = (npiece + NB - 1) // NB
    every = max(1, (NT + 2) // (per_qb + 1))
    for h in range(8):
        s = h % 2
        gen = prologue(h + 1) if h + 1 < 8 else iter(())
        for qb in range(NB):
            qsl = slice(qb * 512, (qb + 1) * 512)
            ab = accs.next()
            sg, sgd = sgr.next()
            B.load("sp", sg, S["sg_mla"][h * 128:(h + 1) * 128, qsl], sgd)
            db = attn_qblock(B, qsl, [(kn[s], qn[s]), (kpe, qp[s])], [vv[s]], [ab],
                             Sring, Pring, pb, ones_f, scale, NT, SEG, qb * 512, SPring,
                             inject=[(lambda gen=gen: next(gen, None))] * per_qb, every=every)
            if qb == NB - 1:
                for _ in gen:
                    pass
            rd, _ = tmpf.next()
            B.recip(rd, pb[db])
            o, _ = tmpf.next()
            B.tt("dve", o, pb[ab], rd, ALU.mult)
            og, od = ost.next()
            B.tt("dve", og, o, sg, ALU.mult)
            B.store("sp", S["br_mla"][h * 128:(h + 1) * 128, qsl], og, od)
    B.dbg("qn", qn[1]); B.dbg("qp", qp[1], 64); B.dbg("kn", kn[1]); B.dbg("vv", vv[1]); B.dbg("kpe", kpe, 64)


def phase_C(B, l, W, C):
    nc, P, S, T, NT, NB, SEG = B.nc, B.P, B.S, B.T, B.NT, B.NB, B.SEG
    B.reset_arena()
    lam_init = 0.8 - 0.6 * math.exp(-0.3 * l)
    ones_b = B.alloc([128], BF16, "ones_b", const=True)
    ones_f = B.alloc([128], F32, "ones_f", const=True)
    subw = B.alloc([2], F32, "subw", const=True)
    lp = B.alloc([512], F32, "lp")
    lt = B.alloc([256], F32, "lt")
    ls = B.alloc([8], F32, "ls")
    nlam = B.alloc([1], F32, "nlam")
    qk = [[B.alloc([T], BF16, f"qk{i}{j}") for j in range(4)] for i in range(2)]
    vv = [B.alloc([NT, 256], BF16, f"v{i}") for i in range(2)]
    hd = [P.dsem() for _ in range(2)]
    Pring = B.ring(6, [512], BF16, "pt")
    SPring = B.ring(3, [512], F32, "sp")
    on = [B.alloc([2, 512], F32, f"on{i}") for i in range(2)]
    oo = B.alloc([2, 512], F32, "oo")
    tmpf = B.ring(5, [512], F32, "tmpf")
    sqr = B.ring(2, [512], BF16, "sq")
    sgr = B.ring(3, [2, 512], BF16, "sg")
    ost = B.ring(2, [2, 512], BF16, "ost")
    cd = P.dsem()
    B.memset("dve", ones_b, 1.0)
    B.memset("dve", ones_f, 1.0)
    B.load("sp", subw, W["diff_subln"][l].rearrange("(c p) -> p c", p=128), cd)
    B.load("sp", lp[0:1], W["diff_lambda"][l].rearrange("(o a) d -> o (a d)", o=1), cd)
    pb = [B.psum_tl(i) for i in range(8)]
    B.tt("dve", lt[0:1, 0:128], lp[0:1, 0:128], lp[0:1, 128:256], ALU.mult)
    B.tt("dve", lt[0:1, 128:256], lp[0:1, 256:384], lp[0:1, 384:512], ALU.mult)
    e = nc.vector
    for j in range(2):
        o_, a_ = ls[0:1, j:j + 1], lt[0:1, j * 128:(j + 1) * 128]
        P.op("dve", (lambda o_=o_, a_=a_: e.reduce_sum(out=o_.ap, in_=a_.ap, axis=AX.X)), reads=[a_], writes=[o_])
    B.act(ls[0:1, 2:4], ls[0:1, 0:2], AF.Exp)
    B.tt("dve", ls[0:1, 4:5], ls[0:1, 3:4], ls[0:1, 2:3], ALU.subtract)
    B.ts("dve", ls[0:1, 5:6], ls[0:1, 4:5], -lam_init, None, op0=ALU.add)
    B.mm(pb[7][:, 0:1], ones_f[0:1, :], ls[0:1, 5:6])
    B.copy("dve", nlam, pb[7][:, 0:1])
    Sring = _Cycle([4, 5, 6, 7])
    accsets = _Cycle([(0, 1), (2, 3)])
    scale = float(128 ** -0.5)

    def loadhead(h):
        s = h % 2
        for j, (nm, r0) in enumerate((("dq", 2 * h), ("dq", 2 * h + 1), ("dk", 2 * h), ("dk", 2 * h + 1))):
            B.load("sp", qk[s][j], S[nm][r0 * 128:(r0 + 1) * 128, :], hd[s])
        B.load("sp", vv[s], S["dv"][:, h * 256:(h + 1) * 256].rearrange("(n p) c -> p n c", p=128), hd[s])

    loadhead(0)
    pending = []
    for h in range(4):
        s = h % 2
        for qb in range(NB):
            qsl = slice(qb * 512, (qb + 1) * 512)
            sg, sgd = sgr.next()
            B.load("sp", sg, S["sg_diff"][h * 256:(h + 1) * 256, qsl].rearrange("(c p) t -> p c t", p=128), sgd)
            for sm in range(2):
                a0, a1 = accsets.next()
                inj = [pending.pop(0)] if (pending and sm == 0) else None
                db = attn_qblock(B, qsl, [(qk[s][2 + sm], qk[s][sm])], [vv[s][:, :, 0:128], vv[s][:, :, 128:256]],
                                 [a0, a1], Sring, Pring, pb, ones_f, scale, NT, SEG, qb * 512, SPring, inject=inj, every=4)
                if qb == 0 and sm == 0 and h + 1 < 4:
                    loadhead(h + 1)
                rd, _ = tmpf.next()
                B.recip(rd, pb[db])
                B.tt("dve", on[sm][:, 0], pb[a0], rd, ALU.mult)
                B.tt("dve", on[sm][:, 1], pb[a1], rd, ALU.mult)

            def post(h=h, qsl=qsl, sg=sg):
                B.stt("dve", oo, on[1], nlam[:, 0:1], on[0], ALU.mult, ALU.add)
                sbk = Sring.next()
                for c in range(2):
                    sq, _ = sqr.next()
                    B.act(sq, oo[:, c], AF.Square)
                    B.mm(pb[sbk], ones_b, sq, start=(c == 0), stop=(c == 1))
                rs, _ = tmpf.next()
                B.rsqrt_act(rs, pb[sbk], 1.0 / 256, EPS)
                og, od = ost.next()
                for c in range(2):
                    tf, _ = tmpf.next()
                    B.stt("dve", tf, oo[:, c], subw[:, c:c + 1], rs, ALU.mult, ALU.mult)
                    B.stt("dve", og[:, c], tf, 1.0 - lam_init, sg[:, c], ALU.mult, ALU.mult)
                B.store("sp", S["br_diff"][h * 256:(h + 1) * 256, qsl].rearrange("(c p) t -> p c t", p=128), og, od)

            pending.append(post)
    for f in pending:
        f()


def phase_E(B, l, W, C):
    nc, P, S, T, NT, NB, SEG = B.nc, B.P, B.S, B.T, B.NT, B.NB, B.SEG
    B.reset_arena()
    PADW = SEG + 16
    ubr = B.ring(2, [2, SEG], BF16, "ub")
    bufs = [B.alloc([2, PADW], F32, f"pbuf{i}") for i in range(2)]
    rcb = B.alloc([2, SEG], F32, "rcb")
    pooled = [B.alloc([T], BF16, f"pooled{i}") for i in range(2)]
    mean = B.alloc([2, SEG], F32, "mean")
    pw = B.alloc([2, 256], BF16, "pw")
    psc = B.alloc([8], F32, "psc", const=True)
    sgr = B.ring(2, [512], BF16, "sg")
    ost = B.ring(2, [512], BF16, "ost")
    tmpf = B.ring(2, [512], F32, "tmpf")
    cd = P.dsem()
    rcd = P.dsem()
    pwd = P.dsem()
    B.load("sp", psc, W["pool_scale"][l].rearrange("(c p) -> p c", p=128), cd)
    pb = [B.psum_tl(i) for i in range(8)]
    banks = _Cycle(list(range(8)))
    shifts = [(-1, 0, 1, PADW), (-1, 1, 2, PADW - 1), (-2, 2, 4, PADW - 3), (-4, 4, 8, PADW - 7)]
    for g in range(4):
        B.load("sp", rcb, C["rcnt"][g].rearrange("(s t) -> s t", s=2).partition_broadcast(128), rcd)
        B.load("pool", pw, W["pool_w"][l, g].rearrange("(c p) d -> p c d", p=128), pwd)
        for j in range(2):
            c = 2 * g + j
            ub, ud = ubr.next()
            B.load("sp", ub, S["u"][c * 128:(c + 1) * 128, :].rearrange("p (s t) -> p s t", s=2), ud)
            a = bufs[0]
            B.memset("pool", a[:, 0, 0:8], 0.0)
            B.memset("pool", a[:, 1, 8 + SEG:PADW], 0.0)
            B.copy("pool", a[:, :, 8:8 + SEG], ub)
            B.ts("pool", a[:, 0, 8 + SEG:PADW], ub[:, 1, 0:8], B.link[:, 0:1], None, op0=ALU.mult)
            B.ts("pool", a[:, 1, 0:8], ub[:, 0, SEG - 8:SEG], B.link[:, 0:1], None, op0=ALU.mult)
            cur = 0
            for st in range(g + 1):
                s0, s1, lo, hi = shifts[st]
                src, dst = bufs[cur], bufs[1 - cur]
                B.tt("dve" if st % 2 == 0 else "pool", dst[:, :, lo:hi], src[:, :, lo + s0:hi + s0], src[:, :, lo + s1:hi + s1], ALU.add)
                cur = 1 - cur
            B.tt("dve", mean, bufs[cur][:, :, 8:8 + SEG], rcb, ALU.mult)
            B.tt("dve", pooled[j].v(lambda a_: a_.rearrange("p (s t) -> p s t", s=2)), mean, ub, ALU.subtract)
        for dc in range(2):
            co = 2 * g + dc
            for sb in range(NB):
                sl = slice(sb * 512, (sb + 1) * 512)
                bk = banks.next()
                for j in range(2):
                    B.mm(pb[bk], pw[:, j, dc * 128:(dc + 1) * 128], pooled[j][:, sl], start=(j == 0), stop=(j == 1))
                sg, sgd = sgr.next()
                B.load("sp", sg, S["sg_pool"][co * 128:(co + 1) * 128, sl], sgd)
                og, od = ost.next()
                B.stt("dve", og, pb[bk], psc[:, co:co + 1], sg, ALU.mult, ALU.mult)
                B.store("sp", S["br_pool"][co * 128:(co + 1) * 128, sl], og, od)


def phase_F(B, l, x_src, x_dst, W, C, last):
    nc, P, S, T, NT, NB, SEG = B.nc, B.P, B.S, B.T, B.NT, B.NB, B.SEG
    pb = [B.psum_tl(i) for i in range(8)]
    B.reset_arena()
    TB1 = min(2048, T)
    nsb = TB1 // 512
    brT = B.alloc([32, TB1], BF16, "brT")
    wbr = B.ring(2, [32, 128], BF16, "wbr")
    mgr = B.ring(4, [4, 512], BF16, "mg")
    tmpf = B.ring(8, [512], F32, "tmpf")
    ost = B.ring(3, [512], BF16, "ost")
    brd = P.dsem()
    banks = _Cycle(list(range(8)))
    wb_src = W["w_branch"][l].rearrange("i (k p) n -> p i k n", p=128)
    mg_src = S["mg"].rearrange("(i d) t -> d i t", i=4)
    brs = [S["br_mla"], S["br_diff"], S["br_ssd"], S["br_pool"]]
    for blk in range(T // TB1):
        t0 = blk * TB1
        for i in range(4):
            B.load("sp", brT[:, i * 8:(i + 1) * 8, :], brs[i][:, t0:t0 + TB1].rearrange("(k p) t -> p k t", p=128), brd)
        wtiles = {}
        mgt = {}

        def f1_load(it, t0=t0, wtiles=wtiles, mgt=mgt):
            dmc, sb = divmod(it, nsb)
            if sb == 0:
                wb, wd = wbr.next()
                for i in range(4):
                    B.load("pool", wb[:, i * 8:(i + 1) * 8, :], wb_src[:, i, :, dmc * 128:(dmc + 1) * 128], wd)
                wtiles[dmc] = wb
            sl = slice(t0 + sb * 512, t0 + (sb + 1) * 512)
            mg, mgd = mgr.next()
            B.load("sp", mg, mg_src[dmc * 128:(dmc + 1) * 128, :, sl], mgd)
            mgt[it] = mg

        def f1_compute(it, t0=t0, wtiles=wtiles, mgt=mgt):
            dmc, sb = divmod(it, nsb)
            wb = wtiles[dmc]
            mg = mgt.pop(it)
            sl = slice(t0 + sb * 512, t0 + (sb + 1) * 512)
            lsl = slice(sb * 512, (sb + 1) * 512)
            ts_ = []
            for i in range(4):
                bk = banks.next()
                for k in range(8):
                    B.mm(pb[bk], wb[:, i * 8 + k, :], brT[:, i * 8 + k, lsl], start=(k == 0), stop=(k == 7))
                tf, _ = tmpf.next()
                B.tt("dve", tf, pb[bk], mg[:, i], ALU.mult)
                ts_.append(tf)
            B.tt("dve", ts_[0], ts_[0], ts_[1], ALU.add)
            B.tt("dve", ts_[2], ts_[2], ts_[3], ALU.add)
            og, od = ost.next()
            B.tt("dve", og, ts_[0], ts_[2], ALU.add)
            B.store("sp", S["mT"][dmc * 128:(dmc + 1) * 128, sl], og, od)

        prefetch_loop(16 * nsb, 2, f1_load, f1_compute)
    P.barrier()
    B.reset_arena()
    wout = B.alloc([16, D], BF16, "wout", const=True)
    mTr = B.ring(2, [16, 512], BF16, "mT")
    xring = B.ring(4, [D], F32, "x")
    stat = B.ring(2, [4], F32, "stat")
    sqj = B.alloc([D], BF16, "sqj")
    if last:
        fnw = B.alloc([D], F32, "fnw", const=True)
    cd = P.dsem()
    for k4 in range(4):
        B.load("pool", wout[:, k4 * 4:(k4 + 1) * 4, :],
               W["w_out"][l].rearrange("(k p) n -> p k n", p=128)[:, k4 * 4:(k4 + 1) * 4, :], cd)
    if last:
        B.load("sp", fnw, W["final_norm"].partition_broadcast(128), P.dsem())
    banks = _Cycle(list(range(8)))
    mts = {}
    xts = {}

    def f2_load(it):
        tb, i = divmod(it, 4)
        if i == 0:
            sl = slice(tb * 512, (tb + 1) * 512)
            mT, mTd = mTr.next()
            B.load("sp", mT, S["mT"][:, sl].rearrange("(k p) t -> p k t", p=128), mTd)
            mts[tb] = mT
        rows = slice(tb * 512 + i * 128, tb * 512 + (i + 1) * 128)
        xt, xd = xring.next()
        B.load("sp", xt, x_src[rows, :], xd)
        xts[it] = (xt, xd)

    def f2_compute(it):
        tb, i = divmod(it, 4)
        mT = mts[tb]
        xt, xd = xts.pop(it)
        rows = slice(tb * 512 + i * 128, tb * 512 + (i + 1) * 128)
        for nb in range(4):
            bk = banks.next()
            for k in range(16):
                B.mm(pb[bk], mT[:, k, i * 128:(i + 1) * 128], wout[:, k, nb * 512:(nb + 1) * 512],
                     start=(k == 0), stop=(k == 15))
            B.tt("dve", xt[:, nb * 512:(nb + 1) * 512], xt[:, nb * 512:(nb + 1) * 512], pb[bk], ALU.add)
        if last:
            st, _ = stat.next()
            B.act(sqj, xt, AF.Square, accum=st[:, 0:1])
            B.rsqrt_act(st[:, 1:2], st[:, 0:1], 1.0 / D, EPS)
            B.stt("dve", xt, xt, st[:, 1:2], fnw, ALU.mult, ALU.mult)
        B.store("sp", x_dst[rows, :], xt, xd)

    prefetch_loop(NB * 4, 2, f2_load, f2_compute)


def phase_D(B, l, W, C):
    nc, P, S, T, NT, NB, SEG = B.nc, B.P, B.S, B.T, B.NT, B.NB, B.SEG
    pb = [B.psum_tl(i) for i in range(8)]
    pbb = [B.psum_tl(i, BF16) for i in range(8)]
    bc3 = lambda n: (lambda a: a.unsqueeze(2).broadcast_to([a.shape[0], a.shape[1], n]))
    B.reset_arena()
    cw = B.alloc([4, 12], F32, "cw", const=True)
    cbv = B.alloc([12], F32, "cbv", const=True)
    identb = B.alloc([128], BF16, "identb", const=True)
    xpad = [B.alloc([2, SEG + 3], F32, f"xpad{i}") for i in range(2)]
    acc = [B.alloc([2, SEG], F32, f"acc{i}") for i in range(2)]
    xc = B.ring(3, [T], BF16, "xc")
    stg = B.ring(4, [8, 128], BF16, "stg")
    cd = P.dsem()
    for j in range(4):
        B.load("sp", cw[:, j, :], W["ssd_conv_w"][l, j].rearrange("(c p) -> p c", p=128), cd)
    B.load("sp", cbv, W["ssd_conv_b"][l].rearrange("(c p) -> p c", p=128), cd)
    B.load("pool", identb, C["ident"][:, :], P.dsem())
    trb = _Cycle([0, 1, 2, 3])
    xpd = [P.dsem() for _ in range(2)]
    for c in range(12):
        xp, ac = xpad[c % 2], acc[c % 2]
        B.load("pool", xp[:, :, 2:2 + SEG], S["xbc"][c * 128:(c + 1) * 128, :].rearrange("p (s t) -> p s t", s=2), xpd[c % 2])
        B.memset("pool", xp[:, 0, 0:2], 0.0)
        B.memset("pool", xp[:, 1, SEG + 2:SEG + 3], 0.0)
        B.ts("pool", xp[:, 0, SEG + 2:SEG + 3], xp[:, 1, 2:3], B.link[:, 0:1], None, op0=ALU.mult)
        B.ts("pool", xp[:, 1, 0:2], xp[:, 0, SEG:SEG + 2], B.link[:, 0:1], None, op0=ALU.mult)
        B.ts("dve", ac, xp[:, :, 0:SEG], cw[:, 0, c:c + 1], cbv[:, c:c + 1], op0=ALU.mult, op1=ALU.add)
        for j in range(1, 4):
            B.stt("dve", ac, xp[:, :, j:j + SEG], cw[:, j, c:c + 1], ac, ALU.mult, ALU.add)
        xo, xod = xc.next()
        B.act(xo.v(lambda a: a.rearrange("p (s t) -> p s t", s=2)), ac, AF.Silu)
        if c >= 8:
            B.store("sp", S["bcT"][(c - 8) * 128:(c - 7) * 128, :], xo, xod)
        if c < 10:
            for i0 in range(0, NT, 8):
                n = min(8, NT - i0)
                bk = trb.next()
                for i in range(n):
                    B.tr(pbb[bk][:, i * 128:(i + 1) * 128], xo[:, (i0 + i) * 128:(i0 + i + 1) * 128], identb)
                sg_, sgd = stg.next()
                B.copy("act", sg_[:, 0:n, :],
                       pbb[bk][:, 0:n * 128].v(lambda a: a.rearrange("p (i c) -> p i c", c=128)))
                B.store("sp", S["xsb"][i0 * 128:(i0 + n) * 128, c * 128:(c + 1) * 128].rearrange("(n p) c -> p n c", p=128),
                        sg_[:, 0:n, :], sgd)
    P.barrier()
    B.reset_arena()
    tri = B.alloc([5, 128], F32, "tri", const=True)
    trib = B.alloc([2, 128], BF16, "trib", const=True)
    identb = B.alloc([128], BF16, "identb", const=True)
    dt = B.alloc([NT, 32], F32, "dt")
    da = B.alloc([NT, 32], F32, "da")
    dtb = B.alloc([32], F32, "dtb", const=True)
    av = B.alloc([32], F32, "av")
    dvec = B.alloc([16], F32, "dvec", const=True)
    nrmw = B.alloc([1024], F32, "nrmw", const=True)
    stats = B.alloc([NT, 5, 32], F32, "stats")
    est = B.alloc([NT, 5, 32], F32, "est")
    coef = B.alloc([NT, 32], F32, "coef")
    cd = P.dsem()
    B.load("sp", tri, C["tri"].rearrange("j p c -> p j c"), cd)
    cdp = P.dsem()
    B.load("pool", trib, C["tri"][0:2].rearrange("j p c -> p j c"), cdp)
    B.load("pool", identb, C["ident"][:, :], cdp)
    B.load("sp", dt, S["dt"].rearrange("(n p) c -> p n c", p=128), cd)
    B.load("sp", dtb, W["ssd_dt_bias"][l].rearrange("a b -> (a b)").partition_broadcast(128), cd)
    B.load("sp", av, W["ssd_a_log"][l].rearrange("a b -> (a b)").partition_broadcast(128), cd)
    B.load("sp", dvec, W["ssd_d"][l].partition_broadcast(128), cd)
    B.load("sp", nrmw, W["ssd_norm"][l].partition_broadcast(128), cd)
    bc_nt = lambda a: a.unsqueeze(1).broadcast_to([128, NT, 32])
    B.tt("dve", dt, dt, dtb.v(bc_nt), ALU.add)
    B.act(dt, dt, AF.Exp)
    B.act(dt, dt, AF.Ln, bias=B.one_col[:, 0:1])
    B.act(av, av, AF.Exp)
    B.ts("dve", av, av, -1.0, None, op0=ALU.mult)
    B.tt("dve", da, dt, av.v(bc_nt), ALU.mult)
    sbk = _Cycle([0, 1, 2, 3])
    for c in range(NT):
        bk = sbk.next()
        for j in range(5):
            B.mm(pb[bk][:, j * 32:(j + 1) * 32], tri[:, j, :], da[:, c, :])
        B.copy("dve" if c % 2 == 0 else "act", stats[:, c].v(lambda a: a.rearrange("p j h -> p (j h)")), pb[bk][:, 0:160])
    B.act(est, stats, AF.Exp)
    B.tt("dve", coef[:, :, 0:16], dt[:, :, 0:16], est[:, :, 2, 0:16], ALU.mult)
    B.tt("dve", coef[:, :, 16:32], dt[:, :, 16:32], est[:, :, 3, 16:32], ALU.mult)
    B.dbg("est", est); B.dbg("dt", dt); B.dbg("da", da); B.dbg("coef", coef)
    mark = B.off
    xsr = B.ring(4, [1280], BF16, "xs")
    xdw = B.ring(2, [1024], BF16, "xdw")
    Hst = [B.alloc([1024], F32, f"H{i}") for i in range(2)]
    Hsv = B.ring(3, [1024], BF16, "Hsv")
    B.memset("dve", Hst[0], 0.0)
    B.memset("dve", Hst[1], 0.0)
    stb = _Cycle([4, 5, 6, 7])
    d2x = {}

    def d2_load(it):
        i, d = divmod(it, 2)
        c = i if d == 0 else NT - 1 - i
        xs, xsd = xsr.next()
        B.load("sp", xs, S["xsb"][c * 128:(c + 1) * 128, :], xsd)
        d2x[it] = xs

    def d2_compute(it):
            i, d = divmod(it, 2)
            c = i if d == 0 else NT - 1 - i
            xs = d2x.pop(it)
            xw, _ = xdw.next()
            B.tt("dve", xw.v(lambda a: a.rearrange("p (h q) -> p h q", q=64)),
                 xs[:, 0:1024].v(lambda a: a.rearrange("p (h q) -> p h q", q=64)),
                 coef[:, c, d * 16:(d + 1) * 16].v(bc3(64)), ALU.mult)
            H = Hst[d]
            if (d == 0 and c == NT // 2) or (d == 1 and c == NT // 2 - 1):
                B.ts("dve", H, H, B.link[:, 0:1], None, op0=ALU.mult)
            hs, hsd = Hsv.next()
            B.copy("act", hs, H)
            B.store("sp", S["H"][d, c], hs, hsd)
            B.tt("dve", H.v(lambda a: a.rearrange("p (h q) -> p h q", q=64)),
                 H.v(lambda a: a.rearrange("p (h q) -> p h q", q=64)),
                 est[:, c, 4, d * 16:(d + 1) * 16].v(bc3(64)), ALU.mult)
            for g in range(2):
                bk = stb.next()
                B.mm(pb[bk], xs[:, 1024 + g * 128:1024 + (g + 1) * 128], xw[:, g * 512:(g + 1) * 512])
                B.tt("dve", H[:, g * 512:(g + 1) * 512], H[:, g * 512:(g + 1) * 512], pb[bk], ALU.add)

    prefetch_loop(NT * 2, 2, d2_load, d2_compute)
    P.barrier()
    B.off = mark
    xsr = B.ring(3, [1024], BF16, "xs")
    bct = B.ring(3, [4, 128], BF16, "bct")
    Hr = B.ring(3, [2, 1024], BF16, "Hr")
    szr = B.ring(4, [1024], BF16, "sz")
    Dm = B.ring(3, [16, 128], F32, "Dm")
    cbm = B.ring(2, [2, 2, 128], BF16, "cbm")
    Lx = B.ring(4, [4, 128], BF16, "Lx")
    Mt = B.ring(5, [4, 128], BF16, "Mt")
    xdr = B.ring(3, [1024], BF16, "xd")
    yo = B.ring(3, [1024], F32, "yo")
    yo2 = B.alloc([1024], F32, "yo2")
    t3r = B.ring(3, [1024], F32, "t3")
    sqj = B.alloc([512], BF16, "sqj")
    ss = B.ring(2, [4], F32, "ss")
    yn = B.ring(2, [1024], BF16, "yn")
    stg = B.ring(2, [8, 128], BF16, "stg")
    cbk = _Cycle([0, 1])
    sgk = _Cycle([2, 3, 7])
    ydk = [4, 5]
    yfk = _Cycle([6])
    hq = lambda a: a.rearrange("p (h q) -> p h q", q=64)
    d3t = {}

    def d3_load(c):
        rows = slice(c * 128, (c + 1) * 128)
        xs, xsd = xsr.next()
        B.load("sp", xs, S["xsb"][rows, 0:1024], xsd)
        bt, btd = bct.next()
        B.load("sp", bt, S["bcT"][:, rows].rearrange("(j p) t -> p j t", p=128), btd)
        Hc, Hd = Hr.next()
        B.load("sp", Hc[:, 0], S["H"][0, c], Hd)
        B.load("sp", Hc[:, 1], S["H"][1, c], Hd)
        sz, szd = szr.next()
        B.load("sp", sz, S["sz"][rows, :], szd)
        d3t[c] = (xs, bt, Hc, sz)

    d3m = {}

    def d3_front(c):
        xs, bt, Hc, sz = d3t.pop(c)
        bk = 0
        for g in range(2):
            B.mm(pb[bk][:, g * 128:(g + 1) * 128], bt[:, g, :], bt[:, 2 + g, :])
        cm, _ = cbm.next()
        for d in range(2):
            B.tt("dve", cm[:, d], pb[bk][:, 0:256].v(lambda a: a.rearrange("p (g l) -> p g l", g=2)),
                 trib[:, d, :].v(lambda a: a.unsqueeze(1).broadcast_to([128, 2, 128])), ALU.mult)
        yoc, _ = yo.next()
        t3, _ = t3r.next()
        B.tt("pool", t3.v(hq), xs.v(hq), dvec.v(bc3(64)), ALU.mult)
        dms, xds = [], []
        for d in range(2):
            dm, _ = Dm.next()
            B.tt("pool", dm, tri[:, d, :].v(lambda a: a.unsqueeze(1).broadcast_to([128, 16, 128])),
                 da[:, c, d * 16:(d + 1) * 16].v(bc3(128)), ALU.mult)
            xd, _ = xdr.next()
            B.tt("dve", xd.v(hq), xs.v(hq), dt[:, c, d * 16:(d + 1) * 16].v(bc3(64)), ALU.mult)
            dms.append(dm)
            xds.append(xd)

        def yoff(d, g):
            fk = yfk.next()
            B.mm(pb[fk], bt[:, 2 + g, :], Hc[:, d, g * 512:(g + 1) * 512])
            dst = (yoc if d == 0 else yo2)[:, g * 512:(g + 1) * 512]
            eai = est[:, c, d, d * 16 + g * 8:d * 16 + (g + 1) * 8]
            B.tt("dve", dst.v(hq), pb[fk].v(hq), eai.v(bc3(64)), ALU.mult)

        items = [(d, q) for d in range(2) for q in range(4)]
        yq = [(0, 0), (0, 1), (1, 0), (1, 1)]
        LAq = 2
        pendq = []
        for step in range(len(items) + LAq):
            if step < len(items):
                d, q = items[step]
                g = q // 2
                sk = sgk.next()
                B.mm(pb[sk], tri[:, 2 + d, :], dms[d][:, q * 4:(q + 1) * 4, :].v(lambda a: a.rearrange("p h l -> p (h l)")))
                lx, _ = Lx.next()
                B.act(lx.v(lambda a: a.rearrange("p h l -> p (h l)")), pb[sk], AF.Exp)
                mt, _ = Mt.next()
                B.tt("dve", mt, lx, cm[:, d, g, :].v(lambda a: a.unsqueeze(1).broadcast_to([128, 4, 128])), ALU.mult)
                pendq.append((d, q, mt))
                if step % 2 == 1:
                    yoff(*yq.pop(0))
            if step >= LAq:
                d, q, mt = pendq.pop(0)
                g = q // 2
                for hh in range(4):
                    hd = q * 4 + hh
                    B.mm(pb[ydk[g]][:, (hd % 8) * 64:(hd % 8 + 1) * 64], mt[:, hh, :], xds[d][:, hd * 64:(hd + 1) * 64],
                         start=(d == 0 and hd % 8 == 0), stop=(d == 1 and hd % 8 == 7), skip=True)
        B.tt("dve", yoc, yoc, yo2, ALU.add)
        for g in range(2):
            B.tt("dve", yoc[:, g * 512:(g + 1) * 512], yoc[:, g * 512:(g + 1) * 512], pb[ydk[g]], ALU.add)
        d3m[c] = (yoc, t3, sz)

    def d3_back(c):
        rows = slice(c * 128, (c + 1) * 128)
        yoc, t3, sz = d3m.pop(c)
        B.tt("dve", yoc, yoc, t3, ALU.add)
        B.tt("dve", yoc, yoc, sz, ALU.mult)
        st, _ = ss.next()
        for g in range(2):
            B.act(sqj, yoc[:, g * 512:(g + 1) * 512], AF.Square, accum=st[:, g:g + 1])
        B.rsqrt_act(st[:, 2:4], st[:, 0:2], 1.0 / 512, EPS)
        ynt, _ = yn.next()
        for g in range(2):
            B.stt("dve", ynt[:, g * 512:(g + 1) * 512], yoc[:, g * 512:(g + 1) * 512],
                  st[:, 2 + g:3 + g], nrmw[:, g * 512:(g + 1) * 512], ALU.mult, ALU.mult)
        bk = 1
        for k in range(8):
            B.tr(pbb[bk][:, k * 128:(k + 1) * 128], ynt[:, k * 128:(k + 1) * 128], identb)
        sg_, sgd = stg.next()
        B.copy("act", sg_, pbb[bk].v(lambda a: a.rearrange("p (k t) -> p k t", k=8)))
        B.store("sp", S["br_ssd"][:, rows].rearrange("(k p) t -> p k t", p=128), sg_, sgd)

    d3_load(0)
    for c in range(NT + 1):
        if c + 1 < NT:
            d3_load(c + 1)
        if c < NT:
            d3_front(c)
        if c >= 1:
            d3_back(c - 1)


def prefetch_loop(n, depth, load, compute):
    for i in range(n + depth):
        if i < n:
            load(i)
        if i >= depth:
            compute(i - depth)


class _Cycle:
    def __init__(self, items):
        self.items = items
        self.i = -1

    def next(self):
        self.i = (self.i + 1) % len(self.items)
        return self.items[self.i]


def host_consts(T, link):
    SEG = T // 2
    seqlen = T if link else SEG
    pos = np.arange(T) if link else np.concatenate([np.arange(SEG), np.arange(SEG)])
    out = {}

    def tables(rot_dim):
        half = rot_dim // 2
        inv = np.power(np.float32(500000.0), -np.arange(half, dtype=np.float32) * np.float32(2.0) / np.float32(rot_dim)).astype(np.float32)
        ang = pos.astype(np.float32)[:, None] * inv[None, :]
        c = np.cos(ang).astype(np.float32).T
        s = np.sin(ang).astype(np.float32).T
        return np.stack([np.concatenate([c, c], 0), np.concatenate([s, s], 0)], 0).astype(np.float32)

    out["c_cs64"] = np.ascontiguousarray(tables(64))
    out["c_cs32"] = np.ascontiguousarray(tables(32))
    rc = np.zeros((4, T), np.float32)
    for i, w in enumerate((2, 4, 8, 16)):
        lo = w // 2
        hi = w - 1 - lo
        start = np.clip(pos - lo, 0, seqlen)
        end = np.clip(pos + hi + 1, 0, seqlen)
        rc[i] = 1.0 / (end - start).astype(np.float32)
    out["c_rcnt"] = rc
    t = np.arange(128)[:, None]
    l_ = np.arange(128)[None, :]
    out["c_tri"] = np.stack([(t <= l_), (t >= l_), (t > l_), (t < l_), np.ones((128, 128), bool)], 0).astype(np.float32)

    def rot(n):
        h = n // 2
        m = np.zeros((n, n), np.float32)
        for i in range(h):
            m[i + h, i] = -1.0
            m[i, i + h] = 1.0
        return m

    out["c_rot64"] = rot(64)
    out["c_rot32"] = rot(32)
    out["c_ident"] = np.eye(128, dtype=np.float32)
    out["c_link"] = np.full((128, 1), 1.0 if link else 0.0, np.float32)
    out["c_cbias"] = np.full((128, 1), 0.0 if link else NEG, np.float32)
    return out


_CACHE = {}


def kernel(**inputs):
    T = 4096
    xp = np.asarray(inputs["x_prompt"], dtype=np.float32)
    xs = np.asarray(inputs["x_sample"], dtype=np.float32)
    slots = []
    for c in range(8):
        if c < 2:
            slots.append((np.ascontiguousarray(xs[c]), 1))
        elif c < 6:
            i = (c - 2) * 2
            slots.append((np.ascontiguousarray(np.concatenate([xp[i], xp[i + 1]], axis=0)), 0))
        else:
            slots.append((np.zeros((T, D), np.float32), 0))
    if T not in _CACHE:
        _CACHE[T] = build(T)
    B = _CACHE[T]
    wts = {n: np.ascontiguousarray(np.asarray(inputs[n], dtype=np.float32)) for n, _ in W_NAMES}
    hc = {1: host_consts(T, 1), 0: host_consts(T, 0)}
    in_maps = []
    for x, link in slots:
        m = {"x": x}
        m.update(wts)
        m.update(hc[link])
        in_maps.append(m)
    res = run_bass_kernel_spmd(B.nc, in_maps, core_ids=list(range(8)))
    ys = [np.asarray(r["y"], dtype=np.float32) for r in res.results]
    y_sample = np.stack([ys[0], ys[1]], axis=0)
    yp = []
    for c in range(2, 6):
        yp.append(ys[c][:2048])
        yp.append(ys[c][2048:])
    y_prompt = np.stack(yp, axis=0)
    return (y_prompt, y_sample)
```

```python
import contextlib
import math
import numpy as np
import concourse.bass as bass
import concourse.mybir as mybir
from concourse.bass_utils import run_bass_kernel_spmd

F32 = mybir.dt.float32
BF16 = mybir.dt.bfloat16
AF = mybir.ActivationFunctionType
ALU = mybir.AluOpType
AX = mybir.AxisListType

D = 2048
BW = 1024
IN_DIM = 18784
DEPTH = 2
EPS = 1e-6
SEM_CAP = 32000
NEG = -30000.0


class Buf:
    __slots__ = ("name", "last_w", "readers", "const")

    def __init__(self, name="", const=False):
        self.name = name
        self.last_w = None
        self.readers = []
        self.const = const


class TL:
    __slots__ = ("ap", "buf")

    def __init__(self, ap, buf):
        self.ap = ap
        self.buf = buf

    def __getitem__(self, k):
        return TL(self.ap[k], self.buf)

    def v(self, fn):
        return TL(fn(self.ap), self.buf)


class Op:
    __slots__ = ("eng", "fn", "deps", "signal", "sigval", "is_dma", "dsem", "dtarget")

    def __init__(self, eng, fn, is_dma=False):
        self.eng = eng
        self.fn = fn
        self.deps = []
        self.signal = False
        self.sigval = None
        self.is_dma = is_dma
        self.dsem = None
        self.dtarget = None


class DmaSem:
    def __init__(self):
        self.count = 0
        self.eng = None


class Prog:
    ENGS = ("pe", "act", "dve", "pool", "sp")

    def __init__(self, nc):
        self.nc = nc
        self.eng_obj = {"pe": nc.tensor, "act": nc.scalar, "dve": nc.vector,
                        "pool": nc.gpsimd, "sp": nc.sync}
        self.ops = {e: [] for e in self.ENGS}
        self.dma_last = {}
        self.dsem_pool = []
        self.dsem_i = 0

    def dsem(self):
        if self.dsem_i >= len(self.dsem_pool):
            self.dsem_pool.append(DmaSem())
        s = self.dsem_pool[self.dsem_i]
        self.dsem_i += 1
        return s

    def _track(self, o, reads, writes, nowaw=False):
        seen = set()
        for b in reads:
            w = b.last_w
            if w is not None and id(w) not in seen:
                o.deps.append((w, "raw", w.dsem.count if w.is_dma else 0))
                seen.add(id(w))
        for b in writes:
            w = b.last_w
            if w is not None and id(w) not in seen:
                if not (nowaw and w.is_dma and w.dsem is o.dsem):
                    o.deps.append((w, "waw", w.dsem.count if w.is_dma else 0))
                    seen.add(id(w))
            for r in b.readers:
                if id(r) not in seen:
                    o.deps.append((r, "war", r.dsem.count if r.is_dma else 0))
                    seen.add(id(r))
        for b in reads:
            if not b.const:
                b.readers.append(o)
        for b in writes:
            b.last_w = o
            b.readers = []
        self.ops[o.eng].append(o)

    def op(self, eng, fn, reads=(), writes=()):
        o = Op(eng, fn)
        self._track(o, [t.buf for t in reads], [t.buf for t in writes])
        return o

    def dma(self, eng, out, in_, dsem, reads=(), writes=()):
        nc = self.nc
        eo = self.eng_obj[eng]
        oa = out.ap if isinstance(out, TL) else out
        ia = in_.ap if isinstance(in_, TL) else in_
        o = Op(eng, lambda: eo.dma_start(out=oa, in_=ia, allow_slow_non_contiguous=True), is_dma=True)
        o.dsem = dsem
        assert dsem.eng in (None, eng), "DMA semaphore shared between queues"
        dsem.eng = eng
        o.dtarget = dsem.count + 1
        rd = [t.buf for t in reads] + ([in_.buf] if isinstance(in_, TL) else [])
        wr = [t.buf for t in writes] + ([out.buf] if isinstance(out, TL) else [])
        self._track(o, rd, wr, nowaw=True)
        dsem.count += 1
        self.dma_last[id(dsem)] = o
        return o

    def barrier(self):
        lasts = []
        for e in self.ENGS:
            for o in reversed(self.ops[e]):
                if not o.is_dma and o.fn is not None:
                    lasts.append(o)
                    break
        lasts += list(self.dma_last.values())
        for e in self.ENGS:
            o = Op(e, None)
            for l in lasts:
                o.deps.append((l, "raw", l.dsem.count if l.is_dma else 0))
            self.ops[e].append(o)
        self.dsem_i = 0
        for d in self.dsem_pool:
            d.eng = None

    @staticmethod
    def _needs_sem(p, c, kind):
        if p.is_dma:
            return True
        if p.eng == c.eng:
            if p.eng in ("pe", "sp"):
                return False
            return kind == "raw"
        return True

    def emit(self, es):
        nc = self.nc
        for e in self.ENGS:
            for c in self.ops[e]:
                for (p, kind, cnt) in c.deps:
                    if not p.is_dma and self._needs_sem(p, c, kind):
                        p.signal = True
        pool = {}

        def getsem(key):
            if key not in pool:
                pool[key] = es.enter_context(nc.semaphore(f"s{len(pool)}"))
            return pool[key]

        for e in self.ENGS:
            k = 0
            for o in self.ops[e]:
                if not o.is_dma and o.signal:
                    o.sigval = k
                    k += 1
        dcap = SEM_CAP // 16
        nw = 0
        ni = 0
        for e in self.ENGS:
            eng = self.eng_obj[e]
            waited = {}
            for o in self.ops[e]:
                need = {}
                for (p, kind, cnt) in o.deps:
                    if not self._needs_sem(p, o, kind):
                        continue
                    if p.is_dma:
                        tot = cnt
                        key = ("d", id(p.dsem), (tot - 1) // dcap)
                        v = ((tot - 1) % dcap + 1) * 16
                    else:
                        key = ("c", p.eng, p.sigval // SEM_CAP)
                        v = p.sigval % SEM_CAP + 1
                    if need.get(key, 0) < v:
                        need[key] = v
                for key, v in need.items():
                    if waited.get(key, 0) >= v:
                        continue
                    waited[key] = v
                    eng.wait_ge(getsem(key), v)
                    nw += 1
                if o.fn is None:
                    continue
                ins = o.fn()
                ni += 1
                if o.is_dma:
                    ins.then_inc(getsem(("d", id(o.dsem), (o.dtarget - 1) // dcap)), 16)
                elif o.signal:
                    ins.then_inc(getsem(("c", o.eng, o.sigval // SEM_CAP)), 1)
        self.stats = dict(waits=nw, insts=ni, sems=len(pool))
        return self.stats


IN_GROUPS = [
    ("cq", 0, 512, "F", "lat"), ("ckv", 512, 256, "F", "lat"), ("kr", 768, 64, "F", "lat"),
    ("g_mla", 832, 1024, "F", "silu"), ("dq", 1856, 1024, "F", "rope"), ("dk", 2880, 1024, "F", "rope"),
    ("dv", 3904, 1024, "T", "copy"), ("g_diff", 4928, 1024, "F", "silu"), ("z", 5952, 1024, "T", "silu"),
    ("xbc", 6976, 1536, "F", "copy"), ("dt", 8512, 32, "T", "copyf"), ("u", 8544, 1024, "F", "copy"),
    ("g_pool", 9568, 1024, "F", "silu"), ("mg", 10592, 8192, "F", "sigmoid"),
]


class Builder:
    def __init__(self, T, debug=False, nlayers=DEPTH, phases=None):
        self.T = T
        self.SEG = T // 2
        self.NT = T // 128
        self.NB = T // 512
        self.debug = debug
        self.nlayers = nlayers
        self.phases = phases
        self.nc = bass.Bass("TRN2", target_bir_lowering=False)
        self.P = Prog(self.nc)
        self.es = contextlib.ExitStack()

    def dram_in(self, name, shape, dt=F32):
        return self.nc.dram_tensor(name, list(shape), dt, kind="ExternalInput").ap()

    def dram_out(self, name, shape, dt=F32):
        return self.nc.dram_tensor(name, list(shape), dt, kind="ExternalOutput").ap()

    def dram_scr(self, name, shape, dt):
        kind = "ExternalOutput" if self.debug else "Internal"
        return self.nc.dram_tensor(name, list(shape), dt, kind=kind).ap()

    def reset_arena(self):
        self.off = 0

    def alloc(self, shape, dt, name="", const=False):
        n = int(np.prod(shape))
        nbytes = n * (4 if dt == F32 else 2)
        nw = (nbytes + 3) // 4
        nw = (nw + 7) // 8 * 8
        assert self.off + nw <= self.BIGW, f"SBUF arena overflow {name} {self.off + nw}"
        ap = self.big[:, self.off:self.off + nw]
        self.off += nw
        if dt == BF16:
            ap = ap.bitcast(BF16)[:, 0:n]
        else:
            ap = ap[:, 0:n]
        if len(shape) > 1:
            names = " ".join(f"a{i}" for i in range(len(shape)))
            kw = {f"a{i}": int(s) for i, s in enumerate(shape)}
            ap = ap.rearrange(f"p ({names}) -> p {names}", **kw)
        return TL(ap, Buf(name, const=const))

    def ring(self, n, shape, dt, name=""):
        tl = [self.alloc(shape, dt, f"{name}{i}") for i in range(n)]
        ds = [self.P.dsem() for _ in range(n)]
        return _Ring(tl, ds)

    def psum_tl(self, i, dt=F32):
        ap = self.banks[i][:]
        if dt == BF16:
            ap = ap.bitcast(BF16)
        return TL(ap, self.bank_bufs[i])

    def mm(self, out, lhsT, rhs, start=True, stop=True, skip=False):
        nc = self.nc
        o, a, b = out.ap, lhsT.ap, rhs.ap
        if skip:
            return self.P.op("pe", lambda: nc.tensor.matmul(o, a, b, start=start, stop=stop, skip_group_check=True),
                             reads=[lhsT, rhs], writes=[out])
        return self.P.op("pe", lambda: nc.tensor.matmul(o, a, b, start=start, stop=stop),
                         reads=[lhsT, rhs], writes=[out])

    def tr(self, out, in_, ident):
        nc = self.nc
        o, a, b = out.ap, in_.ap, ident.ap
        return self.P.op("pe", lambda: nc.tensor.transpose(o, a, b), reads=[in_, ident], writes=[out])

    def act(self, out, in_, func, bias=None, scale=1.0, accum=None, eng="act"):
        nc = self.nc
        o, a = out.ap, in_.ap
        kw = {}
        rd = [in_]
        wr = [out]
        if bias is not None:
            if isinstance(bias, TL):
                kw["bias"] = bias.ap
                rd.append(bias)
            else:
                kw["bias"] = float(bias)
        if isinstance(scale, TL):
            kw["scale"] = scale.ap
            rd.append(scale)
        else:
            kw["scale"] = float(scale)
        if accum is not None:
            kw["accum_out"] = accum.ap
            wr.append(accum)
        return self.P.op("act", lambda: nc.scalar.activation(out=o, in_=a, func=func, **kw), reads=rd, writes=wr)

    def _e(self, eng):
        return self.P.eng_obj[eng]

    def tt(self, eng, out, in0, in1, op):
        e = self._e(eng)
        o, a, b = out.ap, in0.ap, in1.ap
        return self.P.op(eng, lambda: e.tensor_tensor(out=o, in0=a, in1=b, op=op), reads=[in0, in1], writes=[out])

    def ts(self, eng, out, in0, s1, s2=None, op0=ALU.mult, op1=None):
        e = self._e(eng)
        o, a = out.ap, in0.ap
        rd = [in0]
        v1 = s1.ap if isinstance(s1, TL) else float(s1)
        if isinstance(s1, TL):
            rd.append(s1)
        v2 = None
        if s2 is not None:
            v2 = s2.ap if isinstance(s2, TL) else float(s2)
            if isinstance(s2, TL):
                rd.append(s2)
        if op1 is None:
            return self.P.op(eng, lambda: e.tensor_scalar(out=o, in0=a, scalar1=v1, scalar2=None, op0=op0),
                             reads=rd, writes=[out])
        return self.P.op(eng, lambda: e.tensor_scalar(out=o, in0=a, scalar1=v1, scalar2=v2, op0=op0, op1=op1),
                         reads=rd, writes=[out])

    def stt(self, eng, out, in0, scalar, in1, op0, op1):
        eng = "dve"
        e = self._e(eng)
        o, a, b = out.ap, in0.ap, in1.ap
        rd = [in0, in1]
        sv = scalar.ap if isinstance(scalar, TL) else float(scalar)
        if isinstance(scalar, TL):
            rd.append(scalar)
        return self.P.op(eng, lambda: e.scalar_tensor_tensor(out=o, in0=a, scalar=sv, in1=b, op0=op0, op1=op1),
                         reads=rd, writes=[out])

    def copy(self, eng, out, in_):
        if eng == "act":
            return self.act(out, in_, AF.Copy)
        e = self._e(eng)
        o, a = out.ap, in_.ap
        return self.P.op(eng, lambda: e.tensor_copy(o, a), reads=[in_], writes=[out])

    def recip(self, out, in_):
        nc = self.nc
        o, a = out.ap, in_.ap
        return self.P.op("dve", lambda: nc.vector.reciprocal(o, a), reads=[in_], writes=[out])

    def memset(self, eng, out, val):
        e = self._e(eng)
        o = out.ap
        return self.P.op(eng, lambda: e.memset(o, float(val)), writes=[out])

    def load(self, eng, dst, src_ap, dsem):
        return self.P.dma(eng, dst, src_ap, dsem)

    def store(self, eng, dst_ap, src, dsem):
        return self.P.dma(eng, dst_ap, src, dsem)

    def dbg(self, name, tl, parts=128):
        if not self.debug:
            return
        shp = [parts] + list(tl.ap.shape[1:])
        d = self.nc.dram_tensor("dbg_" + name, shp, tl.ap.dtype, kind="ExternalOutput").ap()
        self.P.dma("sp", d, tl[0:parts], self.P.dsem())

    def rsqrt_act(self, out, in_, scale, eps):
        self.act(out, in_, AF.Ln, bias=self.eps_col[:, 0:1] if eps == EPS else eps, scale=scale)
        self.act(out, out, AF.Exp, scale=-0.5)


class _Ring:
    def __init__(self, tl, ds):
        self.tl = tl
        self.ds = ds
        self.i = -1

    def next(self):
        self.i = (self.i + 1) % len(self.tl)
        return self.tl[self.i], self.ds[self.i]


W_NAMES = [
    ("norm_w", (DEPTH, D)), ("w_in", (DEPTH, D, IN_DIM)), ("mla_q_norm", (DEPTH, 512)),
    ("mla_w_uq", (DEPTH, 512, 1536)), ("mla_kv_norm", (DEPTH, 256)), ("mla_w_ukv", (DEPTH, 256, 2048)),
    ("diff_lambda", (DEPTH, 4, 128)), ("diff_subln", (DEPTH, 256)), ("ssd_conv_w", (DEPTH, 4, 1536)),
    ("ssd_conv_b", (DEPTH, 1536)), ("ssd_dt_bias", (DEPTH, 2, 16)), ("ssd_a_log", (DEPTH, 2, 16)),
    ("ssd_d", (DEPTH, 16)), ("ssd_norm", (DEPTH, 1024)), ("pool_w", (DEPTH, 4, 256, 256)),
    ("pool_scale", (DEPTH, 1024)), ("w_branch", (DEPTH, 4, 1024, D)), ("w_out", (DEPTH, D, D)),
    ("final_norm", (D,)),
]


def build(T, debug=False, nlayers=DEPTH, phases=None):
    B = Builder(T, debug, nlayers, phases)
    nc, P, es = B.nc, B.P, B.es
    NT, NB, SEG = B.NT, B.NB, B.SEG
    x_in = B.dram_in("x", (T, D))
    W = {n: B.dram_in(n, s) for n, s in W_NAMES}
    c_link = B.dram_in("c_link", (128, 1))
    c_cbias = B.dram_in("c_cbias", (128, 1))
    c_ident = B.dram_in("c_ident", (128, 128))
    c_tri = B.dram_in("c_tri", (5, 128, 128))
    c_rot64 = B.dram_in("c_rot64", (64, 64))
    c_rot32 = B.dram_in("c_rot32", (32, 32))
    c_cs64 = B.dram_in("c_cs64", (2, 64, T))
    c_cs32 = B.dram_in("c_cs32", (2, 32, T))
    c_rcnt = B.dram_in("c_rcnt", (4, T))
    y_out = B.dram_out("y", (T, D))
    S = {}
    S["x1"] = B.dram_scr("s_x1", (T, D), F32)
    for nm, f in [("cqn", 512), ("ckvn", 256), ("kpe", 64), ("sg_mla", 1024), ("dq", 1024), ("dk", 1024),
                  ("sg_diff", 1024), ("xbc", 1536), ("u", 1024), ("sg_pool", 1024), ("mg", 8192),
                  ("br_mla", 1024), ("br_diff", 1024), ("br_ssd", 1024), ("br_pool", 1024)]:
        S[nm] = B.dram_scr("s_" + nm, (f, T), BF16)
    S["dv"] = B.dram_scr("s_dv", (T, 1024), BF16)
    S["sz"] = B.dram_scr("s_sz", (T, 1024), BF16)
    S["dt"] = B.dram_scr("s_dt", (T, 32), F32)
    S["xsb"] = B.dram_scr("s_xsb", (T, 1280), BF16)
    S["bcT"] = B.dram_scr("s_bcT", (512, T), BF16)
    S["H"] = B.dram_scr("s_H", (2, T // 128, 128, 1024), BF16)
    S["mT"] = B.dram_scr("s_mT", (D, T), BF16)
    B.S = S

    B.BIGW = 49152 - 1024
    B.big = es.enter_context(nc.sbuf_tensor("big", [128, B.BIGW + 64], F32))
    B.banks = [es.enter_context(nc.psum_tensor(f"ps{i}", [128, 512], F32)) for i in range(8)]
    B.bank_bufs = [Buf(f"ps{i}") for i in range(8)]
    cbase = B.BIGW
    B.eps_col = TL(B.big[:, cbase:cbase + 1], Buf("eps", const=True))
    B.link = TL(B.big[:, cbase + 1:cbase + 2], Buf("link", const=True))
    B.cbias = TL(B.big[:, cbase + 2:cbase + 3], Buf("cbias", const=True))
    B.zero_col = TL(B.big[:, cbase + 3:cbase + 4], Buf("zero", const=True))
    B.one_col = TL(B.big[:, cbase + 4:cbase + 5], Buf("one", const=True))
    B.memset("dve", B.eps_col, EPS)
    B.memset("dve", B.zero_col, 0.0)
    B.memset("dve", B.one_col, 1.0)
    ds0 = DmaSem()
    B.load("sp", B.link, c_link[:, :], ds0)
    B.load("sp", B.cbias, c_cbias[:, :], ds0)
    P.barrier()

    consts = dict(ident=c_ident, tri=c_tri, rot64=c_rot64, rot32=c_rot32, cs64=c_cs64, cs32=c_cs32, rcnt=c_rcnt)
    for l in range(nlayers):
        x_src = x_in if l == 0 else S["x1"]
        last = (l == nlayers - 1)
        if phases is None or "A" in phases:
            phase_A(B, l, x_src, W, consts)
            P.barrier()
        if phases is None or "B" in phases:
            phase_B(B, l, W, consts)
            P.barrier()
        if phases is None or "C" in phases:
            phase_C(B, l, W, consts)
            P.barrier()
        if phases is None or "D" in phases:
            phase_D(B, l, W, consts)
            P.barrier()
        if phases is None or "E" in phases:
            phase_E(B, l, W, consts)
            P.barrier()
        if phases is None or "F" in phases:
            phase_F(B, l, x_src, y_out if last else S["x1"], W, consts, last)
            P.barrier()
    P.barrier()
    st = P.emit(es)
    B.stats = st
    return B


def phase_A(B, l, x_src, W, C):
    nc, P, S, T = B.nc, B.P, B.S, B.T
    B.reset_arena()
    TBA = min(1024, T)
    nblk = T // TBA
    nsub = TBA // 512
    ntile = TBA // 128
    CWMAX = 544
    normw = B.alloc([D], F32, "normw", const=True)
    identb = B.alloc([128], BF16, "identb", const=True)
    ones_b = B.alloc([128], BF16, "ones_b", const=True)
    rot64 = B.alloc([64], F32, "rot64", const=True)
    rot32 = B.alloc([32], F32, "rot32", const=True)
    qnw = B.alloc([4], F32, "qnw", const=True)
    kvnw = B.alloc([2], F32, "kvnw", const=True)
    hT = B.alloc([16, TBA], BF16, "hT")
    xring = B.ring(2, [D], F32, "x")
    hb = B.alloc([D], BF16, "hb")
    sqj = B.alloc([D], BF16, "sqj")
    stat = B.alloc([4], F32, "stat")
    wring = B.ring(3, [16, CWMAX], BF16, "w")
    ost = B.ring(4, [TBA], BF16, "ost")
    ostT = B.ring(3, [CWMAX], BF16, "ostT")
    ostF = B.ring(2, [32], F32, "ostF")
    lat = B.alloc([7, TBA], F32, "lat")
    rf = B.ring(2, [512], F32, "rf")
    cs32 = B.alloc([2, TBA], F32, "cs32")
    cs64 = B.alloc([2, TBA], F32, "cs64")
    tmpf = B.ring(2, [512], F32, "tmpf")
    tmpb = B.ring(2, [512], BF16, "tmpb")
    cd = P.dsem()
    cd32 = P.dsem()
    cd64 = P.dsem()
    B.load("sp", normw, W["norm_w"][l].partition_broadcast(128), cd)
    B.load("pool", identb, C["ident"][:, :], P.dsem())
    B.load("sp", rot64[0:64], C["rot64"][:, :], cd)
    B.load("sp", rot32[0:32], C["rot32"][:, :], cd)
    B.load("sp", qnw, W["mla_q_norm"][l].rearrange("(c p) -> p c", p=128), cd)
    B.load("sp", kvnw, W["mla_kv_norm"][l].rearrange("(c p) -> p c", p=128), cd)
    B.memset("dve", ones_b, 1.0)
    pb = [B.psum_tl(i) for i in range(8)]
    pbb = [B.psum_tl(i, BF16) for i in range(8)]
    mmbank = _Cycle([0, 1, 2, 3])
    auxbank = _Cycle([4, 5])
    rotbank = _Cycle([6, 7])
    w_in = W["w_in"][l].rearrange("(k p) c -> p k c", p=128)

    wtiles = []
    for (nm, c0, ncol, orient, epi) in IN_GROUPS:
        if nm in ("ckv", "kr", "dt"):
            continue
        if nm == "cq":
            wtiles.append((0, 512, [("cq", 0, 512)]))
            wtiles.append((512, 320, [("ckv", 0, 256), ("kr", 256, 64)]))
            continue
        nt_ = ncol // 512
        for j in range(nt_):
            if nm == "xbc" and j == nt_ - 1:
                wtiles.append((c0 + j * 512, 544, [("xbc", 0, 512), ("dt", 512, 32)]))
            else:
                wtiles.append((c0 + j * 512, 512, [(nm, 0, 512)]))
    ginfo = {g[0]: g for g in IN_GROUPS}
    act_toggle = [0]

    for blk in range(nblk):
        t0 = blk * TBA
        B.load("sp", cs32[0:32], C["cs32"][:, :, t0:t0 + TBA].rearrange("a p t -> p a t"), cd32)
        B.load("sp", cs64[0:64], C["cs64"][:, :, t0:t0 + TBA].rearrange("a p t -> p a t"), cd64)
        for i in range(ntile):
            xt, xd = xring.next()
            B.load("sp", xt, x_src[t0 + i * 128:t0 + (i + 1) * 128, :], xd)
            B.act(sqj, xt, AF.Square, accum=stat[:, 0:1])
            B.rsqrt_act(stat[:, 1:2], stat[:, 0:1], 1.0 / D, EPS)
            B.stt("dve", hb, xt, stat[:, 1:2], normw, ALU.mult, ALU.mult)
            for half in range(2):
                bk = auxbank.next()
                for j in range(8):
                    k = half * 8 + j
                    B.tr(pbb[bk][:, j * 128:(j + 1) * 128], hb[:, k * 128:(k + 1) * 128], identb)
                B.copy("dve" if half == 0 else "act",
                       hT[:, half * 8:(half + 1) * 8, i * 128:(i + 1) * 128],
                       pbb[bk].v(lambda a: a.rearrange("p (j c) -> p j c", j=8)))
        for (c0, cw, parts) in wtiles:
            wt, wd = wring.next()
            B.load("pool", wt[:, :, 0:cw], w_in[:, :, c0:c0 + cw], wd)
            for (nm, po, pn) in parts:
                _, gc0, gn, orient, epi = ginfo[nm]
                gcol = c0 + po - gc0
                if orient == "F":
                    for cc in range(0, pn, 128):
                        m = min(128, pn - cc)
                        feat = gcol + cc
                        if epi != "lat":
                            og, od = ost.next()
                        for sb in range(nsub):
                            bk = mmbank.next()
                            for k in range(16):
                                B.mm(pb[bk][0:m, :], wt[:, k, po + cc:po + cc + m], hT[:, k, sb * 512:(sb + 1) * 512],
                                     start=(k == 0), stop=(k == 15))
                            src = pb[bk][0:m, :]
                            if epi == "lat":
                                li = {"cq": 0, "ckv": 4, "kr": 6}[nm] + cc // 128
                                B.copy("dve", lat[0:m, li, sb * 512:(sb + 1) * 512], src)
                            elif epi == "silu":
                                B.act(og[0:m, sb * 512:(sb + 1) * 512], src, AF.Silu)
                            elif epi == "sigmoid":
                                B.act(og[0:m, sb * 512:(sb + 1) * 512], src, AF.Sigmoid)
                            elif epi == "copy":
                                act_toggle[0] ^= 1
                                B.copy("dve", og[0:m, sb * 512:(sb + 1) * 512], src)
                            elif epi == "rope":
                                r, _ = rf.next()
                                B.copy("dve", r, src)
                                rb = rotbank.next()
                                B.mm(pb[rb][0:32, :], rot32[0:32, 0:32], r[0:32, :])
                                tf, _ = tmpf.next()
                                sl = slice(sb * 512, (sb + 1) * 512)
                                B.tt("dve", tf[0:32], r[0:32], cs32[0:32, 0, sl], ALU.mult)
                                tf2, _ = tmpf.next()
                                B.tt("dve", tf2[0:32], pb[rb][0:32, :], cs32[0:32, 1, sl], ALU.mult)
                                B.copy("act", og[:, sl], r)
                                B.tt("dve", og[0:32, sl], tf[0:32], tf2[0:32], ALU.add)
                        if epi != "lat":
                            sname = {"g_mla": "sg_mla", "g_diff": "sg_diff", "g_pool": "sg_pool"}.get(nm, nm)
                            B.store("sp", S[sname][feat:feat + m, t0:t0 + TBA], og[0:m, :], od)
                else:
                    for i in range(ntile):
                        bk = mmbank.next()
                        for k in range(16):
                            B.mm(pb[bk][:, 0:pn], hT[:, k, i * 128:(i + 1) * 128], wt[:, k, po:po + pn],
                                 start=(k == 0), stop=(k == 15))
                        src = pb[bk][:, 0:pn]
                        rows = slice(t0 + i * 128, t0 + (i + 1) * 128)
                        if epi == "copyf":
                            og, od = ostF.next()
                            B.copy("dve", og[:, 0:pn], src)
                            B.store("sp", S["dt"][rows, :], og[:, 0:pn], od)
                        else:
                            og, od = ostT.next()
                            if epi == "silu":
                                B.act(og[:, 0:pn], src, AF.Silu)
                            else:
                                B.copy("dve", og[:, 0:pn], src)
                            sname = {"z": "sz"}.get(nm, nm)
                            B.store("sp", S[sname][rows, gcol:gcol + pn], og[:, 0:pn], od)
            if parts[0][0] == "ckv":
                for sb in range(nsub):
                    sl = slice(sb * 512, (sb + 1) * 512)
                    for (nm, li0, nch, wcol, dim) in (("cqn", 0, 4, qnw, 512), ("ckvn", 4, 2, kvnw, 256)):
                        bk = auxbank.next()
                        for c in range(nch):
                            tb_, _ = tmpb.next()
                            B.act(tb_, lat[:, li0 + c, sl], AF.Square)
                            B.mm(pb[bk], ones_b, tb_, start=(c == 0), stop=(c == nch - 1))
                        tf, _ = tmpf.next()
                        B.rsqrt_act(tf, pb[bk], 1.0 / dim, EPS)
                        for c in range(nch):
                            og, od = ost.next()
                            B.stt("dve", og[:, 0:512], lat[:, li0 + c, sl], wcol[:, c:c + 1], tf, ALU.mult, ALU.mult)
                            B.store("sp", S[nm][c * 128:(c + 1) * 128, t0 + sb * 512:t0 + (sb + 1) * 512], og[:, 0:512], od)
                    rb = rotbank.next()
                    B.mm(pb[rb][0:64, :], rot64[0:64, 0:64], lat[0:64, 6, sl])
                    tf, _ = tmpf.next()
                    B.tt("dve", tf[0:64], lat[0:64, 6, sl], cs64[0:64, 0, sl], ALU.mult)
                    tf2, _ = tmpf.next()
                    B.tt("dve", tf2[0:64], pb[rb][0:64, :], cs64[0:64, 1, sl], ALU.mult)
                    og, od = ost.next()
                    B.tt("dve", og[0:64, 0:512], tf[0:64], tf2[0:64], ALU.add)
                    B.store("sp", S["kpe"][0:64, t0 + sb * 512:t0 + (sb + 1) * 512], og[0:64, 0:512], od)


def attn_qblock(B, qsl, s_terms, v_list, acc_banks, den_bank, Sring, Pring, pb, ones_b, scale, NT, SEG, q0, P4ring,
                inject=None, every=6, LA=2):
    DB = 4
    pend = []
    denq = []
    first = None
    p4 = None
    ng = NT // DB
    inject = list(inject) if inject else []
    for step in range(NT + LA):
        if inject and step % every == 2:
            inject.pop(0)()
        if step < NT:
            kb = step
            sb = Sring.next()
            ksl = slice(kb * 128, (kb + 1) * 128)
            for i, (kT, qT) in enumerate(s_terms):
                B.mm(pb[sb], kT[:, ksl], qT[:, qsl], start=(i == 0), stop=(i == len(s_terms) - 1))
            cross = (q0 // SEG) != ((kb * 128) // SEG)
            pt, _ = Pring.next()
            B.act(pt, pb[sb], AF.Exp, scale=scale, bias=(B.cbias if cross else None))
            pend.append((kb, pt))
            j = kb % DB
            if j == 0:
                first = pt
            elif j == 1:
                p4, _ = P4ring.next()
                B.tt("dve", p4, first, pt, ALU.add)
            else:
                B.tt("dve", p4, p4, pt, ALU.add)
            if j == DB - 1:
                denq.append((kb // DB, p4))
        if step >= LA:
            kb, pt = pend.pop(0)
            for v, ab in zip(v_list, acc_banks):
                B.mm(pb[ab], v[:, kb, :], pt, start=(kb == 0), stop=(kb == NT - 1))
            if kb % DB == DB - 1:
                g, pp = denq.pop(0)
                B.mm(pb[den_bank], ones_b, pp, start=(g == 0), stop=(g == ng - 1))
    for f in inject:
        f()


def phase_B(B, l, W, C):
    nc, P, S, T, NT, NB, SEG = B.nc, B.P, B.S, B.T, B.NT, B.NB, B.SEG
    B.reset_arena()
    cqn = B.alloc([4, T], BF16, "cqn")
    ckvn = B.alloc([2, T], BF16, "ckvn")
    kpe = B.alloc([T], BF16, "kpe")
    wuq = B.alloc([4, 1536], BF16, "wuq", const=True)
    wukv = B.alloc([2, 2048], BF16, "wukv", const=True)
    ones_b = B.alloc([128], BF16, "ones_b", const=True)
    rot64 = B.alloc([64], F32, "rot64", const=True)
    qn = [B.alloc([T], BF16, f"qn{i}") for i in range(2)]
    qp = [B.alloc([T], BF16, f"qp{i}") for i in range(2)]
    kn = [B.alloc([T], BF16, f"kn{i}") for i in range(2)]
    vv = [B.alloc([NT, 128], BF16, f"v{i}") for i in range(2)]
    csr = B.ring(2, [2, 512], F32, "cs")
    rr = B.ring(2, [512], F32, "rr")
    tmpf = B.ring(4, [512], F32, "tmpf")
    Pring = B.ring(6, [512], BF16, "pt")
    P4ring = B.ring(3, [512], BF16, "p4")
    sgr = B.ring(2, [512], BF16, "sg")
    ost = B.ring(2, [512], BF16, "ost")
    cd = P.dsem()
    B.load("sp", cqn, S["cqn"].rearrange("(c p) t -> p c t", p=128), cd)
    B.load("sp", ckvn, S["ckvn"].rearrange("(c p) t -> p c t", p=128), cd)
    B.load("sp", kpe[0:64], S["kpe"][:, :], cd)
    B.memset("pool", kpe[64:128], 0.0)
    B.memset("pool", qp[0][64:128], 0.0)
    B.memset("pool", qp[1][64:128], 0.0)
    cdp = P.dsem()
    B.load("pool", wuq, W["mla_w_uq"][l].rearrange("(c p) n -> p c n", p=128), cdp)
    B.load("pool", wukv, W["mla_w_ukv"][l].rearrange("(c p) n -> p c n", p=128), cdp)
    B.load("sp", rot64[0:64], C["rot64"][:, :], cd)
    B.memset("dve", ones_b, 1.0)
    pb = [B.psum_tl(i) for i in range(8)]
    Sring = _Cycle([0, 1, 2, 3])
    accs = _Cycle([4, 5])
    db = 6
    scale = float(192 ** -0.5)

    def prologue(h):
        s = h % 2
        for sb in range(NB):
            sl = slice(sb * 512, (sb + 1) * 512)
            for c in range(4):
                B.mm(pb[7], wuq[:, c, h * 192:h * 192 + 128], cqn[:, c, sl], start=(c == 0), stop=(c == 3))
            B.copy("dve", qn[s][:, sl], pb[7])
            yield
            for c in range(4):
                B.mm(pb[7][0:64], wuq[:, c, h * 192 + 128:h * 192 + 192], cqn[:, c, sl], start=(c == 0), stop=(c == 3))
            r, _ = rr.next()
            B.copy("dve", r[0:64], pb[7][0:64])
            cs, cdm = csr.next()
            B.load("sp", cs[0:64], C["cs64"][:, :, sl].rearrange("a p t -> p a t"), cdm)
            yield
            B.mm(pb[7][0:64], rot64[0:64, 0:64], r[0:64])
            tf, _ = tmpf.next()
            B.tt("dve", tf[0:64], r[0:64], cs[0:64, 0], ALU.mult)
            tf2, _ = tmpf.next()
            B.tt("dve", tf2[0:64], pb[7][0:64], cs[0:64, 1], ALU.mult)
            B.tt("dve", qp[s][0:64, sl], tf[0:64], tf2[0:64], ALU.add)
            yield
            for c in range(2):
                B.mm(pb[7], wukv[:, c, h * 256:h * 256 + 128], ckvn[:, c, sl], start=(c == 0), stop=(c == 1))
            B.copy("act", kn[s][:, sl], pb[7])
            yield
            for i in range(4):
                tsl = slice(sb * 512 + i * 128, sb * 512 + (i + 1) * 128)
                for c in range(2):
                    B.mm(pb[7][:, i * 128:(i + 1) * 128], ckvn[:, c, tsl], wukv[:, c, h * 256 + 128:h * 256 + 256],
                         start=(c == 0), stop=(c == 1))
            B.copy("act", vv[s][:, sb * 4:(sb + 1) * 4, :], pb[7].v(lambda a: a.rearrange("p (i d) -> p i d", i=4)))
            yield

    for _ in prologue(0):
        pass
    npiece = 5 * NB
    per_qb = (npiece + NB - 1) // NB
    every = max(1, (NT + 2) // (per_qb + 1))
    for h in range(8):
        s = h % 2
        gen = prologue(h + 1) if h + 1 < 8 else iter(())
        for qb in range(NB):
            qsl = slice(qb * 512, (qb + 1) * 512)
            ab = accs.next()
            sg, sgd = sgr.next()
            B.load("sp", sg, S["sg_mla"][h * 128:(h + 1) * 128, qsl], sgd)
            attn_qblock(B, qsl, [(kn[s], qn[s]), (kpe, qp[s])], [vv[s]], [ab], db,
                        Sring, Pring, pb, ones_b, scale, NT, SEG, qb * 512, P4ring,
                        inject=[(lambda gen=gen: next(gen, None))] * per_qb, every=every, LA=3)
            if qb == NB - 1:
                for _ in gen:
                    pass
            rd, _ = tmpf.next()
            B.recip(rd, pb[db])
            o, _ = tmpf.next()
            B.tt("dve", o, pb[ab], rd, ALU.mult)
            og, od = ost.next()
            B.tt("dve", og, o, sg, ALU.mult)
            B.store("sp", S["br_mla"][h * 128:(h + 1) * 128, qsl], og, od)
    B.dbg("qn", qn[1]); B.dbg("qp", qp[1], 64); B.dbg("kn", kn[1]); B.dbg("vv", vv[1]); B.dbg("kpe", kpe, 64)


def phase_C(B, l, W, C):
    nc, P, S, T, NT, NB, SEG = B.nc, B.P, B.S, B.T, B.NT, B.NB, B.SEG
    B.reset_arena()
    lam_init = 0.8 - 0.6 * math.exp(-0.3 * l)
    ones_b = B.alloc([128], BF16, "ones_b", const=True)
    ones_f = B.alloc([128], F32, "ones_f", const=True)
    subw = B.alloc([2], F32, "subw", const=True)
    lp = B.alloc([512], F32, "lp")
    lt = B.alloc([256], F32, "lt")
    ls = B.alloc([8], F32, "ls")
    nlam = B.alloc([1], F32, "nlam")
    qk = [[B.alloc([T], BF16, f"qk{i}{j}") for j in range(4)] for i in range(2)]
    vv = [B.alloc([NT, 256], BF16, f"v{i}") for i in range(2)]
    hd = [P.dsem() for _ in range(2)]
    Pring = B.ring(5, [512], BF16, "pt")
    P4ring = B.ring(3, [512], BF16, "p4")
    on = [B.alloc([2, 512], F32, f"on{i}") for i in range(2)]
    oo = B.alloc([2, 512], F32, "oo")
    tmpf = B.ring(5, [512], F32, "tmpf")
    sqr = B.ring(2, [512], BF16, "sq")
    sgr = B.ring(3, [2, 512], BF16, "sg")
    ost = B.ring(2, [2, 512], BF16, "ost")
    cd = P.dsem()
    B.memset("dve", ones_b, 1.0)
    B.memset("dve", ones_f, 1.0)
    B.load("sp", subw, W["diff_subln"][l].rearrange("(c p) -> p c", p=128), cd)
    B.load("sp", lp[0:1], W["diff_lambda"][l].rearrange("(o a) d -> o (a d)", o=1), cd)
    pb = [B.psum_tl(i) for i in range(8)]
    B.tt("dve", lt[0:1, 0:128], lp[0:1, 0:128], lp[0:1, 128:256], ALU.mult)
    B.tt("dve", lt[0:1, 128:256], lp[0:1, 256:384], lp[0:1, 384:512], ALU.mult)
    e = nc.vector
    for j in range(2):
        o_, a_ = ls[0:1, j:j + 1], lt[0:1, j * 128:(j + 1) * 128]
        P.op("dve", (lambda o_=o_, a_=a_: e.reduce_sum(out=o_.ap, in_=a_.ap, axis=AX.X)), reads=[a_], writes=[o_])
    B.act(ls[0:1, 2:4], ls[0:1, 0:2], AF.Exp)
    B.tt("dve", ls[0:1, 4:5], ls[0:1, 3:4], ls[0:1, 2:3], ALU.subtract)
    B.ts("dve", ls[0:1, 5:6], ls[0:1, 4:5], -lam_init, None, op0=ALU.add)
    B.mm(pb[7][:, 0:1], ones_f[0:1, :], ls[0:1, 5:6])
    B.copy("dve", nlam, pb[7][:, 0:1])
    Sring = _Cycle([5, 6, 7])
    accsets = _Cycle([(0, 1), (2, 3)])
    db = 4
    scale = float(128 ** -0.5)

    def loadhead(h):
        s = h % 2
        for j, (nm, r0) in enumerate((("dq", 2 * h), ("dq", 2 * h + 1), ("dk", 2 * h), ("dk", 2 * h + 1))):
            B.load("sp", qk[s][j], S[nm][r0 * 128:(r0 + 1) * 128, :], hd[s])
        B.load("sp", vv[s], S["dv"][:, h * 256:(h + 1) * 256].rearrange("(n p) c -> p n c", p=128), hd[s])

    loadhead(0)
    pending = []
    for h in range(4):
        s = h % 2
        for qb in range(NB):
            qsl = slice(qb * 512, (qb + 1) * 512)
            sg, sgd = sgr.next()
            B.load("sp", sg, S["sg_diff"][h * 256:(h + 1) * 256, qsl].rearrange("(c p) t -> p c t", p=128), sgd)
            for sm in range(2):
                a0, a1 = accsets.next()
                inj = [pending.pop(0)] if (pending and sm == 0) else None
                attn_qblock(B, qsl, [(qk[s][2 + sm], qk[s][sm])], [vv[s][:, :, 0:128], vv[s][:, :, 128:256]],
                            [a0, a1], db, Sring, Pring, pb, ones_b, scale, NT, SEG, qb * 512, P4ring, inject=inj, every=4, LA=2)
                if qb == 0 and sm == 0 and h + 1 < 4:
                    loadhead(h + 1)
                rd, _ = tmpf.next()
                B.recip(rd, pb[db])
                B.tt("dve", on[sm][:, 0], pb[a0], rd, ALU.mult)
                B.tt("dve", on[sm][:, 1], pb[a1], rd, ALU.mult)

            def post(h=h, qsl=qsl, sg=sg):
                B.stt("dve", oo, on[1], nlam[:, 0:1], on[0], ALU.mult, ALU.add)
                sbk = Sring.next()
                for c in range(2):
                    sq, _ = sqr.next()
                    B.act(sq, oo[:, c], AF.Square)
                    B.mm(pb[sbk], ones_b, sq, start=(c == 0), stop=(c == 1))
                rs, _ = tmpf.next()
                B.rsqrt_act(rs, pb[sbk], 1.0 / 256, EPS)
                og, od = ost.next()
                for c in range(2):
                    tf, _ = tmpf.next()
                    B.stt("dve", tf, oo[:, c], subw[:, c:c + 1], rs, ALU.mult, ALU.mult)
                    B.stt("dve", og[:, c], tf, 1.0 - lam_init, sg[:, c], ALU.mult, ALU.mult)
                B.store("sp", S["br_diff"][h * 256:(h + 1) * 256, qsl].rearrange("(c p) t -> p c t", p=128), og, od)

            pending.append(post)
    for f in pending:
        f()


def phase_E(B, l, W, C):
    nc, P, S, T, NT, NB, SEG = B.nc, B.P, B.S, B.T, B.NT, B.NB, B.SEG
    B.reset_arena()
    PADW = SEG + 16
    ubr = B.ring(2, [2, SEG], BF16, "ub")
    bufs = [B.alloc([2, PADW], F32, f"pbuf{i}") for i in range(2)]
    rcb = B.alloc([2, SEG], F32, "rcb")
    pooled = [B.alloc([T], BF16, f"pooled{i}") for i in range(2)]
    mean = B.alloc([2, SEG], F32, "mean")
    pw = B.alloc([2, 256], BF16, "pw")
    psc = B.alloc([8], F32, "psc", const=True)
    sgr = B.ring(2, [512], BF16, "sg")
    ost = B.ring(2, [512], BF16, "ost")
    tmpf = B.ring(2, [512], F32, "tmpf")
    cd = P.dsem()
    rcd = P.dsem()
    pwd = P.dsem()
    B.load("sp", psc, W["pool_scale"][l].rearrange("(c p) -> p c", p=128), cd)
    pb = [B.psum_tl(i) for i in range(8)]
    banks = _Cycle(list(range(8)))
    shifts = [(-1, 0, 1, PADW), (-1, 1, 2, PADW - 1), (-2, 2, 4, PADW - 3), (-4, 4, 8, PADW - 7)]
    for g in range(4):
        B.load("sp", rcb, C["rcnt"][g].rearrange("(s t) -> s t", s=2).partition_broadcast(128), rcd)
        B.load("pool", pw, W["pool_w"][l, g].rearrange("(c p) d -> p c d", p=128), pwd)
        for j in range(2):
            c = 2 * g + j
            ub, ud = ubr.next()
            B.load("sp", ub, S["u"][c * 128:(c + 1) * 128, :].rearrange("p (s t) -> p s t", s=2), ud)
            a = bufs[0]
            B.memset("pool", a[:, 0, 0:8], 0.0)
            B.memset("pool", a[:, 1, 8 + SEG:PADW], 0.0)
            B.copy("pool", a[:, :, 8:8 + SEG], ub)
            B.ts("pool", a[:, 0, 8 + SEG:PADW], ub[:, 1, 0:8], B.link[:, 0:1], None, op0=ALU.mult)
            B.ts("pool", a[:, 1, 0:8], ub[:, 0, SEG - 8:SEG], B.link[:, 0:1], None, op0=ALU.mult)
            cur = 0
            for st in range(g + 1):
                s0, s1, lo, hi = shifts[st]
                src, dst = bufs[cur], bufs[1 - cur]
                B.tt("dve" if st % 2 == 0 else "pool", dst[:, :, lo:hi], src[:, :, lo + s0:hi + s0], src[:, :, lo + s1:hi + s1], ALU.add)
                cur = 1 - cur
            B.tt("dve", mean, bufs[cur][:, :, 8:8 + SEG], rcb, ALU.mult)
            B.tt("dve", pooled[j].v(lambda a_: a_.rearrange("p (s t) -> p s t", s=2)), mean, ub, ALU.subtract)
        for dc in range(2):
            co = 2 * g + dc
            for sb in range(NB):
                sl = slice(sb * 512, (sb + 1) * 512)
                bk = banks.next()
                for j in range(2):
                    B.mm(pb[bk], pw[:, j, dc * 128:(dc + 1) * 128], pooled[j][:, sl], start=(j == 0), stop=(j == 1))
                sg, sgd = sgr.next()
                B.load("sp", sg, S["sg_pool"][co * 128:(co + 1) * 128, sl], sgd)
                og, od = ost.next()
                B.stt("dve", og, pb[bk], psc[:, co:co + 1], sg, ALU.mult, ALU.mult)
                B.store("sp", S["br_pool"][co * 128:(co + 1) * 128, sl], og, od)


def phase_F(B, l, x_src, x_dst, W, C, last):
    nc, P, S, T, NT, NB, SEG = B.nc, B.P, B.S, B.T, B.NT, B.NB, B.SEG
    pb = [B.psum_tl(i) for i in range(8)]
    B.reset_arena()
    TB1 = min(2048, T)
    nsb = TB1 // 512
    brT = B.alloc([32, TB1], BF16, "brT")
    wbr = B.ring(2, [32, 128], BF16, "wbr")
    mgr = B.ring(4, [4, 512], BF16, "mg")
    tmpf = B.ring(8, [512], F32, "tmpf")
    ost = B.ring(3, [512], BF16, "ost")
    brd = P.dsem()
    banks = _Cycle(list(range(8)))
    wb_src = W["w_branch"][l].rearrange("i (k p) n -> p i k n", p=128)
    mg_src = S["mg"].rearrange("(i d) t -> d i t", i=4)
    brs = [S["br_mla"], S["br_diff"], S["br_ssd"], S["br_pool"]]
    for blk in range(T // TB1):
        t0 = blk * TB1
        for i in range(4):
            B.load("sp", brT[:, i * 8:(i + 1) * 8, :], brs[i][:, t0:t0 + TB1].rearrange("(k p) t -> p k t", p=128), brd)
        wtiles = {}
        mgt = {}

        def f1_load(it, t0=t0, wtiles=wtiles, mgt=mgt):
            dmc, sb = divmod(it, nsb)
            if sb == 0:
                wb, wd = wbr.next()
                for i in range(4):
                    B.load("pool", wb[:, i * 8:(i + 1) * 8, :], wb_src[:, i, :, dmc * 128:(dmc + 1) * 128], wd)
                wtiles[dmc] = wb
            sl = slice(t0 + sb * 512, t0 + (sb + 1) * 512)
            mg, mgd = mgr.next()
            B.load("sp", mg, mg_src[dmc * 128:(dmc + 1) * 128, :, sl], mgd)
            mgt[it] = mg

        def f1_compute(it, t0=t0, wtiles=wtiles, mgt=mgt):
            dmc, sb = divmod(it, nsb)
            wb = wtiles[dmc]
            mg = mgt.pop(it)
            sl = slice(t0 + sb * 512, t0 + (sb + 1) * 512)
            lsl = slice(sb * 512, (sb + 1) * 512)
            ts_ = []
            for i in range(4):
                bk = banks.next()
                for k in range(8):
                    B.mm(pb[bk], wb[:, i * 8 + k, :], brT[:, i * 8 + k, lsl], start=(k == 0), stop=(k == 7))
                tf, _ = tmpf.next()
                B.tt("dve", tf, pb[bk], mg[:, i], ALU.mult)
                ts_.append(tf)
            B.tt("dve", ts_[0], ts_[0], ts_[1], ALU.add)
            B.tt("dve", ts_[2], ts_[2], ts_[3], ALU.add)
            og, od = ost.next()
            B.tt("dve", og, ts_[0], ts_[2], ALU.add)
            B.store("sp", S["mT"][dmc * 128:(dmc + 1) * 128, sl], og, od)

        prefetch_loop(16 * nsb, 2, f1_load, f1_compute)
    P.barrier()
    B.reset_arena()
    wout = B.alloc([16, D], BF16, "wout", const=True)
    mTr = B.ring(2, [16, 512], BF16, "mT")
    xring = B.ring(4, [D], F32, "x")
    stat = B.ring(2, [4], F32, "stat")
    sqj = B.alloc([D], BF16, "sqj")
    if last:
        fnw = B.alloc([D], F32, "fnw", const=True)
    cd = P.dsem()
    for k4 in range(4):
        B.load("pool", wout[:, k4 * 4:(k4 + 1) * 4, :],
               W["w_out"][l].rearrange("(k p) n -> p k n", p=128)[:, k4 * 4:(k4 + 1) * 4, :], cd)
    if last:
        B.load("sp", fnw, W["final_norm"].partition_broadcast(128), P.dsem())
    banks = _Cycle(list(range(8)))
    mts = {}
    xts = {}

    def f2_load(it):
        tb, i = divmod(it, 4)
        if i == 0:
            sl = slice(tb * 512, (tb + 1) * 512)
            mT, mTd = mTr.next()
            B.load("sp", mT, S["mT"][:, sl].rearrange("(k p) t -> p k t", p=128), mTd)
            mts[tb] = mT
        rows = slice(tb * 512 + i * 128, tb * 512 + (i + 1) * 128)
        xt, xd = xring.next()
        B.load("sp", xt, x_src[rows, :], xd)
        xts[it] = (xt, xd)

    def f2_compute(it):
        tb, i = divmod(it, 4)
        mT = mts[tb]
        xt, xd = xts.pop(it)
        rows = slice(tb * 512 + i * 128, tb * 512 + (i + 1) * 128)
        for nb in range(4):
            bk = banks.next()
            for k in range(16):
                B.mm(pb[bk], mT[:, k, i * 128:(i + 1) * 128], wout[:, k, nb * 512:(nb + 1) * 512],
                     start=(k == 0), stop=(k == 15))
            B.tt("dve", xt[:, nb * 512:(nb + 1) * 512], xt[:, nb * 512:(nb + 1) * 512], pb[bk], ALU.add)
        if last:
            st, _ = stat.next()
            B.act(sqj, xt, AF.Square, accum=st[:, 0:1])
            B.rsqrt_act(st[:, 1:2], st[:, 0:1], 1.0 / D, EPS)
            B.stt("dve", xt, xt, st[:, 1:2], fnw, ALU.mult, ALU.mult)
        B.store("sp", x_dst[rows, :], xt, xd)

    prefetch_loop(NB * 4, 2, f2_load, f2_compute)


def phase_D(B, l, W, C):
    nc, P, S, T, NT, NB, SEG = B.nc, B.P, B.S, B.T, B.NT, B.NB, B.SEG
    pb = [B.psum_tl(i) for i in range(8)]
    pbb = [B.psum_tl(i, BF16) for i in range(8)]
    bc3 = lambda n: (lambda a: a.unsqueeze(2).broadcast_to([a.shape[0], a.shape[1], n]))
    B.reset_arena()
    cw = B.alloc([4, 12], F32, "cw", const=True)
    cbv = B.alloc([12], F32, "cbv", const=True)
    identb = B.alloc([128], BF16, "identb", const=True)
    xpad = [B.alloc([2, SEG + 3], F32, f"xpad{i}") for i in range(2)]
    acc = [B.alloc([2, SEG], F32, f"acc{i}") for i in range(2)]
    xc = B.ring(3, [T], BF16, "xc")
    stg = B.ring(4, [8, 128], BF16, "stg")
    cd = P.dsem()
    for j in range(4):
        B.load("sp", cw[:, j, :], W["ssd_conv_w"][l, j].rearrange("(c p) -> p c", p=128), cd)
    B.load("sp", cbv, W["ssd_conv_b"][l].rearrange("(c p) -> p c", p=128), cd)
    B.load("pool", identb, C["ident"][:, :], P.dsem())
    trb = _Cycle([0, 1, 2, 3])
    xpd = [P.dsem() for _ in range(2)]
    for c in range(12):
        xp, ac = xpad[c % 2], acc[c % 2]
        B.load("pool", xp[:, :, 2:2 + SEG], S["xbc"][c * 128:(c + 1) * 128, :].rearrange("p (s t) -> p s t", s=2), xpd[c % 2])
        B.memset("pool", xp[:, 0, 0:2], 0.0)
        B.memset("pool", xp[:, 1, SEG + 2:SEG + 3], 0.0)
        B.ts("pool", xp[:, 0, SEG + 2:SEG + 3], xp[:, 1, 2:3], B.link[:, 0:1], None, op0=ALU.mult)
        B.ts("pool", xp[:, 1, 0:2], xp[:, 0, SEG:SEG + 2], B.link[:, 0:1], None, op0=ALU.mult)
        B.ts("dve", ac, xp[:, :, 0:SEG], cw[:, 0, c:c + 1], cbv[:, c:c + 1], op0=ALU.mult, op1=ALU.add)
        for j in range(1, 4):
            B.stt("dve", ac, xp[:, :, j:j + SEG], cw[:, j, c:c + 1], ac, ALU.mult, ALU.add)
        xo, xod = xc.next()
        B.act(xo.v(lambda a: a.rearrange("p (s t) -> p s t", s=2)), ac, AF.Silu)
        if c >= 8:
            B.store("sp", S["bcT"][(c - 8) * 128:(c - 7) * 128, :], xo, xod)
        if c < 10:
            for i0 in range(0, NT, 8):
                n = min(8, NT - i0)
                bk = trb.next()
                for i in range(n):
                    B.tr(pbb[bk][:, i * 128:(i + 1) * 128], xo[:, (i0 + i) * 128:(i0 + i + 1) * 128], identb)
                sg_, sgd = stg.next()
                B.copy("act", sg_[:, 0:n, :],
                       pbb[bk][:, 0:n * 128].v(lambda a: a.rearrange("p (i c) -> p i c", c=128)))
                B.store("sp", S["xsb"][i0 * 128:(i0 + n) * 128, c * 128:(c + 1) * 128].rearrange("(n p) c -> p n c", p=128),
                        sg_[:, 0:n, :], sgd)
    P.barrier()
    B.reset_arena()
    tri = B.alloc([5, 128], F32, "tri", const=True)
    trib = B.alloc([2, 128], BF16, "trib", const=True)
    identb = B.alloc([128], BF16, "identb", const=True)
    dt = B.alloc([NT, 32], F32, "dt")
    da = B.alloc([NT, 32], F32, "da")
    dtb = B.alloc([32], F32, "dtb", const=True)
    av = B.alloc([32], F32, "av")
    dvec = B.alloc([16], F32, "dvec", const=True)
    nrmw = B.alloc([1024], F32, "nrmw", const=True)
    stats = B.alloc([NT, 5, 32], F32, "stats")
    est = B.alloc([NT, 5, 32], F32, "est")
    coef = B.alloc([NT, 32], F32, "coef")
    cd = P.dsem()
    B.load("sp", tri, C["tri"].rearrange("j p c -> p j c"), cd)
    cdp = P.dsem()
    B.load("pool", trib, C["tri"][0:2].rearrange("j p c -> p j c"), cdp)
    B.load("pool", identb, C["ident"][:, :], cdp)
    B.load("sp", dt, S["dt"].rearrange("(n p) c -> p n c", p=128), cd)
    B.load("sp", dtb, W["ssd_dt_bias"][l].rearrange("a b -> (a b)").partition_broadcast(128), cd)
    B.load("sp", av, W["ssd_a_log"][l].rearrange("a b -> (a b)").partition_broadcast(128), cd)
    B.load("sp", dvec, W["ssd_d"][l].partition_broadcast(128), cd)
    B.load("sp", nrmw, W["ssd_norm"][l].partition_broadcast(128), cd)
    bc_nt = lambda a: a.unsqueeze(1).broadcast_to([128, NT, 32])
    B.tt("dve", dt, dt, dtb.v(bc_nt), ALU.add)
    B.act(dt, dt, AF.Exp)
    B.act(dt, dt, AF.Ln, bias=B.one_col[:, 0:1])
    B.act(av, av, AF.Exp)
    B.ts("dve", av, av, -1.0, None, op0=ALU.mult)
    B.tt("dve", da, dt, av.v(bc_nt), ALU.mult)
    sbk = _Cycle([0, 1, 2, 3])
    for c in range(NT):
        bk = sbk.next()
        for j in range(5):
            B.mm(pb[bk][:, j * 32:(j + 1) * 32], tri[:, j, :], da[:, c, :])
        B.copy("dve" if c % 2 == 0 else "act", stats[:, c].v(lambda a: a.rearrange("p j h -> p (j h)")), pb[bk][:, 0:160])
    B.act(est, stats, AF.Exp)
    B.tt("dve", coef[:, :, 0:16], dt[:, :, 0:16], est[:, :, 2, 0:16], ALU.mult)
    B.tt("dve", coef[:, :, 16:32], dt[:, :, 16:32], est[:, :, 3, 16:32], ALU.mult)
    B.dbg("est", est); B.dbg("dt", dt); B.dbg("da", da); B.dbg("coef", coef)
    mark = B.off
    xsr = B.ring(4, [1280], BF16, "xs")
    xdw = B.ring(2, [1024], BF16, "xdw")
    Hst = [B.alloc([1024], F32, f"H{i}") for i in range(2)]
    Hsv = B.ring(3, [1024], BF16, "Hsv")
    B.memset("dve", Hst[0], 0.0)
    B.memset("dve", Hst[1], 0.0)
    stb = _Cycle([4, 5, 6, 7])
    d2x = {}

    def d2_load(it):
        i, d = divmod(it, 2)
        c = i if d == 0 else NT - 1 - i
        xs, xsd = xsr.next()
        B.load("sp", xs, S["xsb"][c * 128:(c + 1) * 128, :], xsd)
        d2x[it] = xs

    def d2_compute(it):
            i, d = divmod(it, 2)
            c = i if d == 0 else NT - 1 - i
            xs = d2x.pop(it)
            xw, _ = xdw.next()
            B.tt("dve", xw.v(lambda a: a.rearrange("p (h q) -> p h q", q=64)),
                 xs[:, 0:1024].v(lambda a: a.rearrange("p (h q) -> p h q", q=64)),
                 coef[:, c, d * 16:(d + 1) * 16].v(bc3(64)), ALU.mult)
            H = Hst[d]
            if (d == 0 and c == NT // 2) or (d == 1 and c == NT // 2 - 1):
                B.ts("dve", H, H, B.link[:, 0:1], None, op0=ALU.mult)
            hs, hsd = Hsv.next()
            B.copy("act", hs, H)
            B.store("sp", S["H"][d, c], hs, hsd)
            B.tt("dve", H.v(lambda a: a.rearrange("p (h q) -> p h q", q=64)),
                 H.v(lambda a: a.rearrange("p (h q) -> p h q", q=64)),
                 est[:, c, 4, d * 16:(d + 1) * 16].v(bc3(64)), ALU.mult)
            for g in range(2):
                bk = stb.next()
                B.mm(pb[bk], xs[:, 1024 + g * 128:1024 + (g + 1) * 128], xw[:, g * 512:(g + 1) * 512])
                B.tt("dve", H[:, g * 512:(g + 1) * 512], H[:, g * 512:(g + 1) * 512], pb[bk], ALU.add)

    prefetch_loop(NT * 2, 2, d2_load, d2_compute)
    P.barrier()
    B.off = mark
    xsr = B.ring(3, [1024], BF16, "xs")
    bct = B.ring(3, [4, 128], BF16, "bct")
    Hr = B.ring(3, [2, 1024], BF16, "Hr")
    szr = B.ring(4, [1024], BF16, "sz")
    Dm = B.ring(3, [16, 128], F32, "Dm")
    cbm = B.ring(2, [2, 2, 128], BF16, "cbm")
    Lx = B.ring(4, [4, 128], BF16, "Lx")
    Mt = B.ring(5, [4, 128], BF16, "Mt")
    xdr = B.ring(3, [1024], BF16, "xd")
    yo = B.ring(3, [1024], F32, "yo")
    yo2 = B.alloc([1024], F32, "yo2")
    t3r = B.ring(3, [1024], F32, "t3")
    sqj = B.alloc([512], BF16, "sqj")
    ss = B.ring(2, [4], F32, "ss")
    yn = B.ring(2, [1024], BF16, "yn")
    stg = B.ring(2, [8, 128], BF16, "stg")
    cbk = _Cycle([0, 1])
    sgk = _Cycle([2, 3, 7])
    ydk = [4, 5]
    yfk = _Cycle([6])
    hq = lambda a: a.rearrange("p (h q) -> p h q", q=64)
    d3t = {}

    def d3_load(c):
        rows = slice(c * 128, (c + 1) * 128)
        xs, xsd = xsr.next()
        B.load("sp", xs, S["xsb"][rows, 0:1024], xsd)
        bt, btd = bct.next()
        B.load("sp", bt, S["bcT"][:, rows].rearrange("(j p) t -> p j t", p=128), btd)
        Hc, Hd = Hr.next()
        B.load("sp", Hc[:, 0], S["H"][0, c], Hd)
        B.load("sp", Hc[:, 1], S["H"][1, c], Hd)
        sz, szd = szr.next()
        B.load("sp", sz, S["sz"][rows, :], szd)
        d3t[c] = (xs, bt, Hc, sz)

    d3m = {}

    def d3_front(c):
        xs, bt, Hc, sz = d3t.pop(c)
        bk = 0
        for g in range(2):
            B.mm(pb[bk][:, g * 128:(g + 1) * 128], bt[:, g, :], bt[:, 2 + g, :])
        cm, _ = cbm.next()
        for d in range(2):
            B.tt("dve", cm[:, d], pb[bk][:, 0:256].v(lambda a: a.rearrange("p (g l) -> p g l", g=2)),
                 trib[:, d, :].v(lambda a: a.unsqueeze(1).broadcast_to([128, 2, 128])), ALU.mult)
        yoc, _ = yo.next()
        t3, _ = t3r.next()
        B.tt("pool", t3.v(hq), xs.v(hq), dvec.v(bc3(64)), ALU.mult)
        dms, xds = [], []
        for d in range(2):
            dm, _ = Dm.next()
            B.tt("pool", dm, tri[:, d, :].v(lambda a: a.unsqueeze(1).broadcast_to([128, 16, 128])),
                 da[:, c, d * 16:(d + 1) * 16].v(bc3(128)), ALU.mult)
            xd, _ = xdr.next()
            B.tt("dve", xd.v(hq), xs.v(hq), dt[:, c, d * 16:(d + 1) * 16].v(bc3(64)), ALU.mult)
            dms.append(dm)
            xds.append(xd)

        def yoff(d, g):
            fk = yfk.next()
            B.mm(pb[fk], bt[:, 2 + g, :], Hc[:, d, g * 512:(g + 1) * 512])
            dst = (yoc if d == 0 else yo2)[:, g * 512:(g + 1) * 512]
            eai = est[:, c, d, d * 16 + g * 8:d * 16 + (g + 1) * 8]
            B.tt("dve", dst.v(hq), pb[fk].v(hq), eai.v(bc3(64)), ALU.mult)

        items = [(d, q) for d in range(2) for q in range(4)]
        yq = [(0, 0), (0, 1), (1, 0), (1, 1)]
        LAq = 2
        pendq = []
        for step in range(len(items) + LAq):
            if step < len(items):
                d, q = items[step]
                g = q // 2
                sk = sgk.next()
                B.mm(pb[sk], tri[:, 2 + d, :], dms[d][:, q * 4:(q + 1) * 4, :].v(lambda a: a.rearrange("p h l -> p (h l)")))
                lx, _ = Lx.next()
                B.act(lx.v(lambda a: a.rearrange("p h l -> p (h l)")), pb[sk], AF.Exp)
                mt, _ = Mt.next()
                B.tt("dve", mt, lx, cm[:, d, g, :].v(lambda a: a.unsqueeze(1).broadcast_to([128, 4, 128])), ALU.mult)
                pendq.append((d, q, mt))
                if step % 2 == 1:
                    yoff(*yq.pop(0))
            if step >= LAq:
                d, q, mt = pendq.pop(0)
                g = q // 2
                for hh in range(4):
                    hd = q * 4 + hh
                    B.mm(pb[ydk[g]][:, (hd % 8) * 64:(hd % 8 + 1) * 64], mt[:, hh, :], xds[d][:, hd * 64:(hd + 1) * 64],
                         start=(d == 0 and hd % 8 == 0), stop=(d == 1 and hd % 8 == 7), skip=True)
        B.tt("dve", yoc, yoc, yo2, ALU.add)
        for g in range(2):
            B.tt("dve", yoc[:, g * 512:(g + 1) * 512], yoc[:, g * 512:(g + 1) * 512], pb[ydk[g]], ALU.add)
        d3m[c] = (yoc, t3, sz)

    def d3_back(c):
        rows = slice(c * 128, (c + 1) * 128)
        yoc, t3, sz = d3m.pop(c)
        B.tt("dve", yoc, yoc, t3, ALU.add)
        B.tt("dve", yoc, yoc, sz, ALU.mult)
        st, _ = ss.next()
        for g in range(2):
            B.act(sqj, yoc[:, g * 512:(g + 1) * 512], AF.Square, accum=st[:, g:g + 1])
        B.rsqrt_act(st[:, 2:4], st[:, 0:2], 1.0 / 512, EPS)
        ynt, _ = yn.next()
        for g in range(2):
            B.stt("dve", ynt[:, g * 512:(g + 1) * 512], yoc[:, g * 512:(g + 1) * 512],
                  st[:, 2 + g:3 + g], nrmw[:, g * 512:(g + 1) * 512], ALU.mult, ALU.mult)
        bk = 1
        for k in range(8):
            B.tr(pbb[bk][:, k * 128:(k + 1) * 128], ynt[:, k * 128:(k + 1) * 128], identb)
        sg_, sgd = stg.next()
        B.copy("act", sg_, pbb[bk].v(lambda a: a.rearrange("p (k t) -> p k t", k=8)))
        B.store("sp", S["br_ssd"][:, rows].rearrange("(k p) t -> p k t", p=128), sg_, sgd)

    d3_load(0)
    for c in range(NT + 1):
        if c + 1 < NT:
            d3_load(c + 1)
        if c < NT:
            d3_front(c)
        if c >= 1:
            d3_back(c - 1)


def prefetch_loop(n, depth, load, compute):
    for i in range(n + depth):
        if i < n:
            load(i)
        if i >= depth:
            compute(i - depth)


class _Cycle:
    def __init__(self, items):
        self.items = items
        self.i = -1

    def next(self):
        self.i = (self.i + 1) % len(self.items)
        return self.items[self.i]


def host_consts(T, link):
    SEG = T // 2
    seqlen = T if link else SEG
    pos = np.arange(T) if link else np.concatenate([np.arange(SEG), np.arange(SEG)])
    out = {}

    def tables(rot_dim):
        half = rot_dim // 2
        inv = np.power(np.float32(500000.0), -np.arange(half, dtype=np.float32) * np.float32(2.0) / np.float32(rot_dim)).astype(np.float32)
        ang = pos.astype(np.float32)[:, None] * inv[None, :]
        c = np.cos(ang).astype(np.float32).T
        s = np.sin(ang).astype(np.float32).T
        return np.stack([np.concatenate([c, c], 0), np.concatenate([s, s], 0)], 0).astype(np.float32)

    out["c_cs64"] = np.ascontiguousarray(tables(64))
    out["c_cs32"] = np.ascontiguousarray(tables(32))
    rc = np.zeros((4, T), np.float32)
    for i, w in enumerate((2, 4, 8, 16)):
        lo = w // 2
        hi = w - 1 - lo
        start = np.clip(pos - lo, 0, seqlen)
        end = np.clip(pos + hi + 1, 0, seqlen)
        rc[i] = 1.0 / (end - start).astype(np.float32)
    out["c_rcnt"] = rc
    t = np.arange(128)[:, None]
    l_ = np.arange(128)[None, :]
    out["c_tri"] = np.stack([(t <= l_), (t >= l_), (t > l_), (t < l_), np.ones((128, 128), bool)], 0).astype(np.float32)

    def rot(n):
        h = n // 2
        m = np.zeros((n, n), np.float32)
        for i in range(h):
            m[i + h, i] = -1.0
            m[i, i + h] = 1.0
        return m

    out["c_rot64"] = rot(64)
    out["c_rot32"] = rot(32)
    out["c_ident"] = np.eye(128, dtype=np.float32)
    out["c_link"] = np.full((128, 1), 1.0 if link else 0.0, np.float32)
    out["c_cbias"] = np.full((128, 1), 0.0 if link else NEG, np.float32)
    return out


_CACHE = {}


def kernel(**inputs):
    T = 4096
    xp = np.asarray(inputs["x_prompt"], dtype=np.float32)
    xs = np.asarray(inputs["x_sample"], dtype=np.float32)
    slots = []
    for c in range(8):
        if c < 2:
            slots.append((np.ascontiguousarray(xs[c]), 1))
        elif c < 6:
            i = (c - 2) * 2
            slots.append((np.ascontiguousarray(np.concatenate([xp[i], xp[i + 1]], axis=0)), 0))
        else:
            slots.append((np.zeros((T, D), np.float32), 0))
    if T not in _CACHE:
        _CACHE[T] = build(T)
    B = _CACHE[T]
    wts = {n: np.ascontiguousarray(np.asarray(inputs[n], dtype=np.float32)) for n, _ in W_NAMES}
    hc = {1: host_consts(T, 1), 0: host_consts(T, 0)}
    in_maps = []
    for x, link in slots:
        m = {"x": x}
        m.update(wts)
        m.update(hc[link])
        in_maps.append(m)
    res = run_bass_kernel_spmd(B.nc, in_maps, core_ids=list(range(8)))
    ys = [np.asarray(r["y"], dtype=np.float32) for r in res.results]
    y_sample = np.stack([ys[0], ys[1]], axis=0)
    yp = []
    for c in range(2, 6):
        yp.append(ys[c][:2048])
        yp.append(ys[c][2048:])
    y_prompt = np.stack(yp, axis=0)
    return (y_prompt, y_sample)
```

```python
import contextlib
import math
import numpy as np
import concourse.bass as bass
import concourse.mybir as mybir
from concourse.bass_utils import run_bass_kernel_spmd

F32 = mybir.dt.float32
BF16 = mybir.dt.bfloat16
AF = mybir.ActivationFunctionType
ALU = mybir.AluOpType
AX = mybir.AxisListType

D = 2048
BW = 1024
IN_DIM = 18784
DEPTH = 2
EPS = 1e-6
SEM_CAP = 32000
NEG = -30000.0


class Buf:
    __slots__ = ("name", "last_w", "readers", "const")

    def __init__(self, name="", const=False):
        self.name = name
        self.last_w = None
        self.readers = []
        self.const = const


class TL:
    __slots__ = ("ap", "buf")

    def __init__(self, ap, buf):
        self.ap = ap
        self.buf = buf

    def __getitem__(self, k):
        return TL(self.ap[k], self.buf)

    def v(self, fn):
        return TL(fn(self.ap), self.buf)


class Op:
    __slots__ = ("eng", "fn", "deps", "signal", "sigval", "is_dma", "dsem", "dtarget")

    def __init__(self, eng, fn, is_dma=False):
        self.eng = eng
        self.fn = fn
        self.deps = []
        self.signal = False
        self.sigval = None
        self.is_dma = is_dma
        self.dsem = None
        self.dtarget = None


class DmaSem:
    def __init__(self):
        self.count = 0
        self.eng = None


class Prog:
    ENGS = ("pe", "act", "dve", "pool", "sp")

    def __init__(self, nc):
        self.nc = nc
        self.eng_obj = {"pe": nc.tensor, "act": nc.scalar, "dve": nc.vector,
                        "pool": nc.gpsimd, "sp": nc.sync}
        self.ops = {e: [] for e in self.ENGS}
        self.dma_last = {}
        self.dsem_pool = []
        self.dsem_i = 0

    def dsem(self):
        if self.dsem_i >= len(self.dsem_pool):
            self.dsem_pool.append(DmaSem())
        s = self.dsem_pool[self.dsem_i]
        self.dsem_i += 1
        return s

    def _track(self, o, reads, writes, nowaw=False):
        seen = set()
        for b in reads:
            w = b.last_w
            if w is not None and id(w) not in seen:
                o.deps.append((w, "raw", w.dsem.count if w.is_dma else 0))
                seen.add(id(w))
        for b in writes:
            w = b.last_w
            if w is not None and id(w) not in seen:
                if not (nowaw and w.is_dma and w.dsem is o.dsem):
                    o.deps.append((w, "waw", w.dsem.count if w.is_dma else 0))
                    seen.add(id(w))
            for r in b.readers:
                if id(r) not in seen:
                    o.deps.append((r, "war", r.dsem.count if r.is_dma else 0))
                    seen.add(id(r))
        for b in reads:
            if not b.const:
                b.readers.append(o)
        for b in writes:
            b.last_w = o
            b.readers = []
        self.ops[o.eng].append(o)

    def op(self, eng, fn, reads=(), writes=()):
        o = Op(eng, fn)
        self._track(o, [t.buf for t in reads], [t.buf for t in writes])
        return o

    def dma(self, eng, out, in_, dsem, reads=(), writes=()):
        nc = self.nc
        eo = self.eng_obj[eng]
        oa = out.ap if isinstance(out, TL) else out
        ia = in_.ap if isinstance(in_, TL) else in_
        o = Op(eng, lambda: eo.dma_start(out=oa, in_=ia, allow_slow_non_contiguous=True), is_dma=True)
        o.dsem = dsem
        assert dsem.eng in (None, eng), "DMA semaphore shared between queues"
        dsem.eng = eng
        o.dtarget = dsem.count + 1
        rd = [t.buf for t in reads] + ([in_.buf] if isinstance(in_, TL) else [])
        wr = [t.buf for t in writes] + ([out.buf] if isinstance(out, TL) else [])
        self._track(o, rd, wr, nowaw=True)
        dsem.count += 1
        self.dma_last[id(dsem)] = o
        return o

    def barrier(self):
        lasts = []
        for e in self.ENGS:
            for o in reversed(self.ops[e]):
                if not o.is_dma and o.fn is not None:
                    lasts.append(o)
                    break
        lasts += list(self.dma_last.values())
        for e in self.ENGS:
            o = Op(e, None)
            for l in lasts:
                o.deps.append((l, "raw", l.dsem.count if l.is_dma else 0))
            self.ops[e].append(o)
        self.dsem_i = 0
        for d in self.dsem_pool:
            d.eng = None

    @staticmethod
    def _needs_sem(p, c, kind):
        if p.is_dma:
            return True
        if p.eng == c.eng:
            if p.eng in ("pe", "sp"):
                return False
            return kind == "raw"
        return True

    def emit(self, es):
        nc = self.nc
        for e in self.ENGS:
            for c in self.ops[e]:
                for (p, kind, cnt) in c.deps:
                    if not p.is_dma and self._needs_sem(p, c, kind):
                        p.signal = True
        pool = {}

        def getsem(key):
            if key not in pool:
                pool[key] = es.enter_context(nc.semaphore(f"s{len(pool)}"))
            return pool[key]

        for e in self.ENGS:
            k = 0
            for o in self.ops[e]:
                if not o.is_dma and o.signal:
                    o.sigval = k
                    k += 1
        dcap = SEM_CAP // 16
        nw = 0
        ni = 0
        for e in self.ENGS:
            eng = self.eng_obj[e]
            waited = {}
            for o in self.ops[e]:
                need = {}
                for (p, kind, cnt) in o.deps:
                    if not self._needs_sem(p, o, kind):
                        continue
                    if p.is_dma:
                        tot = cnt
                        key = ("d", id(p.dsem), (tot - 1) // dcap)
                        v = ((tot - 1) % dcap + 1) * 16
                    else:
                        key = ("c", p.eng, p.sigval // SEM_CAP)
                        v = p.sigval % SEM_CAP + 1
                    if need.get(key, 0) < v:
                        need[key] = v
                for key, v in need.items():
                    if waited.get(key, 0) >= v:
                        continue
                    waited[key] = v
                    eng.wait_ge(getsem(key), v)
                    nw += 1
                if o.fn is None:
                    continue
                ins = o.fn()
                ni += 1
                if o.is_dma:
                    ins.then_inc(getsem(("d", id(o.dsem), (o.dtarget - 1) // dcap)), 16)
                elif o.signal:
                    ins.then_inc(getsem(("c", o.eng, o.sigval // SEM_CAP)), 1)
        self.stats = dict(waits=nw, insts=ni, sems=len(pool))
        return self.stats


IN_GROUPS = [
    ("cq", 0, 512, "F", "lat"), ("ckv", 512, 256, "F", "lat"), ("kr", 768, 64, "F", "lat"),
    ("g_mla", 832, 1024, "F", "silu"), ("dq", 1856, 1024, "F", "rope"), ("dk", 2880, 1024, "F", "rope"),
    ("dv", 3904, 1024, "T", "copy"), ("g_diff", 4928, 1024, "F", "silu"), ("z", 5952, 1024, "T", "silu"),
    ("xbc", 6976, 1536, "F", "copy"), ("dt", 8512, 32, "T", "copyf"), ("u", 8544, 1024, "F", "copy"),
    ("g_pool", 9568, 1024, "F", "silu"), ("mg", 10592, 8192, "F", "sigmoid"),
]


class Builder:
    def __init__(self, T, debug=False, nlayers=DEPTH, phases=None):
        self.T = T
        self.SEG = T // 2
        self.NT = T // 128
        self.NB = T // 512
        self.debug = debug
        self.nlayers = nlayers
        self.phases = phases
        self.nc = bass.Bass("TRN2", target_bir_lowering=False)
        self.P = Prog(self.nc)
        self.es = contextlib.ExitStack()

    def dram_in(self, name, shape, dt=F32):
        return self.nc.dram_tensor(name, list(shape), dt, kind="ExternalInput").ap()

    def dram_out(self, name, shape, dt=F32):
        return self.nc.dram_tensor(name, list(shape), dt, kind="ExternalOutput").ap()

    def dram_scr(self, name, shape, dt):
        kind = "ExternalOutput" if self.debug else "Internal"
        return self.nc.dram_tensor(name, list(shape), dt, kind=kind).ap()

    def reset_arena(self):
        self.off = 0

    def alloc(self, shape, dt, name="", const=False):
        n = int(np.prod(shape))
        nbytes = n * (4 if dt == F32 else 2)
        nw = (nbytes + 3) // 4
        nw = (nw + 7) // 8 * 8
        assert self.off + nw <= self.BIGW, f"SBUF arena overflow {name} {self.off + nw}"
        ap = self.big[:, self.off:self.off + nw]
        self.off += nw
        if dt == BF16:
            ap = ap.bitcast(BF16)[:, 0:n]
        else:
            ap = ap[:, 0:n]
        if len(shape) > 1:
            names = " ".join(f"a{i}" for i in range(len(shape)))
            kw = {f"a{i}": int(s) for i, s in enumerate(shape)}
            ap = ap.rearrange(f"p ({names}) -> p {names}", **kw)
        return TL(ap, Buf(name, const=const))

    def ring(self, n, shape, dt, name=""):
        tl = [self.alloc(shape, dt, f"{name}{i}") for i in range(n)]
        ds = [self.P.dsem() for _ in range(n)]
        return _Ring(tl, ds)

    def psum_tl(self, i, dt=F32):
        ap = self.banks[i][:]
        if dt == BF16:
            ap = ap.bitcast(BF16)
        return TL(ap, self.bank_bufs[i])

    def mm(self, out, lhsT, rhs, start=True, stop=True, skip=False):
        nc = self.nc
        o, a, b = out.ap, lhsT.ap, rhs.ap
        if skip:
            return self.P.op("pe", lambda: nc.tensor.matmul(o, a, b, start=start, stop=stop, skip_group_check=True),
                             reads=[lhsT, rhs], writes=[out])
        return self.P.op("pe", lambda: nc.tensor.matmul(o, a, b, start=start, stop=stop),
                         reads=[lhsT, rhs], writes=[out])

    def tr(self, out, in_, ident):
        nc = self.nc
        o, a, b = out.ap, in_.ap, ident.ap
        return self.P.op("pe", lambda: nc.tensor.transpose(o, a, b), reads=[in_, ident], writes=[out])

    def act(self, out, in_, func, bias=None, scale=1.0, accum=None, eng="act"):
        nc = self.nc
        o, a = out.ap, in_.ap
        kw = {}
        rd = [in_]
        wr = [out]
        if bias is not None:
            if isinstance(bias, TL):
                kw["bias"] = bias.ap
                rd.append(bias)
            else:
                kw["bias"] = float(bias)
        if isinstance(scale, TL):
            kw["scale"] = scale.ap
            rd.append(scale)
        else:
            kw["scale"] = float(scale)
        if accum is not None:
            kw["accum_out"] = accum.ap
            wr.append(accum)
        return self.P.op("act", lambda: nc.scalar.activation(out=o, in_=a, func=func, **kw), reads=rd, writes=wr)

    def _e(self, eng):
        return self.P.eng_obj[eng]

    def tt(self, eng, out, in0, in1, op):
        e = self._e(eng)
        o, a, b = out.ap, in0.ap, in1.ap
        return self.P.op(eng, lambda: e.tensor_tensor(out=o, in0=a, in1=b, op=op), reads=[in0, in1], writes=[out])

    def ts(self, eng, out, in0, s1, s2=None, op0=ALU.mult, op1=None):
        e = self._e(eng)
        o, a = out.ap, in0.ap
        rd = [in0]
        v1 = s1.ap if isinstance(s1, TL) else float(s1)
        if isinstance(s1, TL):
            rd.append(s1)
        v2 = None
        if s2 is not None:
            v2 = s2.ap if isinstance(s2, TL) else float(s2)
            if isinstance(s2, TL):
                rd.append(s2)
        if op1 is None:
            return self.P.op(eng, lambda: e.tensor_scalar(out=o, in0=a, scalar1=v1, scalar2=None, op0=op0),
                             reads=rd, writes=[out])
        return self.P.op(eng, lambda: e.tensor_scalar(out=o, in0=a, scalar1=v1, scalar2=v2, op0=op0, op1=op1),
                         reads=rd, writes=[out])

    def stt(self, eng, out, in0, scalar, in1, op0, op1):
        eng = "dve"
        e = self._e(eng)
        o, a, b = out.ap, in0.ap, in1.ap
        rd = [in0, in1]
        sv = scalar.ap if isinstance(scalar, TL) else float(scalar)
        if isinstance(scalar, TL):
            rd.append(scalar)
        return self.P.op(eng, lambda: e.scalar_tensor_tensor(out=o, in0=a, scalar=sv, in1=b, op0=op0, op1=op1),
                         reads=rd, writes=[out])

    def copy(self, eng, out, in_):
        if eng == "act":
            return self.act(out, in_, AF.Copy)
        e = self._e(eng)
        o, a = out.ap, in_.ap
        return self.P.op(eng, lambda: e.tensor_copy(o, a), reads=[in_], writes=[out])

    def recip(self, out, in_):
        nc = self.nc
        o, a = out.ap, in_.ap
        return self.P.op("dve", lambda: nc.vector.reciprocal(o, a), reads=[in_], writes=[out])

    def memset(self, eng, out, val):
        e = self._e(eng)
        o = out.ap
        return self.P.op(eng, lambda: e.memset(o, float(val)), writes=[out])

    def load(self, eng, dst, src_ap, dsem):
        return self.P.dma(eng, dst, src_ap, dsem)

    def store(self, eng, dst_ap, src, dsem):
        return self.P.dma(eng, dst_ap, src, dsem)

    def dbg(self, name, tl, parts=128):
        if not self.debug:
            return
        shp = [parts] + list(tl.ap.shape[1:])
        d = self.nc.dram_tensor("dbg_" + name, shp, tl.ap.dtype, kind="ExternalOutput").ap()
        self.P.dma("sp", d, tl[0:parts], self.P.dsem())

    def rsqrt_act(self, out, in_, scale, eps):
        self.act(out, in_, AF.Ln, bias=self.eps_col[:, 0:1] if eps == EPS else eps, scale=scale)
        self.act(out, out, AF.Exp, scale=-0.5)


class _Ring:
    def __init__(self, tl, ds):
        self.tl = tl
        self.ds = ds
        self.i = -1

    def next(self):
        self.i = (self.i + 1) % len(self.tl)
        return self.tl[self.i], self.ds[self.i]


W_NAMES = [
    ("norm_w", (DEPTH, D)), ("w_in", (DEPTH, D, IN_DIM)), ("mla_q_norm", (DEPTH, 512)),
    ("mla_w_uq", (DEPTH, 512, 1536)), ("mla_kv_norm", (DEPTH, 256)), ("mla_w_ukv", (DEPTH, 256, 2048)),
    ("diff_lambda", (DEPTH, 4, 128)), ("diff_subln", (DEPTH, 256)), ("ssd_conv_w", (DEPTH, 4, 1536)),
    ("ssd_conv_b", (DEPTH, 1536)), ("ssd_dt_bias", (DEPTH, 2, 16)), ("ssd_a_log", (DEPTH, 2, 16)),
    ("ssd_d", (DEPTH, 16)), ("ssd_norm", (DEPTH, 1024)), ("pool_w", (DEPTH, 4, 256, 256)),
    ("pool_scale", (DEPTH, 1024)), ("w_branch", (DEPTH, 4, 1024, D)), ("w_out", (DEPTH, D, D)),
    ("final_norm", (D,)),
]


def build(T, debug=False, nlayers=DEPTH, phases=None):
    B = Builder(T, debug, nlayers, phases)
    nc, P, es = B.nc, B.P, B.es
    NT, NB, SEG = B.NT, B.NB, B.SEG
    x_in = B.dram_in("x", (T, D))
    W = {n: B.dram_in(n, s) for n, s in W_NAMES}
    c_link = B.dram_in("c_link", (128, 1))
    c_cbias = B.dram_in("c_cbias", (128, 1))
    c_ident = B.dram_in("c_ident", (128, 128))
    c_tri = B.dram_in("c_tri", (5, 128, 128))
    c_rot64 = B.dram_in("c_rot64", (64, 64))
    c_rot32 = B.dram_in("c_rot32", (32, 32))
    c_cs64 = B.dram_in("c_cs64", (2, 64, T))
    c_cs32 = B.dram_in("c_cs32", (2, 32, T))
    c_rcnt = B.dram_in("c_rcnt", (4, T))
    y_out = B.dram_out("y", (T, D))
    S = {}
    S["x1"] = B.dram_scr("s_x1", (T, D), F32)
    for nm, f in [("cqn", 512), ("ckvn", 256), ("kpe", 64), ("sg_mla", 1024), ("dq", 1024), ("dk", 1024),
                  ("sg_diff", 1024), ("xbc", 1536), ("u", 1024), ("sg_pool", 1024), ("mg", 8192),
                  ("br_mla", 1024), ("br_diff", 1024), ("br_ssd", 1024), ("br_pool", 1024)]:
        S[nm] = B.dram_scr("s_" + nm, (f, T), BF16)
    S["dv"] = B.dram_scr("s_dv", (T, 1024), BF16)
    S["sz"] = B.dram_scr("s_sz", (T, 1024), BF16)
    S["dt"] = B.dram_scr("s_dt", (T, 32), F32)
    S["xsb"] = B.dram_scr("s_xsb", (T, 1280), BF16)
    S["bcT"] = B.dram_scr("s_bcT", (512, T), BF16)
    S["H"] = B.dram_scr("s_H", (2, T // 128, 128, 1024), BF16)
    S["mT"] = B.dram_scr("s_mT", (D, T), BF16)
    B.S = S

    B.BIGW = 49152 - 1024
    B.big = es.enter_context(nc.sbuf_tensor("big", [128, B.BIGW + 64], F32))
    B.banks = [es.enter_context(nc.psum_tensor(f"ps{i}", [128, 512], F32)) for i in range(8)]
    B.bank_bufs = [Buf(f"ps{i}") for i in range(8)]
    cbase = B.BIGW
    B.eps_col = TL(B.big[:, cbase:cbase + 1], Buf("eps", const=True))
    B.link = TL(B.big[:, cbase + 1:cbase + 2], Buf("link", const=True))
    B.cbias = TL(B.big[:, cbase + 2:cbase + 3], Buf("cbias", const=True))
    B.zero_col = TL(B.big[:, cbase + 3:cbase + 4], Buf("zero", const=True))
    B.one_col = TL(B.big[:, cbase + 4:cbase + 5], Buf("one", const=True))
    B.memset("dve", B.eps_col, EPS)
    B.memset("dve", B.zero_col, 0.0)
    B.memset("dve", B.one_col, 1.0)
    ds0 = DmaSem()
    B.load("sp", B.link, c_link[:, :], ds0)
    B.load("sp", B.cbias, c_cbias[:, :], ds0)
    P.barrier()

    consts = dict(ident=c_ident, tri=c_tri, rot64=c_rot64, rot32=c_rot32, cs64=c_cs64, cs32=c_cs32, rcnt=c_rcnt)
    for l in range(nlayers):
        x_src = x_in if l == 0 else S["x1"]
        last = (l == nlayers - 1)
        if phases is None or "A" in phases:
            phase_A(B, l, x_src, W, consts)
            P.barrier()
        if phases is None or "B" in phases:
            phase_B(B, l, W, consts)
            P.barrier()
        if phases is None or "C" in phases:
            phase_C(B, l, W, consts)
            P.barrier()
        if phases is None or "D" in phases:
            phase_D(B, l, W, consts)
            P.barrier()
        if phases is None or "E" in phases:
            phase_E(B, l, W, consts)
            P.barrier()
        if phases is None or "F" in phases:
            phase_F(B, l, x_src, y_out if last else S["x1"], W, consts, last)
            P.barrier()
    P.barrier()
    st = P.emit(es)
    B.stats = st
    return B


def phase_A(B, l, x_src, W, C):
    nc, P, S, T = B.nc, B.P, B.S, B.T
    B.reset_arena()
    TBA = min(1024, T)
    nblk = T // TBA
    nsub = TBA // 512
    ntile = TBA // 128
    CWMAX = 544
    normw = B.alloc([D], F32, "normw", const=True)
    identb = B.alloc([128], BF16, "identb", const=True)
    ones_b = B.alloc([128], BF16, "ones_b", const=True)
    rot64 = B.alloc([64], F32, "rot64", const=True)
    rot32 = B.alloc([32], F32, "rot32", const=True)
    qnw = B.alloc([4], F32, "qnw", const=True)
    kvnw = B.alloc([2], F32, "kvnw", const=True)
    hT = B.alloc([16, TBA], BF16, "hT")
    xring = B.ring(2, [D], F32, "x")
    hb = B.alloc([D], BF16, "hb")
    sqj = B.alloc([D], BF16, "sqj")
    stat = B.alloc([4], F32, "stat")
    wring = B.ring(3, [16, CWMAX], BF16, "w")
    ost = B.ring(4, [TBA], BF16, "ost")
    ostT = B.ring(3, [CWMAX], BF16, "ostT")
    ostF = B.ring(2, [32], F32, "ostF")
    lat = B.alloc([7, TBA], F32, "lat")
    rf = B.ring(2, [512], F32, "rf")
    cs32 = B.alloc([2, TBA], F32, "cs32")
    cs64 = B.alloc([2, TBA], F32, "cs64")
    tmpf = B.ring(2, [512], F32, "tmpf")
    tmpb = B.ring(2, [512], BF16, "tmpb")
    cd = P.dsem()
    cd32 = P.dsem()
    cd64 = P.dsem()
    B.load("sp", normw, W["norm_w"][l].partition_broadcast(128), cd)
    B.load("pool", identb, C["ident"][:, :], P.dsem())
    B.load("sp", rot64[0:64], C["rot64"][:, :], cd)
    B.load("sp", rot32[0:32], C["rot32"][:, :], cd)
    B.load("sp", qnw, W["mla_q_norm"][l].rearrange("(c p) -> p c", p=128), cd)
    B.load("sp", kvnw, W["mla_kv_norm"][l].rearrange("(c p) -> p c", p=128), cd)
    B.memset("dve", ones_b, 1.0)
    pb = [B.psum_tl(i) for i in range(8)]
    pbb = [B.psum_tl(i, BF16) for i in range(8)]
    mmbank = _Cycle([0, 1, 2, 3])
    auxbank = _Cycle([4, 5])
    rotbank = _Cycle([6, 7])
    w_in = W["w_in"][l].rearrange("(k p) c -> p k c", p=128)

    wtiles = []
    for (nm, c0, ncol, orient, epi) in IN_GROUPS:
        if nm in ("ckv", "kr", "dt"):
            continue
        if nm == "cq":
            wtiles.append((0, 512, [("cq", 0, 512)]))
            wtiles.append((512, 320, [("ckv", 0, 256), ("kr", 256, 64)]))
            continue
        nt_ = ncol // 512
        for j in range(nt_):
            if nm == "xbc" and j == nt_ - 1:
                wtiles.append((c0 + j * 512, 544, [("xbc", 0, 512), ("dt", 512, 32)]))
            else:
                wtiles.append((c0 + j * 512, 512, [(nm, 0, 512)]))
    ginfo = {g[0]: g for g in IN_GROUPS}
    act_toggle = [0]

    for blk in range(nblk):
        t0 = blk * TBA
        B.load("sp", cs32[0:32], C["cs32"][:, :, t0:t0 + TBA].rearrange("a p t -> p a t"), cd32)
        B.load("sp", cs64[0:64], C["cs64"][:, :, t0:t0 + TBA].rearrange("a p t -> p a t"), cd64)
        for i in range(ntile):
            xt, xd = xring.next()
            B.load("sp", xt, x_src[t0 + i * 128:t0 + (i + 1) * 128, :], xd)
            B.act(sqj, xt, AF.Square, accum=stat[:, 0:1])
            B.rsqrt_act(stat[:, 1:2], stat[:, 0:1], 1.0 / D, EPS)
            B.stt("dve", hb, xt, stat[:, 1:2], normw, ALU.mult, ALU.mult)
            for half in range(2):
                bk = auxbank.next()
                for j in range(8):
                    k = half * 8 + j
                    B.tr(pbb[bk][:, j * 128:(j + 1) * 128], hb[:, k * 128:(k + 1) * 128], identb)
                B.copy("dve" if half == 0 else "act",
                       hT[:, half * 8:(half + 1) * 8, i * 128:(i + 1) * 128],
                       pbb[bk].v(lambda a: a.rearrange("p (j c) -> p j c", j=8)))
        for (c0, cw, parts) in wtiles:
            wt, wd = wring.next()
            B.load("pool", wt[:, :, 0:cw], w_in[:, :, c0:c0 + cw], wd)
            for (nm, po, pn) in parts:
                _, gc0, gn, orient, epi = ginfo[nm]
                gcol = c0 + po - gc0
                if orient == "F":
                    for cc in range(0, pn, 128):
                        m = min(128, pn - cc)
                        feat = gcol + cc
                        if epi != "lat":
                            og, od = ost.next()
                        for sb in range(nsub):
                            bk = mmbank.next()
                            for k in range(16):
                                B.mm(pb[bk][0:m, :], wt[:, k, po + cc:po + cc + m], hT[:, k, sb * 512:(sb + 1) * 512],
                                     start=(k == 0), stop=(k == 15))
                            src = pb[bk][0:m, :]
                            if epi == "lat":
                                li = {"cq": 0, "ckv": 4, "kr": 6}[nm] + cc // 128
                                B.copy("dve", lat[0:m, li, sb * 512:(sb + 1) * 512], src)
                            elif epi == "silu":
                                B.act(og[0:m, sb * 512:(sb + 1) * 512], src, AF.Silu)
                            elif epi == "sigmoid":
                                B.act(og[0:m, sb * 512:(sb + 1) * 512], src, AF.Sigmoid)
                            elif epi == "copy":
                                act_toggle[0] ^= 1
                                B.copy("dve", og[0:m, sb * 512:(sb + 1) * 512], src)
                            elif epi == "rope":
                                r, _ = rf.next()
                                B.copy("dve", r, src)
                                rb = rotbank.next()
                                B.mm(pb[rb][0:32, :], rot32[0:32, 0:32], r[0:32, :])
                                tf, _ = tmpf.next()
                                sl = slice(sb * 512, (sb + 1) * 512)
                                B.tt("dve", tf[0:32], r[0:32], cs32[0:32, 0, sl], ALU.mult)
                                tf2, _ = tmpf.next()
                                B.tt("dve", tf2[0:32], pb[rb][0:32, :], cs32[0:32, 1, sl], ALU.mult)
                                B.copy("act", og[:, sl], r)
                                B.tt("dve", og[0:32, sl], tf[0:32], tf2[0:32], ALU.add)
                        if epi != "lat":
                            sname = {"g_mla": "sg_mla", "g_diff": "sg_diff", "g_pool": "sg_pool"}.get(nm, nm)
                            B.store("sp", S[sname][feat:feat + m, t0:t0 + TBA], og[0:m, :], od)
                else:
                    for i in range(ntile):
                        bk = mmbank.next()
                        for k in range(16):
                            B.mm(pb[bk][:, 0:pn], hT[:, k, i * 128:(i + 1) * 128], wt[:, k, po:po + pn],
                                 start=(k == 0), stop=(k == 15))
                        src = pb[bk][:, 0:pn]
                        rows = slice(t0 + i * 128, t0 + (i + 1) * 128)
                        if epi == "copyf":
                            og, od = ostF.next()
                            B.copy("dve", og[:, 0:pn], src)
                            B.store("sp", S["dt"][rows, :], og[:, 0:pn], od)
                        else:
                            og, od = ostT.next()
                            if epi == "silu":
                                B.act(og[:, 0:pn], src, AF.Silu)
                            else:
                                B.copy("dve", og[:, 0:pn], src)
                            sname = {"z": "sz"}.get(nm, nm)
                            B.store("sp", S[sname][rows, gcol:gcol + pn], og[:, 0:pn], od)
            if parts[0][0] == "ckv":
                for sb in range(nsub):
                    sl = slice(sb * 512, (sb + 1) * 512)
                    for (nm, li0, nch, wcol, dim) in (("cqn", 0, 4, qnw, 512), ("ckvn", 4, 2, kvnw, 256)):
                        bk = auxbank.next()
                        for c in range(nch):
                            tb_, _ = tmpb.next()
                            B.act(tb_, lat[:, li0 + c, sl], AF.Square)
                            B.mm(pb[bk], ones_b, tb_, start=(c == 0), stop=(c == nch - 1))
                        tf, _ = tmpf.next()
                        B.rsqrt_act(tf, pb[bk], 1.0 / dim, EPS)
                        for c in range(nch):
                            og, od = ost.next()
                            B.stt("dve", og[:, 0:512], lat[:, li0 + c, sl], wcol[:, c:c + 1], tf, ALU.mult, ALU.mult)
                            B.store("sp", S[nm][c * 128:(c + 1) * 128, t0 + sb * 512:t0 + (sb + 1) * 512], og[:, 0:512], od)
                    rb = rotbank.next()
                    B.mm(pb[rb][0:64, :], rot64[0:64, 0:64], lat[0:64, 6, sl])
                    tf, _ = tmpf.next()
                    B.tt("dve", tf[0:64], lat[0:64, 6, sl], cs64[0:64, 0, sl], ALU.mult)
                    tf2, _ = tmpf.next()
                    B.tt("dve", tf2[0:64], pb[rb][0:64, :], cs64[0:64, 1, sl], ALU.mult)
                    og, od = ost.next()
                    B.tt("dve", og[0:64, 0:512], tf[0:64], tf2[0:64], ALU.add)
                    B.store("sp", S["kpe"][0:64, t0 + sb * 512:t0 + (sb + 1) * 512], og[0:64, 0:512], od)


def attn_qblock(B, qsl, s_terms, v_list, acc_banks, den_bank, Sring, Pring, pb, ones_b, scale, NT, SEG, q0, P4ring,
                inject=None, every=6, LA=2):
    DB = 4
    pend = []
    denq = []
    first = None
    p4 = None
    ng = NT // DB
    inject = list(inject) if inject else []
    for step in range(NT + LA):
        if inject and step % every == 2:
            inject.pop(0)()
        if step < NT:
            kb = step
            sb = Sring.next()
            ksl = slice(kb * 128, (kb + 1) * 128)
            for i, (kT, qT) in enumerate(s_terms):
                B.mm(pb[sb], kT[:, ksl], qT[:, qsl], start=(i == 0), stop=(i == len(s_terms) - 1))
            cross = (q0 // SEG) != ((kb * 128) // SEG)
            pt, _ = Pring.next()
            B.act(pt, pb[sb], AF.Exp, scale=scale, bias=(B.cbias if cross else None))
            pend.append((kb, pt))
            j = kb % DB
            if j == 0:
                first = pt
            elif j == 1:
                p4, _ = P4ring.next()
                B.tt("dve", p4, first, pt, ALU.add)
            else:
                B.tt("dve", p4, p4, pt, ALU.add)
            if j == DB - 1:
                denq.append((kb // DB, p4))
        if step >= LA:
            kb, pt = pend.pop(0)
            for v, ab in zip(v_list, acc_banks):
                B.mm(pb[ab], v[:, kb, :], pt, start=(kb == 0), stop=(kb == NT - 1))
            if kb % DB == DB - 1:
                g, pp = denq.pop(0)
                B.mm(pb[den_bank], ones_b, pp, start=(g == 0), stop=(g == ng - 1))
    for f in inject:
        f()


def phase_B(B, l, W, C):
    nc, P, S, T, NT, NB, SEG = B.nc, B.P, B.S, B.T, B.NT, B.NB, B.SEG
    B.reset_arena()
    cqn = B.alloc([4, T], BF16, "cqn")
    ckvn = B.alloc([2, T], BF16, "ckvn")
    kpe = B.alloc([T], BF16, "kpe")
    wuq = B.alloc([4, 1536], BF16, "wuq", const=True)
    wukv = B.alloc([2, 2048], BF16, "wukv", const=True)
    ones_b = B.alloc([128], BF16, "ones_b", const=True)
    rot64 = B.alloc([64], F32, "rot64", const=True)
    qn = [B.alloc([T], BF16, f"qn{i}") for i in range(2)]
    qp = [B.alloc([T], BF16, f"qp{i}") for i in range(2)]
    kn = [B.alloc([T], BF16, f"kn{i}") for i in range(2)]
    vv = [B.alloc([NT, 128], BF16, f"v{i}") for i in range(2)]
    csr = B.ring(2, [2, 512], F32, "cs")
    rr = B.ring(2, [512], F32, "rr")
    tmpf = B.ring(4, [512], F32, "tmpf")
    Pring = B.ring(6, [512], BF16, "pt")
    P4ring = B.ring(3, [512], BF16, "p4")
    sgr = B.ring(2, [512], BF16, "sg")
    ost = B.ring(2, [512], BF16, "ost")
    cd = P.dsem()
    B.load("sp", cqn, S["cqn"].rearrange("(c p) t -> p c t", p=128), cd)
    B.load("sp", ckvn, S["ckvn"].rearrange("(c p) t -> p c t", p=128), cd)
    B.load("sp", kpe[0:64], S["kpe"][:, :], cd)
    B.memset("pool", kpe[64:128], 0.0)
    B.memset("pool", qp[0][64:128], 0.0)
    B.memset("pool", qp[1][64:128], 0.0)
    cdp = P.dsem()
    B.load("pool", wuq, W["mla_w_uq"][l].rearrange("(c p) n -> p c n", p=128), cdp)
    B.load("pool", wukv, W["mla_w_ukv"][l].rearrange("(c p) n -> p c n", p=128), cdp)
    B.load("sp", rot64[0:64], C["rot64"][:, :], cd)
    B.memset("dve", ones_b, 1.0)
    pb = [B.psum_tl(i) for i in range(8)]
    Sring = _Cycle([0, 1, 2, 3])
    accs = _Cycle([4, 5])
    db = 6
    scale = float(192 ** -0.5)

    def prologue(h):
        s = h % 2
        for sb in range(NB):
            sl = slice(sb * 512, (sb + 1) * 512)
            for c in range(4):
                B.mm(pb[7], wuq[:, c, h * 192:h * 192 + 128], cqn[:, c, sl], start=(c == 0), stop=(c == 3))
            B.copy("dve", qn[s][:, sl], pb[7])
            yield
            for c in range(4):
                B.mm(pb[7][0:64], wuq[:, c, h * 192 + 128:h * 192 + 192], cqn[:, c, sl], start=(c == 0), stop=(c == 3))
            r, _ = rr.next()
            B.copy("dve", r[0:64], pb[7][0:64])
            cs, cdm = csr.next()
            B.load("sp", cs[0:64], C["cs64"][:, :, sl].rearrange("a p t -> p a t"), cdm)
            yield
            B.mm(pb[7][0:64], rot64[0:64, 0:64], r[0:64])
            tf, _ = tmpf.next()
            B.tt("dve", tf[0:64], r[0:64], cs[0:64, 0], ALU.mult)
            tf2, _ = tmpf.next()
            B.tt("dve", tf2[0:64], pb[7][0:64], cs[0:64, 1], ALU.mult)
            B.tt("dve", qp[s][0:64, sl], tf[0:64], tf2[0:64], ALU.add)
            yield
            for c in range(2):
                B.mm(pb[7], wukv[:, c, h * 256:h * 256 + 128], ckvn[:, c, sl], start=(c == 0), stop=(c == 1))
            B.copy("act", kn[s][:, sl], pb[7])
            yield
            for i in range(4):
                tsl = slice(sb * 512 + i * 128, sb * 512 + (i + 1) * 128)
                for c in range(2):
                    B.mm(pb[7][:, i * 128:(i + 1) * 128], ckvn[:, c, tsl], wukv[:, c, h * 256 + 128:h * 256 + 256],
                         start=(c == 0), stop=(c == 1))
            B.copy("act", vv[s][:, sb * 4:(sb + 1) * 4, :], pb[7].v(lambda a: a.rearrange("p (i d) -> p i d", i=4)))
            yield

    for _ in prologue(0):
        pass
    npiece = 5 * NB
    per_qb = (npiece + NB - 1) // NB
    every = max(1, (NT + 2) // (per_qb + 1))
    for h in range(8):
        s = h % 2
        gen = prologue(h + 1) if h + 1 < 8 else iter(())
        for qb in range(NB):
            qsl = slice(qb * 512, (qb + 1) * 512)
            ab = accs.next()
            sg, sgd = sgr.next()
            B.load("sp", sg, S["sg_mla"][h * 128:(h + 1) * 128, qsl], sgd)
            attn_qblock(B, qsl, [(kn[s], qn[s]), (kpe, qp[s])], [vv[s]], [ab], db,
                        Sring, Pring, pb, ones_b, scale, NT, SEG, qb * 512, P4ring,
                        inject=[(lambda gen=gen: next(gen, None))] * per_qb, every=every, LA=3)
            if qb == NB - 1:
                for _ in gen:
                    pass
            rd, _ = tmpf.next()
            B.recip(rd, pb[db])
            o, _ = tmpf.next()
            B.tt("dve", o, pb[ab], rd, ALU.mult)
            og, od = ost.next()
            B.tt("dve", og, o, sg, ALU.mult)
            B.store("sp", S["br_mla"][h * 128:(h + 1) * 128, qsl], og, od)
    B.dbg("qn", qn[1]); B.dbg("qp", qp[1], 64); B.dbg("kn", kn[1]); B.dbg("vv", vv[1]); B.dbg("kpe", kpe, 64)


def phase_C(B, l, W, C):
    nc, P, S, T, NT, NB, SEG = B.nc, B.P, B.S, B.T, B.NT, B.NB, B.SEG
    B.reset_arena()
    lam_init = 0.8 - 0.6 * math.exp(-0.3 * l)
    ones_b = B.alloc([128], BF16, "ones_b", const=True)
    ones_f = B.alloc([128], F32, "ones_f", const=True)
    subw = B.alloc([2], F32, "subw", const=True)
    lp = B.alloc([512], F32, "lp")
    lt = B.alloc([256], F32, "lt")
    ls = B.alloc([8], F32, "ls")
    nlam = B.alloc([1], F32, "nlam")
    qk = [[B.alloc([T], BF16, f"qk{i}{j}") for j in range(4)] for i in range(2)]
    vv = [B.alloc([NT, 256], BF16, f"v{i}") for i in range(2)]
    hd = [P.dsem() for _ in range(2)]
    Pring = B.ring(5, [512], BF16, "pt")
    P4ring = B.ring(3, [512], BF16, "p4")
    on = [B.alloc([2, 512], F32, f"on{i}") for i in range(2)]
    oo = B.alloc([2, 512], F32, "oo")
    tmpf = B.ring(5, [512], F32, "tmpf")
    sqr = B.ring(2, [512], BF16, "sq")
    sgr = B.ring(3, [2, 512], BF16, "sg")
    ost = B.ring(2, [2, 512], BF16, "ost")
    cd = P.dsem()
    B.memset("dve", ones_b, 1.0)
    B.memset("dve", ones_f, 1.0)
    B.load("sp", subw, W["diff_subln"][l].rearrange("(c p) -> p c", p=128), cd)
    B.load("sp", lp[0:1], W["diff_lambda"][l].rearrange("(o a) d -> o (a d)", o=1), cd)
    pb = [B.psum_tl(i) for i in range(8)]
    B.tt("dve", lt[0:1, 0:128], lp[0:1, 0:128], lp[0:1, 128:256], ALU.mult)
    B.tt("dve", lt[0:1, 128:256], lp[0:1, 256:384], lp[0:1, 384:512], ALU.mult)
    e = nc.vector
    for j in range(2):
        o_, a_ = ls[0:1, j:j + 1], lt[0:1, j * 128:(j + 1) * 128]
        P.op("dve", (lambda o_=o_, a_=a_: e.reduce_sum(out=o_.ap, in_=a_.ap, axis=AX.X)), reads=[a_], writes=[o_])
    B.act(ls[0:1, 2:4], ls[0:1, 0:2], AF.Exp)
    B.tt("dve", ls[0:1, 4:5], ls[0:1, 3:4], ls[0:1, 2:3], ALU.subtract)
    B.ts("dve", ls[0:1, 5:6], ls[0:1, 4:5], -lam_init, None, op0=ALU.add)
    B.mm(pb[7][:, 0:1], ones_f[0:1, :], ls[0:1, 5:6])
    B.copy("dve", nlam, pb[7][:, 0:1])
    Sring = _Cycle([5, 6, 7])
    accsets = _Cycle([(0, 1), (2, 3)])
    db = 4
    scale = float(128 ** -0.5)

    def loadhead(h):
        s = h % 2
        for j, (nm, r0) in enumerate((("dq", 2 * h), ("dq", 2 * h + 1), ("dk", 2 * h), ("dk", 2 * h + 1))):
            B.load("sp", qk[s][j], S[nm][r0 * 128:(r0 + 1) * 128, :], hd[s])
        B.load("sp", vv[s], S["dv"][:, h * 256:(h + 1) * 256].rearrange("(n p) c -> p n c", p=128), hd[s])

    loadhead(0)
    pending = []
    for h in range(4):
        s = h % 2
        for qb in range(NB):
            qsl = slice(qb * 512, (qb + 1) * 512)
            sg, sgd = sgr.next()
            B.load("sp", sg, S["sg_diff"][h * 256:(h + 1) * 256, qsl].rearrange("(c p) t -> p c t", p=128), sgd)
            for sm in range(2):
                a0, a1 = accsets.next()
                inj = [pending.pop(0)] if (pending and sm == 0) else None
                attn_qblock(B, qsl, [(qk[s][2 + sm], qk[s][sm])], [vv[s][:, :, 0:128], vv[s][:, :, 128:256]],
                            [a0, a1], db, Sring, Pring, pb, ones_b, scale, NT, SEG, qb * 512, P4ring, inject=inj, every=4, LA=2)
                if qb == 0 and sm == 0 and h + 1 < 4:
                    loadhead(h + 1)
                rd, _ = tmpf.next()
                B.recip(rd, pb[db])
                B.tt("dve", on[sm][:, 0], pb[a0], rd, ALU.mult)
                B.tt("dve", on[sm][:, 1], pb[a1], rd, ALU.mult)

            def post(h=h, qsl=qsl, sg=sg):
                B.stt("dve", oo, on[1], nlam[:, 0:1], on[0], ALU.mult, ALU.add)
                sbk = Sring.next()
                for c in range(2):
                    sq, _ = sqr.next()
                    B.act(sq, oo[:, c], AF.Square)
                    B.mm(pb[sbk], ones_b, sq, start=(c == 0), stop=(c == 1))
                rs, _ = tmpf.next()
                B.rsqrt_act(rs, pb[sbk], 1.0 / 256, EPS)
                og, od = ost.next()
                for c in range(2):
                    tf, _ = tmpf.next()
                    B.stt("dve", tf, oo[:, c], subw[:, c:c + 1], rs, ALU.mult, ALU.mult)
                    B.stt("dve", og[:, c], tf, 1.0 - lam_init, sg[:, c], ALU.mult, ALU.mult)
                B.store("sp", S["br_diff"][h * 256:(h + 1) * 256, qsl].rearrange("(c p) t -> p c t", p=128), og, od)

            pending.append(post)
    for f in pending:
        f()


def phase_E(B, l, W, C):
    nc, P, S, T, NT, NB, SEG = B.nc, B.P, B.S, B.T, B.NT, B.NB, B.SEG
    B.reset_arena()
    PADW = SEG + 16
    ubr = B.ring(2, [2, SEG], BF16, "ub")
    bufs = [B.alloc([2, PADW], F32, f"pbuf{i}") for i in range(2)]
    rcb = B.alloc([2, SEG], F32, "rcb")
    pooled = [B.alloc([T], BF16, f"pooled{i}") for i in range(2)]
    mean = B.alloc([2, SEG], F32, "mean")
    pw = B.alloc([2, 256], BF16, "pw")
    psc = B.alloc([8], F32, "psc", const=True)
    sgr = B.ring(2, [512], BF16, "sg")
    ost = B.ring(2, [512], BF16, "ost")
    tmpf = B.ring(2, [512], F32, "tmpf")
    cd = P.dsem()
    rcd = P.dsem()
    pwd = P.dsem()
    apd = P.dsem()
    B.load("sp", psc, W["pool_scale"][l].rearrange("(c p) -> p c", p=128), cd)
    pb = [B.psum_tl(i) for i in range(8)]
    banks = _Cycle(list(range(8)))
    shifts = [(-1, 0, 1, PADW), (-1, 1, 2, PADW - 1), (-2, 2, 4, PADW - 3), (-4, 4, 8, PADW - 7)]
    for g in range(4):
        B.load("sp", rcb, C["rcnt"][g].rearrange("(s t) -> s t", s=2).partition_broadcast(128), rcd)
        B.load("pool", pw, W["pool_w"][l, g].rearrange("(c p) d -> p c d", p=128), pwd)
        for j in range(2):
            c = 2 * g + j
            ub, ud = ubr.next()
            usrc = S["u"][c * 128:(c + 1) * 128, :].rearrange("p (s t) -> p s t", s=2)
            B.load("sp", ub, usrc, ud)
            a = bufs[0]
            B.load("pool", a[:, :, 8:8 + SEG], usrc, apd)
            B.memset("pool", a[:, 0, 0:8], 0.0)
            B.memset("pool", a[:, 1, 8 + SEG:PADW], 0.0)
            B.ts("pool", a[:, 0, 8 + SEG:PADW], a[:, 1, 8:16], B.link[:, 0:1], None, op0=ALU.mult)
            B.ts("pool", a[:, 1, 0:8], a[:, 0, SEG:SEG + 8], B.link[:, 0:1], None, op0=ALU.mult)
            cur = 0
            for st in range(g + 1):
                s0, s1, lo, hi = shifts[st]
                src, dst = bufs[cur], bufs[1 - cur]
                B.tt("dve", dst[:, :, lo:hi], src[:, :, lo + s0:hi + s0], src[:, :, lo + s1:hi + s1], ALU.add)
                cur = 1 - cur
            B.tt("dve", mean, bufs[cur][:, :, 8:8 + SEG], rcb, ALU.mult)
            B.tt("dve", pooled[j].v(lambda a_: a_.rearrange("p (s t) -> p s t", s=2)), mean, ub, ALU.subtract)
        for dc in range(2):
            co = 2 * g + dc
            for sb in range(NB):
                sl = slice(sb * 512, (sb + 1) * 512)
                bk = banks.next()
                for j in range(2):
                    B.mm(pb[bk], pw[:, j, dc * 128:(dc + 1) * 128], pooled[j][:, sl], start=(j == 0), stop=(j == 1))
                sg, sgd = sgr.next()
                B.load("sp", sg, S["sg_pool"][co * 128:(co + 1) * 128, sl], sgd)
                og, od = ost.next()
                B.stt("dve", og, pb[bk], psc[:, co:co + 1], sg, ALU.mult, ALU.mult)
                B.store("sp", S["br_pool"][co * 128:(co + 1) * 128, sl], og, od)


def phase_F(B, l, x_src, x_dst, W, C, last):
    nc, P, S, T, NT, NB, SEG = B.nc, B.P, B.S, B.T, B.NT, B.NB, B.SEG
    pb = [B.psum_tl(i) for i in range(8)]
    B.reset_arena()
    TB1 = min(2048, T)
    nsb = TB1 // 512
    brT = B.alloc([32, TB1], BF16, "brT")
    wbr = B.ring(2, [32, 128], BF16, "wbr")
    mgr = B.ring(4, [4, 512], BF16, "mg")
    tmpf = B.ring(8, [512], F32, "tmpf")
    ost = B.ring(3, [512], BF16, "ost")
    brd = P.dsem()
    banks = _Cycle(list(range(8)))
    wb_src = W["w_branch"][l].rearrange("i (k p) n -> p i k n", p=128)
    mg_src = S["mg"].rearrange("(i d) t -> d i t", i=4)
    brs = [S["br_mla"], S["br_diff"], S["br_ssd"], S["br_pool"]]
    for blk in range(T // TB1):
        t0 = blk * TB1
        for i in range(4):
            B.load("sp", brT[:, i * 8:(i + 1) * 8, :], brs[i][:, t0:t0 + TB1].rearrange("(k p) t -> p k t", p=128), brd)
        wtiles = {}
        mgt = {}

        def f1_load(it, t0=t0, wtiles=wtiles, mgt=mgt):
            dmc, sb = divmod(it, nsb)
            if sb == 0:
                wb, wd = wbr.next()
                for i in range(4):
                    B.load("pool", wb[:, i * 8:(i + 1) * 8, :], wb_src[:, i, :, dmc * 128:(dmc + 1) * 128], wd)
                wtiles[dmc] = wb
            sl = slice(t0 + sb * 512, t0 + (sb + 1) * 512)
            mg, mgd = mgr.next()
            B.load("sp", mg, mg_src[dmc * 128:(dmc + 1) * 128, :, sl], mgd)
            mgt[it] = mg

        def f1_compute(it, t0=t0, wtiles=wtiles, mgt=mgt):
            dmc, sb = divmod(it, nsb)
            wb = wtiles[dmc]
            mg = mgt.pop(it)
            sl = slice(t0 + sb * 512, t0 + (sb + 1) * 512)
            lsl = slice(sb * 512, (sb + 1) * 512)
            ts_ = []
            for i in range(4):
                bk = banks.next()
                for k in range(8):
                    B.mm(pb[bk], wb[:, i * 8 + k, :], brT[:, i * 8 + k, lsl], start=(k == 0), stop=(k == 7))
                tf, _ = tmpf.next()
                B.tt("dve", tf, pb[bk], mg[:, i], ALU.mult)
                ts_.append(tf)
            B.tt("dve", ts_[0], ts_[0], ts_[1], ALU.add)
            B.tt("dve", ts_[2], ts_[2], ts_[3], ALU.add)
            og, od = ost.next()
            B.tt("dve", og, ts_[0], ts_[2], ALU.add)
            B.store("sp", S["mT"][dmc * 128:(dmc + 1) * 128, sl], og, od)

        prefetch_loop(16 * nsb, 2, f1_load, f1_compute)
    P.barrier()
    B.reset_arena()
    wout = B.alloc([16, D], BF16, "wout", const=True)
    mTr = B.ring(2, [16, 512], BF16, "mT")
    xring = B.ring(4, [D], F32, "x")
    stat = B.ring(2, [4], F32, "stat")
    sqj = B.alloc([D], BF16, "sqj")
    if last:
        fnw = B.alloc([D], F32, "fnw", const=True)
    cd = P.dsem()
    for k4 in range(4):
        B.load("pool", wout[:, k4 * 4:(k4 + 1) * 4, :],
               W["w_out"][l].rearrange("(k p) n -> p k n", p=128)[:, k4 * 4:(k4 + 1) * 4, :], cd)
    if last:
        B.load("sp", fnw, W["final_norm"].partition_broadcast(128), P.dsem())
    banks = _Cycle(list(range(8)))
    mts = {}
    xts = {}

    def f2_load(it):
        tb, i = divmod(it, 4)
        if i == 0:
            sl = slice(tb * 512, (tb + 1) * 512)
            mT, mTd = mTr.next()
            B.load("sp", mT, S["mT"][:, sl].rearrange("(k p) t -> p k t", p=128), mTd)
            mts[tb] = mT
        rows = slice(tb * 512 + i * 128, tb * 512 + (i + 1) * 128)
        xt, xd = xring.next()
        B.load("sp", xt, x_src[rows, :], xd)
        xts[it] = (xt, xd)

    def f2_compute(it):
        tb, i = divmod(it, 4)
        mT = mts[tb]
        xt, xd = xts.pop(it)
        rows = slice(tb * 512 + i * 128, tb * 512 + (i + 1) * 128)
        for nb in range(4):
            bk = banks.next()
            for k in range(16):
                B.mm(pb[bk], mT[:, k, i * 128:(i + 1) * 128], wout[:, k, nb * 512:(nb + 1) * 512],
                     start=(k == 0), stop=(k == 15))
            B.tt("dve", xt[:, nb * 512:(nb + 1) * 512], xt[:, nb * 512:(nb + 1) * 512], pb[bk], ALU.add)
        if last:
            st, _ = stat.next()
            B.act(sqj, xt, AF.Square, accum=st[:, 0:1])
            B.rsqrt_act(st[:, 1:2], st[:, 0:1], 1.0 / D, EPS)
            B.stt("dve", xt, xt, st[:, 1:2], fnw, ALU.mult, ALU.mult)
        B.store("sp", x_dst[rows, :], xt, xd)

    prefetch_loop(NB * 4, 2, f2_load, f2_compute)


def phase_D(B, l, W, C):
    nc, P, S, T, NT, NB, SEG = B.nc, B.P, B.S, B.T, B.NT, B.NB, B.SEG
    pb = [B.psum_tl(i) for i in range(8)]
    pbb = [B.psum_tl(i, BF16) for i in range(8)]
    bc3 = lambda n: (lambda a: a.unsqueeze(2).broadcast_to([a.shape[0], a.shape[1], n]))
    B.reset_arena()
    cw = B.alloc([4, 12], F32, "cw", const=True)
    cbv = B.alloc([12], F32, "cbv", const=True)
    identb = B.alloc([128], BF16, "identb", const=True)
    xpad = [B.alloc([2, SEG + 3], F32, f"xpad{i}") for i in range(2)]
    acc = [B.alloc([2, SEG], F32, f"acc{i}") for i in range(2)]
    xc = B.ring(3, [T], BF16, "xc")
    stg = B.ring(4, [8, 128], BF16, "stg")
    cd = P.dsem()
    for j in range(4):
        B.load("sp", cw[:, j, :], W["ssd_conv_w"][l, j].rearrange("(c p) -> p c", p=128), cd)
    B.load("sp", cbv, W["ssd_conv_b"][l].rearrange("(c p) -> p c", p=128), cd)
    B.load("pool", identb, C["ident"][:, :], P.dsem())
    trb = _Cycle([0, 1, 2, 3])
    xpd = [P.dsem() for _ in range(2)]
    for c in range(12):
        xp, ac = xpad[c % 2], acc[c % 2]
        B.load("pool", xp[:, :, 2:2 + SEG], S["xbc"][c * 128:(c + 1) * 128, :].rearrange("p (s t) -> p s t", s=2), xpd[c % 2])
        B.memset("pool", xp[:, 0, 0:2], 0.0)
        B.memset("pool", xp[:, 1, SEG + 2:SEG + 3], 0.0)
        B.ts("pool", xp[:, 0, SEG + 2:SEG + 3], xp[:, 1, 2:3], B.link[:, 0:1], None, op0=ALU.mult)
        B.ts("pool", xp[:, 1, 0:2], xp[:, 0, SEG:SEG + 2], B.link[:, 0:1], None, op0=ALU.mult)
        B.ts("dve", ac, xp[:, :, 0:SEG], cw[:, 0, c:c + 1], cbv[:, c:c + 1], op0=ALU.mult, op1=ALU.add)
        for j in range(1, 4):
            B.stt("dve", ac, xp[:, :, j:j + SEG], cw[:, j, c:c + 1], ac, ALU.mult, ALU.add)
        xo, xod = xc.next()
        B.act(xo.v(lambda a: a.rearrange("p (s t) -> p s t", s=2)), ac, AF.Silu)
        if c >= 8:
            B.store("sp", S["bcT"][(c - 8) * 128:(c - 7) * 128, :], xo, xod)
        if c < 10:
            for i0 in range(0, NT, 8):
                n = min(8, NT - i0)
                bk = trb.next()
                for i in range(n):
                    B.tr(pbb[bk][:, i * 128:(i + 1) * 128], xo[:, (i0 + i) * 128:(i0 + i + 1) * 128], identb)
                sg_, sgd = stg.next()
                B.copy("act", sg_[:, 0:n, :],
                       pbb[bk][:, 0:n * 128].v(lambda a: a.rearrange("p (i c) -> p i c", c=128)))
                B.store("sp", S["xsb"][i0 * 128:(i0 + n) * 128, c * 128:(c + 1) * 128].rearrange("(n p) c -> p n c", p=128),
                        sg_[:, 0:n, :], sgd)
    P.barrier()
    B.reset_arena()
    tri = B.alloc([5, 128], F32, "tri", const=True)
    trib = B.alloc([2, 128], BF16, "trib", const=True)
    identb = B.alloc([128], BF16, "identb", const=True)
    dt = B.alloc([NT, 32], F32, "dt")
    da = B.alloc([NT, 32], F32, "da")
    dtb = B.alloc([32], F32, "dtb", const=True)
    av = B.alloc([32], F32, "av")
    dvec = B.alloc([16], F32, "dvec", const=True)
    nrmw = B.alloc([1024], F32, "nrmw", const=True)
    stats = B.alloc([NT, 5, 32], F32, "stats")
    est = B.alloc([NT, 5, 32], F32, "est")
    coef = B.alloc([NT, 32], F32, "coef")
    cd = P.dsem()
    B.load("sp", tri, C["tri"].rearrange("j p c -> p j c"), cd)
    cdp = P.dsem()
    B.load("pool", trib, C["tri"][0:2].rearrange("j p c -> p j c"), cdp)
    B.load("pool", identb, C["ident"][:, :], cdp)
    B.load("sp", dt, S["dt"].rearrange("(n p) c -> p n c", p=128), cd)
    B.load("sp", dtb, W["ssd_dt_bias"][l].rearrange("a b -> (a b)").partition_broadcast(128), cd)
    B.load("sp", av, W["ssd_a_log"][l].rearrange("a b -> (a b)").partition_broadcast(128), cd)
    B.load("sp", dvec, W["ssd_d"][l].partition_broadcast(128), cd)
    B.load("sp", nrmw, W["ssd_norm"][l].partition_broadcast(128), cd)
    bc_nt = lambda a: a.unsqueeze(1).broadcast_to([128, NT, 32])
    B.tt("dve", dt, dt, dtb.v(bc_nt), ALU.add)
    B.act(dt, dt, AF.Exp)
    B.act(dt, dt, AF.Ln, bias=B.one_col[:, 0:1])
    B.act(av, av, AF.Exp)
    B.ts("dve", av, av, -1.0, None, op0=ALU.mult)
    B.tt("dve", da, dt, av.v(bc_nt), ALU.mult)
    sbk = _Cycle([0, 1, 2, 3])
    for c in range(NT):
        bk = sbk.next()
        for j in range(5):
            B.mm(pb[bk][:, j * 32:(j + 1) * 32], tri[:, j, :], da[:, c, :])
        B.copy("dve" if c % 2 == 0 else "act", stats[:, c].v(lambda a: a.rearrange("p j h -> p (j h)")), pb[bk][:, 0:160])
    B.act(est, stats, AF.Exp)
    B.tt("dve", coef[:, :, 0:16], dt[:, :, 0:16], est[:, :, 2, 0:16], ALU.mult)
    B.tt("dve", coef[:, :, 16:32], dt[:, :, 16:32], est[:, :, 3, 16:32], ALU.mult)
    B.dbg("est", est); B.dbg("dt", dt); B.dbg("da", da); B.dbg("coef", coef)
    mark = B.off
    xsr = B.ring(4, [1280], BF16, "xs")
    xdw = B.ring(2, [1024], BF16, "xdw")
    Hst = [B.alloc([1024], F32, f"H{i}") for i in range(2)]
    Hsv = B.ring(3, [1024], BF16, "Hsv")
    B.memset("dve", Hst[0], 0.0)
    B.memset("dve", Hst[1], 0.0)
    stb = _Cycle([4, 5, 6, 7])
    d2x = {}

    def d2_load(it):
        i, d = divmod(it, 2)
        c = i if d == 0 else NT - 1 - i
        xs, xsd = xsr.next()
        B.load("sp", xs, S["xsb"][c * 128:(c + 1) * 128, :], xsd)
        d2x[it] = xs

    def d2_compute(it):
            i, d = divmod(it, 2)
            c = i if d == 0 else NT - 1 - i
            xs = d2x.pop(it)
            xw, _ = xdw.next()
            B.tt("dve", xw.v(lambda a: a.rearrange("p (h q) -> p h q", q=64)),
                 xs[:, 0:1024].v(lambda a: a.rearrange("p (h q) -> p h q", q=64)),
                 coef[:, c, d * 16:(d + 1) * 16].v(bc3(64)), ALU.mult)
            H = Hst[d]
            if (d == 0 and c == NT // 2) or (d == 1 and c == NT // 2 - 1):
                B.ts("dve", H, H, B.link[:, 0:1], None, op0=ALU.mult)
            hs, hsd = Hsv.next()
            B.copy("act", hs, H)
            B.store("sp", S["H"][d, c], hs, hsd)
            B.tt("dve", H.v(lambda a: a.rearrange("p (h q) -> p h q", q=64)),
                 H.v(lambda a: a.rearrange("p (h q) -> p h q", q=64)),
                 est[:, c, 4, d * 16:(d + 1) * 16].v(bc3(64)), ALU.mult)
            for g in range(2):
                bk = stb.next()
                B.mm(pb[bk], xs[:, 1024 + g * 128:1024 + (g + 1) * 128], xw[:, g * 512:(g + 1) * 512])
                B.tt("dve", H[:, g * 512:(g + 1) * 512], H[:, g * 512:(g + 1) * 512], pb[bk], ALU.add)

    prefetch_loop(NT * 2, 2, d2_load, d2_compute)
    P.barrier()
    B.off = mark
    xsr = B.ring(3, [1024], BF16, "xs")
    bct = B.ring(3, [4, 128], BF16, "bct")
    Hr = B.ring(3, [2, 1024], BF16, "Hr")
    szr = B.ring(4, [1024], BF16, "sz")
    Dm = B.ring(3, [16, 128], F32, "Dm")
    cbm = B.ring(2, [2, 2, 128], BF16, "cbm")
    Lx = B.ring(4, [4, 128], BF16, "Lx")
    Mt = B.ring(5, [4, 128], BF16, "Mt")
    xdr = B.ring(3, [1024], BF16, "xd")
    yo = B.ring(3, [1024], F32, "yo")
    yo2 = B.alloc([1024], F32, "yo2")
    t3r = B.ring(3, [1024], F32, "t3")
    sqj = B.alloc([512], BF16, "sqj")
    ss = B.ring(2, [4], F32, "ss")
    yn = B.ring(2, [1024], BF16, "yn")
    stg = B.ring(2, [8, 128], BF16, "stg")
    cbk = _Cycle([0, 1])
    sgk = _Cycle([2, 3, 7])
    ydk = [4, 5]
    yfk = _Cycle([6])
    hq = lambda a: a.rearrange("p (h q) -> p h q", q=64)
    d3t = {}

    def d3_load(c):
        rows = slice(c * 128, (c + 1) * 128)
        xs, xsd = xsr.next()
        B.load("sp", xs, S["xsb"][rows, 0:1024], xsd)
        bt, btd = bct.next()
        B.load("sp", bt, S["bcT"][:, rows].rearrange("(j p) t -> p j t", p=128), btd)
        Hc, Hd = Hr.next()
        B.load("sp", Hc[:, 0], S["H"][0, c], Hd)
        B.load("sp", Hc[:, 1], S["H"][1, c], Hd)
        sz, szd = szr.next()
        B.load("sp", sz, S["sz"][rows, :], szd)
        d3t[c] = (xs, bt, Hc, sz)

    d3m = {}

    def d3_front(c):
        xs, bt, Hc, sz = d3t.pop(c)
        bk = 0
        for g in range(2):
            B.mm(pb[bk][:, g * 128:(g + 1) * 128], bt[:, g, :], bt[:, 2 + g, :])
        cm, _ = cbm.next()
        for d in range(2):
            B.tt("dve", cm[:, d], pb[bk][:, 0:256].v(lambda a: a.rearrange("p (g l) -> p g l", g=2)),
                 trib[:, d, :].v(lambda a: a.unsqueeze(1).broadcast_to([128, 2, 128])), ALU.mult)
        yoc, _ = yo.next()
        t3, _ = t3r.next()
        B.tt("pool", t3.v(hq), xs.v(hq), dvec.v(bc3(64)), ALU.mult)
        dms, xds = [], []
        for d in range(2):
            dm, _ = Dm.next()
            B.tt("pool", dm, tri[:, d, :].v(lambda a: a.unsqueeze(1).broadcast_to([128, 16, 128])),
                 da[:, c, d * 16:(d + 1) * 16].v(bc3(128)), ALU.mult)
            xd, _ = xdr.next()
            B.tt("dve", xd.v(hq), xs.v(hq), dt[:, c, d * 16:(d + 1) * 16].v(bc3(64)), ALU.mult)
            dms.append(dm)
            xds.append(xd)

        def yoff(d, g):
            fk = yfk.next()
            B.mm(pb[fk], bt[:, 2 + g, :], Hc[:, d, g * 512:(g + 1) * 512])
            dst = (yoc if d == 0 else yo2)[:, g * 512:(g + 1) * 512]
            eai = est[:, c, d, d * 16 + g * 8:d * 16 + (g + 1) * 8]
            B.tt("dve", dst.v(hq), pb[fk].v(hq), eai.v(bc3(64)), ALU.mult)

        items = [(d, q) for d in range(2) for q in range(4)]
        yq = [(0, 0), (0, 1), (1, 0), (1, 1)]
        LAq = 2
        pendq = []
        for step in range(len(items) + LAq):
            if step < len(items):
                d, q = items[step]
                g = q // 2
                sk = sgk.next()
                B.mm(pb[sk], tri[:, 2 + d, :], dms[d][:, q * 4:(q + 1) * 4, :].v(lambda a: a.rearrange("p h l -> p (h l)")))
                lx, _ = Lx.next()
                B.act(lx.v(lambda a: a.rearrange("p h l -> p (h l)")), pb[sk], AF.Exp)
                mt, _ = Mt.next()
                B.tt("dve", mt, lx, cm[:, d, g, :].v(lambda a: a.unsqueeze(1).broadcast_to([128, 4, 128])), ALU.mult)
                pendq.append((d, q, mt))
                if step % 2 == 1:
                    yoff(*yq.pop(0))
            if step >= LAq:
                d, q, mt = pendq.pop(0)
                g = q // 2
                for hh in range(4):
                    hd = q * 4 + hh
                    B.mm(pb[ydk[g]][:, (hd % 8) * 64:(hd % 8 + 1) * 64], mt[:, hh, :], xds[d][:, hd * 64:(hd + 1) * 64],
                         start=(d == 0 and hd % 8 == 0), stop=(d == 1 and hd % 8 == 7), skip=True)
        B.tt("dve", yoc, yoc, yo2, ALU.add)
        for g in range(2):
            B.tt("dve", yoc[:, g * 512:(g + 1) * 512], yoc[:, g * 512:(g + 1) * 512], pb[ydk[g]], ALU.add)
        d3m[c] = (yoc, t3, sz)

    def d3_back(c):
        rows = slice(c * 128, (c + 1) * 128)
        yoc, t3, sz = d3m.pop(c)
        B.tt("dve", yoc, yoc, t3, ALU.add)
        B.tt("dve", yoc, yoc, sz, ALU.mult)
        st, _ = ss.next()
        for g in range(2):
            B.act(sqj, yoc[:, g * 512:(g + 1) * 512], AF.Square, accum=st[:, g:g + 1])
        B.rsqrt_act(st[:, 2:4], st[:, 0:2], 1.0 / 512, EPS)
        ynt, _ = yn.next()
        for g in range(2):
            B.stt("dve", ynt[:, g * 512:(g + 1) * 512], yoc[:, g * 512:(g + 1) * 512],
                  st[:, 2 + g:3 + g], nrmw[:, g * 512:(g + 1) * 512], ALU.mult, ALU.mult)
        bk = 1
        for k in range(8):
            B.tr(pbb[bk][:, k * 128:(k + 1) * 128], ynt[:, k * 128:(k + 1) * 128], identb)
        sg_, sgd = stg.next()
        B.copy("act", sg_, pbb[bk].v(lambda a: a.rearrange("p (k t) -> p k t", k=8)))
        B.store("sp", S["br_ssd"][:, rows].rearrange("(k p) t -> p k t", p=128), sg_, sgd)

    d3_load(0)
    for c in range(NT + 1):
        if c + 1 < NT:
            d3_load(c + 1)
        if c < NT:
            d3_front(c)
        if c >= 1:
            d3_back(c - 1)


def prefetch_loop(n, depth, load, compute):
    for i in range(n + depth):
        if i < n:
            load(i)
        if i >= depth:
            compute(i - depth)


class _Cycle:
    def __init__(self, items):
        self.items = items
        self.i = -1

    def next(self):
        self.i = (self.i + 1) % len(self.items)
        return self.items[self.i]


def host_consts(T, link):
    SEG = T // 2
    seqlen = T if link else SEG
    pos = np.arange(T) if link else np.concatenate([np.arange(SEG), np.arange(SEG)])
    out = {}

    def tables(rot_dim):
        half = rot_dim // 2
        inv = np.power(np.float32(500000.0), -np.arange(half, dtype=np.float32) * np.float32(2.0) / np.float32(rot_dim)).astype(np.float32)
        ang = pos.astype(np.float32)[:, None] * inv[None, :]
        c = np.cos(ang).astype(np.float32).T
        s = np.sin(ang).astype(np.float32).T
        return np.stack([np.concatenate([c, c], 0), np.concatenate([s, s], 0)], 0).astype(np.float32)

    out["c_cs64"] = np.ascontiguousarray(tables(64))
    out["c_cs32"] = np.ascontiguousarray(tables(32))
    rc = np.zeros((4, T), np.float32)
    for i, w in enumerate((2, 4, 8, 16)):
        lo = w // 2
        hi = w - 1 - lo
        start = np.clip(pos - lo, 0, seqlen)
        end = np.clip(pos + hi + 1, 0, seqlen)
        rc[i] = 1.0 / (end - start).astype(np.float32)
    out["c_rcnt"] = rc
    t = np.arange(128)[:, None]
    l_ = np.arange(128)[None, :]
    out["c_tri"] = np.stack([(t <= l_), (t >= l_), (t > l_), (t < l_), np.ones((128, 128), bool)], 0).astype(np.float32)

    def rot(n):
        h = n // 2
        m = np.zeros((n, n), np.float32)
        for i in range(h):
            m[i + h, i] = -1.0
            m[i, i + h] = 1.0
        return m

    out["c_rot64"] = rot(64)
    out["c_rot32"] = rot(32)
    out["c_ident"] = np.eye(128, dtype=np.float32)
    out["c_link"] = np.full((128, 1), 1.0 if link else 0.0, np.float32)
    out["c_cbias"] = np.full((128, 1), 0.0 if link else NEG, np.float32)
    return out


_CACHE = {}


def kernel(**inputs):
    T = 4096
    xp = np.asarray(inputs["x_prompt"], dtype=np.float32)
    xs = np.asarray(inputs["x_sample"], dtype=np.float32)
    slots = []
    for c in range(8):
        if c < 2:
            slots.append((np.ascontiguousarray(xs[c]), 1))
        elif c < 6:
            i = (c - 2) * 2
            slots.append((np.ascontiguousarray(np.concatenate([xp[i], xp[i + 1]], axis=0)), 0))
        else:
            slots.append((np.zeros((T, D), np.float32), 0))
    if T not in _CACHE:
        _CACHE[T] = build(T)
    B = _CACHE[T]
    wts = {n: np.ascontiguousarray(np.asarray(inputs[n], dtype=np.float32)) for n, _ in W_NAMES}
    hc = {1: host_consts(T, 1), 0: host_consts(T, 0)}
    in_maps = []
    for x, link in slots:
        m = {"x": x}
        m.update(wts)
        m.update(hc[link])
        in_maps.append(m)
    res = run_bass_kernel_spmd(B.nc, in_maps, core_ids=list(range(8)))
    ys = [np.asarray(r["y"], dtype=np.float32) for r in res.results]
    y_sample = np.stack([ys[0], ys[1]], axis=0)
    yp = []
    for c in range(2, 6):
        yp.append(ys[c][:2048])
        yp.append(ys[c][2048:])
    y_prompt = np.stack(yp, axis=0)
    return (y_prompt, y_sample)
```
